# Optimizing a Trainium2 kernel written in Bass

```python
import math
import jax, jax.numpy as jnp
from jax import lax
import numpy as np

D_MODEL = 1024
BATCH = 8
SEQ = 2048
DEPTH = 2

CHUNK = 64
N_MEM = 256
D_MIX = D_MODEL
RET_HEADS = 4
RET_HD = 128
RET_WIDTH = RET_HEADS * RET_HD
SB_HEADS = 8
SB_HD = 64
SB_WIDTH = SB_HEADS * SB_HD
IN_COLS = 4 * RET_WIDTH + 3 * SB_WIDTH
X_HEADS = 4
X_HD = D_MODEL // X_HEADS
D_FF = 4 * D_MODEL
SB_BLOCK = 128
EPS = 1e-6
ROPE_BASE = 10000.0

kernel_name = "hymba_retention_stickbreaking_block"


def _rmsnorm(x, g):
    xf = x.astype(jnp.float32)
    y = xf * lax.rsqrt(jnp.mean(xf * xf, axis=-1, keepdims=True) + EPS)
    return (y * g.astype(jnp.float32)).astype(x.dtype)


def _rotary(x):
    s, hd = x.shape[1], x.shape[-1]
    half = hd // 2
    inv = 1.0 / (ROPE_BASE ** jnp.linspace(0.0, 1.0, half, dtype=jnp.float32))
    ang = jnp.arange(s, dtype=jnp.float32)[:, None] * inv[None, :]
    cos = jnp.cos(ang)[None, :, None, :]
    sin = jnp.sin(ang)[None, :, None, :]
    xf = x.astype(jnp.float32)
    x1, x2 = xf[..., :half], xf[..., half:]
    return jnp.concatenate([x1 * cos - x2 * sin, x1 * sin + x2 * cos], axis=-1).astype(x.dtype)


def _retention(q, k, v):
    b, s, h, d = q.shape
    nc = s // CHUNK
    f32 = jnp.float32
    log_gamma = jnp.log1p(-jnp.exp2(-5.0 - jnp.arange(h, dtype=f32)))
    qf = q.astype(f32).reshape(b, nc, CHUNK, h, d)
    kf = k.astype(f32).reshape(b, nc, CHUNK, h, d) * (d ** -0.5)
    vf = v.astype(f32).reshape(b, nc, CHUNK, h, d)
    idx = jnp.arange(CHUNK, dtype=f32)
    dist = jnp.abs(idx[:, None] - idx[None, :])
    dmask = jnp.exp(log_gamma[:, None, None] * dist)
    scores = jnp.einsum('bnihd,bnjhd->bnhij', qf, kf) * dmask
    o_intra = jnp.einsum('bnhij,bnjhe->bnihe', scores, vf)
    k_dec = jnp.exp(log_gamma[None, :] * (CHUNK - 1.0 - idx)[:, None])
    u = jnp.einsum('bnjhd,bnjhe->nbhde', kf * k_dec[:, :, None], vf)
    chunk_decay = jnp.exp(log_gamma * CHUNK)[None, :, None, None]

    def step(state, u_c):
        return state * chunk_decay + u_c, state

    _, s_before = lax.scan(step, jnp.zeros_like(u[0]), u)
    q_dec = jnp.exp(log_gamma[None, :] * (idx + 1.0)[:, None])
    o_inter = jnp.einsum('bnihd,nbhde->bnihe', qf * q_dec[:, :, None], s_before)
    return (o_intra + o_inter).reshape(b, s, h, d)


def _stick_breaking(q, k, v):
    b, s, h, d = q.shape
    scale = d ** -0.5
    outs = []
    for blk in range(s // SB_BLOCK):
        q0 = blk * SB_BLOCK
        end = q0 + SB_BLOCK
        qb = q[:, q0:end].astype(jnp.float32)
        kb = k[:, :end].astype(jnp.float32)
        vb = v[:, :end].astype(jnp.float32)
        z = jnp.einsum('bthd,bshd->bhts', qb, kb) * scale
        t_idx = q0 + jnp.arange(SB_BLOCK)[:, None]
        s_idx = jnp.arange(end)[None, :]
        strict = s_idx < t_idx
        log_beta = jax.nn.log_sigmoid(z)
        log_1m = jnp.where(strict, jax.nn.log_sigmoid(-z), 0.0)
        after = lax.cumsum(log_1m, axis=3, reverse=True) - log_1m
        a = jnp.where(strict, jnp.exp(log_beta + after), 0.0)
        outs.append(jnp.einsum('bhts,bshd->bthd', a, vb))
    return jnp.concatenate(outs, axis=1)


def setup_inputs(seed: int = 0) -> dict:
    key = jax.random.key(seed)
    ks = jax.random.split(key, 20)
    f32 = jnp.float32

    def w(k, shape, fan_in, gain=1.0):
        return jax.random.normal(k, shape, f32) * (gain * fan_in ** -0.5)

    def g(k, shape):
        return 1.0 + 0.02 * jax.random.normal(k, shape, f32)

    return {
        "x": jax.random.normal(ks[0], (BATCH, SEQ, D_MODEL), f32),
        "mem": jax.random.normal(ks[1], (BATCH, N_MEM, D_MODEL), f32),
        "g_mix": g(ks[2], (DEPTH, D_MODEL)),
        "w_in": w(ks[3], (DEPTH, D_MODEL, IN_COLS), D_MODEL),
        "g_ret_out": g(ks[4], (DEPTH, RET_HEADS, RET_HD)),
        "g_sb_out": g(ks[5], (DEPTH, SB_WIDTH)),
        "w_mix_out": w(ks[6], (DEPTH, D_MIX, D_MODEL), D_MIX, 0.5),
        "g_cross": g(ks[7], (DEPTH, D_MODEL)),
        "g_mem": g(ks[8], (DEPTH, D_MODEL)),
        "w_xq": w(ks[9], (DEPTH, D_MODEL, D_MODEL), D_MODEL),
        "w_xkv": w(ks[10], (DEPTH, D_MODEL, 2 * D_MODEL), D_MODEL),
        "g_qn": g(ks[11], (DEPTH, X_HD)),
        "g_kn": g(ks[12], (DEPTH, X_HD)),
        "w_xo": w(ks[13], (DEPTH, D_MODEL, D_MODEL), D_MODEL, 0.5),
        "g_mlp": g(ks[14], (DEPTH, D_MODEL)),
        "w_up": w(ks[15], (DEPTH, D_MODEL, D_FF), D_MODEL),
        "w_down": w(ks[16], (DEPTH, D_FF, D_MODEL), D_FF, 0.5),
    }


def reference(x, mem, g_mix, w_in, g_ret_out, g_sb_out, w_mix_out, g_cross, g_mem,
              w_xq, w_xkv, g_qn, g_kn, w_xo, g_mlp, w_up, w_down):
    b, s, _ = x.shape
    m = mem.shape[1]
    splits = [RET_WIDTH, 2 * RET_WIDTH, 3 * RET_WIDTH, 4 * RET_WIDTH,
              4 * RET_WIDTH + SB_WIDTH, 4 * RET_WIDTH + 2 * SB_WIDTH]
    for l in range(DEPTH):
        h = _rmsnorm(x, g_mix[l])
        proj = h @ w_in[l]
        rq, rk, rv, rg, sq, sk, sv = jnp.split(proj, splits, axis=-1)
        rq = _rotary(rq.reshape(b, s, RET_HEADS, RET_HD))
        rk = _rotary(rk.reshape(b, s, RET_HEADS, RET_HD))
        rv = rv.reshape(b, s, RET_HEADS, RET_HD)
        o_ret = _rmsnorm(_retention(rq, rk, rv), g_ret_out[l])
        o_ret = o_ret.reshape(b, s, RET_WIDTH).astype(x.dtype) * jax.nn.silu(rg)
        o_sb = _stick_breaking(sq.reshape(b, s, SB_HEADS, SB_HD),
                               sk.reshape(b, s, SB_HEADS, SB_HD),
                               sv.reshape(b, s, SB_HEADS, SB_HD))
        o_sb = _rmsnorm(o_sb.reshape(b, s, SB_WIDTH), g_sb_out[l]).astype(x.dtype)
        x = x + jnp.concatenate([o_ret, o_sb], axis=-1) @ w_mix_out[l]

        hq = _rmsnorm(x, g_cross[l])
        mn = _rmsnorm(mem, g_mem[l])
        q = _rmsnorm((hq @ w_xq[l]).reshape(b, s, X_HEADS, X_HD), g_qn[l])
        kv = mn @ w_xkv[l]
        k, v = jnp.split(kv, 2, axis=-1)
        k = _rmsnorm(k.reshape(b, m, X_HEADS, X_HD), g_kn[l])
        v = v.reshape(b, m, X_HEADS, X_HD)
        sc = jnp.einsum('bthd,bmhd->bhtm', q, k).astype(jnp.float32) * (X_HD ** -0.5)
        p = jax.nn.softmax(sc, axis=-1).astype(v.dtype)
        o = jnp.einsum('bhtm,bmhd->bthd', p, v).reshape(b, s, D_MODEL)
        x = x + o @ w_xo[l]

        hm = _rmsnorm(x, g_mlp[l])
        x = x + jnp.square(jax.nn.relu(hm @ w_up[l])) @ w_down[l]
    return x
```

```python
import contextlib
import numpy as np
import concourse.bass as bass
import concourse.mybir as mybir
from concourse.bass_utils import run_bass_kernel_spmd

dt = mybir.dt
F32, BF16 = dt.float32, dt.bfloat16
AF = mybir.ActivationFunctionType
ALU = mybir.AluOpType
AX = mybir.AxisListType

ENGS = ('pe', 'act', 'dve', 'pool', 'sp')


class Sched:
    EPOCH = 20000
    NDMA = 8

    def __init__(self, nc):
        self.nc = nc
        self.stack = contextlib.ExitStack()
        self.ops = {e: [] for e in ENGS}
        self.ncomp = {e: 0 for e in ENGS}
        self.esem = {e: {} for e in ENGS}
        self.sems = []
        self.lastw = {}
        self.readers = {}
        self.seen = {e: {} for e in ENGS}
        self.dsem = []
        self.dnext = 0
        self.finals = []

    def __enter__(self):
        self.stack.__enter__()
        return self

    def __exit__(self, *a):
        return self.stack.__exit__(*a)

    def sbuf(self, name, shape, dtype):
        return self.stack.enter_context(self.nc.sbuf_tensor(name, list(shape), dtype))

    def psum(self, name, shape, dtype):
        return self.stack.enter_context(self.nc.psum_tensor(name, list(shape), dtype))

    def _newsem(self, name):
        s = self.stack.enter_context(self.nc.semaphore(name))
        self.sems.append(s)
        return len(self.sems) - 1

    def _deps(self, eng, reads, writes, is_dma):
        deps = set()
        for k in reads:
            t = self.lastw.get(k)
            if t is not None:
                deps.add(t)
        for k in writes:
            t = self.lastw.get(k)
            if t is not None:
                deps.add(t)
            for t in self.readers.get(k, ()):
                if t[2] == eng and not is_dma:
                    continue
                deps.add(t)
        return deps

    def _waits(self, eng, deps, is_dma):
        waits = []
        seen = self.seen[eng]
        for (sid, v, deng) in sorted(deps):
            if deng == eng and eng == 'pe' and not is_dma:
                continue
            if seen.get(sid, 0) >= v:
                continue
            seen[sid] = v
            waits.append((sid, v))
        return waits

    def _commit(self, tok, reads, writes):
        for k in reads:
            self.readers.setdefault(k, []).append(tok)
        for k in writes:
            self.lastw[k] = tok
            self.readers[k] = []

    def op(self, eng, fn, reads=(), writes=()):
        deps = self._deps(eng, reads, writes, False)
        waits = self._waits(eng, deps, False)
        idx = self.ncomp[eng]
        ep = idx // self.EPOCH
        if ep not in self.esem[eng]:
            self.esem[eng][ep] = self._newsem(f"s_{eng}_{ep}")
        sid = self.esem[eng][ep]
        tok = (sid, idx % self.EPOCH + 1, eng)
        self.ncomp[eng] += 1
        self._commit(tok, reads, writes)
        self.ops[eng].append((fn, waits, sid, 1))
        return tok

    def dma(self, eng, fn, reads=(), writes=(), final=False):
        if eng == 'pool':
            return self._dma_sw(fn, reads, writes)
        deps = self._deps(eng, reads, writes, True)
        if len(self.dsem) < self.NDMA:
            self.dsem.append([self._newsem(f"s_dma_{len(self.dsem)}"), 0, None])
        ent = self.dsem[self.dnext % self.NDMA]
        self.dnext += 1
        if ent[2] is not None:
            deps.add(ent[2])
        waits = self._waits(eng, deps, True)
        ent[1] += 16
        tok = (ent[0], ent[1], 'dma')
        ent[2] = tok
        self._commit(tok, reads, writes)
        self.ops[eng].append((fn, waits, ent[0], 16))
        if final:
            self.finals.append(tok)
        return tok

    def _dma_sw(self, fn, reads, writes):
        deps = self._deps('pool', reads, writes, True)
        waits = self._waits('pool', deps, True)
        if not hasattr(self, 'nsw'):
            self.nsw = 0
        sid = self._newsem(f"s_sw_{self.nsw}")
        self.nsw += 1
        tok = (sid, 16, 'dma')
        self._commit(tok, reads, writes)
        self.ops['pool'].append((fn, waits, sid, 16))
        return tok

    def emit(self):
        nc = self.nc
        fin = [(sid, v) for (sid, v, _) in self.finals]
        sems = self.sems
        ops = self.ops

        def replay(name, e, tail=()):
            for ent in ops[name]:
                if ent[0] == 'relay':
                    e.wait_ge(sems[ent[1]], 16)
                    e.sem_clear(sems[ent[1]])
                    e.sem_inc(sems[ent[2]], 1)
                    continue
                (fn, waits, sid, n) = ent
                for (ws, wv) in waits:
                    e.wait_ge(sems[ws], wv)
                fn(e).then_inc(sems[sid], n)
            for (ws, wv) in tail:
                e.wait_ge(sems[ws], wv)

        with nc.Block() as block:
            @block.sync
            def _(e):
                replay('sp', e, fin)

            @block.tensor
            def _(e):
                replay('pe', e)

            @block.vector
            def _(e):
                replay('dve', e)

            @block.scalar
            def _(e):
                replay('act', e)

            @block.gpsimd
            def _(e):
                replay('pool', e)


SEQ, DM, DEPTH = 2048, 1024, 2
NT = 16
EPS = 1e-6
NPC = 44
GAM = [1.0 - 2.0 ** (-5 - h) for h in range(4)]


def _const_tables():
    p = np.arange(128)
    c = {}
    c["ident"] = np.eye(128, dtype=np.float32)
    c["tri"] = (p[:, None] >= p[None, :]).astype(np.float32)
    c["ones"] = np.ones((128, 128), np.float32)
    c["mstr"] = (p[:, None] < p[None, :]).astype(np.float32)
    inv = (1.0 / (np.float32(10000.0) ** np.linspace(0.0, 1.0, 64, dtype=np.float32))).astype(np.float32)
    pos = np.arange(SEQ, dtype=np.float32)
    ang = (pos[:, None] * inv[None, :]).astype(np.float32)
    c["cos"] = np.cos(ang.astype(np.float64)).astype(np.float32).reshape(NT, 128, 64).transpose(1, 0, 2).copy()
    c["sin"] = np.sin(ang.astype(np.float64)).astype(np.float32).reshape(NT, 128, 64).transpose(1, 0, 2).copy()
    lg = np.log1p(-np.exp2(-5.0 - np.arange(4, dtype=np.float64)))
    dist = np.abs(p[:, None] - p[None, :]).astype(np.float64)
    same = (p[:, None] // 64) == (p[None, :] // 64)
    dm = np.zeros((128, 4, 128), np.float64)
    qd = np.zeros((128, 4, 128), np.float64)
    kd = np.zeros((128, 4, 128), np.float64)
    for h in range(4):
        dm[:, h, :] = np.where(same, np.exp(lg[h] * dist), 0.0) * (128.0 ** -0.5)
        qd[:, h, :] = np.exp(lg[h] * ((p % 64) + 1.0))[None, :]
        kd[:, h, :] = (np.exp(lg[h] * (63.0 - (p % 64))) * (128.0 ** -0.5))[:, None]
    c["dmk"] = dm.reshape(128, 512).astype(np.float32)
    c["qdc"] = qd.reshape(128, 512).astype(np.float32)
    c["kdc"] = kd.reshape(128, 512).astype(np.float32)
    return c


def _param_table(g_mix, g_cross, g_mem, g_mlp, g_ret_out, g_sb_out, g_qn, g_kn):
    pt = np.zeros((128, DEPTH, NPC), np.float32)
    for l in range(DEPTH):
        pt[:, l, 0:8] = g_mix[l].reshape(8, 128).T
        pt[:, l, 8:16] = g_cross[l].reshape(8, 128).T
        pt[:, l, 16:24] = g_mem[l].reshape(8, 128).T
        pt[:, l, 24:32] = g_mlp[l].reshape(8, 128).T
        pt[:, l, 32:36] = g_ret_out[l].reshape(4, 128).T
        pt[:, l, 36:40] = g_sb_out[l].reshape(4, 128).T
        pt[:, l, 40:42] = g_qn[l].reshape(2, 128).T
        pt[:, l, 42:44] = g_kn[l].reshape(2, 128).T
    return pt


def _bv(ap, dims):
    return bass.AP(ap.tensor, ap.offset, [list(ap.ap[0])] + [list(d) for d in dims])


def build_program(n_layers=DEPTH, stop=None):
    nc = bass.Bass("TRN2", target_bir_lowering=False)
    D = {}
    def din(name, shape):
        D[name] = nc.dram_tensor(name, list(shape), F32, kind="ExternalInput").ap()
    din("x", [SEQ, DM]); din("mem", [256, DM])
    din("w_in", [DEPTH, DM, 3584]); din("w_mix_out", [DEPTH, DM, DM]); din("w_xq", [DEPTH, DM, DM])
    din("w_xkv", [DEPTH, DM, 2 * DM]); din("w_xo", [DEPTH, DM, DM]); din("w_up", [DEPTH, DM, 4 * DM])
    din("w_down", [DEPTH, 4 * DM, DM]); din("pt", [128, DEPTH, NPC])
    for nm in ("ident", "tri", "ones", "mstr"):
        din(nm, [128, 128])
    din("cos", [128, NT, 64]); din("sin", [128, NT, 64])
    for nm in ("dmk", "qdc", "kdc"):
        din(nm, [128, 512])
    out_d = nc.dram_tensor("out", [SEQ, DM], F32, kind="ExternalOutput").ap()

    S = Sched(nc)
    with S:
        xs = S.sbuf("xs", [128, NT, DM], F32)
        hT = S.sbuf("hT", [128, 8, SEQ], BF16)
        cT = S.sbuf("cT", [128, 8, SEQ], BF16)
        WA = S.sbuf("WA", [128, 8, 4096], BF16)
        ident = S.sbuf("ident_s", [128, 128], BF16)
        tri = S.sbuf("tri_s", [128, 128], BF16)
        ones = S.sbuf("ones_s", [128, 128], BF16)
        mstr = S.sbuf("mstr_s", [128, 128], BF16)
        zer = S.sbuf("zer_s", [128, 128], BF16)
        cosT = S.sbuf("cos_s", [128, NT, 64], BF16)
        sinT = S.sbuf("sin_s", [128, NT, 64], BF16)
        dmk = S.sbuf("dmk_s", [128, 512], BF16)
        qdc = S.sbuf("qdc_s", [128, 512], BF16)
        kdc = S.sbuf("kdc_s", [128, 512], BF16)
        pt = S.sbuf("pt_s", [128, DEPTH, NPC], F32)
        st = S.sbuf("st_s", [128, 4, 128], F32)
        stb = S.sbuf("stb_s", [128, 2, 4, 128], BF16)
        sm = S.sbuf("sm_s", [128, 128], F32)
        PS = S.psum("ps", [128, 8, 512], F32)
        ps = [PS[:, i, :] for i in range(8)]
        psb = [p.bitcast(BF16) for p in ps]

        rr = {"dq": 0, "bank": 0, "ring": 0, "ev": 0}

        def dq():
            rr["dq"] += 1
            return 'sp' if rr["dq"] % 2 else 'act'

        def PK(b):
            return ('ps', b)

        def unit(buf, name, u, n=1, dtype=BF16):
            kc, g = u // 4, u % 4
            assert g + n <= 4
            ap = buf[:, kc, g * 512:(g + n) * 512]
            if dtype == F32:
                ap = ap.bitcast(F32)
            return ap, [(name, kc, g + i) for i in range(n)]

        def tunit(i, n=1, dtype=BF16):
            ap = WA[:, 3, i * 512:(i + n) * 512]
            if dtype == F32:
                ap = ap.bitcast(F32)
            return ap, [('t', i + k) for k in range(n)]

        def nextbank(allowed):
            rr["bank"] += 1
            return allowed[rr["bank"] % len(allowed)]

        ring = {"slots": [0, 1, 2]}

        def load_piece(w2d, nk, ncols):
            assert nk * ncols <= 4096
            rr["ring"] += 1
            s = ring["slots"][rr["ring"] % len(ring["slots"])]
            dst = WA[:, s, 0:nk * ncols].rearrange("p (k n) -> p k n", k=nk)
            src = w2d.rearrange("(k p) n -> p k n", p=128)
            S.dma('pool', lambda e: e.dma_start(out=dst, in_=src), reads=[], writes=[('hs', s)])
            return dst, ('hs', s)

        def evac_engine():
            rr["ev"] += 1
            return 'act' if rr["ev"] % 2 else 'dve'

        def copy_op(eng, out, in_, reads, writes):
            if eng == 'act':
                S.op('act', lambda e: e.copy(out=out, in_=in_), reads=reads, writes=writes)
            else:
                S.op(eng, lambda e: e.tensor_copy(out=out, in_=in_), reads=reads, writes=writes)

        def rstd_from_ssq(ssq_ap, inv_n, key):
            S.op('dve', lambda e: e.tensor_scalar(out=ssq_ap, in0=ssq_ap, scalar1=inv_n, scalar2=EPS,
                                                  op0=ALU.mult, op1=ALU.add), reads=[key], writes=[key])
            S.op('act', lambda e: e.activation(out=ssq_ap, in_=ssq_ap, func=AF.Sqrt), reads=[key], writes=[key])
            S.op('dve', lambda e: e.reciprocal(out=ssq_ap, in_=ssq_ap), reads=[key], writes=[key])

        def transposes(src_fn, n, bank, reads):
            for c in range(n):
                S.op('pe', lambda e, c=c: e.transpose(out=psb[bank][:, c * 128:(c + 1) * 128], in_=src_fn(c),
                                                      identity=ident[:]),
                     reads=reads, writes=[PK(bank)])

        for tt in range(NT):
            S.dma('sp', lambda e, tt=tt: e.dma_start(out=xs[:, tt, :], in_=D["x"][tt * 128:(tt + 1) * 128, :]),
                  writes=[('xs', tt)])
        for (sb_t, nm) in ((ident, "ident"), (tri, "tri"), (ones, "ones"), (mstr, "mstr"), (cosT, "cos"),
                           (sinT, "sin"), (dmk, "dmk"), (qdc, "qdc"), (kdc, "kdc")):
            S.dma('pool', lambda e, sb_t=sb_t, nm=nm: e.dma_start(out=sb_t[:], in_=D[nm]), writes=['const'])
        S.dma('sp', lambda e: e.dma_start(out=pt[:], in_=D["pt"]), writes=['const'])
        S.op('pool', lambda e: e.memset(zer[:], 0.0), writes=['const'])

        ALLB = list(range(8))

        def norm_to_hT(l, gcol):
            junk, jk = tunit(0, 2)
            for tt in range(NT):
                S.op('act', lambda e, tt=tt: e.activation(out=junk, in_=xs[:, tt, :], func=AF.Square,
                                                          accum_out=sm[:, tt:tt + 1]),
                     reads=[('xs', tt)], writes=['sm'] + jk)
            rstd_from_ssq(sm[:, 0:NT], 1.0 / DM, 'sm')
            for tt in range(NT):
                hn, hk = tunit(2 + 2 * (tt % 2), 2)
                S.op('dve', lambda e, tt=tt, hn=hn: e.tensor_scalar(out=hn, in0=xs[:, tt, :], scalar1=sm[:, tt:tt + 1],
                                                                    scalar2=None, op0=ALU.mult),
                     reads=[('xs', tt), 'sm'], writes=hk)
                b = nextbank(ALLB)
                transposes(lambda c, hn=hn: hn[:, c * 128:(c + 1) * 128], 8, b, hk + ['const'])
                gsl = pt[:, l, gcol:gcol + 8]
                gb = _bv(gsl, [[1, 8], [0, 128]])
                S.op('dve', lambda e, tt=tt, b=b, gb=gb: e.tensor_tensor(
                    out=hT[:, :, tt * 128:(tt + 1) * 128],
                    in0=psb[b][:, :].rearrange("p (k n) -> p k n", k=8), in1=gb, op=ALU.mult),
                     reads=[PK(b), 'const'], writes=[('hT', kc, tt // 4) for kc in range(8)])

        def gemm_tm(lhs_fn, lhs_keys_fn, pieces_fn, n_slices, epilogue, banks=ALLB, ntt=NT):
            for ns in range(n_slices):
                pcs = [load_piece(w2d, nk, 512) for (w2d, nk) in pieces_fn(ns)]
                for tt in range(ntt):
                    b = nextbank(banks)
                    tot = sum(p[0].shape[1] for p in pcs)
                    i = 0
                    for pi, (wap, wkey) in enumerate(pcs):
                        for k in range(wap.shape[1]):
                            kc = sum(p[0].shape[1] for p in pcs[:pi]) + k
                            S.op('pe', lambda e, wap=wap, k=k, kc=kc, i=i, b=b, tt=tt: e.matmul(
                                ps[b][:, :], lhsT=lhs_fn(kc, tt), rhs=wap[:, k, :], start=(i == 0), stop=(i == tot - 1)),
                                 reads=[wkey] + lhs_keys_fn(kc, tt), writes=[PK(b)])
                            i += 1
                    epilogue(tt, ns, b)

        def resid_add(tt, ns, b):
            S.op('dve', lambda e: e.tensor_tensor(out=xs[:, tt, ns * 512:(ns + 1) * 512], in0=ps[b][:, :],
                                                  in1=xs[:, tt, ns * 512:(ns + 1) * 512], op=ALU.add),
                 reads=[PK(b), ('xs', tt)], writes=[('xs', tt)])

        def mix_block(l):
            w_in = D["w_in"][l]
            norm_to_hT(l, 0)
            for j in range(4):
                dst = WA[:, 4 + j, :].rearrange("p (k n) -> p k n", k=8)
                src = w_in[:, j * 512:(j + 1) * 512].rearrange("(k p) n -> p k n", p=128)
                S.dma('pool', lambda e, dst=dst, src=src: e.dma_start(out=dst, in_=src), writes=[('hs', 4 + j)])
            S.op('pool', lambda e: e.memset(st[:], 0.0), writes=['st'])
            S.op('pool', lambda e: e.memset(stb[:], 0.0), writes=['stb0', 'stb1'])
            def cu(i, n=1, dtype=BF16):
                return unit(cT, 'cT', 16 + i, n, dtype)
            def ret_tile(tt):
                pb = [nextbank(ALLB) for _ in range(4)]
                for j in range(4):
                    wv = WA[:, 4 + j, :].rearrange("p (k n) -> p k n", k=8)
                    for kc in range(8):
                        S.op('pe', lambda e, j=j, kc=kc, wv=wv, tt=tt: e.matmul(
                            ps[pb[j]][:, :], lhsT=hT[:, kc, tt * 128:(tt + 1) * 128], rhs=wv[:, kc, :],
                            start=(kc == 0), stop=(kc == 7)),
                             reads=[('hs', 4 + j), ('hT', kc, tt // 4)], writes=[PK(pb[j])])
                A, Ak = cu(0, 2, F32)
                Bm, Bk = cu(2, 2, F32)
                qr, qrk = cu(4); kr, krk = cu(5); kdt, kdk = cu(6); vb, vbk = cu(7); sg, sgk = cu(8)
                qTt, qTk = cu(9); qdT, qdk = cu(10); kTt, kTk = cu(11); STm, STk = cu(12); on, onk = cu(13)
                cosb = _bv(cosT[:, tt, :], [[0, 4], [0, 2], [1, 64]])
                sinb = _bv(sinT[:, tt, :], [[0, 4], [1, 64]])
                def rot(bank, dstt, dk):
                    P4 = ps[bank][:, :].rearrange("p (h x d) -> p h x d", h=4, x=2)
                    A4 = A.rearrange("p (h x d) -> p h x d", h=4, x=2)
                    B4 = Bm.rearrange("p (x h d) -> p x h d", x=2, h=4)
                    D4 = dstt.rearrange("p (h x d) -> p h x d", h=4, x=2)
                    S.op('dve', lambda e, P4=P4, A4=A4: e.tensor_tensor(out=A4, in0=P4, in1=cosb, op=ALU.mult),
                         reads=[PK(bank), 'const'], writes=Ak)
                    S.op('dve', lambda e, P4=P4, B4=B4: e.tensor_tensor(out=B4[:, 0], in0=P4[:, :, 1, :], in1=sinb, op=ALU.mult),
                         reads=[PK(bank), 'const'], writes=[Bk[0]])
                    S.op('dve', lambda e, P4=P4, B4=B4: e.tensor_tensor(out=B4[:, 1], in0=P4[:, :, 0, :], in1=sinb, op=ALU.mult),
                         reads=[PK(bank), 'const'], writes=[Bk[1]])
                    S.op('pool', lambda e, A4=A4, B4=B4, D4=D4: e.tensor_tensor(out=D4[:, :, 0, :], in0=A4[:, :, 0, :], in1=B4[:, 0], op=ALU.subtract),
                         reads=Ak + Bk, writes=dk)
                    S.op('pool', lambda e, A4=A4, B4=B4, D4=D4: e.tensor_tensor(out=D4[:, :, 1, :], in0=A4[:, :, 1, :], in1=B4[:, 1], op=ALU.add),
                         reads=Ak + Bk, writes=dk)
                rot(pb[0], qr, qrk)
                rot(pb[1], kr, krk)
                S.op('pool', lambda e: e.tensor_tensor(out=kdt, in0=kr, in1=kdc[:], op=ALU.mult), reads=krk + ['const'], writes=kdk)
                S.op('act', lambda e: e.copy(out=vb, in_=ps[pb[2]][:, :]), reads=[PK(pb[2])], writes=vbk)
                S.op('act', lambda e: e.activation(out=sg, in_=ps[pb[3]][:, :], func=AF.Silu), reads=[PK(pb[3])], writes=sgk)
                bq = nextbank(ALLB)
                transposes(lambda c: qr[:, c * 128:(c + 1) * 128], 4, bq, qrk + ['const'])
                S.op('act', lambda e: e.copy(out=qTt, in_=psb[bq][:, 0:512]), reads=[PK(bq)], writes=qTk)
                S.op('dve', lambda e: e.tensor_tensor(out=qdT, in0=psb[bq][:, 0:512], in1=qdc[:], op=ALU.mult),
                     reads=[PK(bq), 'const'], writes=qdk)
                bk = nextbank(ALLB)
                transposes(lambda c: kr[:, c * 128:(c + 1) * 128], 4, bk, krk + ['const'])
                S.op('act', lambda e: e.copy(out=kTt, in_=psb[bk][:, 0:512]), reads=[PK(bk)], writes=kTk)
                bs = nextbank(ALLB)
                for h in range(4):
                    S.op('pe', lambda e, h=h: e.matmul(ps[bs][:, h * 128:(h + 1) * 128], lhsT=kTt[:, h * 128:(h + 1) * 128],
                                                       rhs=qTt[:, h * 128:(h + 1) * 128], start=True, stop=True),
                         reads=kTk + qTk, writes=[PK(bs)])
                S.op('dve', lambda e: e.tensor_tensor(out=STm, in0=ps[bs][:, :], in1=dmk[:], op=ALU.mult),
                     reads=[PK(bs), 'const'], writes=STk)
                bu = [nextbank(ALLB), nextbank(ALLB)]
                for c in range(2):
                    for h in range(4):
                        S.op('pe', lambda e, c=c, h=h: e.matmul(
                            ps[bu[c]][:, h * 128:(h + 1) * 128], lhsT=kdt[c * 64:(c + 1) * 64, h * 128:(h + 1) * 128],
                            rhs=vb[c * 64:(c + 1) * 64, h * 128:(h + 1) * 128], start=True, stop=True),
                             reads=kdk + vbk, writes=[PK(bu[c])])
                stf = st[:].rearrange("p h e -> p (h e)")
                for h in range(4):
                    S.op('dve', lambda e, h=h: e.scalar_tensor_tensor(
                        out=st[:, h, :], in0=st[:, h, :], scalar=float(GAM[h] ** 64), in1=ps[bu[0]][:, h * 128:(h + 1) * 128],
                        op0=ALU.mult, op1=ALU.add), reads=['st', PK(bu[0])], writes=['st'])
                S.op('act', lambda e: e.copy(out=stb[:, 1].rearrange("p h e -> p (h e)"), in_=stf), reads=['st'], writes=['stb1'])
                bo = nextbank(ALLB)
                for h in range(4):
                    S.op('pe', lambda e, h=h: e.matmul(ps[bo][:, h * 128:(h + 1) * 128], lhsT=STm[:, h * 128:(h + 1) * 128],
                                                       rhs=vb[:, h * 128:(h + 1) * 128], start=True, stop=False),
                         reads=STk + vbk, writes=[PK(bo)])
                    for c in range(2):
                        S.op('pe', lambda e, h=h, c=c: e.matmul(
                            ps[bo][c * 64:(c + 1) * 64, h * 128:(h + 1) * 128],
                            lhsT=qdT[:, h * 128 + c * 64:h * 128 + (c + 1) * 64], rhs=stb[:, c, h, :],
                            start=False, stop=True),
                             reads=qdk + [f'stb{c}'], writes=[PK(bo)])
                for h in range(4):
                    S.op('dve', lambda e, h=h: e.scalar_tensor_tensor(
                        out=st[:, h, :], in0=st[:, h, :], scalar=float(GAM[h] ** 64), in1=ps[bu[1]][:, h * 128:(h + 1) * 128],
                        op0=ALU.mult, op1=ALU.add), reads=['st', PK(bu[1])], writes=['st'])
                S.op('act', lambda e: e.copy(out=stb[:, 0].rearrange("p h e -> p (h e)"), in_=stf), reads=['st'], writes=['stb0'])
                junk, jk = tunit(0, 1)
                for h in range(4):
                    S.op('act', lambda e, h=h: e.activation(out=junk[:, 0:128], in_=ps[bo][:, h * 128:(h + 1) * 128], func=AF.Square,
                                                            accum_out=sm[:, 32 + h:33 + h]), reads=[PK(bo)], writes=['sm2'] + jk)
                rstd_from_ssq(sm[:, 32:36], 1.0 / 128, 'sm2')
                for h in range(4):
                    S.op('dve', lambda e, h=h: e.scalar_tensor_tensor(
                        out=on[:, h * 128:(h + 1) * 128], in0=ps[bo][:, h * 128:(h + 1) * 128], scalar=sm[:, 32 + h:33 + h],
                        in1=sg[:, h * 128:(h + 1) * 128], op0=ALU.mult, op1=ALU.mult),
                         reads=[PK(bo), 'sm2'] + sgk, writes=onk)
                bt = nextbank(ALLB)
                transposes(lambda c: on[:, c * 128:(c + 1) * 128], 4, bt, onk + ['const'])
                gb = _bv(pt[:, l, 32:36], [[1, 4], [0, 128]])
                S.op('dve', lambda e, tt=tt, bt=bt, gb=gb: e.tensor_tensor(
                    out=cT[:, 0:4, tt * 128:(tt + 1) * 128], in0=psb[bt][:, 0:512].rearrange("p (k n) -> p k n", k=4),
                    in1=gb, op=ALU.mult), reads=[PK(bt), 'const'], writes=[('cT', kc, tt // 4) for kc in range(4)])
            for tt in range(NT):
                ret_tile(tt)
            if stop == 'ret':
                return
            skT = WA[:, 4:6, :].rearrange("p a (b n) -> p (a b) n", b=2)
            svv = WA[:, 6:8, :].rearrange("p a (t n) -> p (a t) n", t=8)
            for (which, c0) in (("q", 2048), ("k", 2560)):
                wap, wkey = load_piece(w_in[:, c0:c0 + 512], 8, 512)
                for m in range(4):
                    for g in range(4):
                        b = nextbank(ALLB)
                        for kc in range(8):
                            S.op('pe', lambda e, m=m, g=g, kc=kc, b=b, wap=wap: e.matmul(
                                ps[b][:, :], lhsT=wap[:, kc, m * 128:(m + 1) * 128], rhs=hT[:, kc, g * 512:(g + 1) * 512],
                                start=(kc == 0), stop=(kc == 7)),
                                 reads=[wkey, ('hT', kc, g)], writes=[PK(b)])
                        if which == "q":
                            copy_op(evac_engine(), cT[:, 4 + m, g * 512:(g + 1) * 512], ps[b][:, :], [PK(b)], [('cT', 4 + m, g)])
                        else:
                            copy_op(evac_engine(), skT[:, m, g * 512:(g + 1) * 512], ps[b][:, :], [PK(b)], [('hs', 4 + m // 2)])
            wap, wkey = load_piece(w_in[:, 3072:3584], 8, 512)
            for tt in range(NT):
                b = nextbank(ALLB)
                for kc in range(8):
                    S.op('pe', lambda e, tt=tt, kc=kc, b=b, wap=wap: e.matmul(
                        ps[b][:, :], lhsT=hT[:, kc, tt * 128:(tt + 1) * 128], rhs=wap[:, kc, :], start=(kc == 0), stop=(kc == 7)),
                         reads=[wkey, ('hT', kc, tt // 4)], writes=[PK(b)])
                copy_op(evac_engine(), svv[:, tt, :], ps[b][:, :], [PK(b)], [('hs', 6 + tt // 8)])
            PSall = PS
            ZBK = [[0, 1], [2, 3]]
            CBK = [4, 5]
            ACC, SSQ = 6, 7

            def hrow(row, lo, hi, dtype=BF16):
                ap = hT[:, row, lo:hi]
                if dtype == F32:
                    ap = ap.bitcast(F32)
                return ap, [('hT', row, g) for g in range(lo // 512, (hi + 511) // 512)]
            Ebuf = [hrow(0, 0, 2048, F32), hrow(1, 0, 2048, F32)]
            Gbuf = hrow(2, 0, 2048, F32)
            spbuf = [hrow(3, 0, 1024), hrow(3, 1024, 2048)]
            Abuf = [hrow(4, 0, 1024), hrow(4, 1024, 2048)]
            Rbuf = hrow(5, 0, 1024)
            rsb = hrow(5, 1024, 2048, F32)
            rawb = hrow(6, 0, 2048)
            sqb = hrow(7, 0, 512)

            def v2(ap):
                return ap.rearrange("p (c n) -> p c n", c=2)
            mstr2 = _bv(mstr[:], [[0, 2], [1, 128]])

            def sweep_group(G):
                nkb = 4 * G + 4
                items = [(hp, kb) for hp in range(4) for kb in range(nkb - 1, -1, -1)]
                n = len(items)

                def c0_of(i):
                    return max(0, items[i][1] - 4 * G) * 128

                def zmm(i):
                    hp, kb = items[i]
                    c0 = c0_of(i)
                    for ch in range(2):
                        ho = ch * 64
                        zb = ZBK[i % 2][ch]
                        S.op('pe', lambda e, ho=ho, zb=zb: e.matmul(
                            ps[zb][:, c0:512], lhsT=skT[ho:ho + 64, hp, kb * 128:(kb + 1) * 128],
                            rhs=cT[ho:ho + 64, 4 + hp, G * 512 + c0:(G + 1) * 512], start=True, stop=True),
                             reads=[('hs', 4 + hp // 2), ('cT', 4 + hp, G)], writes=[PK(zb)])

                def esp(i):
                    hp, kb = items[i]
                    c0 = c0_of(i)
                    par = i % 2
                    E, Ek = Ebuf[par]; sp, spk = spbuf[par]
                    zb = ZBK[par]
                    S.op('act', lambda e: e.activation(out=v2(E)[:, :, c0:512], in_=PSall[:, zb[0]:zb[0] + 2, c0:512], func=AF.Exp, scale=0.125),
                         reads=[PK(zb[0]), PK(zb[1])], writes=Ek)
                    S.op('act', lambda e: e.activation(out=v2(sp)[:, :, c0:512], in_=v2(E)[:, :, c0:512], func=AF.Ln, bias=1.0, scale=1.0),
                         reads=Ek, writes=spk)
                    if kb >= 4 * G:
                        S.op('pool', lambda e: e.tensor_tensor(out=v2(sp)[:, :, c0:c0 + 128], in0=v2(sp)[:, :, c0:c0 + 128], in1=mstr2, op=ALU.mult),
                             reads=spk + ['const'], writes=spk)

                def cmm(i):
                    hp, kb = items[i]
                    c0 = c0_of(i)
                    first = (kb == nkb - 1)
                    sp, spk = spbuf[i % 2]; R, Rk = Rbuf
                    for ch in range(2):
                        S.op('pe', lambda e, ch=ch: e.matmul(ps[CBK[ch]][:, c0:512], lhsT=tri[:], rhs=v2(sp)[:, ch, c0:512], start=True, stop=first),
                             reads=spk + ['const'], writes=[PK(CBK[ch])])
                        if not first:
                            S.op('pe', lambda e, ch=ch: e.matmul(ps[CBK[ch]][:, c0:512], lhsT=ones[:], rhs=v2(R)[:, ch, c0:512], start=False, stop=True),
                                 reads=Rk + ['const'], writes=[PK(CBK[ch])])

                def rest(i):
                    hp, kb = items[i]
                    c0 = c0_of(i)
                    par = i % 2
                    first = (kb == nkb - 1)
                    E, Ek = Ebuf[par]; sp, spk = spbuf[par]; Gt, Gk = Gbuf; A, Akk = Abuf[par]; R, Rk = Rbuf
                    S.op('act', lambda e: e.activation(out=v2(Gt)[:, :, c0:512], in_=PSall[:, CBK[0]:CBK[0] + 2, c0:512], func=AF.Exp, scale=-1.0),
                         reads=[PK(CBK[0]), PK(CBK[1])], writes=Gk)
                    S.op('dve', lambda e: e.tensor_tensor(out=v2(A)[:, :, c0:512], in0=v2(E)[:, :, c0:512], in1=v2(Gt)[:, :, c0:512], op=ALU.mult),
                         reads=Ek + Gk, writes=Akk)
                    if kb >= 4 * G:
                        S.op('pool', lambda e: e.tensor_tensor(out=v2(A)[:, :, c0:c0 + 128], in0=v2(A)[:, :, c0:c0 + 128], in1=mstr2, op=ALU.mult),
                             reads=Akk + ['const'], writes=Akk)
                    if first:
                        S.op('pool', lambda e: e.memset(R, 0.0), writes=Rk)
                    if kb > 0:
                        S.op('pool', lambda e: e.tensor_tensor(out=v2(R)[:, :, c0:512], in0=v2(R)[:, :, c0:512], in1=v2(sp)[:, :, c0:512], op=ALU.add),
                             reads=Rk + spk, writes=Rk)

                def avmm(i):
                    hp, kb = items[i]
                    c0 = c0_of(i)
                    first = (kb == nkb - 1)
                    A, Akk = Abuf[i % 2]
                    if first:
                        S.op('pe', lambda e: e.matmul(ps[ACC][:, :], lhsT=zer[:], rhs=dmk[:], start=True, stop=False),
                             reads=['const'], writes=[PK(ACC)])
                    for ch in range(2):
                        h = 2 * hp + ch
                        S.op('pe', lambda e, ch=ch, h=h: e.matmul(
                            ps[ACC][ch * 64:(ch + 1) * 64, c0:512], lhsT=svv[:, kb, h * 64:(h + 1) * 64], rhs=v2(A)[:, ch, c0:512],
                            start=False, stop=(kb == 0), skip_group_check=True),
                             reads=Akk + [('hs', 6 + kb // 8)], writes=[PK(ACC)])
                    if kb == 0:
                        raw, rawk = rawb; sq, sqk = sqb
                        S.op('act', lambda e: e.copy(out=raw[:, hp * 512:(hp + 1) * 512], in_=ps[ACC][:, :]), reads=[PK(ACC)], writes=[rawk[hp]])
                        S.op('act', lambda e: e.activation(out=sq, in_=ps[ACC][:, :], func=AF.Square), reads=[PK(ACC)], writes=sqk)
                        S.op('pe', lambda e: e.matmul(ps[SSQ][:, :], lhsT=ones[:], rhs=sq, start=(hp == 0), stop=(hp == 3)),
                             reads=sqk + ['const'], writes=[PK(SSQ)])

                zmm(0)
                if n > 1:
                    zmm(1)
                esp(0)
                cmm(0)
                for i in range(n):
                    if i + 1 < n:
                        esp(i + 1)
                    rest(i)
                    if i + 1 < n:
                        cmm(i + 1)
                    if i + 2 < n:
                        zmm(i + 2)
                    avmm(i)
                rs, rsk = rsb; raw, rawk = rawb
                S.op('dve', lambda e: e.tensor_scalar(out=rs, in0=ps[SSQ][:, :], scalar1=1.0 / 512, scalar2=EPS, op0=ALU.mult, op1=ALU.add),
                     reads=[PK(SSQ)], writes=rsk)
                S.op('act', lambda e: e.activation(out=rs, in_=rs, func=AF.Sqrt), reads=rsk, writes=rsk)
                S.op('dve', lambda e: e.reciprocal(out=rs, in_=rs), reads=rsk, writes=rsk)
                for hp in range(4):
                    S.op('dve', lambda e, hp=hp: e.scalar_tensor_tensor(
                        out=cT[:, 4 + hp, G * 512:(G + 1) * 512], in0=raw[:, hp * 512:(hp + 1) * 512], scalar=pt[:, l, 36 + hp:37 + hp],
                        in1=rs, op0=ALU.mult, op1=ALU.mult), reads=[rawk[hp], 'const'] + rsk, writes=[('cT', 4 + hp, G)])
            for G in range(4):
                sweep_group(G)
            if stop == 'sb':
                return
            wm = D["w_mix_out"][l]
            gemm_tm(lambda kc, tt: cT[:, kc, tt * 128:(tt + 1) * 128], lambda kc, tt: [('cT', kc, tt // 4)],
                    lambda ns: [(wm[:, ns * 512:(ns + 1) * 512], 8)], 2, resid_add)

        def cross_block(l):
            norm_to_hT(l, 8)
            memf = WA[:, 7, 0:4096].bitcast(F32).rearrange("p (t d) -> p t d", t=2)
            S.dma('sp', lambda e: e.dma_start(out=memf, in_=D["mem"].rearrange("(t p) d -> p t d", p=128)), writes=[('hs', 7)])
            junk, jk = tunit(0, 2)
            for t in range(2):
                S.op('act', lambda e, t=t: e.activation(out=junk, in_=memf[:, t, :], func=AF.Square, accum_out=sm[:, 64 + t:65 + t]),
                     reads=[('hs', 7)], writes=['sm4'] + jk)
            rstd_from_ssq(sm[:, 64:66], 1.0 / DM, 'sm4')
            mnT = WA[:, 4, 0:2048].rearrange("p (k n) -> p k n", k=8)
            for t in range(2):
                hn, hk = tunit(2 + 2 * t, 2)
                S.op('dve', lambda e, t=t, hn=hn: e.tensor_scalar(out=hn, in0=memf[:, t, :], scalar1=sm[:, 64 + t:65 + t], scalar2=None, op0=ALU.mult),
                     reads=[('hs', 7), 'sm4'], writes=hk)
                b = nextbank(ALLB)
                transposes(lambda c, hn=hn: hn[:, c * 128:(c + 1) * 128], 8, b, hk + ['const'])
                gb = _bv(pt[:, l, 16:24], [[1, 8], [0, 128]])
                S.op('dve', lambda e, t=t, b=b, gb=gb: e.tensor_tensor(out=mnT[:, :, t * 128:(t + 1) * 128],
                                                                       in0=psb[b][:, :].rearrange("p (k n) -> p k n", k=8), in1=gb, op=ALU.mult),
                     reads=[PK(b), 'const'], writes=[('hs', 4)])
            kTm = WA[:, 6, 0:2048].rearrange("p (k n) -> p k n", k=8)
            vm = WA[:, 6, 2048:4096].rearrange("p (t n) -> p t n", t=2)
            wkv = D["w_xkv"][l]
            def kv_epi(tt, ns, b):
                if ns < 2:
                    kn, knk = tunit(6 + tt % 2, 1)
                    junk1, jk1 = tunit(0, 1)
                    for hh in range(2):
                        col = 66 + (tt * 4 + ns * 2 + hh)
                        S.op('act', lambda e, hh=hh, col=col: e.activation(out=junk1[:, 0:256], in_=ps[b][:, hh * 256:(hh + 1) * 256], func=AF.Square,
                                                                           accum_out=sm[:, col:col + 1]), reads=[PK(b)], writes=[('sm5', tt, ns)] + jk1)
                    c0 = 66 + tt * 4 + ns * 2
                    rstd_from_ssq(sm[:, c0:c0 + 2], 1.0 / 256, ('sm5', tt, ns))
                    for hh in range(2):
                        S.op('dve', lambda e, hh=hh, kn=kn: e.tensor_scalar(out=kn[:, hh * 256:(hh + 1) * 256], in0=ps[b][:, hh * 256:(hh + 1) * 256],
                                                                            scalar1=sm[:, c0 + hh:c0 + hh + 1], scalar2=None, op0=ALU.mult),
                             reads=[PK(b), ('sm5', tt, ns)], writes=knk)
                    bt = nextbank(ALLB)
                    transposes(lambda c, kn=kn: kn[:, c * 128:(c + 1) * 128], 4, bt, knk + ['const'])
                    gb = _bv(pt[:, l, 42:44], [[0, 2], [1, 2], [0, 128]])
                    S.op('dve', lambda e, bt=bt, gb=gb: e.tensor_tensor(
                        out=kTm[:, ns * 4:(ns + 1) * 4, tt * 128:(tt + 1) * 128].rearrange("p (a c) n -> p a c n", a=2),
                        in0=psb[bt][:, 0:512].rearrange("p (a c n) -> p a c n", a=2, c=2), in1=gb, op=ALU.mult),
                         reads=[PK(bt), 'const'], writes=[('hs', 6)])
                else:
                    copy_op('act', vm[:, tt, (ns - 2) * 512:(ns - 1) * 512], ps[b][:, :], [PK(b)], [('hs', 6)])
            gemm_tm(lambda kc, tt: mnT[:, kc, tt * 128:(tt + 1) * 128], lambda kc, tt: [('hs', 4)],
                    lambda ns: [(wkv[:, ns * 512:(ns + 1) * 512], 8)], 4, kv_epi, ntt=2)
            wq = D["w_xq"][l]
            qT = WA[:, 4:6, :].rearrange("p a (b n) -> p (a b) n", b=2)
            for hpair in range(2):
                def q_epi(tt, ns, b):
                    qn, qnk = tunit(6 + tt % 2, 1)
                    junk1, jk1 = tunit(0, 1)
                    for hh in range(2):
                        S.op('act', lambda e, hh=hh: e.activation(out=junk1[:, 0:256], in_=ps[b][:, hh * 256:(hh + 1) * 256], func=AF.Square,
                                                                  accum_out=sm[:, 80 + 2 * (tt % 2) + hh:81 + 2 * (tt % 2) + hh]),
                             reads=[PK(b)], writes=[('sm6', tt % 2)] + jk1)
                    c0 = 80 + 2 * (tt % 2)
                    rstd_from_ssq(sm[:, c0:c0 + 2], 1.0 / 256, ('sm6', tt % 2))
                    for hh in range(2):
                        S.op('dve', lambda e, hh=hh, qn=qn: e.tensor_scalar(out=qn[:, hh * 256:(hh + 1) * 256], in0=ps[b][:, hh * 256:(hh + 1) * 256],
                                                                            scalar1=sm[:, c0 + hh:c0 + hh + 1], scalar2=None, op0=ALU.mult),
                             reads=[PK(b), ('sm6', tt % 2)], writes=qnk)
                    bt = nextbank(ALLB)
                    transposes(lambda c, qn=qn: qn[:, c * 128:(c + 1) * 128], 4, bt, qnk + ['const'])
                    gb = _bv(pt[:, l, 40:42], [[0, 2], [1, 2], [0, 128]])
                    S.op('dve', lambda e, bt=bt, gb=gb, tt=tt: e.tensor_tensor(
                        out=qT[:, :, tt * 128:(tt + 1) * 128].rearrange("p (a c) n -> p a c n", a=2),
                        in0=psb[bt][:, 0:512].rearrange("p (a c n) -> p a c n", a=2, c=2), in1=gb, op=ALU.mult),
                         reads=[PK(bt), 'const'], writes=[('hs', 4 + (c // 2)) for c in range(4)])
                gemm_tm(lambda kc, tt: hT[:, kc, tt * 128:(tt + 1) * 128], lambda kc, tt: [('hT', kc, tt // 4)],
                        lambda ns: [(wq[:, hpair * 512:(hpair + 1) * 512], 8)], 1, q_epi)
                for hh in range(2):
                    h = hpair * 2 + hh
                    def attn(g, h=h, hh=hh):
                        pT, pTk = tunit(2 + 2 * (g % 2), 2)
                        bsc = [nextbank(ALLB), nextbank(ALLB)]
                        for mc in range(2):
                            for c in range(2):
                                S.op('pe', lambda e, mc=mc, c=c: e.matmul(
                                    ps[bsc[mc]][:, :], lhsT=kTm[:, h * 2 + c, mc * 128:(mc + 1) * 128],
                                    rhs=qT[:, hh * 2 + c, g * 512:(g + 1) * 512], start=(c == 0), stop=(c == 1)),
                                     reads=[('hs', 6), ('hs', 4 + hh)], writes=[PK(bsc[mc])])
                            S.op('act', lambda e, mc=mc, pT=pT: e.activation(out=pT[:, mc * 512:(mc + 1) * 512], in_=ps[bsc[mc]][:, :], func=AF.Exp, scale=1.0 / 16),
                                 reads=[PK(bsc[mc])], writes=pTk)
                        bd = nextbank(ALLB)
                        for mc in range(2):
                            S.op('pe', lambda e, mc=mc, pT=pT: e.matmul(ps[bd][:, :], lhsT=ones[:], rhs=pT[:, mc * 512:(mc + 1) * 512],
                                                                       start=(mc == 0), stop=(mc == 1)), reads=pTk + ['const'], writes=[PK(bd)])
                        rden, rdk = tunit(0, 2, F32)
                        S.op('dve', lambda e, rden=rden: e.reciprocal(out=rden, in_=ps[bd][:, :]), reads=[PK(bd)], writes=rdk)
                        for c in range(2):
                            bo = nextbank(ALLB)
                            for mc in range(2):
                                S.op('pe', lambda e, mc=mc, c=c, pT=pT, bo=bo: e.matmul(
                                    ps[bo][:, :], lhsT=vm[:, mc, h * 256 + c * 128:h * 256 + (c + 1) * 128],
                                    rhs=pT[:, mc * 512:(mc + 1) * 512], start=(mc == 0), stop=(mc == 1)),
                                     reads=pTk + [('hs', 6)], writes=[PK(bo)])
                            S.op('dve', lambda e, c=c, bo=bo, rden=rden: e.tensor_tensor(
                                out=cT[:, h * 2 + c, g * 512:(g + 1) * 512], in0=ps[bo][:, :], in1=rden, op=ALU.mult),
                                 reads=[PK(bo)] + rdk, writes=[('cT', h * 2 + c, g)])
                    for g in range(4):
                        attn(g)
            if stop == 'xattn':
                return
            wo = D["w_xo"][l]
            gemm_tm(lambda kc, tt: cT[:, kc, tt * 128:(tt + 1) * 128], lambda kc, tt: [('cT', kc, tt // 4)],
                    lambda ns: [(wo[:, ns * 512:(ns + 1) * 512], 8)], 2, resid_add)

        def mlp_block(l):
            norm_to_hT(l, 24)
            ring["slots"] = [0, 1, 2, 4, 5, 6, 7]
            wu = D["w_up"][l]; wd = D["w_down"][l]
            for fs in range(4):
                for half in range(2):
                    wap, wkey = load_piece(wu[:, fs * 1024 + half * 512:fs * 1024 + (half + 1) * 512], 8, 512)
                    for m in range(4):
                        fc = half * 4 + m
                        for g in range(4):
                            b = nextbank(ALLB)
                            for kc in range(8):
                                S.op('pe', lambda e, m=m, g=g, kc=kc, b=b, wap=wap: e.matmul(
                                    ps[b][:, :], lhsT=wap[:, kc, m * 128:(m + 1) * 128], rhs=hT[:, kc, g * 512:(g + 1) * 512],
                                    start=(kc == 0), stop=(kc == 7)), reads=[wkey, ('hT', kc, g)], writes=[PK(b)])
                            r, rk = tunit((fc * 4 + g) % 4, 1)
                            S.op('act', lambda e, b=b, r=r: e.activation(out=r, in_=ps[b][:, :], func=AF.Relu), reads=[PK(b)], writes=rk)
                            S.op('pool', lambda e, r=r, fc=fc, g=g: e.tensor_tensor(out=cT[:, fc, g * 512:(g + 1) * 512], in0=r, in1=r, op=ALU.mult),
                                 reads=rk, writes=[('cT', fc, g)])
                gemm_tm(lambda kc, tt: cT[:, kc, tt * 128:(tt + 1) * 128], lambda kc, tt: [('cT', kc, tt // 4)],
                        lambda ns: [(wd[fs * 1024:(fs + 1) * 1024, ns * 512:(ns + 1) * 512], 8)], 2, resid_add)
            ring["slots"] = [0, 1, 2]

        done = False
        for l in range(n_layers):
            mix_block(l)
            if stop in ('ret', 'sb') or stop == f'mix{l}':
                break
            cross_block(l)
            if stop == 'xattn' or stop == f'cross{l}':
                break
            mlp_block(l)
            if stop == f'mlp{l}':
                break

        if stop in ('ret', 'sb', 'xattn'):
            for kc in range(4 if stop == 'ret' else 8):
                for g in range(4):
                    S.op('dve', lambda e, kc=kc, g=g: e.tensor_copy(out=xs[:, kc * 2 + g // 2, (g % 2) * 512:(g % 2 + 1) * 512],
                                                                    in_=cT[:, kc, g * 512:(g + 1) * 512]),
                         reads=[('cT', kc, g)], writes=[('xs', kc * 2 + g // 2)])
        for tt in range(NT):
            S.dma('sp', lambda e, tt=tt: e.dma_start(out=out_d[tt * 128:(tt + 1) * 128, :], in_=xs[:, tt, :]),
                  reads=[('xs', tt)], writes=[('out', tt)], final=True)
        S.emit()
    return nc


_CONSTS = None


def make_in_maps(inputs, n_cores=8):
    global _CONSTS
    if _CONSTS is None:
        _CONSTS = _const_tables()
    f = lambda a: np.ascontiguousarray(np.asarray(a, dtype=np.float32))
    pt = _param_table(*[np.asarray(inputs[k], dtype=np.float32) for k in
                        ("g_mix", "g_cross", "g_mem", "g_mlp", "g_ret_out", "g_sb_out", "g_qn", "g_kn")])
    shared = {k: f(inputs[k]) for k in ("w_in", "w_mix_out", "w_xq", "w_xkv", "w_xo", "w_up", "w_down")}
    shared["pt"] = pt
    shared.update(_CONSTS)
    x = f(inputs["x"]); mem = f(inputs["mem"])
    maps = []
    for c in range(n_cores):
        m = dict(shared)
        m["x"] = x[c]
        m["mem"] = mem[c]
        maps.append(m)
    return maps


def kernel(**inputs):
    nc = build_program()
    maps = make_in_maps(inputs, 8)
    res = run_bass_kernel_spmd(nc, maps, core_ids=list(range(8)))
    return np.stack([np.asarray(r["out"], dtype=np.float32) for r in res.results], axis=0)
```

```python
import contextlib
import numpy as np
import concourse.bass as bass
import concourse.mybir as mybir
from concourse.bass_utils import run_bass_kernel_spmd

dt = mybir.dt
F32, BF16 = dt.float32, dt.bfloat16
AF = mybir.ActivationFunctionType
ALU = mybir.AluOpType
AX = mybir.AxisListType

ENGS = ('pe', 'act', 'dve', 'pool', 'sp')


class Sched:
    EPOCH = 20000
    NDMA = 8

    def __init__(self, nc):
        self.nc = nc
        self.stack = contextlib.ExitStack()
        self.ops = {e: [] for e in ENGS}
        self.ncomp = {e: 0 for e in ENGS}
        self.esem = {e: {} for e in ENGS}
        self.sems = []
        self.lastw = {}
        self.readers = {}
        self.seen = {e: {} for e in ENGS}
        self.dsem = []
        self.dnext = 0
        self.finals = []

    def __enter__(self):
        self.stack.__enter__()
        return self

    def __exit__(self, *a):
        return self.stack.__exit__(*a)

    def sbuf(self, name, shape, dtype):
        return self.stack.enter_context(self.nc.sbuf_tensor(name, list(shape), dtype))

    def psum(self, name, shape, dtype):
        return self.stack.enter_context(self.nc.psum_tensor(name, list(shape), dtype))

    def _newsem(self, name):
        s = self.stack.enter_context(self.nc.semaphore(name))
        self.sems.append(s)
        return len(self.sems) - 1

    def _deps(self, eng, reads, writes, is_dma):
        deps = set()
        for k in reads:
            t = self.lastw.get(k)
            if t is not None:
                deps.add(t)
        for k in writes:
            t = self.lastw.get(k)
            if t is not None:
                deps.add(t)
            for t in self.readers.get(k, ()):
                if t[2] == eng and not is_dma:
                    continue
                deps.add(t)
        return deps

    def _waits(self, eng, deps, is_dma):
        waits = []
        seen = self.seen[eng]
        for (sid, v, deng) in sorted(deps):
            if deng == eng and eng == 'pe' and not is_dma:
                continue
            if seen.get(sid, 0) >= v:
                continue
            seen[sid] = v
            waits.append((sid, v))
        return waits

    def _commit(self, tok, reads, writes):
        for k in reads:
            self.readers.setdefault(k, []).append(tok)
        for k in writes:
            self.lastw[k] = tok
            self.readers[k] = []

    def op(self, eng, fn, reads=(), writes=()):
        deps = self._deps(eng, reads, writes, False)
        waits = self._waits(eng, deps, False)
        idx = self.ncomp[eng]
        ep = idx // self.EPOCH
        if ep not in self.esem[eng]:
            self.esem[eng][ep] = self._newsem(f"s_{eng}_{ep}")
        sid = self.esem[eng][ep]
        tok = (sid, idx % self.EPOCH + 1, eng)
        self.ncomp[eng] += 1
        self._commit(tok, reads, writes)
        self.ops[eng].append((fn, waits, sid, 1))
        return tok

    def dma(self, eng, fn, reads=(), writes=(), final=False):
        if eng == 'pool':
            return self._dma_sw(fn, reads, writes)
        deps = self._deps(eng, reads, writes, True)
        if len(self.dsem) < self.NDMA:
            self.dsem.append([self._newsem(f"s_dma_{len(self.dsem)}"), 0, None])
        ent = self.dsem[self.dnext % self.NDMA]
        self.dnext += 1
        if ent[2] is not None:
            deps.add(ent[2])
        waits = self._waits(eng, deps, True)
        ent[1] += 16
        tok = (ent[0], ent[1], 'dma')
        ent[2] = tok
        self._commit(tok, reads, writes)
        self.ops[eng].append((fn, waits, ent[0], 16))
        if final:
            self.finals.append(tok)
        return tok

    def _dma_sw(self, fn, reads, writes):
        deps = self._deps('pool', reads, writes, True)
        waits = self._waits('pool', deps, True)
        if not hasattr(self, 'nsw'):
            self.nsw = 0
        sid = self._newsem(f"s_sw_{self.nsw}")
        self.nsw += 1
        tok = (sid, 16, 'dma')
        self._commit(tok, reads, writes)
        self.ops['pool'].append((fn, waits, sid, 16))
        return tok

    def emit(self):
        nc = self.nc
        fin = [(sid, v) for (sid, v, _) in self.finals]
        sems = self.sems
        ops = self.ops

        def replay(name, e, tail=()):
            for ent in ops[name]:
                if ent[0] == 'relay':
                    e.wait_ge(sems[ent[1]], 16)
                    e.sem_clear(sems[ent[1]])
                    e.sem_inc(sems[ent[2]], 1)
                    continue
                (fn, waits, sid, n) = ent
                for (ws, wv) in waits:
                    e.wait_ge(sems[ws], wv)
                fn(e).then_inc(sems[sid], n)
            for (ws, wv) in tail:
                e.wait_ge(sems[ws], wv)

        with nc.Block() as block:
            @block.sync
            def _(e):
                replay('sp', e, fin)

            @block.tensor
            def _(e):
                replay('pe', e)

            @block.vector
            def _(e):
                replay('dve', e)

            @block.scalar
            def _(e):
                replay('act', e)

            @block.gpsimd
            def _(e):
                replay('pool', e)


SEQ, DM, DEPTH = 2048, 1024, 2
NT = 16
EPS = 1e-6
NPC = 44
GAM = [1.0 - 2.0 ** (-5 - h) for h in range(4)]


def _const_tables():
    p = np.arange(128)
    c = {}
    c["ident"] = np.eye(128, dtype=np.float32)
    c["tri"] = (p[:, None] >= p[None, :]).astype(np.float32)
    c["ones"] = np.ones((128, 128), np.float32)
    c["mstr"] = (p[:, None] < p[None, :]).astype(np.float32)
    inv = (1.0 / (np.float32(10000.0) ** np.linspace(0.0, 1.0, 64, dtype=np.float32))).astype(np.float32)
    pos = np.arange(SEQ, dtype=np.float32)
    ang = (pos[:, None] * inv[None, :]).astype(np.float32)
    c["cos"] = np.cos(ang.astype(np.float64)).astype(np.float32).reshape(NT, 128, 64).transpose(1, 0, 2).copy()
    c["sin"] = np.sin(ang.astype(np.float64)).astype(np.float32).reshape(NT, 128, 64).transpose(1, 0, 2).copy()
    lg = np.log1p(-np.exp2(-5.0 - np.arange(4, dtype=np.float64)))
    dist = np.abs(p[:, None] - p[None, :]).astype(np.float64)
    same = (p[:, None] // 64) == (p[None, :] // 64)
    dm = np.zeros((128, 4, 128), np.float64)
    qd = np.zeros((128, 4, 128), np.float64)
    kd = np.zeros((128, 4, 128), np.float64)
    for h in range(4):
        dm[:, h, :] = np.where(same, np.exp(lg[h] * dist), 0.0) * (128.0 ** -0.5)
        qd[:, h, :] = np.exp(lg[h] * ((p % 64) + 1.0))[None, :]
        kd[:, h, :] = (np.exp(lg[h] * (63.0 - (p % 64))) * (128.0 ** -0.5))[:, None]
    c["dmk"] = dm.reshape(128, 512).astype(np.float32)
    c["qdc"] = qd.reshape(128, 512).astype(np.float32)
    c["kdc"] = kd.reshape(128, 512).astype(np.float32)
    return c


def _param_table(g_mix, g_cross, g_mem, g_mlp, g_ret_out, g_sb_out, g_qn, g_kn):
    pt = np.zeros((128, DEPTH, NPC), np.float32)
    for l in range(DEPTH):
        pt[:, l, 0:8] = g_mix[l].reshape(8, 128).T
        pt[:, l, 8:16] = g_cross[l].reshape(8, 128).T
        pt[:, l, 16:24] = g_mem[l].reshape(8, 128).T
        pt[:, l, 24:32] = g_mlp[l].reshape(8, 128).T
        pt[:, l, 32:36] = g_ret_out[l].reshape(4, 128).T
        pt[:, l, 36:40] = g_sb_out[l].reshape(4, 128).T
        pt[:, l, 40:42] = g_qn[l].reshape(2, 128).T
        pt[:, l, 42:44] = g_kn[l].reshape(2, 128).T
    return pt


def _bv(ap, dims):
    return bass.AP(ap.tensor, ap.offset, [list(ap.ap[0])] + [list(d) for d in dims])


def build_program(n_layers=DEPTH, stop=None):
    nc = bass.Bass("TRN2", target_bir_lowering=False)
    D = {}
    def din(name, shape):
        D[name] = nc.dram_tensor(name, list(shape), F32, kind="ExternalInput").ap()
    din("x", [SEQ, DM]); din("mem", [256, DM])
    din("w_in", [DEPTH, DM, 3584]); din("w_mix_out", [DEPTH, DM, DM]); din("w_xq", [DEPTH, DM, DM])
    din("w_xkv", [DEPTH, DM, 2 * DM]); din("w_xo", [DEPTH, DM, DM]); din("w_up", [DEPTH, DM, 4 * DM])
    din("w_down", [DEPTH, 4 * DM, DM]); din("pt", [128, DEPTH, NPC])
    for nm in ("ident", "tri", "ones", "mstr"):
        din(nm, [128, 128])
    din("cos", [128, NT, 64]); din("sin", [128, NT, 64])
    for nm in ("dmk", "qdc", "kdc"):
        din(nm, [128, 512])
    out_d = nc.dram_tensor("out", [SEQ, DM], F32, kind="ExternalOutput").ap()

    S = Sched(nc)
    with S:
        xs = S.sbuf("xs", [128, NT, DM], F32)
        hT = S.sbuf("hT", [128, 8, SEQ], BF16)
        cT = S.sbuf("cT", [128, 8, SEQ], BF16)
        WA = S.sbuf("WA", [128, 8, 4096], BF16)
        ident = S.sbuf("ident_s", [128, 128], BF16)
        tri = S.sbuf("tri_s", [128, 128], BF16)
        ones = S.sbuf("ones_s", [128, 128], BF16)
        mstr = S.sbuf("mstr_s", [128, 128], BF16)
        zer = S.sbuf("zer_s", [128, 128], BF16)
        cosT = S.sbuf("cos_s", [128, NT, 64], BF16)
        sinT = S.sbuf("sin_s", [128, NT, 64], BF16)
        dmk = S.sbuf("dmk_s", [128, 512], BF16)
        qdc = S.sbuf("qdc_s", [128, 512], BF16)
        kdc = S.sbuf("kdc_s", [128, 512], BF16)
        pt = S.sbuf("pt_s", [128, DEPTH, NPC], F32)
        st = S.sbuf("st_s", [128, 4, 128], F32)
        stb = S.sbuf("stb_s", [128, 2, 4, 128], BF16)
        sm = S.sbuf("sm_s", [128, 128], F32)
        PS = S.psum("ps", [128, 8, 512], F32)
        ps = [PS[:, i, :] for i in range(8)]
        psb = [p.bitcast(BF16) for p in ps]

        rr = {"dq": 0, "bank": 0, "ring": 0, "ev": 0}

        def dq():
            rr["dq"] += 1
            return 'sp' if rr["dq"] % 2 else 'act'

        def PK(b):
            return ('ps', b)

        def unit(buf, name, u, n=1, dtype=BF16):
            kc, g = u // 4, u % 4
            assert g + n <= 4
            ap = buf[:, kc, g * 512:(g + n) * 512]
            if dtype == F32:
                ap = ap.bitcast(F32)
            return ap, [(name, kc, g + i) for i in range(n)]

        def tunit(i, n=1, dtype=BF16):
            ap = WA[:, 3, i * 512:(i + n) * 512]
            if dtype == F32:
                ap = ap.bitcast(F32)
            return ap, [('t', i + k) for k in range(n)]

        def nextbank(allowed):
            rr["bank"] += 1
            return allowed[rr["bank"] % len(allowed)]

        ring = {"slots": [0, 1, 2]}

        def load_piece(w2d, nk, ncols):
            assert nk * ncols <= 4096
            rr["ring"] += 1
            s = ring["slots"][rr["ring"] % len(ring["slots"])]
            dst = WA[:, s, 0:nk * ncols].rearrange("p (k n) -> p k n", k=nk)
            src = w2d.rearrange("(k p) n -> p k n", p=128)
            S.dma('pool', lambda e: e.dma_start(out=dst, in_=src), reads=[], writes=[('hs', s)])
            return dst, ('hs', s)

        def evac_engine():
            rr["ev"] += 1
            return 'act' if rr["ev"] % 2 else 'dve'

        def copy_op(eng, out, in_, reads, writes):
            if eng == 'act':
                S.op('act', lambda e: e.copy(out=out, in_=in_), reads=reads, writes=writes)
            else:
                S.op(eng, lambda e: e.tensor_copy(out=out, in_=in_), reads=reads, writes=writes)

        def rstd_from_ssq(ssq_ap, inv_n, key):
            S.op('dve', lambda e: e.tensor_scalar(out=ssq_ap, in0=ssq_ap, scalar1=inv_n, scalar2=EPS,
                                                  op0=ALU.mult, op1=ALU.add), reads=[key], writes=[key])
            S.op('act', lambda e: e.activation(out=ssq_ap, in_=ssq_ap, func=AF.Sqrt), reads=[key], writes=[key])
            S.op('dve', lambda e: e.reciprocal(out=ssq_ap, in_=ssq_ap), reads=[key], writes=[key])

        def transposes(src_fn, n, bank, reads):
            for c in range(n):
                S.op('pe', lambda e, c=c: e.transpose(out=psb[bank][:, c * 128:(c + 1) * 128], in_=src_fn(c),
                                                      identity=ident[:]),
                     reads=reads, writes=[PK(bank)])

        for tt in range(NT):
            S.dma('sp', lambda e, tt=tt: e.dma_start(out=xs[:, tt, :], in_=D["x"][tt * 128:(tt + 1) * 128, :]),
                  writes=[('xs', tt)])
        for (sb_t, nm) in ((ident, "ident"), (tri, "tri"), (ones, "ones"), (mstr, "mstr"), (cosT, "cos"),
                           (sinT, "sin"), (dmk, "dmk"), (qdc, "qdc"), (kdc, "kdc")):
            S.dma('pool', lambda e, sb_t=sb_t, nm=nm: e.dma_start(out=sb_t[:], in_=D[nm]), writes=['const'])
        S.dma('sp', lambda e: e.dma_start(out=pt[:], in_=D["pt"]), writes=['const'])
        S.op('pool', lambda e: e.memset(zer[:], 0.0), writes=['const'])

        ALLB = list(range(8))

        def norm_to_hT(l, gcol):
            junk, jk = tunit(0, 2)
            for tt in range(NT):
                S.op('act', lambda e, tt=tt: e.activation(out=junk, in_=xs[:, tt, :], func=AF.Square,
                                                          accum_out=sm[:, tt:tt + 1]),
                     reads=[('xs', tt)], writes=['sm'] + jk)
            rstd_from_ssq(sm[:, 0:NT], 1.0 / DM, 'sm')
            for tt in range(NT):
                hn, hk = tunit(2 + 2 * (tt % 2), 2)
                S.op('dve', lambda e, tt=tt, hn=hn: e.tensor_scalar(out=hn, in0=xs[:, tt, :], scalar1=sm[:, tt:tt + 1],
                                                                    scalar2=None, op0=ALU.mult),
                     reads=[('xs', tt), 'sm'], writes=hk)
                b = nextbank(ALLB)
                transposes(lambda c, hn=hn: hn[:, c * 128:(c + 1) * 128], 8, b, hk + ['const'])
                gsl = pt[:, l, gcol:gcol + 8]
                gb = _bv(gsl, [[1, 8], [0, 128]])
                S.op('dve', lambda e, tt=tt, b=b, gb=gb: e.tensor_tensor(
                    out=hT[:, :, tt * 128:(tt + 1) * 128],
                    in0=psb[b][:, :].rearrange("p (k n) -> p k n", k=8), in1=gb, op=ALU.mult),
                     reads=[PK(b), 'const'], writes=[('hT', kc, tt // 4) for kc in range(8)])

        def gemm_tm(lhs_fn, lhs_keys_fn, pieces_fn, n_slices, epilogue, banks=ALLB, ntt=NT):
            for ns in range(n_slices):
                pcs = [load_piece(w2d, nk, 512) for (w2d, nk) in pieces_fn(ns)]
                for tt in range(ntt):
                    b = nextbank(banks)
                    tot = sum(p[0].shape[1] for p in pcs)
                    i = 0
                    for pi, (wap, wkey) in enumerate(pcs):
                        for k in range(wap.shape[1]):
                            kc = sum(p[0].shape[1] for p in pcs[:pi]) + k
                            S.op('pe', lambda e, wap=wap, k=k, kc=kc, i=i, b=b, tt=tt: e.matmul(
                                ps[b][:, :], lhsT=lhs_fn(kc, tt), rhs=wap[:, k, :], start=(i == 0), stop=(i == tot - 1)),
                                 reads=[wkey] + lhs_keys_fn(kc, tt), writes=[PK(b)])
                            i += 1
                    epilogue(tt, ns, b)

        def resid_add(tt, ns, b):
            S.op('dve', lambda e: e.tensor_tensor(out=xs[:, tt, ns * 512:(ns + 1) * 512], in0=ps[b][:, :],
                                                  in1=xs[:, tt, ns * 512:(ns + 1) * 512], op=ALU.add),
                 reads=[PK(b), ('xs', tt)], writes=[('xs', tt)])

        def mix_block(l):
            w_in = D["w_in"][l]
            norm_to_hT(l, 0)
            for j in range(4):
                dst = WA[:, 4 + j, :].rearrange("p (k n) -> p k n", k=8)
                src = w_in[:, j * 512:(j + 1) * 512].rearrange("(k p) n -> p k n", p=128)
                S.dma('pool', lambda e, dst=dst, src=src: e.dma_start(out=dst, in_=src), writes=[('hs', 4 + j)])
            S.op('pool', lambda e: e.memset(st[:], 0.0), writes=['st'])
            S.op('pool', lambda e: e.memset(stb[:], 0.0), writes=['stb0', 'stb1'])
            def cu(i, n=1, dtype=BF16):
                return unit(cT, 'cT', 16 + i, n, dtype)
            def ret_tile(tt):
                pb = [nextbank(ALLB) for _ in range(4)]
                for j in range(4):
                    wv = WA[:, 4 + j, :].rearrange("p (k n) -> p k n", k=8)
                    for kc in range(8):
                        S.op('pe', lambda e, j=j, kc=kc, wv=wv, tt=tt: e.matmul(
                            ps[pb[j]][:, :], lhsT=hT[:, kc, tt * 128:(tt + 1) * 128], rhs=wv[:, kc, :],
                            start=(kc == 0), stop=(kc == 7)),
                             reads=[('hs', 4 + j), ('hT', kc, tt // 4)], writes=[PK(pb[j])])
                A, Ak = cu(0, 2, F32)
                Bm, Bk = cu(2, 2, F32)
                qr, qrk = cu(4); kr, krk = cu(5); kdt, kdk = cu(6); vb, vbk = cu(7); sg, sgk = cu(8)
                qTt, qTk = cu(9); qdT, qdk = cu(10); kTt, kTk = cu(11); STm, STk = cu(12); on, onk = cu(13)
                cosb = _bv(cosT[:, tt, :], [[0, 4], [0, 2], [1, 64]])
                sinb = _bv(sinT[:, tt, :], [[0, 4], [1, 64]])
                def rot(bank, dstt, dk):
                    P4 = ps[bank][:, :].rearrange("p (h x d) -> p h x d", h=4, x=2)
                    A4 = A.rearrange("p (h x d) -> p h x d", h=4, x=2)
                    B4 = Bm.rearrange("p (x h d) -> p x h d", x=2, h=4)
                    D4 = dstt.rearrange("p (h x d) -> p h x d", h=4, x=2)
                    S.op('dve', lambda e, P4=P4, A4=A4: e.tensor_tensor(out=A4, in0=P4, in1=cosb, op=ALU.mult),
                         reads=[PK(bank), 'const'], writes=Ak)
                    S.op('dve', lambda e, P4=P4, B4=B4: e.tensor_tensor(out=B4[:, 0], in0=P4[:, :, 1, :], in1=sinb, op=ALU.mult),
                         reads=[PK(bank), 'const'], writes=[Bk[0]])
                    S.op('dve', lambda e, P4=P4, B4=B4: e.tensor_tensor(out=B4[:, 1], in0=P4[:, :, 0, :], in1=sinb, op=ALU.mult),
                         reads=[PK(bank), 'const'], writes=[Bk[1]])
                    S.op('pool', lambda e, A4=A4, B4=B4, D4=D4: e.tensor_tensor(out=D4[:, :, 0, :], in0=A4[:, :, 0, :], in1=B4[:, 0], op=ALU.subtract),
                         reads=Ak + Bk, writes=dk)
                    S.op('pool', lambda e, A4=A4, B4=B4, D4=D4: e.tensor_tensor(out=D4[:, :, 1, :], in0=A4[:, :, 1, :], in1=B4[:, 1], op=ALU.add),
                         reads=Ak + Bk, writes=dk)
                rot(pb[0], qr, qrk)
                rot(pb[1], kr, krk)
                S.op('pool', lambda e: e.tensor_tensor(out=kdt, in0=kr, in1=kdc[:], op=ALU.mult), reads=krk + ['const'], writes=kdk)
                S.op('act', lambda e: e.copy(out=vb, in_=ps[pb[2]][:, :]), reads=[PK(pb[2])], writes=vbk)
                S.op('act', lambda e: e.activation(out=sg, in_=ps[pb[3]][:, :], func=AF.Silu), reads=[PK(pb[3])], writes=sgk)
                bq = nextbank(ALLB)
                transposes(lambda c: qr[:, c * 128:(c + 1) * 128], 4, bq, qrk + ['const'])
                S.op('act', lambda e: e.copy(out=qTt, in_=psb[bq][:, 0:512]), reads=[PK(bq)], writes=qTk)
                S.op('dve', lambda e: e.tensor_tensor(out=qdT, in0=psb[bq][:, 0:512], in1=qdc[:], op=ALU.mult),
                     reads=[PK(bq), 'const'], writes=qdk)
                bk = nextbank(ALLB)
                transposes(lambda c: kr[:, c * 128:(c + 1) * 128], 4, bk, krk + ['const'])
                S.op('act', lambda e: e.copy(out=kTt, in_=psb[bk][:, 0:512]), reads=[PK(bk)], writes=kTk)
                bs = nextbank(ALLB)
                for h in range(4):
                    S.op('pe', lambda e, h=h: e.matmul(ps[bs][:, h * 128:(h + 1) * 128], lhsT=kTt[:, h * 128:(h + 1) * 128],
                                                       rhs=qTt[:, h * 128:(h + 1) * 128], start=True, stop=True),
                         reads=kTk + qTk, writes=[PK(bs)])
                S.op('dve', lambda e: e.tensor_tensor(out=STm, in0=ps[bs][:, :], in1=dmk[:], op=ALU.mult),
                     reads=[PK(bs), 'const'], writes=STk)
                bu = [nextbank(ALLB), nextbank(ALLB)]
                for c in range(2):
                    for h in range(4):
                        S.op('pe', lambda e, c=c, h=h: e.matmul(
                            ps[bu[c]][:, h * 128:(h + 1) * 128], lhsT=kdt[c * 64:(c + 1) * 64, h * 128:(h + 1) * 128],
                            rhs=vb[c * 64:(c + 1) * 64, h * 128:(h + 1) * 128], start=True, stop=True),
                             reads=kdk + vbk, writes=[PK(bu[c])])
                stf = st[:].rearrange("p h e -> p (h e)")
                for h in range(4):
                    S.op('dve', lambda e, h=h: e.scalar_tensor_tensor(
                        out=st[:, h, :], in0=st[:, h, :], scalar=float(GAM[h] ** 64), in1=ps[bu[0]][:, h * 128:(h + 1) * 128],
                        op0=ALU.mult, op1=ALU.add), reads=['st', PK(bu[0])], writes=['st'])
                S.op('act', lambda e: e.copy(out=stb[:, 1].rearrange("p h e -> p (h e)"), in_=stf), reads=['st'], writes=['stb1'])
                bo = nextbank(ALLB)
                for h in range(4):
                    S.op('pe', lambda e, h=h: e.matmul(ps[bo][:, h * 128:(h + 1) * 128], lhsT=STm[:, h * 128:(h + 1) * 128],
                                                       rhs=vb[:, h * 128:(h + 1) * 128], start=True, stop=False),
                         reads=STk + vbk, writes=[PK(bo)])
                    for c in range(2):
                        S.op('pe', lambda e, h=h, c=c: e.matmul(
                            ps[bo][c * 64:(c + 1) * 64, h * 128:(h + 1) * 128],
                            lhsT=qdT[:, h * 128 + c * 64:h * 128 + (c + 1) * 64], rhs=stb[:, c, h, :],
                            start=False, stop=True),
                             reads=qdk + [f'stb{c}'], writes=[PK(bo)])
                for h in range(4):
                    S.op('dve', lambda e, h=h: e.scalar_tensor_tensor(
                        out=st[:, h, :], in0=st[:, h, :], scalar=float(GAM[h] ** 64), in1=ps[bu[1]][:, h * 128:(h + 1) * 128],
                        op0=ALU.mult, op1=ALU.add), reads=['st', PK(bu[1])], writes=['st'])
                S.op('act', lambda e: e.copy(out=stb[:, 0].rearrange("p h e -> p (h e)"), in_=stf), reads=['st'], writes=['stb0'])
                junk, jk = tunit(0, 1)
                for h in range(4):
                    S.op('act', lambda e, h=h: e.activation(out=junk[:, 0:128], in_=ps[bo][:, h * 128:(h + 1) * 128], func=AF.Square,
                                                            accum_out=sm[:, 32 + h:33 + h]), reads=[PK(bo)], writes=['sm2'] + jk)
                rstd_from_ssq(sm[:, 32:36], 1.0 / 128, 'sm2')
                for h in range(4):
                    S.op('dve', lambda e, h=h: e.scalar_tensor_tensor(
                        out=on[:, h * 128:(h + 1) * 128], in0=ps[bo][:, h * 128:(h + 1) * 128], scalar=sm[:, 32 + h:33 + h],
                        in1=sg[:, h * 128:(h + 1) * 128], op0=ALU.mult, op1=ALU.mult),
                         reads=[PK(bo), 'sm2'] + sgk, writes=onk)
                bt = nextbank(ALLB)
                transposes(lambda c: on[:, c * 128:(c + 1) * 128], 4, bt, onk + ['const'])
                gb = _bv(pt[:, l, 32:36], [[1, 4], [0, 128]])
                S.op('dve', lambda e, tt=tt, bt=bt, gb=gb: e.tensor_tensor(
                    out=cT[:, 0:4, tt * 128:(tt + 1) * 128], in0=psb[bt][:, 0:512].rearrange("p (k n) -> p k n", k=4),
                    in1=gb, op=ALU.mult), reads=[PK(bt), 'const'], writes=[('cT', kc, tt // 4) for kc in range(4)])
            for tt in range(NT):
                ret_tile(tt)
            if stop == 'ret':
                return
            skT = WA[:, 4:6, :].rearrange("p a (b n) -> p (a b) n", b=2)
            svv = WA[:, 6:8, :].rearrange("p a (t n) -> p (a t) n", t=8)
            for (which, c0) in (("q", 2048), ("k", 2560)):
                wap, wkey = load_piece(w_in[:, c0:c0 + 512], 8, 512)
                for m in range(4):
                    for g in range(4):
                        b = nextbank(ALLB)
                        for kc in range(8):
                            S.op('pe', lambda e, m=m, g=g, kc=kc, b=b, wap=wap: e.matmul(
                                ps[b][:, :], lhsT=wap[:, kc, m * 128:(m + 1) * 128], rhs=hT[:, kc, g * 512:(g + 1) * 512],
                                start=(kc == 0), stop=(kc == 7)),
                                 reads=[wkey, ('hT', kc, g)], writes=[PK(b)])
                        if which == "q":
                            copy_op(evac_engine(), cT[:, 4 + m, g * 512:(g + 1) * 512], ps[b][:, :], [PK(b)], [('cT', 4 + m, g)])
                        else:
                            copy_op(evac_engine(), skT[:, m, g * 512:(g + 1) * 512], ps[b][:, :], [PK(b)], [('hs', 4 + m // 2)])
            wap, wkey = load_piece(w_in[:, 3072:3584], 8, 512)
            for tt in range(NT):
                b = nextbank(ALLB)
                for kc in range(8):
                    S.op('pe', lambda e, tt=tt, kc=kc, b=b, wap=wap: e.matmul(
                        ps[b][:, :], lhsT=hT[:, kc, tt * 128:(tt + 1) * 128], rhs=wap[:, kc, :], start=(kc == 0), stop=(kc == 7)),
                         reads=[wkey, ('hT', kc, tt // 4)], writes=[PK(b)])
                copy_op(evac_engine(), svv[:, tt, :], ps[b][:, :], [PK(b)], [('hs', 6 + tt // 8)])
            PSall = PS
            ZBK = [[0, 1], [2, 3]]
            CBK = [4, 5]
            ACC, SSQ = 6, 7

            def hrow(row, lo, hi, dtype=BF16):
                ap = hT[:, row, lo:hi]
                if dtype == F32:
                    ap = ap.bitcast(F32)
                return ap, [('hT', row, g) for g in range(lo // 512, (hi + 511) // 512)]
            Ebuf = [hrow(0, 0, 2048, F32), hrow(1, 0, 2048, F32)]
            Gbuf = hrow(2, 0, 2048, F32)
            spbuf = [hrow(3, 0, 1024), hrow(3, 1024, 2048)]
            Abuf = [hrow(4, 0, 1024), hrow(4, 1024, 2048)]
            Rbuf = [hrow(5, 0, 1024), hrow(5, 1024, 2048)]
            rsb = hrow(7, 512, 1536, F32)
            rawb = hrow(6, 0, 2048)
            sqb = hrow(7, 0, 512)

            def v2(ap):
                return ap.rearrange("p (c n) -> p c n", c=2)
            mstr2 = _bv(mstr[:], [[0, 2], [1, 128]])

            def sweep_group(G):
                nkb = 4 * G + 4
                items = [(hp, kb) for hp in range(4) for kb in range(nkb - 1, -1, -1)]
                n = len(items)

                def c0_of(i):
                    return max(0, items[i][1] - 4 * G) * 128

                def zmm(i):
                    hp, kb = items[i]
                    c0 = c0_of(i)
                    for ch in range(2):
                        ho = ch * 64
                        zb = ZBK[i % 2][ch]
                        S.op('pe', lambda e, ho=ho, zb=zb: e.matmul(
                            ps[zb][:, c0:512], lhsT=skT[ho:ho + 64, hp, kb * 128:(kb + 1) * 128],
                            rhs=cT[ho:ho + 64, 4 + hp, G * 512 + c0:(G + 1) * 512], start=True, stop=True),
                             reads=[('hs', 4 + hp // 2), ('cT', 4 + hp, G)], writes=[PK(zb)])

                def esp(i):
                    hp, kb = items[i]
                    c0 = c0_of(i)
                    par = i % 2
                    E, Ek = Ebuf[par]; sp, spk = spbuf[par]
                    zb = ZBK[par]
                    S.op('act', lambda e: e.activation(out=v2(E)[:, :, c0:512], in_=PSall[:, zb[0]:zb[0] + 2, c0:512], func=AF.Exp, scale=0.125),
                         reads=[PK(zb[0]), PK(zb[1])], writes=Ek)
                    S.op('act', lambda e: e.activation(out=v2(sp)[:, :, c0:512], in_=v2(E)[:, :, c0:512], func=AF.Ln, bias=1.0, scale=1.0),
                         reads=Ek, writes=spk)
                    if kb >= 4 * G:
                        S.op('dve', lambda e: e.tensor_tensor(out=v2(sp)[:, :, c0:c0 + 128], in0=v2(sp)[:, :, c0:c0 + 128], in1=mstr2, op=ALU.mult),
                             reads=spk + ['const'], writes=spk)
                    first = (kb == nkb - 1)
                    R, Rk = Rbuf[par]; Rn, Rnk = Rbuf[1 - par]
                    if kb > 0:
                        if kb >= 4 * G:
                            S.op('dve', lambda e: e.memset(Rn, 0.0), writes=Rnk)
                        if first:
                            S.op('dve', lambda e: e.tensor_copy(out=v2(Rn)[:, :, c0:512], in_=v2(sp)[:, :, c0:512]), reads=spk, writes=Rnk)
                        else:
                            S.op('dve', lambda e: e.tensor_tensor(out=v2(Rn)[:, :, c0:512], in0=v2(R)[:, :, c0:512], in1=v2(sp)[:, :, c0:512], op=ALU.add),
                                 reads=Rk + spk, writes=Rnk)

                def cmm(i):
                    hp, kb = items[i]
                    c0 = c0_of(i)
                    first = (kb == nkb - 1)
                    sp, spk = spbuf[i % 2]; R, Rk = Rbuf[i % 2]
                    for ch in range(2):
                        S.op('pe', lambda e, ch=ch: e.matmul(ps[CBK[ch]][:, c0:512], lhsT=tri[:], rhs=v2(sp)[:, ch, c0:512], start=True, stop=first),
                             reads=spk + ['const'], writes=[PK(CBK[ch])])
                        if not first:
                            S.op('pe', lambda e, ch=ch: e.matmul(ps[CBK[ch]][:, c0:512], lhsT=ones[:], rhs=v2(R)[:, ch, c0:512], start=False, stop=True),
                                 reads=Rk + ['const'], writes=[PK(CBK[ch])])

                def rest(i):
                    hp, kb = items[i]
                    c0 = c0_of(i)
                    par = i % 2
                    first = (kb == nkb - 1)
                    E, Ek = Ebuf[par]; sp, spk = spbuf[par]; Gt, Gk = Gbuf; A, Akk = Abuf[par]; R, Rk = Rbuf[par]; Rn, Rnk = Rbuf[1 - par]
                    S.op('act', lambda e: e.activation(out=v2(Gt)[:, :, c0:512], in_=PSall[:, CBK[0]:CBK[0] + 2, c0:512], func=AF.Exp, scale=-1.0),
                         reads=[PK(CBK[0]), PK(CBK[1])], writes=Gk)
                    S.op('dve', lambda e: e.tensor_tensor(out=v2(A)[:, :, c0:512], in0=v2(E)[:, :, c0:512], in1=v2(Gt)[:, :, c0:512], op=ALU.mult),
                         reads=Ek + Gk, writes=Akk)
                    if kb >= 4 * G:
                        S.op('pool', lambda e: e.tensor_tensor(out=v2(A)[:, :, c0:c0 + 128], in0=v2(A)[:, :, c0:c0 + 128], in1=mstr2, op=ALU.mult),
                             reads=Akk + ['const'], writes=Akk)

                def avmm(i):
                    hp, kb = items[i]
                    c0 = c0_of(i)
                    first = (kb == nkb - 1)
                    A, Akk = Abuf[i % 2]
                    if first:
                        S.op('pe', lambda e: e.matmul(ps[ACC][:, :], lhsT=zer[:], rhs=dmk[:], start=True, stop=False),
                             reads=['const'], writes=[PK(ACC)])
                    for ch in range(2):
                        h = 2 * hp + ch
                        S.op('pe', lambda e, ch=ch, h=h: e.matmul(
                            ps[ACC][ch * 64:(ch + 1) * 64, c0:512], lhsT=svv[:, kb, h * 64:(h + 1) * 64], rhs=v2(A)[:, ch, c0:512],
                            start=False, stop=(kb == 0), skip_group_check=True),
                             reads=Akk + [('hs', 6 + kb // 8)], writes=[PK(ACC)])
                    if kb == 0:
                        raw, rawk = rawb; sq, sqk = sqb
                        S.op('act', lambda e: e.copy(out=raw[:, hp * 512:(hp + 1) * 512], in_=ps[ACC][:, :]), reads=[PK(ACC)], writes=[rawk[hp]])
                        S.op('act', lambda e: e.activation(out=sq, in_=ps[ACC][:, :], func=AF.Square), reads=[PK(ACC)], writes=sqk)
                        S.op('pe', lambda e: e.matmul(ps[SSQ][:, :], lhsT=ones[:], rhs=sq, start=(hp == 0), stop=(hp == 3)),
                             reads=sqk + ['const'], writes=[PK(SSQ)])

                zmm(0)
                if n > 1:
                    zmm(1)
                esp(0)
                cmm(0)
                for i in range(n):
                    if i + 1 < n:
                        esp(i + 1)
                    rest(i)
                    if i + 1 < n:
                        cmm(i + 1)
                    if i + 2 < n:
                        zmm(i + 2)
                    avmm(i)
                rs, rsk = rsb; raw, rawk = rawb
                S.op('dve', lambda e: e.tensor_scalar(out=rs, in0=ps[SSQ][:, :], scalar1=1.0 / 512, scalar2=EPS, op0=ALU.mult, op1=ALU.add),
                     reads=[PK(SSQ)], writes=rsk)
                S.op('act', lambda e: e.activation(out=rs, in_=rs, func=AF.Sqrt), reads=rsk, writes=rsk)
                S.op('dve', lambda e: e.reciprocal(out=rs, in_=rs), reads=rsk, writes=rsk)
                for hp in range(4):
                    S.op('dve', lambda e, hp=hp: e.scalar_tensor_tensor(
                        out=cT[:, 4 + hp, G * 512:(G + 1) * 512], in0=raw[:, hp * 512:(hp + 1) * 512], scalar=pt[:, l, 36 + hp:37 + hp],
                        in1=rs, op0=ALU.mult, op1=ALU.mult), reads=[rawk[hp], 'const'] + rsk, writes=[('cT', 4 + hp, G)])
            for G in range(4):
                sweep_group(G)
            if stop == 'sb':
                return
            wm = D["w_mix_out"][l]
            gemm_tm(lambda kc, tt: cT[:, kc, tt * 128:(tt + 1) * 128], lambda kc, tt: [('cT', kc, tt // 4)],
                    lambda ns: [(wm[:, ns * 512:(ns + 1) * 512], 8)], 2, resid_add)

        def cross_block(l):
            norm_to_hT(l, 8)
            memf = WA[:, 7, 0:4096].bitcast(F32).rearrange("p (t d) -> p t d", t=2)
            S.dma('sp', lambda e: e.dma_start(out=memf, in_=D["mem"].rearrange("(t p) d -> p t d", p=128)), writes=[('hs', 7)])
            junk, jk = tunit(0, 2)
            for t in range(2):
                S.op('act', lambda e, t=t: e.activation(out=junk, in_=memf[:, t, :], func=AF.Square, accum_out=sm[:, 64 + t:65 + t]),
                     reads=[('hs', 7)], writes=['sm4'] + jk)
            rstd_from_ssq(sm[:, 64:66], 1.0 / DM, 'sm4')
            mnT = WA[:, 4, 0:2048].rearrange("p (k n) -> p k n", k=8)
            for t in range(2):
                hn, hk = tunit(2 + 2 * t, 2)
                S.op('dve', lambda e, t=t, hn=hn: e.tensor_scalar(out=hn, in0=memf[:, t, :], scalar1=sm[:, 64 + t:65 + t], scalar2=None, op0=ALU.mult),
                     reads=[('hs', 7), 'sm4'], writes=hk)
                b = nextbank(ALLB)
                transposes(lambda c, hn=hn: hn[:, c * 128:(c + 1) * 128], 8, b, hk + ['const'])
                gb = _bv(pt[:, l, 16:24], [[1, 8], [0, 128]])
                S.op('dve', lambda e, t=t, b=b, gb=gb: e.tensor_tensor(out=mnT[:, :, t * 128:(t + 1) * 128],
                                                                       in0=psb[b][:, :].rearrange("p (k n) -> p k n", k=8), in1=gb, op=ALU.mult),
                     reads=[PK(b), 'const'], writes=[('hs', 4)])
            kTm = WA[:, 6, 0:2048].rearrange("p (k n) -> p k n", k=8)
            vm = WA[:, 6, 2048:4096].rearrange("p (t n) -> p t n", t=2)
            wkv = D["w_xkv"][l]
            def kv_epi(tt, ns, b):
                if ns < 2:
                    kn, knk = tunit(6 + tt % 2, 1)
                    junk1, jk1 = tunit(0, 1)
                    for hh in range(2):
                        col = 66 + (tt * 4 + ns * 2 + hh)
                        S.op('act', lambda e, hh=hh, col=col: e.activation(out=junk1[:, 0:256], in_=ps[b][:, hh * 256:(hh + 1) * 256], func=AF.Square,
                                                                           accum_out=sm[:, col:col + 1]), reads=[PK(b)], writes=[('sm5', tt, ns)] + jk1)
                    c0 = 66 + tt * 4 + ns * 2
                    rstd_from_ssq(sm[:, c0:c0 + 2], 1.0 / 256, ('sm5', tt, ns))
                    for hh in range(2):
                        S.op('dve', lambda e, hh=hh, kn=kn: e.tensor_scalar(out=kn[:, hh * 256:(hh + 1) * 256], in0=ps[b][:, hh * 256:(hh + 1) * 256],
                                                                            scalar1=sm[:, c0 + hh:c0 + hh + 1], scalar2=None, op0=ALU.mult),
                             reads=[PK(b), ('sm5', tt, ns)], writes=knk)
                    bt = nextbank(ALLB)
                    transposes(lambda c, kn=kn: kn[:, c * 128:(c + 1) * 128], 4, bt, knk + ['const'])
                    gb = _bv(pt[:, l, 42:44], [[0, 2], [1, 2], [0, 128]])
                    S.op('dve', lambda e, bt=bt, gb=gb: e.tensor_tensor(
                        out=kTm[:, ns * 4:(ns + 1) * 4, tt * 128:(tt + 1) * 128].rearrange("p (a c) n -> p a c n", a=2),
                        in0=psb[bt][:, 0:512].rearrange("p (a c n) -> p a c n", a=2, c=2), in1=gb, op=ALU.mult),
                         reads=[PK(bt), 'const'], writes=[('hs', 6)])
                else:
                    copy_op('act', vm[:, tt, (ns - 2) * 512:(ns - 1) * 512], ps[b][:, :], [PK(b)], [('hs', 6)])
            gemm_tm(lambda kc, tt: mnT[:, kc, tt * 128:(tt + 1) * 128], lambda kc, tt: [('hs', 4)],
                    lambda ns: [(wkv[:, ns * 512:(ns + 1) * 512], 8)], 4, kv_epi, ntt=2)
            wq = D["w_xq"][l]
            qT = WA[:, 4:6, :].rearrange("p a (b n) -> p (a b) n", b=2)
            for hpair in range(2):
                def q_epi(tt, ns, b):
                    qn, qnk = tunit(6 + tt % 2, 1)
                    junk1, jk1 = tunit(0, 1)
                    for hh in range(2):
                        S.op('act', lambda e, hh=hh: e.activation(out=junk1[:, 0:256], in_=ps[b][:, hh * 256:(hh + 1) * 256], func=AF.Square,
                                                                  accum_out=sm[:, 80 + 2 * (tt % 2) + hh:81 + 2 * (tt % 2) + hh]),
                             reads=[PK(b)], writes=[('sm6', tt % 2)] + jk1)
                    c0 = 80 + 2 * (tt % 2)
                    rstd_from_ssq(sm[:, c0:c0 + 2], 1.0 / 256, ('sm6', tt % 2))
                    for hh in range(2):
                        S.op('dve', lambda e, hh=hh, qn=qn: e.tensor_scalar(out=qn[:, hh * 256:(hh + 1) * 256], in0=ps[b][:, hh * 256:(hh + 1) * 256],
                                                                            scalar1=sm[:, c0 + hh:c0 + hh + 1], scalar2=None, op0=ALU.mult),
                             reads=[PK(b), ('sm6', tt % 2)], writes=qnk)
                    bt = nextbank(ALLB)
                    transposes(lambda c, qn=qn: qn[:, c * 128:(c + 1) * 128], 4, bt, qnk + ['const'])
                    gb = _bv(pt[:, l, 40:42], [[0, 2], [1, 2], [0, 128]])
                    S.op('dve', lambda e, bt=bt, gb=gb, tt=tt: e.tensor_tensor(
                        out=qT[:, :, tt * 128:(tt + 1) * 128].rearrange("p (a c) n -> p a c n", a=2),
                        in0=psb[bt][:, 0:512].rearrange("p (a c n) -> p a c n", a=2, c=2), in1=gb, op=ALU.mult),
                         reads=[PK(bt), 'const'], writes=[('hs', 4 + (c // 2)) for c in range(4)])
                gemm_tm(lambda kc, tt: hT[:, kc, tt * 128:(tt + 1) * 128], lambda kc, tt: [('hT', kc, tt // 4)],
                        lambda ns: [(wq[:, hpair * 512:(hpair + 1) * 512], 8)], 1, q_epi)
                for hh in range(2):
                    h = hpair * 2 + hh
                    def attn(g, h=h, hh=hh):
                        pT, pTk = tunit(2 + 2 * (g % 2), 2)
                        bsc = [nextbank(ALLB), nextbank(ALLB)]
                        for mc in range(2):
                            for c in range(2):
                                S.op('pe', lambda e, mc=mc, c=c: e.matmul(
                                    ps[bsc[mc]][:, :], lhsT=kTm[:, h * 2 + c, mc * 128:(mc + 1) * 128],
                                    rhs=qT[:, hh * 2 + c, g * 512:(g + 1) * 512], start=(c == 0), stop=(c == 1)),
                                     reads=[('hs', 6), ('hs', 4 + hh)], writes=[PK(bsc[mc])])
                            S.op('act', lambda e, mc=mc, pT=pT: e.activation(out=pT[:, mc * 512:(mc + 1) * 512], in_=ps[bsc[mc]][:, :], func=AF.Exp, scale=1.0 / 16),
                                 reads=[PK(bsc[mc])], writes=pTk)
                        bd = nextbank(ALLB)
                        for mc in range(2):
                            S.op('pe', lambda e, mc=mc, pT=pT: e.matmul(ps[bd][:, :], lhsT=ones[:], rhs=pT[:, mc * 512:(mc + 1) * 512],
                                                                       start=(mc == 0), stop=(mc == 1)), reads=pTk + ['const'], writes=[PK(bd)])
                        rden, rdk = tunit(0, 2, F32)
                        S.op('dve', lambda e, rden=rden: e.reciprocal(out=rden, in_=ps[bd][:, :]), reads=[PK(bd)], writes=rdk)
                        for c in range(2):
                            bo = nextbank(ALLB)
                            for mc in range(2):
                                S.op('pe', lambda e, mc=mc, c=c, pT=pT, bo=bo: e.matmul(
                                    ps[bo][:, :], lhsT=vm[:, mc, h * 256 + c * 128:h * 256 + (c + 1) * 128],
                                    rhs=pT[:, mc * 512:(mc + 1) * 512], start=(mc == 0), stop=(mc == 1)),
                                     reads=pTk + [('hs', 6)], writes=[PK(bo)])
                            S.op('dve', lambda e, c=c, bo=bo, rden=rden: e.tensor_tensor(
                                out=cT[:, h * 2 + c, g * 512:(g + 1) * 512], in0=ps[bo][:, :], in1=rden, op=ALU.mult),
                                 reads=[PK(bo)] + rdk, writes=[('cT', h * 2 + c, g)])
                    for g in range(4):
                        attn(g)
            if stop == 'xattn':
                return
            wo = D["w_xo"][l]
            gemm_tm(lambda kc, tt: cT[:, kc, tt * 128:(tt + 1) * 128], lambda kc, tt: [('cT', kc, tt // 4)],
                    lambda ns: [(wo[:, ns * 512:(ns + 1) * 512], 8)], 2, resid_add)

        def mlp_block(l):
            norm_to_hT(l, 24)
            ring["slots"] = [0, 1, 2, 4, 5, 6, 7]
            wu = D["w_up"][l]; wd = D["w_down"][l]
            for fs in range(4):
                for half in range(2):
                    wap, wkey = load_piece(wu[:, fs * 1024 + half * 512:fs * 1024 + (half + 1) * 512], 8, 512)
                    for m in range(4):
                        fc = half * 4 + m
                        for g in range(4):
                            b = nextbank(ALLB)
                            for kc in range(8):
                                S.op('pe', lambda e, m=m, g=g, kc=kc, b=b, wap=wap: e.matmul(
                                    ps[b][:, :], lhsT=wap[:, kc, m * 128:(m + 1) * 128], rhs=hT[:, kc, g * 512:(g + 1) * 512],
                                    start=(kc == 0), stop=(kc == 7)), reads=[wkey, ('hT', kc, g)], writes=[PK(b)])
                            r, rk = tunit((fc * 4 + g) % 4, 1)
                            S.op('act', lambda e, b=b, r=r: e.activation(out=r, in_=ps[b][:, :], func=AF.Relu), reads=[PK(b)], writes=rk)
                            S.op('pool', lambda e, r=r, fc=fc, g=g: e.tensor_tensor(out=cT[:, fc, g * 512:(g + 1) * 512], in0=r, in1=r, op=ALU.mult),
                                 reads=rk, writes=[('cT', fc, g)])
                gemm_tm(lambda kc, tt: cT[:, kc, tt * 128:(tt + 1) * 128], lambda kc, tt: [('cT', kc, tt // 4)],
                        lambda ns: [(wd[fs * 1024:(fs + 1) * 1024, ns * 512:(ns + 1) * 512], 8)], 2, resid_add)
            ring["slots"] = [0, 1, 2]

        done = False
        for l in range(n_layers):
            mix_block(l)
            if stop in ('ret', 'sb') or stop == f'mix{l}':
                break
            cross_block(l)
            if stop == 'xattn' or stop == f'cross{l}':
                break
            mlp_block(l)
            if stop == f'mlp{l}':
                break

        if stop in ('ret', 'sb', 'xattn'):
            for kc in range(4 if stop == 'ret' else 8):
                for g in range(4):
                    S.op('dve', lambda e, kc=kc, g=g: e.tensor_copy(out=xs[:, kc * 2 + g // 2, (g % 2) * 512:(g % 2 + 1) * 512],
                                                                    in_=cT[:, kc, g * 512:(g + 1) * 512]),
                         reads=[('cT', kc, g)], writes=[('xs', kc * 2 + g // 2)])
        for tt in range(NT):
            S.dma('sp', lambda e, tt=tt: e.dma_start(out=out_d[tt * 128:(tt + 1) * 128, :], in_=xs[:, tt, :]),
                  reads=[('xs', tt)], writes=[('out', tt)], final=True)
        S.emit()
    return nc


_CONSTS = None


def make_in_maps(inputs, n_cores=8):
    global _CONSTS
    if _CONSTS is None:
        _CONSTS = _const_tables()
    f = lambda a: np.ascontiguousarray(np.asarray(a, dtype=np.float32))
    pt = _param_table(*[np.asarray(inputs[k], dtype=np.float32) for k in
                        ("g_mix", "g_cross", "g_mem", "g_mlp", "g_ret_out", "g_sb_out", "g_qn", "g_kn")])
    shared = {k: f(inputs[k]) for k in ("w_in", "w_mix_out", "w_xq", "w_xkv", "w_xo", "w_up", "w_down")}
    shared["pt"] = pt
    shared.update(_CONSTS)
    x = f(inputs["x"]); mem = f(inputs["mem"])
    maps = []
    for c in range(n_cores):
        m = dict(shared)
        m["x"] = x[c]
        m["mem"] = mem[c]
        maps.append(m)
    return maps


def kernel(**inputs):
    nc = build_program()
    maps = make_in_maps(inputs, 8)
    res = run_bass_kernel_spmd(nc, maps, core_ids=list(range(8)))
    return np.stack([np.asarray(r["out"], dtype=np.float32) for r in res.results], axis=0)
```

```python
import contextlib
import numpy as np
import concourse.bass as bass
import concourse.mybir as mybir
from concourse.bass_utils import run_bass_kernel_spmd

dt = mybir.dt
F32, BF16 = dt.float32, dt.bfloat16
AF = mybir.ActivationFunctionType
ALU = mybir.AluOpType
AX = mybir.AxisListType

ENGS = ('pe', 'act', 'dve', 'pool', 'sp')


class Sched:
    EPOCH = 20000
    NDMA = 8

    def __init__(self, nc):
        self.nc = nc
        self.stack = contextlib.ExitStack()
        self.ops = {e: [] for e in ENGS}
        self.ncomp = {e: 0 for e in ENGS}
        self.esem = {e: {} for e in ENGS}
        self.sems = []
        self.lastw = {}
        self.readers = {}
        self.seen = {e: {} for e in ENGS}
        self.dsem = []
        self.dnext = 0
        self.finals = []

    def __enter__(self):
        self.stack.__enter__()
        return self

    def __exit__(self, *a):
        return self.stack.__exit__(*a)

    def sbuf(self, name, shape, dtype):
        return self.stack.enter_context(self.nc.sbuf_tensor(name, list(shape), dtype))

    def psum(self, name, shape, dtype):
        return self.stack.enter_context(self.nc.psum_tensor(name, list(shape), dtype))

    def _newsem(self, name):
        s = self.stack.enter_context(self.nc.semaphore(name))
        self.sems.append(s)
        return len(self.sems) - 1

    def _deps(self, eng, reads, writes, is_dma):
        deps = set()
        for k in reads:
            t = self.lastw.get(k)
            if t is not None:
                deps.add(t)
        for k in writes:
            t = self.lastw.get(k)
            if t is not None:
                deps.add(t)
            for t in self.readers.get(k, ()):
                if t[2] == eng and not is_dma:
                    continue
                deps.add(t)
        return deps

    def _waits(self, eng, deps, is_dma):
        waits = []
        seen = self.seen[eng]
        for (sid, v, deng) in sorted(deps):
            if deng == eng and eng == 'pe' and not is_dma:
                continue
            if seen.get(sid, 0) >= v:
                continue
            seen[sid] = v
            waits.append((sid, v))
        return waits

    def _commit(self, tok, reads, writes):
        for k in reads:
            self.readers.setdefault(k, []).append(tok)
        for k in writes:
            self.lastw[k] = tok
            self.readers[k] = []

    def op(self, eng, fn, reads=(), writes=()):
        deps = self._deps(eng, reads, writes, False)
        waits = self._waits(eng, deps, False)
        idx = self.ncomp[eng]
        ep = idx // self.EPOCH
        if ep not in self.esem[eng]:
            self.esem[eng][ep] = self._newsem(f"s_{eng}_{ep}")
        sid = self.esem[eng][ep]
        tok = (sid, idx % self.EPOCH + 1, eng)
        self.ncomp[eng] += 1
        self._commit(tok, reads, writes)
        self.ops[eng].append((fn, waits, sid, 1))
        return tok

    def dma(self, eng, fn, reads=(), writes=(), final=False):
        if eng == 'pool':
            return self._dma_sw(fn, reads, writes)
        deps = self._deps(eng, reads, writes, True)
        if len(self.dsem) < self.NDMA:
            self.dsem.append([self._newsem(f"s_dma_{len(self.dsem)}"), 0, None])
        ent = self.dsem[self.dnext % self.NDMA]
        self.dnext += 1
        if ent[2] is not None:
            deps.add(ent[2])
        waits = self._waits(eng, deps, True)
        ent[1] += 16
        tok = (ent[0], ent[1], 'dma')
        ent[2] = tok
        self._commit(tok, reads, writes)
        self.ops[eng].append((fn, waits, ent[0], 16))
        if final:
            self.finals.append(tok)
        return tok

    def _dma_sw(self, fn, reads, writes):
        deps = self._deps('pool', reads, writes, True)
        waits = self._waits('pool', deps, True)
        if not hasattr(self, 'nsw'):
            self.nsw = 0
        sid = self._newsem(f"s_sw_{self.nsw}")
        self.nsw += 1
        tok = (sid, 16, 'dma')
        self._commit(tok, reads, writes)
        self.ops['pool'].append((fn, waits, sid, 16))
        return tok

    def emit(self):
        nc = self.nc
        fin = [(sid, v) for (sid, v, _) in self.finals]
        sems = self.sems
        ops = self.ops

        def replay(name, e, tail=()):
            for ent in ops[name]:
                if ent[0] == 'relay':
                    e.wait_ge(sems[ent[1]], 16)
                    e.sem_clear(sems[ent[1]])
                    e.sem_inc(sems[ent[2]], 1)
                    continue
                (fn, waits, sid, n) = ent
                for (ws, wv) in waits:
                    e.wait_ge(sems[ws], wv)
                fn(e).then_inc(sems[sid], n)
            for (ws, wv) in tail:
                e.wait_ge(sems[ws], wv)

        with nc.Block() as block:
            @block.sync
            def _(e):
                replay('sp', e, fin)

            @block.tensor
            def _(e):
                replay('pe', e)

            @block.vector
            def _(e):
                replay('dve', e)

            @block.scalar
            def _(e):
                replay('act', e)

            @block.gpsimd
            def _(e):
                replay('pool', e)


SEQ, DM, DEPTH = 2048, 1024, 2
NT = 16
EPS = 1e-6
NPC = 44
GAM = [1.0 - 2.0 ** (-5 - h) for h in range(4)]


def _const_tables():
    p = np.arange(128)
    c = {}
    c["ident"] = np.eye(128, dtype=np.float32)
    c["tri"] = (p[:, None] >= p[None, :]).astype(np.float32)
    c["ones"] = np.ones((128, 128), np.float32)
    c["mstr"] = (p[:, None] < p[None, :]).astype(np.float32)
    inv = (1.0 / (np.float32(10000.0) ** np.linspace(0.0, 1.0, 64, dtype=np.float32))).astype(np.float32)
    pos = np.arange(SEQ, dtype=np.float32)
    ang = (pos[:, None] * inv[None, :]).astype(np.float32)
    c["cos"] = np.cos(ang.astype(np.float64)).astype(np.float32).reshape(NT, 128, 64).transpose(1, 0, 2).copy()
    c["sin"] = np.sin(ang.astype(np.float64)).astype(np.float32).reshape(NT, 128, 64).transpose(1, 0, 2).copy()
    lg = np.log1p(-np.exp2(-5.0 - np.arange(4, dtype=np.float64)))
    dist = np.abs(p[:, None] - p[None, :]).astype(np.float64)
    same = (p[:, None] // 64) == (p[None, :] // 64)
    dm = np.zeros((128, 4, 128), np.float64)
    qd = np.zeros((128, 4, 128), np.float64)
    kd = np.zeros((128, 4, 128), np.float64)
    for h in range(4):
        dm[:, h, :] = np.where(same, np.exp(lg[h] * dist), 0.0) * (128.0 ** -0.5)
        qd[:, h, :] = np.exp(lg[h] * ((p % 64) + 1.0))[None, :]
        kd[:, h, :] = (np.exp(lg[h] * (63.0 - (p % 64))) * (128.0 ** -0.5))[:, None]
    c["dmk"] = dm.reshape(128, 512).astype(np.float32)
    c["qdc"] = qd.reshape(128, 512).astype(np.float32)
    c["kdc"] = kd.reshape(128, 512).astype(np.float32)
    return c


def _param_table(g_mix, g_cross, g_mem, g_mlp, g_ret_out, g_sb_out, g_qn, g_kn):
    pt = np.zeros((128, DEPTH, NPC), np.float32)
    for l in range(DEPTH):
        pt[:, l, 0:8] = g_mix[l].reshape(8, 128).T
        pt[:, l, 8:16] = g_cross[l].reshape(8, 128).T
        pt[:, l, 16:24] = g_mem[l].reshape(8, 128).T
        pt[:, l, 24:32] = g_mlp[l].reshape(8, 128).T
        pt[:, l, 32:36] = g_ret_out[l].reshape(4, 128).T
        pt[:, l, 36:40] = g_sb_out[l].reshape(4, 128).T
        pt[:, l, 40:42] = g_qn[l].reshape(2, 128).T
        pt[:, l, 42:44] = g_kn[l].reshape(2, 128).T
    return pt


def _bv(ap, dims):
    return bass.AP(ap.tensor, ap.offset, [list(ap.ap[0])] + [list(d) for d in dims])


def build_program(n_layers=DEPTH, stop=None):
    nc = bass.Bass("TRN2", target_bir_lowering=False)
    D = {}
    def din(name, shape):
        D[name] = nc.dram_tensor(name, list(shape), F32, kind="ExternalInput").ap()
    din("x", [SEQ, DM]); din("mem", [256, DM])
    din("w_in", [DEPTH, DM, 3584]); din("w_mix_out", [DEPTH, DM, DM]); din("w_xq", [DEPTH, DM, DM])
    din("w_xkv", [DEPTH, DM, 2 * DM]); din("w_xo", [DEPTH, DM, DM]); din("w_up", [DEPTH, DM, 4 * DM])
    din("w_down", [DEPTH, 4 * DM, DM]); din("pt", [128, DEPTH, NPC])
    for nm in ("ident", "tri", "ones", "mstr"):
        din(nm, [128, 128])
    din("cos", [128, NT, 64]); din("sin", [128, NT, 64])
    for nm in ("dmk", "qdc", "kdc"):
        din(nm, [128, 512])
    out_d = nc.dram_tensor("out", [SEQ, DM], F32, kind="ExternalOutput").ap()

    S = Sched(nc)
    with S:
        xs = S.sbuf("xs", [128, NT, DM], F32)
        hT = S.sbuf("hT", [128, 8, SEQ], BF16)
        cT = S.sbuf("cT", [128, 8, SEQ], BF16)
        WA = S.sbuf("WA", [128, 8, 4096], BF16)
        ident = S.sbuf("ident_s", [128, 128], BF16)
        tri = S.sbuf("tri_s", [128, 128], BF16)
        ones = S.sbuf("ones_s", [128, 128], BF16)
        mstr = S.sbuf("mstr_s", [128, 128], BF16)
        zer = S.sbuf("zer_s", [128, 128], BF16)
        cosT = S.sbuf("cos_s", [128, NT, 64], BF16)
        sinT = S.sbuf("sin_s", [128, NT, 64], BF16)
        dmk = S.sbuf("dmk_s", [128, 512], BF16)
        qdc = S.sbuf("qdc_s", [128, 512], BF16)
        kdc = S.sbuf("kdc_s", [128, 512], BF16)
        pt = S.sbuf("pt_s", [128, DEPTH, NPC], F32)
        st = S.sbuf("st_s", [128, 4, 128], F32)
        stb = S.sbuf("stb_s", [128, 2, 4, 128], BF16)
        sm = S.sbuf("sm_s", [128, 128], F32)
        PS = S.psum("ps", [128, 8, 512], F32)
        ps = [PS[:, i, :] for i in range(8)]
        psb = [p.bitcast(BF16) for p in ps]

        rr = {"dq": 0, "bank": 0, "ring": 0, "ev": 0}

        def dq():
            rr["dq"] += 1
            return 'sp' if rr["dq"] % 2 else 'act'

        def PK(b):
            return ('ps', b)

        def unit(buf, name, u, n=1, dtype=BF16):
            kc, g = u // 4, u % 4
            assert g + n <= 4
            ap = buf[:, kc, g * 512:(g + n) * 512]
            if dtype == F32:
                ap = ap.bitcast(F32)
            return ap, [(name, kc, g + i) for i in range(n)]

        def tunit(i, n=1, dtype=BF16):
            ap = WA[:, 3, i * 512:(i + n) * 512]
            if dtype == F32:
                ap = ap.bitcast(F32)
            return ap, [('t', i + k) for k in range(n)]

        def nextbank(allowed):
            rr["bank"] += 1
            return allowed[rr["bank"] % len(allowed)]

        ring = {"slots": [0, 1, 2]}

        def load_piece(w2d, nk, ncols):
            assert nk * ncols <= 4096
            rr["ring"] += 1
            s = ring["slots"][rr["ring"] % len(ring["slots"])]
            dst = WA[:, s, 0:nk * ncols].rearrange("p (k n) -> p k n", k=nk)
            src = w2d.rearrange("(k p) n -> p k n", p=128)
            S.dma('pool', lambda e: e.dma_start(out=dst, in_=src), reads=[], writes=[('hs', s)])
            return dst, ('hs', s)

        def evac_engine():
            rr["ev"] += 1
            return 'act' if rr["ev"] % 2 else 'dve'

        def copy_op(eng, out, in_, reads, writes):
            if eng == 'act':
                S.op('act', lambda e: e.copy(out=out, in_=in_), reads=reads, writes=writes)
            else:
                S.op(eng, lambda e: e.tensor_copy(out=out, in_=in_), reads=reads, writes=writes)

        def rstd_from_ssq(ssq_ap, inv_n, key):
            S.op('dve', lambda e: e.tensor_scalar(out=ssq_ap, in0=ssq_ap, scalar1=inv_n, scalar2=EPS,
                                                  op0=ALU.mult, op1=ALU.add), reads=[key], writes=[key])
            S.op('act', lambda e: e.activation(out=ssq_ap, in_=ssq_ap, func=AF.Sqrt), reads=[key], writes=[key])
            S.op('dve', lambda e: e.reciprocal(out=ssq_ap, in_=ssq_ap), reads=[key], writes=[key])

        def transposes(src_fn, n, bank, reads):
            for c in range(n):
                S.op('pe', lambda e, c=c: e.transpose(out=psb[bank][:, c * 128:(c + 1) * 128], in_=src_fn(c),
                                                      identity=ident[:]),
                     reads=reads, writes=[PK(bank)])

        for tt in range(NT):
            S.dma('sp', lambda e, tt=tt: e.dma_start(out=xs[:, tt, :], in_=D["x"][tt * 128:(tt + 1) * 128, :]),
                  writes=[('xs', tt)])
        for (sb_t, nm) in ((ident, "ident"), (tri, "tri"), (ones, "ones"), (mstr, "mstr"), (cosT, "cos"),
                           (sinT, "sin"), (dmk, "dmk"), (qdc, "qdc"), (kdc, "kdc")):
            S.dma('pool', lambda e, sb_t=sb_t, nm=nm: e.dma_start(out=sb_t[:], in_=D[nm]), writes=['const'])
        S.dma('sp', lambda e: e.dma_start(out=pt[:], in_=D["pt"]), writes=['const'])
        S.op('pool', lambda e: e.memset(zer[:], 0.0), writes=['const'])

        ALLB = list(range(8))

        def norm_to_hT(l, gcol):
            junk, jk = tunit(0, 2)
            for tt in range(NT):
                S.op('act', lambda e, tt=tt: e.activation(out=junk, in_=xs[:, tt, :], func=AF.Square,
                                                          accum_out=sm[:, tt:tt + 1]),
                     reads=[('xs', tt)], writes=['sm'] + jk)
            rstd_from_ssq(sm[:, 0:NT], 1.0 / DM, 'sm')
            for tt in range(NT):
                hn, hk = tunit(2 + 2 * (tt % 2), 2)
                S.op('dve', lambda e, tt=tt, hn=hn: e.tensor_scalar(out=hn, in0=xs[:, tt, :], scalar1=sm[:, tt:tt + 1],
                                                                    scalar2=None, op0=ALU.mult),
                     reads=[('xs', tt), 'sm'], writes=hk)
                b = nextbank(ALLB)
                transposes(lambda c, hn=hn: hn[:, c * 128:(c + 1) * 128], 8, b, hk + ['const'])
                gsl = pt[:, l, gcol:gcol + 8]
                gb = _bv(gsl, [[1, 8], [0, 128]])
                S.op('dve', lambda e, tt=tt, b=b, gb=gb: e.tensor_tensor(
                    out=hT[:, :, tt * 128:(tt + 1) * 128],
                    in0=psb[b][:, :].rearrange("p (k n) -> p k n", k=8), in1=gb, op=ALU.mult),
                     reads=[PK(b), 'const'], writes=[('hT', kc, tt // 4) for kc in range(8)])

        def gemm_tm(lhs_fn, lhs_keys_fn, pieces_fn, n_slices, epilogue, banks=ALLB, ntt=NT):
            for ns in range(n_slices):
                pcs = [load_piece(w2d, nk, 512) for (w2d, nk) in pieces_fn(ns)]
                for tt in range(ntt):
                    b = nextbank(banks)
                    tot = sum(p[0].shape[1] for p in pcs)
                    i = 0
                    for pi, (wap, wkey) in enumerate(pcs):
                        for k in range(wap.shape[1]):
                            kc = sum(p[0].shape[1] for p in pcs[:pi]) + k
                            S.op('pe', lambda e, wap=wap, k=k, kc=kc, i=i, b=b, tt=tt: e.matmul(
                                ps[b][:, :], lhsT=lhs_fn(kc, tt), rhs=wap[:, k, :], start=(i == 0), stop=(i == tot - 1)),
                                 reads=[wkey] + lhs_keys_fn(kc, tt), writes=[PK(b)])
                            i += 1
                    epilogue(tt, ns, b)

        def resid_add(tt, ns, b):
            S.op('dve', lambda e: e.tensor_tensor(out=xs[:, tt, ns * 512:(ns + 1) * 512], in0=ps[b][:, :],
                                                  in1=xs[:, tt, ns * 512:(ns + 1) * 512], op=ALU.add),
                 reads=[PK(b), ('xs', tt)], writes=[('xs', tt)])

        def mix_block(l):
            w_in = D["w_in"][l]
            norm_to_hT(l, 0)
            for j in range(4):
                dst = WA[:, 4 + j, :].rearrange("p (k n) -> p k n", k=8)
                src = w_in[:, j * 512:(j + 1) * 512].rearrange("(k p) n -> p k n", p=128)
                S.dma('pool', lambda e, dst=dst, src=src: e.dma_start(out=dst, in_=src), writes=[('hs', 4 + j)])
            S.op('pool', lambda e: e.memset(st[:], 0.0), writes=['st'])
            S.op('pool', lambda e: e.memset(stb[:], 0.0), writes=['stb0', 'stb1'])
            def cu(i, n=1, dtype=BF16):
                return unit(cT, 'cT', 16 + i, n, dtype)
            def ret_tile(tt):
                pb = [nextbank(ALLB) for _ in range(4)]
                for j in range(4):
                    wv = WA[:, 4 + j, :].rearrange("p (k n) -> p k n", k=8)
                    for kc in range(8):
                        S.op('pe', lambda e, j=j, kc=kc, wv=wv, tt=tt: e.matmul(
                            ps[pb[j]][:, :], lhsT=hT[:, kc, tt * 128:(tt + 1) * 128], rhs=wv[:, kc, :],
                            start=(kc == 0), stop=(kc == 7)),
                             reads=[('hs', 4 + j), ('hT', kc, tt // 4)], writes=[PK(pb[j])])
                A, Ak = cu(0, 2, F32)
                Bm, Bk = cu(2, 2, F32)
                qr, qrk = cu(4); kr, krk = cu(5); kdt, kdk = cu(6); vb, vbk = cu(7); sg, sgk = cu(8)
                qTt, qTk = cu(9); qdT, qdk = cu(10); kTt, kTk = cu(11); STm, STk = cu(12); on, onk = cu(13)
                cosb = _bv(cosT[:, tt, :], [[0, 4], [0, 2], [1, 64]])
                sinb = _bv(sinT[:, tt, :], [[0, 4], [1, 64]])
                def rot(bank, dstt, dk):
                    P4 = ps[bank][:, :].rearrange("p (h x d) -> p h x d", h=4, x=2)
                    A4 = A.rearrange("p (h x d) -> p h x d", h=4, x=2)
                    B4 = Bm.rearrange("p (x h d) -> p x h d", x=2, h=4)
                    D4 = dstt.rearrange("p (h x d) -> p h x d", h=4, x=2)
                    S.op('dve', lambda e, P4=P4, A4=A4: e.tensor_tensor(out=A4, in0=P4, in1=cosb, op=ALU.mult),
                         reads=[PK(bank), 'const'], writes=Ak)
                    S.op('dve', lambda e, P4=P4, B4=B4: e.tensor_tensor(out=B4[:, 0], in0=P4[:, :, 1, :], in1=sinb, op=ALU.mult),
                         reads=[PK(bank), 'const'], writes=[Bk[0]])
                    S.op('dve', lambda e, P4=P4, B4=B4: e.tensor_tensor(out=B4[:, 1], in0=P4[:, :, 0, :], in1=sinb, op=ALU.mult),
                         reads=[PK(bank), 'const'], writes=[Bk[1]])
                    S.op('pool', lambda e, A4=A4, B4=B4, D4=D4: e.tensor_tensor(out=D4[:, :, 0, :], in0=A4[:, :, 0, :], in1=B4[:, 0], op=ALU.subtract),
                         reads=Ak + Bk, writes=dk)
                    S.op('pool', lambda e, A4=A4, B4=B4, D4=D4: e.tensor_tensor(out=D4[:, :, 1, :], in0=A4[:, :, 1, :], in1=B4[:, 1], op=ALU.add),
                         reads=Ak + Bk, writes=dk)
                rot(pb[0], qr, qrk)
                rot(pb[1], kr, krk)
                S.op('pool', lambda e: e.tensor_tensor(out=kdt, in0=kr, in1=kdc[:], op=ALU.mult), reads=krk + ['const'], writes=kdk)
                S.op('act', lambda e: e.copy(out=vb, in_=ps[pb[2]][:, :]), reads=[PK(pb[2])], writes=vbk)
                S.op('act', lambda e: e.activation(out=sg, in_=ps[pb[3]][:, :], func=AF.Silu), reads=[PK(pb[3])], writes=sgk)
                bq = nextbank(ALLB)
                transposes(lambda c: qr[:, c * 128:(c + 1) * 128], 4, bq, qrk + ['const'])
                S.op('act', lambda e: e.copy(out=qTt, in_=psb[bq][:, 0:512]), reads=[PK(bq)], writes=qTk)
                S.op('dve', lambda e: e.tensor_tensor(out=qdT, in0=psb[bq][:, 0:512], in1=qdc[:], op=ALU.mult),
                     reads=[PK(bq), 'const'], writes=qdk)
                bk = nextbank(ALLB)
                transposes(lambda c: kr[:, c * 128:(c + 1) * 128], 4, bk, krk + ['const'])
                S.op('act', lambda e: e.copy(out=kTt, in_=psb[bk][:, 0:512]), reads=[PK(bk)], writes=kTk)
                bs = nextbank(ALLB)
                for h in range(4):
                    S.op('pe', lambda e, h=h: e.matmul(ps[bs][:, h * 128:(h + 1) * 128], lhsT=kTt[:, h * 128:(h + 1) * 128],
                                                       rhs=qTt[:, h * 128:(h + 1) * 128], start=True, stop=True),
                         reads=kTk + qTk, writes=[PK(bs)])
                S.op('dve', lambda e: e.tensor_tensor(out=STm, in0=ps[bs][:, :], in1=dmk[:], op=ALU.mult),
                     reads=[PK(bs), 'const'], writes=STk)
                bu = [nextbank(ALLB), nextbank(ALLB)]
                for c in range(2):
                    for h in range(4):
                        S.op('pe', lambda e, c=c, h=h: e.matmul(
                            ps[bu[c]][:, h * 128:(h + 1) * 128], lhsT=kdt[c * 64:(c + 1) * 64, h * 128:(h + 1) * 128],
                            rhs=vb[c * 64:(c + 1) * 64, h * 128:(h + 1) * 128], start=True, stop=True),
                             reads=kdk + vbk, writes=[PK(bu[c])])
                stf = st[:].rearrange("p h e -> p (h e)")
                for h in range(4):
                    S.op('dve', lambda e, h=h: e.scalar_tensor_tensor(
                        out=st[:, h, :], in0=st[:, h, :], scalar=float(GAM[h] ** 64), in1=ps[bu[0]][:, h * 128:(h + 1) * 128],
                        op0=ALU.mult, op1=ALU.add), reads=['st', PK(bu[0])], writes=['st'])
                S.op('act', lambda e: e.copy(out=stb[:, 1].rearrange("p h e -> p (h e)"), in_=stf), reads=['st'], writes=['stb1'])
                bo = nextbank(ALLB)
                for h in range(4):
                    S.op('pe', lambda e, h=h: e.matmul(ps[bo][:, h * 128:(h + 1) * 128], lhsT=STm[:, h * 128:(h + 1) * 128],
                                                       rhs=vb[:, h * 128:(h + 1) * 128], start=True, stop=False),
                         reads=STk + vbk, writes=[PK(bo)])
                    for c in range(2):
                        S.op('pe', lambda e, h=h, c=c: e.matmul(
                            ps[bo][c * 64:(c + 1) * 64, h * 128:(h + 1) * 128],
                            lhsT=qdT[:, h * 128 + c * 64:h * 128 + (c + 1) * 64], rhs=stb[:, c, h, :],
                            start=False, stop=True),
                             reads=qdk + [f'stb{c}'], writes=[PK(bo)])
                for h in range(4):
                    S.op('dve', lambda e, h=h: e.scalar_tensor_tensor(
                        out=st[:, h, :], in0=st[:, h, :], scalar=float(GAM[h] ** 64), in1=ps[bu[1]][:, h * 128:(h + 1) * 128],
                        op0=ALU.mult, op1=ALU.add), reads=['st', PK(bu[1])], writes=['st'])
                S.op('act', lambda e: e.copy(out=stb[:, 0].rearrange("p h e -> p (h e)"), in_=stf), reads=['st'], writes=['stb0'])
                junk, jk = tunit(0, 1)
                for h in range(4):
                    S.op('act', lambda e, h=h: e.activation(out=junk[:, 0:128], in_=ps[bo][:, h * 128:(h + 1) * 128], func=AF.Square,
                                                            accum_out=sm[:, 32 + h:33 + h]), reads=[PK(bo)], writes=['sm2'] + jk)
                rstd_from_ssq(sm[:, 32:36], 1.0 / 128, 'sm2')
                for h in range(4):
                    S.op('dve', lambda e, h=h: e.scalar_tensor_tensor(
                        out=on[:, h * 128:(h + 1) * 128], in0=ps[bo][:, h * 128:(h + 1) * 128], scalar=sm[:, 32 + h:33 + h],
                        in1=sg[:, h * 128:(h + 1) * 128], op0=ALU.mult, op1=ALU.mult),
                         reads=[PK(bo), 'sm2'] + sgk, writes=onk)
                bt = nextbank(ALLB)
                transposes(lambda c: on[:, c * 128:(c + 1) * 128], 4, bt, onk + ['const'])
                gb = _bv(pt[:, l, 32:36], [[1, 4], [0, 128]])
                S.op('dve', lambda e, tt=tt, bt=bt, gb=gb: e.tensor_tensor(
                    out=cT[:, 0:4, tt * 128:(tt + 1) * 128], in0=psb[bt][:, 0:512].rearrange("p (k n) -> p k n", k=4),
                    in1=gb, op=ALU.mult), reads=[PK(bt), 'const'], writes=[('cT', kc, tt // 4) for kc in range(4)])
            for tt in range(NT):
                ret_tile(tt)
            if stop == 'ret':
                return
            skT = WA[:, 4:6, :].rearrange("p a (b n) -> p (a b) n", b=2)
            svv = WA[:, 6:8, :].rearrange("p a (t n) -> p (a t) n", t=8)
            for (which, c0) in (("q", 2048), ("k", 2560)):
                wap, wkey = load_piece(w_in[:, c0:c0 + 512], 8, 512)
                for m in range(4):
                    for g in range(4):
                        b = nextbank(ALLB)
                        for kc in range(8):
                            S.op('pe', lambda e, m=m, g=g, kc=kc, b=b, wap=wap: e.matmul(
                                ps[b][:, :], lhsT=wap[:, kc, m * 128:(m + 1) * 128], rhs=hT[:, kc, g * 512:(g + 1) * 512],
                                start=(kc == 0), stop=(kc == 7)),
                                 reads=[wkey, ('hT', kc, g)], writes=[PK(b)])
                        if which == "q":
                            copy_op(evac_engine(), cT[:, 4 + m, g * 512:(g + 1) * 512], ps[b][:, :], [PK(b)], [('cT', 4 + m, g)])
                        else:
                            copy_op(evac_engine(), skT[:, m, g * 512:(g + 1) * 512], ps[b][:, :], [PK(b)], [('hs', 4 + m // 2)])
            wap, wkey = load_piece(w_in[:, 3072:3584], 8, 512)
            for tt in range(NT):
                b = nextbank(ALLB)
                for kc in range(8):
                    S.op('pe', lambda e, tt=tt, kc=kc, b=b, wap=wap: e.matmul(
                        ps[b][:, :], lhsT=hT[:, kc, tt * 128:(tt + 1) * 128], rhs=wap[:, kc, :], start=(kc == 0), stop=(kc == 7)),
                         reads=[wkey, ('hT', kc, tt // 4)], writes=[PK(b)])
                copy_op(evac_engine(), svv[:, tt, :], ps[b][:, :], [PK(b)], [('hs', 6 + tt // 8)])
            PSall = PS
            ZBK = [[0, 1], [2, 3]]
            CBK = [4, 5]
            ACC, SSQ = 6, 7

            def hrow(row, lo, hi, dtype=BF16):
                ap = hT[:, row, lo:hi]
                if dtype == F32:
                    ap = ap.bitcast(F32)
                return ap, [('hT', row, g) for g in range(lo // 512, (hi + 511) // 512)]
            Ebuf = [hrow(0, 0, 2048, F32), hrow(1, 0, 2048, F32)]
            Gbuf = hrow(2, 0, 2048, F32)
            spbuf = [hrow(3, 0, 1024), hrow(3, 1024, 2048)]
            Abuf = [hrow(4, 0, 1024), hrow(4, 1024, 2048)]
            Rbuf = [hrow(5, 0, 1024), hrow(5, 1024, 2048)]
            rsb = hrow(7, 512, 1536, F32)
            rawb = hrow(6, 0, 2048)
            sqb = hrow(7, 0, 512)

            def v2(ap):
                return ap.rearrange("p (c n) -> p c n", c=2)
            mstr2 = _bv(mstr[:], [[0, 2], [1, 128]])

            def sweep_group(G):
                nkb = 4 * G + 4
                items = [(hp, kb) for hp in range(4) for kb in range(nkb - 1, -1, -1)]
                n = len(items)

                def c0_of(i):
                    return max(0, items[i][1] - 4 * G) * 128

                def zmm(i):
                    hp, kb = items[i]
                    c0 = c0_of(i)
                    for ch in range(2):
                        ho = ch * 64
                        zb = ZBK[i % 2][ch]
                        S.op('pe', lambda e, ho=ho, zb=zb: e.matmul(
                            ps[zb][:, c0:512], lhsT=skT[ho:ho + 64, hp, kb * 128:(kb + 1) * 128],
                            rhs=cT[ho:ho + 64, 4 + hp, G * 512 + c0:(G + 1) * 512], start=True, stop=True),
                             reads=[('hs', 4 + hp // 2), ('cT', 4 + hp, G)], writes=[PK(zb)])

                def esp(i):
                    hp, kb = items[i]
                    c0 = c0_of(i)
                    par = i % 2
                    E, Ek = Ebuf[par]; sp, spk = spbuf[par]
                    zb = ZBK[par]
                    S.op('act', lambda e: e.activation(out=v2(E)[:, :, c0:512], in_=PSall[:, zb[0]:zb[0] + 2, c0:512], func=AF.Exp, scale=0.125),
                         reads=[PK(zb[0]), PK(zb[1])], writes=Ek)
                    S.op('act', lambda e: e.activation(out=v2(sp)[:, :, c0:512], in_=v2(E)[:, :, c0:512], func=AF.Ln, bias=1.0, scale=1.0),
                         reads=Ek, writes=spk)
                    if kb >= 4 * G:
                        S.op('dve', lambda e: e.tensor_tensor(out=v2(sp)[:, :, c0:c0 + 128], in0=v2(sp)[:, :, c0:c0 + 128], in1=mstr2, op=ALU.mult),
                             reads=spk + ['const'], writes=spk)
                    first = (kb == nkb - 1)
                    R, Rk = Rbuf[par]; Rn, Rnk = Rbuf[1 - par]
                    if kb > 0:
                        if kb >= 4 * G:
                            if c0 > 0:
                                S.op('dve', lambda e: e.memset(v2(Rn)[:, :, 0:c0], 0.0), writes=Rnk)
                        if first:
                            S.op('dve', lambda e: e.tensor_copy(out=v2(Rn)[:, :, c0:512], in_=v2(sp)[:, :, c0:512]), reads=spk, writes=Rnk)
                        else:
                            S.op('dve', lambda e: e.tensor_tensor(out=v2(Rn)[:, :, c0:512], in0=v2(R)[:, :, c0:512], in1=v2(sp)[:, :, c0:512], op=ALU.add),
                                 reads=Rk + spk, writes=Rnk)

                def cmm(i):
                    hp, kb = items[i]
                    c0 = c0_of(i)
                    first = (kb == nkb - 1)
                    sp, spk = spbuf[i % 2]; R, Rk = Rbuf[i % 2]
                    for ch in range(2):
                        S.op('pe', lambda e, ch=ch: e.matmul(ps[CBK[ch]][:, c0:512], lhsT=tri[:], rhs=v2(sp)[:, ch, c0:512], start=True, stop=first),
                             reads=spk + ['const'], writes=[PK(CBK[ch])])
                        if not first:
                            S.op('pe', lambda e, ch=ch: e.matmul(ps[CBK[ch]][:, c0:512], lhsT=ones[:], rhs=v2(R)[:, ch, c0:512], start=False, stop=True),
                                 reads=Rk + ['const'], writes=[PK(CBK[ch])])

                def rest(i):
                    hp, kb = items[i]
                    c0 = c0_of(i)
                    par = i % 2
                    first = (kb == nkb - 1)
                    E, Ek = Ebuf[par]; sp, spk = spbuf[par]; Gt, Gk = Gbuf; A, Akk = Abuf[par]; R, Rk = Rbuf[par]; Rn, Rnk = Rbuf[1 - par]
                    S.op('act', lambda e: e.activation(out=v2(Gt)[:, :, c0:512], in_=PSall[:, CBK[0]:CBK[0] + 2, c0:512], func=AF.Exp, scale=-1.0),
                         reads=[PK(CBK[0]), PK(CBK[1])], writes=Gk)
                    S.op('dve', lambda e: e.tensor_tensor(out=v2(A)[:, :, c0:512], in0=v2(E)[:, :, c0:512], in1=v2(Gt)[:, :, c0:512], op=ALU.mult),
                         reads=Ek + Gk, writes=Akk)
                    if kb >= 4 * G:
                        S.op('pool', lambda e: e.tensor_tensor(out=v2(A)[:, :, c0:c0 + 128], in0=v2(A)[:, :, c0:c0 + 128], in1=mstr2, op=ALU.mult),
                             reads=Akk + ['const'], writes=Akk)

                def avmm(i):
                    hp, kb = items[i]
                    c0 = c0_of(i)
                    first = (kb == nkb - 1)
                    A, Akk = Abuf[i % 2]
                    if first:
                        S.op('pe', lambda e: e.matmul(ps[ACC][:, :], lhsT=zer[:], rhs=dmk[:], start=True, stop=False),
                             reads=['const'], writes=[PK(ACC)])
                    for ch in range(2):
                        h = 2 * hp + ch
                        S.op('pe', lambda e, ch=ch, h=h: e.matmul(
                            ps[ACC][ch * 64:(ch + 1) * 64, c0:512], lhsT=svv[:, kb, h * 64:(h + 1) * 64], rhs=v2(A)[:, ch, c0:512],
                            start=False, stop=(kb == 0), skip_group_check=True),
                             reads=Akk + [('hs', 6 + kb // 8)], writes=[PK(ACC)])
                    if kb == 0:
                        raw, rawk = rawb; sq, sqk = sqb
                        S.op('act', lambda e: e.copy(out=raw[:, hp * 512:(hp + 1) * 512], in_=ps[ACC][:, :]), reads=[PK(ACC)], writes=[rawk[hp]])
                        S.op('act', lambda e: e.activation(out=sq, in_=ps[ACC][:, :], func=AF.Square), reads=[PK(ACC)], writes=sqk)
                        S.op('pe', lambda e: e.matmul(ps[SSQ][:, :], lhsT=ones[:], rhs=sq, start=(hp == 0), stop=(hp == 3)),
                             reads=sqk + ['const'], writes=[PK(SSQ)])

                zmm(0)
                if n > 1:
                    zmm(1)
                esp(0)
                cmm(0)
                for i in range(n):
                    if i + 2 < n:
                        zmm(i + 2)
                    if i + 1 < n:
                        esp(i + 1)
                    rest(i)
                    if i + 1 < n:
                        cmm(i + 1)
                    avmm(i)
                rs, rsk = rsb; raw, rawk = rawb
                S.op('dve', lambda e: e.tensor_scalar(out=rs, in0=ps[SSQ][:, :], scalar1=1.0 / 512, scalar2=EPS, op0=ALU.mult, op1=ALU.add),
                     reads=[PK(SSQ)], writes=rsk)
                S.op('act', lambda e: e.activation(out=rs, in_=rs, func=AF.Sqrt), reads=rsk, writes=rsk)
                S.op('dve', lambda e: e.reciprocal(out=rs, in_=rs), reads=rsk, writes=rsk)
                for hp in range(4):
                    S.op('dve', lambda e, hp=hp: e.scalar_tensor_tensor(
                        out=cT[:, 4 + hp, G * 512:(G + 1) * 512], in0=raw[:, hp * 512:(hp + 1) * 512], scalar=pt[:, l, 36 + hp:37 + hp],
                        in1=rs, op0=ALU.mult, op1=ALU.mult), reads=[rawk[hp], 'const'] + rsk, writes=[('cT', 4 + hp, G)])
            for G in range(4):
                sweep_group(G)
            if stop == 'sb':
                return
            wm = D["w_mix_out"][l]
            gemm_tm(lambda kc, tt: cT[:, kc, tt * 128:(tt + 1) * 128], lambda kc, tt: [('cT', kc, tt // 4)],
                    lambda ns: [(wm[:, ns * 512:(ns + 1) * 512], 8)], 2, resid_add)

        def cross_block(l):
            norm_to_hT(l, 8)
            memf = WA[:, 7, 0:4096].bitcast(F32).rearrange("p (t d) -> p t d", t=2)
            S.dma('sp', lambda e: e.dma_start(out=memf, in_=D["mem"].rearrange("(t p) d -> p t d", p=128)), writes=[('hs', 7)])
            junk, jk = tunit(0, 2)
            for t in range(2):
                S.op('act', lambda e, t=t: e.activation(out=junk, in_=memf[:, t, :], func=AF.Square, accum_out=sm[:, 64 + t:65 + t]),
                     reads=[('hs', 7)], writes=['sm4'] + jk)
            rstd_from_ssq(sm[:, 64:66], 1.0 / DM, 'sm4')
            mnT = WA[:, 4, 0:2048].rearrange("p (k n) -> p k n", k=8)
            for t in range(2):
                hn, hk = tunit(2 + 2 * t, 2)
                S.op('dve', lambda e, t=t, hn=hn: e.tensor_scalar(out=hn, in0=memf[:, t, :], scalar1=sm[:, 64 + t:65 + t], scalar2=None, op0=ALU.mult),
                     reads=[('hs', 7), 'sm4'], writes=hk)
                b = nextbank(ALLB)
                transposes(lambda c, hn=hn: hn[:, c * 128:(c + 1) * 128], 8, b, hk + ['const'])
                gb = _bv(pt[:, l, 16:24], [[1, 8], [0, 128]])
                S.op('dve', lambda e, t=t, b=b, gb=gb: e.tensor_tensor(out=mnT[:, :, t * 128:(t + 1) * 128],
                                                                       in0=psb[b][:, :].rearrange("p (k n) -> p k n", k=8), in1=gb, op=ALU.mult),
                     reads=[PK(b), 'const'], writes=[('hs', 4)])
            kTm = WA[:, 6, 0:2048].rearrange("p (k n) -> p k n", k=8)
            vm = WA[:, 6, 2048:4096].rearrange("p (t n) -> p t n", t=2)
            wkv = D["w_xkv"][l]
            def kv_epi(tt, ns, b):
                if ns < 2:
                    kn, knk = tunit(6 + tt % 2, 1)
                    junk1, jk1 = tunit(0, 1)
                    for hh in range(2):
                        col = 66 + (tt * 4 + ns * 2 + hh)
                        S.op('act', lambda e, hh=hh, col=col: e.activation(out=junk1[:, 0:256], in_=ps[b][:, hh * 256:(hh + 1) * 256], func=AF.Square,
                                                                           accum_out=sm[:, col:col + 1]), reads=[PK(b)], writes=[('sm5', tt, ns)] + jk1)
                    c0 = 66 + tt * 4 + ns * 2
                    rstd_from_ssq(sm[:, c0:c0 + 2], 1.0 / 256, ('sm5', tt, ns))
                    for hh in range(2):
                        S.op('dve', lambda e, hh=hh, kn=kn: e.tensor_scalar(out=kn[:, hh * 256:(hh + 1) * 256], in0=ps[b][:, hh * 256:(hh + 1) * 256],
                                                                            scalar1=sm[:, c0 + hh:c0 + hh + 1], scalar2=None, op0=ALU.mult),
                             reads=[PK(b), ('sm5', tt, ns)], writes=knk)
                    bt = nextbank(ALLB)
                    transposes(lambda c, kn=kn: kn[:, c * 128:(c + 1) * 128], 4, bt, knk + ['const'])
                    gb = _bv(pt[:, l, 42:44], [[0, 2], [1, 2], [0, 128]])
                    S.op('dve', lambda e, bt=bt, gb=gb: e.tensor_tensor(
                        out=kTm[:, ns * 4:(ns + 1) * 4, tt * 128:(tt + 1) * 128].rearrange("p (a c) n -> p a c n", a=2),
                        in0=psb[bt][:, 0:512].rearrange("p (a c n) -> p a c n", a=2, c=2), in1=gb, op=ALU.mult),
                         reads=[PK(bt), 'const'], writes=[('hs', 6)])
                else:
                    copy_op('act', vm[:, tt, (ns - 2) * 512:(ns - 1) * 512], ps[b][:, :], [PK(b)], [('hs', 6)])
            gemm_tm(lambda kc, tt: mnT[:, kc, tt * 128:(tt + 1) * 128], lambda kc, tt: [('hs', 4)],
                    lambda ns: [(wkv[:, ns * 512:(ns + 1) * 512], 8)], 4, kv_epi, ntt=2)
            wq = D["w_xq"][l]
            qT = WA[:, 4:6, :].rearrange("p a (b n) -> p (a b) n", b=2)
            for hpair in range(2):
                def q_epi(tt, ns, b):
                    qn, qnk = tunit(6 + tt % 2, 1)
                    junk1, jk1 = tunit(0, 1)
                    for hh in range(2):
                        S.op('act', lambda e, hh=hh: e.activation(out=junk1[:, 0:256], in_=ps[b][:, hh * 256:(hh + 1) * 256], func=AF.Square,
                                                                  accum_out=sm[:, 80 + 2 * (tt % 2) + hh:81 + 2 * (tt % 2) + hh]),
                             reads=[PK(b)], writes=[('sm6', tt % 2)] + jk1)
                    c0 = 80 + 2 * (tt % 2)
                    rstd_from_ssq(sm[:, c0:c0 + 2], 1.0 / 256, ('sm6', tt % 2))
                    for hh in range(2):
                        S.op('dve', lambda e, hh=hh, qn=qn: e.tensor_scalar(out=qn[:, hh * 256:(hh + 1) * 256], in0=ps[b][:, hh * 256:(hh + 1) * 256],
                                                                            scalar1=sm[:, c0 + hh:c0 + hh + 1], scalar2=None, op0=ALU.mult),
                             reads=[PK(b), ('sm6', tt % 2)], writes=qnk)
                    bt = nextbank(ALLB)
                    transposes(lambda c, qn=qn: qn[:, c * 128:(c + 1) * 128], 4, bt, qnk + ['const'])
                    gb = _bv(pt[:, l, 40:42], [[0, 2], [1, 2], [0, 128]])
                    S.op('dve', lambda e, bt=bt, gb=gb, tt=tt: e.tensor_tensor(
                        out=qT[:, :, tt * 128:(tt + 1) * 128].rearrange("p (a c) n -> p a c n", a=2),
                        in0=psb[bt][:, 0:512].rearrange("p (a c n) -> p a c n", a=2, c=2), in1=gb, op=ALU.mult),
                         reads=[PK(bt), 'const'], writes=[('hs', 4 + (c // 2)) for c in range(4)])
                gemm_tm(lambda kc, tt: hT[:, kc, tt * 128:(tt + 1) * 128], lambda kc, tt: [('hT', kc, tt // 4)],
                        lambda ns: [(wq[:, hpair * 512:(hpair + 1) * 512], 8)], 1, q_epi)
                for hh in range(2):
                    h = hpair * 2 + hh
                    def attn(g, h=h, hh=hh):
                        pT, pTk = tunit(2 + 2 * (g % 2), 2)
                        bsc = [nextbank(ALLB), nextbank(ALLB)]
                        for mc in range(2):
                            for c in range(2):
                                S.op('pe', lambda e, mc=mc, c=c: e.matmul(
                                    ps[bsc[mc]][:, :], lhsT=kTm[:, h * 2 + c, mc * 128:(mc + 1) * 128],
                                    rhs=qT[:, hh * 2 + c, g * 512:(g + 1) * 512], start=(c == 0), stop=(c == 1)),
                                     reads=[('hs', 6), ('hs', 4 + hh)], writes=[PK(bsc[mc])])
                            S.op('act', lambda e, mc=mc, pT=pT: e.activation(out=pT[:, mc * 512:(mc + 1) * 512], in_=ps[bsc[mc]][:, :], func=AF.Exp, scale=1.0 / 16),
                                 reads=[PK(bsc[mc])], writes=pTk)
                        bd = nextbank(ALLB)
                        for mc in range(2):
                            S.op('pe', lambda e, mc=mc, pT=pT: e.matmul(ps[bd][:, :], lhsT=ones[:], rhs=pT[:, mc * 512:(mc + 1) * 512],
                                                                       start=(mc == 0), stop=(mc == 1)), reads=pTk + ['const'], writes=[PK(bd)])
                        rden, rdk = tunit(0, 2, F32)
                        S.op('dve', lambda e, rden=rden: e.reciprocal(out=rden, in_=ps[bd][:, :]), reads=[PK(bd)], writes=rdk)
                        for c in range(2):
                            bo = nextbank(ALLB)
                            for mc in range(2):
                                S.op('pe', lambda e, mc=mc, c=c, pT=pT, bo=bo: e.matmul(
                                    ps[bo][:, :], lhsT=vm[:, mc, h * 256 + c * 128:h * 256 + (c + 1) * 128],
                                    rhs=pT[:, mc * 512:(mc + 1) * 512], start=(mc == 0), stop=(mc == 1)),
                                     reads=pTk + [('hs', 6)], writes=[PK(bo)])
                            S.op('dve', lambda e, c=c, bo=bo, rden=rden: e.tensor_tensor(
                                out=cT[:, h * 2 + c, g * 512:(g + 1) * 512], in0=ps[bo][:, :], in1=rden, op=ALU.mult),
                                 reads=[PK(bo)] + rdk, writes=[('cT', h * 2 + c, g)])
                    for g in range(4):
                        attn(g)
            if stop == 'xattn':
                return
            wo = D["w_xo"][l]
            gemm_tm(lambda kc, tt: cT[:, kc, tt * 128:(tt + 1) * 128], lambda kc, tt: [('cT', kc, tt // 4)],
                    lambda ns: [(wo[:, ns * 512:(ns + 1) * 512], 8)], 2, resid_add)

        def mlp_block(l):
            norm_to_hT(l, 24)
            ring["slots"] = [0, 1, 2, 4, 5, 6, 7]
            wu = D["w_up"][l]; wd = D["w_down"][l]
            for fs in range(4):
                for half in range(2):
                    wap, wkey = load_piece(wu[:, fs * 1024 + half * 512:fs * 1024 + (half + 1) * 512], 8, 512)
                    for m in range(4):
                        fc = half * 4 + m
                        for g in range(4):
                            b = nextbank(ALLB)
                            for kc in range(8):
                                S.op('pe', lambda e, m=m, g=g, kc=kc, b=b, wap=wap: e.matmul(
                                    ps[b][:, :], lhsT=wap[:, kc, m * 128:(m + 1) * 128], rhs=hT[:, kc, g * 512:(g + 1) * 512],
                                    start=(kc == 0), stop=(kc == 7)), reads=[wkey, ('hT', kc, g)], writes=[PK(b)])
                            r, rk = tunit((fc * 4 + g) % 4, 1)
                            S.op('act', lambda e, b=b, r=r: e.activation(out=r, in_=ps[b][:, :], func=AF.Relu), reads=[PK(b)], writes=rk)
                            S.op('pool', lambda e, r=r, fc=fc, g=g: e.tensor_tensor(out=cT[:, fc, g * 512:(g + 1) * 512], in0=r, in1=r, op=ALU.mult),
                                 reads=rk, writes=[('cT', fc, g)])
                gemm_tm(lambda kc, tt: cT[:, kc, tt * 128:(tt + 1) * 128], lambda kc, tt: [('cT', kc, tt // 4)],
                        lambda ns: [(wd[fs * 1024:(fs + 1) * 1024, ns * 512:(ns + 1) * 512], 8)], 2, resid_add)
            ring["slots"] = [0, 1, 2]

        done = False
        for l in range(n_layers):
            mix_block(l)
            if stop in ('ret', 'sb') or stop == f'mix{l}':
                break
            cross_block(l)
            if stop == 'xattn' or stop == f'cross{l}':
                break
            mlp_block(l)
            if stop == f'mlp{l}':
                break

        if stop in ('ret', 'sb', 'xattn'):
            for kc in range(4 if stop == 'ret' else 8):
                for g in range(4):
                    S.op('dve', lambda e, kc=kc, g=g: e.tensor_copy(out=xs[:, kc * 2 + g // 2, (g % 2) * 512:(g % 2 + 1) * 512],
                                                                    in_=cT[:, kc, g * 512:(g + 1) * 512]),
                         reads=[('cT', kc, g)], writes=[('xs', kc * 2 + g // 2)])
        for tt in range(NT):
            S.dma('sp', lambda e, tt=tt: e.dma_start(out=out_d[tt * 128:(tt + 1) * 128, :], in_=xs[:, tt, :]),
                  reads=[('xs', tt)], writes=[('out', tt)], final=True)
        S.emit()
    return nc


_CONSTS = None


def make_in_maps(inputs, n_cores=8):
    global _CONSTS
    if _CONSTS is None:
        _CONSTS = _const_tables()
    f = lambda a: np.ascontiguousarray(np.asarray(a, dtype=np.float32))
    pt = _param_table(*[np.asarray(inputs[k], dtype=np.float32) for k in
                        ("g_mix", "g_cross", "g_mem", "g_mlp", "g_ret_out", "g_sb_out", "g_qn", "g_kn")])
    shared = {k: f(inputs[k]) for k in ("w_in", "w_mix_out", "w_xq", "w_xkv", "w_xo", "w_up", "w_down")}
    shared["pt"] = pt
    shared.update(_CONSTS)
    x = f(inputs["x"]); mem = f(inputs["mem"])
    maps = []
    for c in range(n_cores):
        m = dict(shared)
        m["x"] = x[c]
        m["mem"] = mem[c]
        maps.append(m)
    return maps


def kernel(**inputs):
    nc = build_program()
    maps = make_in_maps(inputs, 8)
    res = run_bass_kernel_spmd(nc, maps, core_ids=list(range(8)))
    return np.stack([np.asarray(r["out"], dtype=np.float32) for r in res.results], axis=0)
```

```python
import contextlib
import numpy as np
import concourse.bass as bass
import concourse.mybir as mybir
from concourse.bass_utils import run_bass_kernel_spmd

dt = mybir.dt
F32, BF16 = dt.float32, dt.bfloat16
AF = mybir.ActivationFunctionType
ALU = mybir.AluOpType
AX = mybir.AxisListType

ENGS = ('pe', 'act', 'dve', 'pool', 'sp')


class Sched:
    EPOCH = 20000
    NDMA = 8

    def __init__(self, nc):
        self.nc = nc
        self.stack = contextlib.ExitStack()
        self.ops = {e: [] for e in ENGS}
        self.ncomp = {e: 0 for e in ENGS}
        self.esem = {e: {} for e in ENGS}
        self.sems = []
        self.lastw = {}
        self.readers = {}
        self.seen = {e: {} for e in ENGS}
        self.dsem = []
        self.dnext = 0
        self.finals = []

    def __enter__(self):
        self.stack.__enter__()
        return self

    def __exit__(self, *a):
        return self.stack.__exit__(*a)

    def sbuf(self, name, shape, dtype):
        return self.stack.enter_context(self.nc.sbuf_tensor(name, list(shape), dtype))

    def psum(self, name, shape, dtype):
        return self.stack.enter_context(self.nc.psum_tensor(name, list(shape), dtype))

    def _newsem(self, name):
        s = self.stack.enter_context(self.nc.semaphore(name))
        self.sems.append(s)
        return len(self.sems) - 1

    def _deps(self, eng, reads, writes, is_dma):
        deps = set()
        for k in reads:
            t = self.lastw.get(k)
            if t is not None:
                deps.add(t)
        for k in writes:
            t = self.lastw.get(k)
            if t is not None:
                deps.add(t)
            for t in self.readers.get(k, ()):
                if t[2] == eng and not is_dma:
                    continue
                deps.add(t)
        return deps

    def _waits(self, eng, deps, is_dma):
        waits = []
        seen = self.seen[eng]
        for (sid, v, deng) in sorted(deps):
            if deng == eng and eng == 'pe' and not is_dma:
                continue
            if seen.get(sid, 0) >= v:
                continue
            seen[sid] = v
            waits.append((sid, v))
        return waits

    def _commit(self, tok, reads, writes):
        for k in reads:
            self.readers.setdefault(k, []).append(tok)
        for k in writes:
            self.lastw[k] = tok
            self.readers[k] = []

    def op(self, eng, fn, reads=(), writes=()):
        deps = self._deps(eng, reads, writes, False)
        waits = self._waits(eng, deps, False)
        idx = self.ncomp[eng]
        ep = idx // self.EPOCH
        if ep not in self.esem[eng]:
            self.esem[eng][ep] = self._newsem(f"s_{eng}_{ep}")
        sid = self.esem[eng][ep]
        tok = (sid, idx % self.EPOCH + 1, eng)
        self.ncomp[eng] += 1
        self._commit(tok, reads, writes)
        self.ops[eng].append((fn, waits, sid, 1))
        return tok

    def dma(self, eng, fn, reads=(), writes=(), final=False):
        if eng == 'pool':
            return self._dma_sw(fn, reads, writes)
        deps = self._deps(eng, reads, writes, True)
        if len(self.dsem) < self.NDMA:
            self.dsem.append([self._newsem(f"s_dma_{len(self.dsem)}"), 0, None])
        ent = self.dsem[self.dnext % self.NDMA]
        self.dnext += 1
        if ent[2] is not None:
            deps.add(ent[2])
        waits = self._waits(eng, deps, True)
        ent[1] += 16
        tok = (ent[0], ent[1], 'dma')
        ent[2] = tok
        self._commit(tok, reads, writes)
        self.ops[eng].append((fn, waits, ent[0], 16))
        if final:
            self.finals.append(tok)
        return tok

    def _dma_sw(self, fn, reads, writes):
        deps = self._deps('pool', reads, writes, True)
        waits = self._waits('pool', deps, True)
        if not hasattr(self, 'nsw'):
            self.nsw = 0
        sid = self._newsem(f"s_sw_{self.nsw}")
        self.nsw += 1
        tok = (sid, 16, 'dma')
        self._commit(tok, reads, writes)
        self.ops['pool'].append((fn, waits, sid, 16))
        return tok

    def emit(self):
        nc = self.nc
        fin = [(sid, v) for (sid, v, _) in self.finals]
        sems = self.sems
        ops = self.ops

        def replay(name, e, tail=()):
            for ent in ops[name]:
                if ent[0] == 'relay':
                    e.wait_ge(sems[ent[1]], 16)
                    e.sem_clear(sems[ent[1]])
                    e.sem_inc(sems[ent[2]], 1)
                    continue
                (fn, waits, sid, n) = ent
                for (ws, wv) in waits:
                    e.wait_ge(sems[ws], wv)
                fn(e).then_inc(sems[sid], n)
            for (ws, wv) in tail:
                e.wait_ge(sems[ws], wv)

        with nc.Block() as block:
            @block.sync
            def _(e):
                replay('sp', e, fin)

            @block.tensor
            def _(e):
                replay('pe', e)

            @block.vector
            def _(e):
                replay('dve', e)

            @block.scalar
            def _(e):
                replay('act', e)

            @block.gpsimd
            def _(e):
                replay('pool', e)


SEQ, DM, DEPTH = 2048, 1024, 2
NT = 16
EPS = 1e-6
NPC = 44
GAM = [1.0 - 2.0 ** (-5 - h) for h in range(4)]


def _const_tables():
    p = np.arange(128)
    c = {}
    c["ident"] = np.eye(128, dtype=np.float32)
    c["tri"] = (p[:, None] >= p[None, :]).astype(np.float32)
    c["ones"] = np.ones((128, 128), np.float32)
    c["mstr"] = (p[:, None] < p[None, :]).astype(np.float32)
    inv = (1.0 / (np.float32(10000.0) ** np.linspace(0.0, 1.0, 64, dtype=np.float32))).astype(np.float32)
    pos = np.arange(SEQ, dtype=np.float32)
    ang = (pos[:, None] * inv[None, :]).astype(np.float32)
    c["cos"] = np.cos(ang.astype(np.float64)).astype(np.float32).reshape(NT, 128, 64).transpose(1, 0, 2).copy()
    c["sin"] = np.sin(ang.astype(np.float64)).astype(np.float32).reshape(NT, 128, 64).transpose(1, 0, 2).copy()
    lg = np.log1p(-np.exp2(-5.0 - np.arange(4, dtype=np.float64)))
    dist = np.abs(p[:, None] - p[None, :]).astype(np.float64)
    same = (p[:, None] // 64) == (p[None, :] // 64)
    dm = np.zeros((128, 4, 128), np.float64)
    qd = np.zeros((128, 4, 128), np.float64)
    kd = np.zeros((128, 4, 128), np.float64)
    for h in range(4):
        dm[:, h, :] = np.where(same, np.exp(lg[h] * dist), 0.0) * (128.0 ** -0.5)
        qd[:, h, :] = np.exp(lg[h] * ((p % 64) + 1.0))[None, :]
        kd[:, h, :] = (np.exp(lg[h] * (63.0 - (p % 64))) * (128.0 ** -0.5))[:, None]
    c["dmk"] = dm.reshape(128, 512).astype(np.float32)
    c["qdc"] = qd.reshape(128, 512).astype(np.float32)
    c["kdc"] = kd.reshape(128, 512).astype(np.float32)
    return c


def _param_table(g_mix, g_cross, g_mem, g_mlp, g_ret_out, g_sb_out, g_qn, g_kn):
    pt = np.zeros((128, DEPTH, NPC), np.float32)
    for l in range(DEPTH):
        pt[:, l, 0:8] = g_mix[l].reshape(8, 128).T
        pt[:, l, 8:16] = g_cross[l].reshape(8, 128).T
        pt[:, l, 16:24] = g_mem[l].reshape(8, 128).T
        pt[:, l, 24:32] = g_mlp[l].reshape(8, 128).T
        pt[:, l, 32:36] = g_ret_out[l].reshape(4, 128).T
        pt[:, l, 36:40] = g_sb_out[l].reshape(4, 128).T
        pt[:, l, 40:42] = g_qn[l].reshape(2, 128).T
        pt[:, l, 42:44] = g_kn[l].reshape(2, 128).T
    return pt


def _bv(ap, dims):
    return bass.AP(ap.tensor, ap.offset, [list(ap.ap[0])] + [list(d) for d in dims])


def build_program(n_layers=DEPTH, stop=None):
    nc = bass.Bass("TRN2", target_bir_lowering=False)
    D = {}
    def din(name, shape):
        D[name] = nc.dram_tensor(name, list(shape), F32, kind="ExternalInput").ap()
    din("x", [SEQ, DM]); din("mem", [256, DM])
    din("w_in", [DEPTH, DM, 3584]); din("w_mix_out", [DEPTH, DM, DM]); din("w_xq", [DEPTH, DM, DM])
    din("w_xkv", [DEPTH, DM, 2 * DM]); din("w_xo", [DEPTH, DM, DM]); din("w_up", [DEPTH, DM, 4 * DM])
    din("w_down", [DEPTH, 4 * DM, DM]); din("pt", [128, DEPTH, NPC])
    for nm in ("ident", "tri", "ones", "mstr"):
        din(nm, [128, 128])
    din("cos", [128, NT, 64]); din("sin", [128, NT, 64])
    for nm in ("dmk", "qdc", "kdc"):
        din(nm, [128, 512])
    out_d = nc.dram_tensor("out", [SEQ, DM], F32, kind="ExternalOutput").ap()

    S = Sched(nc)
    with S:
        xs = S.sbuf("xs", [128, NT, DM], F32)
        hT = S.sbuf("hT", [128, 8, SEQ], BF16)
        cT = S.sbuf("cT", [128, 8, SEQ], BF16)
        WA = S.sbuf("WA", [128, 8, 4096], BF16)
        ident = S.sbuf("ident_s", [128, 128], BF16)
        tri = S.sbuf("tri_s", [128, 128], BF16)
        ones = S.sbuf("ones_s", [128, 128], BF16)
        mstr = S.sbuf("mstr_s", [128, 128], BF16)
        zer = S.sbuf("zer_s", [128, 128], BF16)
        cosT = S.sbuf("cos_s", [128, NT, 64], BF16)
        sinT = S.sbuf("sin_s", [128, NT, 64], BF16)
        dmk = S.sbuf("dmk_s", [128, 512], BF16)
        qdc = S.sbuf("qdc_s", [128, 512], BF16)
        kdc = S.sbuf("kdc_s", [128, 512], BF16)
        pt = S.sbuf("pt_s", [128, DEPTH, NPC], F32)
        st = S.sbuf("st_s", [128, 4, 128], F32)
        stb = S.sbuf("stb_s", [128, 2, 4, 128], BF16)
        sm = S.sbuf("sm_s", [128, 128], F32)
        PS = S.psum("ps", [128, 8, 512], F32)
        ps = [PS[:, i, :] for i in range(8)]
        psb = [p.bitcast(BF16) for p in ps]

        rr = {"dq": 0, "bank": 0, "ring": 0, "ev": 0}

        def dq():
            rr["dq"] += 1
            return 'sp' if rr["dq"] % 2 else 'act'

        def PK(b):
            return ('ps', b)

        def unit(buf, name, u, n=1, dtype=BF16):
            kc, g = u // 4, u % 4
            assert g + n <= 4
            ap = buf[:, kc, g * 512:(g + n) * 512]
            if dtype == F32:
                ap = ap.bitcast(F32)
            return ap, [(name, kc, g + i) for i in range(n)]

        def tunit(i, n=1, dtype=BF16):
            ap = WA[:, 3, i * 512:(i + n) * 512]
            if dtype == F32:
                ap = ap.bitcast(F32)
            return ap, [('t', i + k) for k in range(n)]

        def nextbank(allowed):
            rr["bank"] += 1
            return allowed[rr["bank"] % len(allowed)]

        ring = {"slots": [0, 1, 2]}

        def load_piece(w2d, nk, ncols):
            assert nk * ncols <= 4096
            rr["ring"] += 1
            s = ring["slots"][rr["ring"] % len(ring["slots"])]
            dst = WA[:, s, 0:nk * ncols].rearrange("p (k n) -> p k n", k=nk)
            src = w2d.rearrange("(k p) n -> p k n", p=128)
            S.dma('pool', lambda e: e.dma_start(out=dst, in_=src), reads=[], writes=[('hs', s)])
            return dst, ('hs', s)

        def evac_engine():
            rr["ev"] += 1
            return 'act' if rr["ev"] % 2 else 'dve'

        def copy_op(eng, out, in_, reads, writes):
            if eng == 'act':
                S.op('act', lambda e: e.copy(out=out, in_=in_), reads=reads, writes=writes)
            else:
                S.op(eng, lambda e: e.tensor_copy(out=out, in_=in_), reads=reads, writes=writes)

        def rstd_from_ssq(ssq_ap, inv_n, key):
            S.op('dve', lambda e: e.tensor_scalar(out=ssq_ap, in0=ssq_ap, scalar1=inv_n, scalar2=EPS,
                                                  op0=ALU.mult, op1=ALU.add), reads=[key], writes=[key])
            S.op('act', lambda e: e.activation(out=ssq_ap, in_=ssq_ap, func=AF.Sqrt), reads=[key], writes=[key])
            S.op('dve', lambda e: e.reciprocal(out=ssq_ap, in_=ssq_ap), reads=[key], writes=[key])

        def transposes(src_fn, n, bank, reads):
            for c in range(n):
                S.op('pe', lambda e, c=c: e.transpose(out=psb[bank][:, c * 128:(c + 1) * 128], in_=src_fn(c),
                                                      identity=ident[:]),
                     reads=reads, writes=[PK(bank)])

        for tt in range(NT):
            S.dma('sp', lambda e, tt=tt: e.dma_start(out=xs[:, tt, :], in_=D["x"][tt * 128:(tt + 1) * 128, :]),
                  writes=[('xs', tt)])
        for (sb_t, nm) in ((ident, "ident"), (tri, "tri"), (ones, "ones"), (mstr, "mstr"), (cosT, "cos"),
                           (sinT, "sin"), (dmk, "dmk"), (qdc, "qdc"), (kdc, "kdc")):
            S.dma('pool', lambda e, sb_t=sb_t, nm=nm: e.dma_start(out=sb_t[:], in_=D[nm]), writes=['const'])
        S.dma('sp', lambda e: e.dma_start(out=pt[:], in_=D["pt"]), writes=['const'])
        S.op('pool', lambda e: e.memset(zer[:], 0.0), writes=['const'])

        ALLB = list(range(8))

        def norm_to_hT(l, gcol):
            junk, jk = tunit(0, 2)
            for tt in range(NT):
                S.op('act', lambda e, tt=tt: e.activation(out=junk, in_=xs[:, tt, :], func=AF.Square,
                                                          accum_out=sm[:, tt:tt + 1]),
                     reads=[('xs', tt)], writes=['sm'] + jk)
            rstd_from_ssq(sm[:, 0:NT], 1.0 / DM, 'sm')
            for tt in range(NT):
                hn, hk = tunit(2 + 2 * (tt % 2), 2)
                S.op('dve', lambda e, tt=tt, hn=hn: e.tensor_scalar(out=hn, in0=xs[:, tt, :], scalar1=sm[:, tt:tt + 1],
                                                                    scalar2=None, op0=ALU.mult),
                     reads=[('xs', tt), 'sm'], writes=hk)
                b = nextbank(ALLB)
                transposes(lambda c, hn=hn: hn[:, c * 128:(c + 1) * 128], 8, b, hk + ['const'])
                gsl = pt[:, l, gcol:gcol + 8]
                gb = _bv(gsl, [[1, 8], [0, 128]])
                S.op('dve', lambda e, tt=tt, b=b, gb=gb: e.tensor_tensor(
                    out=hT[:, :, tt * 128:(tt + 1) * 128],
                    in0=psb[b][:, :].rearrange("p (k n) -> p k n", k=8), in1=gb, op=ALU.mult),
                     reads=[PK(b), 'const'], writes=[('hT', kc, tt // 4) for kc in range(8)])

        def gemm_tm(lhs_fn, lhs_keys_fn, pieces_fn, n_slices, epilogue, banks=ALLB, ntt=NT):
            for ns in range(n_slices):
                pcs = [load_piece(w2d, nk, 512) for (w2d, nk) in pieces_fn(ns)]
                for tt in range(ntt):
                    b = nextbank(banks)
                    tot = sum(p[0].shape[1] for p in pcs)
                    i = 0
                    for pi, (wap, wkey) in enumerate(pcs):
                        for k in range(wap.shape[1]):
                            kc = sum(p[0].shape[1] for p in pcs[:pi]) + k
                            S.op('pe', lambda e, wap=wap, k=k, kc=kc, i=i, b=b, tt=tt: e.matmul(
                                ps[b][:, :], lhsT=lhs_fn(kc, tt), rhs=wap[:, k, :], start=(i == 0), stop=(i == tot - 1)),
                                 reads=[wkey] + lhs_keys_fn(kc, tt), writes=[PK(b)])
                            i += 1
                    epilogue(tt, ns, b)

        def resid_add(tt, ns, b):
            S.op('dve', lambda e: e.tensor_tensor(out=xs[:, tt, ns * 512:(ns + 1) * 512], in0=ps[b][:, :],
                                                  in1=xs[:, tt, ns * 512:(ns + 1) * 512], op=ALU.add),
                 reads=[PK(b), ('xs', tt)], writes=[('xs', tt)])

        def mix_block(l):
            w_in = D["w_in"][l]
            norm_to_hT(l, 0)
            for j in range(4):
                dst = WA[:, 4 + j, :].rearrange("p (k n) -> p k n", k=8)
                src = w_in[:, j * 512:(j + 1) * 512].rearrange("(k p) n -> p k n", p=128)
                S.dma('pool', lambda e, dst=dst, src=src: e.dma_start(out=dst, in_=src), writes=[('hs', 4 + j)])
            S.op('pool', lambda e: e.memset(st[:], 0.0), writes=['st'])
            S.op('pool', lambda e: e.memset(stb[:], 0.0), writes=['stb0', 'stb1'])
            def cu(i, n=1, dtype=BF16):
                return unit(cT, 'cT', 16 + i, n, dtype)
            def ret_tile(tt):
                pb = [nextbank(ALLB) for _ in range(4)]
                for j in range(4):
                    wv = WA[:, 4 + j, :].rearrange("p (k n) -> p k n", k=8)
                    for kc in range(8):
                        S.op('pe', lambda e, j=j, kc=kc, wv=wv, tt=tt: e.matmul(
                            ps[pb[j]][:, :], lhsT=hT[:, kc, tt * 128:(tt + 1) * 128], rhs=wv[:, kc, :],
                            start=(kc == 0), stop=(kc == 7)),
                             reads=[('hs', 4 + j), ('hT', kc, tt // 4)], writes=[PK(pb[j])])
                yield
                pp = tt % 2
                A, Ak = cu(0, 2, F32)
                Bm, Bk = cu(2, 2, F32)
                qr, qrk = cu(4); kr, krk = cu(5); STm, STk = cu(6); on, onk = cu(7)
                kdt, kdk = cu(8 + pp); vb, vbk = cu(10 + pp); sg, sgk = cu(12 + pp); qTt, qTk = cu(14 + pp)
                qdT, qdk = tunit(1 + pp); kTt, kTk = tunit(3 + pp)
                cosb = _bv(cosT[:, tt, :], [[0, 4], [0, 2], [1, 64]])
                sinb = _bv(sinT[:, tt, :], [[0, 4], [1, 64]])
                def rot(bank, dstt, dk):
                    P4 = ps[bank][:, :].rearrange("p (h x d) -> p h x d", h=4, x=2)
                    A4 = A.rearrange("p (h x d) -> p h x d", h=4, x=2)
                    B4 = Bm.rearrange("p (x h d) -> p x h d", x=2, h=4)
                    D4 = dstt.rearrange("p (h x d) -> p h x d", h=4, x=2)
                    S.op('dve', lambda e, P4=P4, A4=A4: e.tensor_tensor(out=A4, in0=P4, in1=cosb, op=ALU.mult),
                         reads=[PK(bank), 'const'], writes=Ak)
                    S.op('dve', lambda e, P4=P4, B4=B4: e.tensor_tensor(out=B4[:, 0], in0=P4[:, :, 1, :], in1=sinb, op=ALU.mult),
                         reads=[PK(bank), 'const'], writes=[Bk[0]])
                    S.op('dve', lambda e, P4=P4, B4=B4: e.tensor_tensor(out=B4[:, 1], in0=P4[:, :, 0, :], in1=sinb, op=ALU.mult),
                         reads=[PK(bank), 'const'], writes=[Bk[1]])
                    S.op('pool', lambda e, A4=A4, B4=B4, D4=D4: e.tensor_tensor(out=D4[:, :, 0, :], in0=A4[:, :, 0, :], in1=B4[:, 0], op=ALU.subtract),
                         reads=Ak + Bk, writes=dk)
                    S.op('pool', lambda e, A4=A4, B4=B4, D4=D4: e.tensor_tensor(out=D4[:, :, 1, :], in0=A4[:, :, 1, :], in1=B4[:, 1], op=ALU.add),
                         reads=Ak + Bk, writes=dk)
                rot(pb[0], qr, qrk)
                yield
                S.op('act', lambda e: e.copy(out=vb, in_=ps[pb[2]][:, :]), reads=[PK(pb[2])], writes=vbk)
                yield
                rot(pb[1], kr, krk)
                yield
                S.op('act', lambda e: e.activation(out=sg, in_=ps[pb[3]][:, :], func=AF.Silu), reads=[PK(pb[3])], writes=sgk)
                S.op('pool', lambda e: e.tensor_tensor(out=kdt, in0=kr, in1=kdc[:], op=ALU.mult), reads=krk + ['const'], writes=kdk)
                yield
                bq = nextbank(ALLB)
                transposes(lambda c: qr[:, c * 128:(c + 1) * 128], 4, bq, qrk + ['const'])
                S.op('act', lambda e: e.copy(out=qTt, in_=psb[bq][:, 0:512]), reads=[PK(bq)], writes=qTk)
                S.op('dve', lambda e: e.tensor_tensor(out=qdT, in0=psb[bq][:, 0:512], in1=qdc[:], op=ALU.mult),
                     reads=[PK(bq), 'const'], writes=qdk)
                yield
                bk = nextbank(ALLB)
                transposes(lambda c: kr[:, c * 128:(c + 1) * 128], 4, bk, krk + ['const'])
                S.op('act', lambda e: e.copy(out=kTt, in_=psb[bk][:, 0:512]), reads=[PK(bk)], writes=kTk)
                yield 'split'
                bs = nextbank(ALLB)
                for h in range(4):
                    S.op('pe', lambda e, h=h: e.matmul(ps[bs][:, h * 128:(h + 1) * 128], lhsT=kTt[:, h * 128:(h + 1) * 128],
                                                       rhs=qTt[:, h * 128:(h + 1) * 128], start=True, stop=True),
                         reads=kTk + qTk, writes=[PK(bs)])
                S.op('dve', lambda e: e.tensor_tensor(out=STm, in0=ps[bs][:, :], in1=dmk[:], op=ALU.mult),
                     reads=[PK(bs), 'const'], writes=STk)
                yield
                bu = [nextbank(ALLB), nextbank(ALLB)]
                for c in range(2):
                    for h in range(4):
                        S.op('pe', lambda e, c=c, h=h: e.matmul(
                            ps[bu[c]][:, h * 128:(h + 1) * 128], lhsT=kdt[c * 64:(c + 1) * 64, h * 128:(h + 1) * 128],
                            rhs=vb[c * 64:(c + 1) * 64, h * 128:(h + 1) * 128], start=True, stop=True),
                             reads=kdk + vbk, writes=[PK(bu[c])])
                yield
                stf = st[:].rearrange("p h e -> p (h e)")
                for h in range(4):
                    S.op('dve', lambda e, h=h: e.scalar_tensor_tensor(
                        out=st[:, h, :], in0=st[:, h, :], scalar=float(GAM[h] ** 64), in1=ps[bu[0]][:, h * 128:(h + 1) * 128],
                        op0=ALU.mult, op1=ALU.add), reads=['st', PK(bu[0])], writes=['st'])
                S.op('act', lambda e: e.copy(out=stb[:, 1].rearrange("p h e -> p (h e)"), in_=stf), reads=['st'], writes=['stb1'])
                yield
                bo = nextbank(ALLB)
                for h in range(4):
                    S.op('pe', lambda e, h=h: e.matmul(ps[bo][:, h * 128:(h + 1) * 128], lhsT=STm[:, h * 128:(h + 1) * 128],
                                                       rhs=vb[:, h * 128:(h + 1) * 128], start=True, stop=False),
                         reads=STk + vbk, writes=[PK(bo)])
                    for c in range(2):
                        S.op('pe', lambda e, h=h, c=c: e.matmul(
                            ps[bo][c * 64:(c + 1) * 64, h * 128:(h + 1) * 128],
                            lhsT=qdT[:, h * 128 + c * 64:h * 128 + (c + 1) * 64], rhs=stb[:, c, h, :],
                            start=False, stop=True),
                             reads=qdk + [f'stb{c}'], writes=[PK(bo)])
                yield
                for h in range(4):
                    S.op('dve', lambda e, h=h: e.scalar_tensor_tensor(
                        out=st[:, h, :], in0=st[:, h, :], scalar=float(GAM[h] ** 64), in1=ps[bu[1]][:, h * 128:(h + 1) * 128],
                        op0=ALU.mult, op1=ALU.add), reads=['st', PK(bu[1])], writes=['st'])
                S.op('act', lambda e: e.copy(out=stb[:, 0].rearrange("p h e -> p (h e)"), in_=stf), reads=['st'], writes=['stb0'])
                yield
                junk, jk = tunit(0, 1)
                for h in range(4):
                    S.op('act', lambda e, h=h: e.activation(out=junk[:, 0:128], in_=ps[bo][:, h * 128:(h + 1) * 128], func=AF.Square,
                                                            accum_out=sm[:, 32 + h:33 + h]), reads=[PK(bo)], writes=['sm2'] + jk)
                yield
                rstd_from_ssq(sm[:, 32:36], 1.0 / 128, 'sm2')
                yield
                for h in range(4):
                    S.op('dve', lambda e, h=h: e.scalar_tensor_tensor(
                        out=on[:, h * 128:(h + 1) * 128], in0=ps[bo][:, h * 128:(h + 1) * 128], scalar=sm[:, 32 + h:33 + h],
                        in1=sg[:, h * 128:(h + 1) * 128], op0=ALU.mult, op1=ALU.mult),
                         reads=[PK(bo), 'sm2'] + sgk, writes=onk)
                yield
                bt = nextbank(ALLB)
                transposes(lambda c: on[:, c * 128:(c + 1) * 128], 4, bt, onk + ['const'])
                yield
                gb = _bv(pt[:, l, 32:36], [[1, 4], [0, 128]])
                S.op('dve', lambda e, tt=tt, bt=bt, gb=gb: e.tensor_tensor(
                    out=cT[:, 0:4, tt * 128:(tt + 1) * 128], in0=psb[bt][:, 0:512].rearrange("p (k n) -> p k n", k=4),
                    in1=gb, op=ALU.mult), reads=[PK(bt), 'const'], writes=[('cT', kc, tt // 4) for kc in range(4)])
            gens = [ret_tile(tt) for tt in range(NT)]
            for v in gens[0]:
                if v == 'split':
                    break
            for tt in range(NT):
                gB = gens[tt]
                gA = gens[tt + 1] if tt + 1 < NT else None
                doneA, doneB = gA is None, False
                while not (doneA and doneB):
                    if not doneB:
                        try:
                            next(gB)
                        except StopIteration:
                            doneB = True
                    if not doneA:
                        try:
                            if next(gA) == 'split':
                                doneA = True
                        except StopIteration:
                            doneA = True
            if stop == 'ret':
                return
            skT = WA[:, 4:6, :].rearrange("p a (b n) -> p (a b) n", b=2)
            svv = WA[:, 6:8, :].rearrange("p a (t n) -> p (a t) n", t=8)
            for (which, c0) in (("q", 2048), ("k", 2560)):
                wap, wkey = load_piece(w_in[:, c0:c0 + 512], 8, 512)
                for m in range(4):
                    for g in range(4):
                        b = nextbank(ALLB)
                        for kc in range(8):
                            S.op('pe', lambda e, m=m, g=g, kc=kc, b=b, wap=wap: e.matmul(
                                ps[b][:, :], lhsT=wap[:, kc, m * 128:(m + 1) * 128], rhs=hT[:, kc, g * 512:(g + 1) * 512],
                                start=(kc == 0), stop=(kc == 7)),
                                 reads=[wkey, ('hT', kc, g)], writes=[PK(b)])
                        if which == "q":
                            copy_op(evac_engine(), cT[:, 4 + m, g * 512:(g + 1) * 512], ps[b][:, :], [PK(b)], [('cT', 4 + m, g)])
                        else:
                            copy_op(evac_engine(), skT[:, m, g * 512:(g + 1) * 512], ps[b][:, :], [PK(b)], [('hs', 4 + m // 2)])
            wap, wkey = load_piece(w_in[:, 3072:3584], 8, 512)
            for tt in range(NT):
                b = nextbank(ALLB)
                for kc in range(8):
                    S.op('pe', lambda e, tt=tt, kc=kc, b=b, wap=wap: e.matmul(
                        ps[b][:, :], lhsT=hT[:, kc, tt * 128:(tt + 1) * 128], rhs=wap[:, kc, :], start=(kc == 0), stop=(kc == 7)),
                         reads=[wkey, ('hT', kc, tt // 4)], writes=[PK(b)])
                copy_op(evac_engine(), svv[:, tt, :], ps[b][:, :], [PK(b)], [('hs', 6 + tt // 8)])
            PSall = PS
            ZBK = [[0, 1], [2, 3]]
            CBK = [4, 5]
            ACC, SSQ = 6, 7

            def hrow(row, lo, hi, dtype=BF16):
                ap = hT[:, row, lo:hi]
                if dtype == F32:
                    ap = ap.bitcast(F32)
                return ap, [('hT', row, g) for g in range(lo // 512, (hi + 511) // 512)]
            Ebuf = [hrow(0, 0, 2048, F32), hrow(1, 0, 2048, F32), tunit(0, 4, F32)]
            Gbuf = hrow(2, 0, 2048, F32)
            spbuf = [hrow(3, 0, 1024), hrow(3, 1024, 2048)]
            Abuf = [hrow(4, 0, 1024), hrow(4, 1024, 2048)]
            Rbuf = [hrow(5, 0, 1024), hrow(5, 1024, 2048)]
            rsb = hrow(7, 512, 1536, F32)
            rawb = hrow(6, 0, 2048)
            sqb = hrow(7, 0, 512)

            def v2(ap):
                return ap.rearrange("p (c n) -> p c n", c=2)
            mstr2 = _bv(mstr[:], [[0, 2], [1, 128]])

            def sweep_group(G):
                nkb = 4 * G + 4
                items = [(hp, kb) for hp in range(4) for kb in range(nkb - 1, -1, -1)]
                n = len(items)

                def c0_of(i):
                    return max(0, items[i][1] - 4 * G) * 128

                def zmm(i):
                    hp, kb = items[i]
                    c0 = c0_of(i)
                    for ch in range(2):
                        ho = ch * 64
                        zb = ZBK[i % 2][ch]
                        S.op('pe', lambda e, ho=ho, zb=zb: e.matmul(
                            ps[zb][:, c0:512], lhsT=skT[ho:ho + 64, hp, kb * 128:(kb + 1) * 128],
                            rhs=cT[ho:ho + 64, 4 + hp, G * 512 + c0:(G + 1) * 512], start=True, stop=True),
                             reads=[('hs', 4 + hp // 2), ('cT', 4 + hp, G)], writes=[PK(zb)])

                def esp(i):
                    hp, kb = items[i]
                    c0 = c0_of(i)
                    par = i % 2
                    E, Ek = Ebuf[i % 3]; sp, spk = spbuf[par]
                    zb = ZBK[par]
                    S.op('act', lambda e: e.activation(out=v2(E)[:, :, c0:512], in_=PSall[:, zb[0]:zb[0] + 2, c0:512], func=AF.Exp, scale=0.125),
                         reads=[PK(zb[0]), PK(zb[1])], writes=Ek)
                    S.op('act', lambda e: e.activation(out=v2(sp)[:, :, c0:512], in_=v2(E)[:, :, c0:512], func=AF.Ln, bias=1.0, scale=1.0),
                         reads=Ek, writes=spk)
                    if kb >= 4 * G:
                        S.op('dve', lambda e: e.tensor_tensor(out=v2(sp)[:, :, c0:c0 + 128], in0=v2(sp)[:, :, c0:c0 + 128], in1=mstr2, op=ALU.mult),
                             reads=spk + ['const'], writes=spk)
                    first = (kb == nkb - 1)
                    R, Rk = Rbuf[par]; Rn, Rnk = Rbuf[1 - par]
                    if kb > 0:
                        if kb >= 4 * G:
                            if c0 > 0:
                                S.op('dve', lambda e: e.memset(v2(Rn)[:, :, 0:c0], 0.0), writes=Rnk)
                        if first:
                            S.op('dve', lambda e: e.tensor_copy(out=v2(Rn)[:, :, c0:512], in_=v2(sp)[:, :, c0:512]), reads=spk, writes=Rnk)
                        else:
                            S.op('dve', lambda e: e.tensor_tensor(out=v2(Rn)[:, :, c0:512], in0=v2(R)[:, :, c0:512], in1=v2(sp)[:, :, c0:512], op=ALU.add),
                                 reads=Rk + spk, writes=Rnk)

                def cmm(i):
                    hp, kb = items[i]
                    c0 = c0_of(i)
                    first = (kb == nkb - 1)
                    sp, spk = spbuf[i % 2]; R, Rk = Rbuf[i % 2]
                    for ch in range(2):
                        S.op('pe', lambda e, ch=ch: e.matmul(ps[CBK[ch]][:, c0:512], lhsT=tri[:], rhs=v2(sp)[:, ch, c0:512], start=True, stop=first),
                             reads=spk + ['const'], writes=[PK(CBK[ch])])
                        if not first:
                            S.op('pe', lambda e, ch=ch: e.matmul(ps[CBK[ch]][:, c0:512], lhsT=ones[:], rhs=v2(R)[:, ch, c0:512], start=False, stop=True),
                                 reads=Rk + ['const'], writes=[PK(CBK[ch])])

                def rest(i):
                    hp, kb = items[i]
                    c0 = c0_of(i)
                    par = i % 2
                    first = (kb == nkb - 1)
                    E, Ek = Ebuf[i % 3]; sp, spk = spbuf[par]; Gt, Gk = Gbuf; A, Akk = Abuf[par]; R, Rk = Rbuf[par]; Rn, Rnk = Rbuf[1 - par]
                    S.op('act', lambda e: e.activation(out=v2(Gt)[:, :, c0:512], in_=PSall[:, CBK[0]:CBK[0] + 2, c0:512], func=AF.Exp, scale=-1.0),
                         reads=[PK(CBK[0]), PK(CBK[1])], writes=Gk)
                    S.op('dve', lambda e: e.tensor_tensor(out=v2(A)[:, :, c0:512], in0=v2(E)[:, :, c0:512], in1=v2(Gt)[:, :, c0:512], op=ALU.mult),
                         reads=Ek + Gk, writes=Akk)
                    if kb >= 4 * G:
                        S.op('pool', lambda e: e.tensor_tensor(out=v2(A)[:, :, c0:c0 + 128], in0=v2(A)[:, :, c0:c0 + 128], in1=mstr2, op=ALU.mult),
                             reads=Akk + ['const'], writes=Akk)

                def avmm(i):
                    hp, kb = items[i]
                    c0 = c0_of(i)
                    first = (kb == nkb - 1)
                    A, Akk = Abuf[i % 2]
                    if first:
                        S.op('pe', lambda e: e.matmul(ps[ACC][:, :], lhsT=zer[:], rhs=dmk[:], start=True, stop=False),
                             reads=['const'], writes=[PK(ACC)])
                    for ch in range(2):
                        h = 2 * hp + ch
                        S.op('pe', lambda e, ch=ch, h=h: e.matmul(
                            ps[ACC][ch * 64:(ch + 1) * 64, c0:512], lhsT=svv[:, kb, h * 64:(h + 1) * 64], rhs=v2(A)[:, ch, c0:512],
                            start=False, stop=(kb == 0), skip_group_check=True),
                             reads=Akk + [('hs', 6 + kb // 8)], writes=[PK(ACC)])
                    if kb == 0:
                        raw, rawk = rawb; sq, sqk = sqb
                        S.op('act', lambda e: e.copy(out=raw[:, hp * 512:(hp + 1) * 512], in_=ps[ACC][:, :]), reads=[PK(ACC)], writes=[rawk[hp]])
                        S.op('act', lambda e: e.activation(out=sq, in_=ps[ACC][:, :], func=AF.Square), reads=[PK(ACC)], writes=sqk)
                        S.op('pe', lambda e: e.matmul(ps[SSQ][:, :], lhsT=ones[:], rhs=sq, start=(hp == 0), stop=(hp == 3)),
                             reads=sqk + ['const'], writes=[PK(SSQ)])

                zmm(0)
                if n > 1:
                    zmm(1)
                esp(0)
                cmm(0)
                for i in range(n):
                    if i + 2 < n:
                        zmm(i + 2)
                    if i + 1 < n:
                        esp(i + 1)
                    rest(i)
                    if i + 1 < n:
                        cmm(i + 1)
                    avmm(i)
                rs, rsk = rsb; raw, rawk = rawb
                S.op('dve', lambda e: e.tensor_scalar(out=rs, in0=ps[SSQ][:, :], scalar1=1.0 / 512, scalar2=EPS, op0=ALU.mult, op1=ALU.add),
                     reads=[PK(SSQ)], writes=rsk)
                S.op('act', lambda e: e.activation(out=rs, in_=rs, func=AF.Sqrt), reads=rsk, writes=rsk)
                S.op('dve', lambda e: e.reciprocal(out=rs, in_=rs), reads=rsk, writes=rsk)
                for hp in range(4):
                    S.op('dve', lambda e, hp=hp: e.scalar_tensor_tensor(
                        out=cT[:, 4 + hp, G * 512:(G + 1) * 512], in0=raw[:, hp * 512:(hp + 1) * 512], scalar=pt[:, l, 36 + hp:37 + hp],
                        in1=rs, op0=ALU.mult, op1=ALU.mult), reads=[rawk[hp], 'const'] + rsk, writes=[('cT', 4 + hp, G)])
            for G in range(4):
                sweep_group(G)
            if stop == 'sb':
                return
            wm = D["w_mix_out"][l]
            gemm_tm(lambda kc, tt: cT[:, kc, tt * 128:(tt + 1) * 128], lambda kc, tt: [('cT', kc, tt // 4)],
                    lambda ns: [(wm[:, ns * 512:(ns + 1) * 512], 8)], 2, resid_add)

        def cross_block(l):
            norm_to_hT(l, 8)
            memf = WA[:, 7, 0:4096].bitcast(F32).rearrange("p (t d) -> p t d", t=2)
            S.dma('sp', lambda e: e.dma_start(out=memf, in_=D["mem"].rearrange("(t p) d -> p t d", p=128)), writes=[('hs', 7)])
            junk, jk = tunit(0, 2)
            for t in range(2):
                S.op('act', lambda e, t=t: e.activation(out=junk, in_=memf[:, t, :], func=AF.Square, accum_out=sm[:, 64 + t:65 + t]),
                     reads=[('hs', 7)], writes=['sm4'] + jk)
            rstd_from_ssq(sm[:, 64:66], 1.0 / DM, 'sm4')
            mnT = WA[:, 4, 0:2048].rearrange("p (k n) -> p k n", k=8)
            for t in range(2):
                hn, hk = tunit(2 + 2 * t, 2)
                S.op('dve', lambda e, t=t, hn=hn: e.tensor_scalar(out=hn, in0=memf[:, t, :], scalar1=sm[:, 64 + t:65 + t], scalar2=None, op0=ALU.mult),
                     reads=[('hs', 7), 'sm4'], writes=hk)
                b = nextbank(ALLB)
                transposes(lambda c, hn=hn: hn[:, c * 128:(c + 1) * 128], 8, b, hk + ['const'])
                gb = _bv(pt[:, l, 16:24], [[1, 8], [0, 128]])
                S.op('dve', lambda e, t=t, b=b, gb=gb: e.tensor_tensor(out=mnT[:, :, t * 128:(t + 1) * 128],
                                                                       in0=psb[b][:, :].rearrange("p (k n) -> p k n", k=8), in1=gb, op=ALU.mult),
                     reads=[PK(b), 'const'], writes=[('hs', 4)])
            kTm = WA[:, 6, 0:2048].rearrange("p (k n) -> p k n", k=8)
            vm = WA[:, 6, 2048:4096].rearrange("p (t n) -> p t n", t=2)
            wkv = D["w_xkv"][l]
            def kv_epi(tt, ns, b):
                if ns < 2:
                    kn, knk = tunit(6 + tt % 2, 1)
                    junk1, jk1 = tunit(0, 1)
                    for hh in range(2):
                        col = 66 + (tt * 4 + ns * 2 + hh)
                        S.op('act', lambda e, hh=hh, col=col: e.activation(out=junk1[:, 0:256], in_=ps[b][:, hh * 256:(hh + 1) * 256], func=AF.Square,
                                                                           accum_out=sm[:, col:col + 1]), reads=[PK(b)], writes=[('sm5', tt, ns)] + jk1)
                    c0 = 66 + tt * 4 + ns * 2
                    rstd_from_ssq(sm[:, c0:c0 + 2], 1.0 / 256, ('sm5', tt, ns))
                    for hh in range(2):
                        S.op('dve', lambda e, hh=hh, kn=kn: e.tensor_scalar(out=kn[:, hh * 256:(hh + 1) * 256], in0=ps[b][:, hh * 256:(hh + 1) * 256],
                                                                            scalar1=sm[:, c0 + hh:c0 + hh + 1], scalar2=None, op0=ALU.mult),
                             reads=[PK(b), ('sm5', tt, ns)], writes=knk)
                    bt = nextbank(ALLB)
                    transposes(lambda c, kn=kn: kn[:, c * 128:(c + 1) * 128], 4, bt, knk + ['const'])
                    gb = _bv(pt[:, l, 42:44], [[0, 2], [1, 2], [0, 128]])
                    S.op('dve', lambda e, bt=bt, gb=gb: e.tensor_tensor(
                        out=kTm[:, ns * 4:(ns + 1) * 4, tt * 128:(tt + 1) * 128].rearrange("p (a c) n -> p a c n", a=2),
                        in0=psb[bt][:, 0:512].rearrange("p (a c n) -> p a c n", a=2, c=2), in1=gb, op=ALU.mult),
                         reads=[PK(bt), 'const'], writes=[('hs', 6)])
                else:
                    copy_op('act', vm[:, tt, (ns - 2) * 512:(ns - 1) * 512], ps[b][:, :], [PK(b)], [('hs', 6)])
            gemm_tm(lambda kc, tt: mnT[:, kc, tt * 128:(tt + 1) * 128], lambda kc, tt: [('hs', 4)],
                    lambda ns: [(wkv[:, ns * 512:(ns + 1) * 512], 8)], 4, kv_epi, ntt=2)
            wq = D["w_xq"][l]
            qT = WA[:, 4:6, :].rearrange("p a (b n) -> p (a b) n", b=2)
            for hpair in range(2):
                def q_epi(tt, ns, b):
                    qn, qnk = tunit(6 + tt % 2, 1)
                    junk1, jk1 = tunit(0, 1)
                    for hh in range(2):
                        S.op('act', lambda e, hh=hh: e.activation(out=junk1[:, 0:256], in_=ps[b][:, hh * 256:(hh + 1) * 256], func=AF.Square,
                                                                  accum_out=sm[:, 80 + 2 * (tt % 2) + hh:81 + 2 * (tt % 2) + hh]),
                             reads=[PK(b)], writes=[('sm6', tt % 2)] + jk1)
                    c0 = 80 + 2 * (tt % 2)
                    rstd_from_ssq(sm[:, c0:c0 + 2], 1.0 / 256, ('sm6', tt % 2))
                    for hh in range(2):
                        S.op('dve', lambda e, hh=hh, qn=qn: e.tensor_scalar(out=qn[:, hh * 256:(hh + 1) * 256], in0=ps[b][:, hh * 256:(hh + 1) * 256],
                                                                            scalar1=sm[:, c0 + hh:c0 + hh + 1], scalar2=None, op0=ALU.mult),
                             reads=[PK(b), ('sm6', tt % 2)], writes=qnk)
                    bt = nextbank(ALLB)
                    transposes(lambda c, qn=qn: qn[:, c * 128:(c + 1) * 128], 4, bt, qnk + ['const'])
                    gb = _bv(pt[:, l, 40:42], [[0, 2], [1, 2], [0, 128]])
                    S.op('dve', lambda e, bt=bt, gb=gb, tt=tt: e.tensor_tensor(
                        out=qT[:, :, tt * 128:(tt + 1) * 128].rearrange("p (a c) n -> p a c n", a=2),
                        in0=psb[bt][:, 0:512].rearrange("p (a c n) -> p a c n", a=2, c=2), in1=gb, op=ALU.mult),
                         reads=[PK(bt), 'const'], writes=[('hs', 4 + (c // 2)) for c in range(4)])
                gemm_tm(lambda kc, tt: hT[:, kc, tt * 128:(tt + 1) * 128], lambda kc, tt: [('hT', kc, tt // 4)],
                        lambda ns: [(wq[:, hpair * 512:(hpair + 1) * 512], 8)], 1, q_epi)
                for hh in range(2):
                    h = hpair * 2 + hh
                    def attn(g, h=h, hh=hh):
                        pT, pTk = tunit(2 + 2 * (g % 2), 2)
                        bsc = [nextbank(ALLB), nextbank(ALLB)]
                        for mc in range(2):
                            for c in range(2):
                                S.op('pe', lambda e, mc=mc, c=c: e.matmul(
                                    ps[bsc[mc]][:, :], lhsT=kTm[:, h * 2 + c, mc * 128:(mc + 1) * 128],
                                    rhs=qT[:, hh * 2 + c, g * 512:(g + 1) * 512], start=(c == 0), stop=(c == 1)),
                                     reads=[('hs', 6), ('hs', 4 + hh)], writes=[PK(bsc[mc])])
                            S.op('act', lambda e, mc=mc, pT=pT: e.activation(out=pT[:, mc * 512:(mc + 1) * 512], in_=ps[bsc[mc]][:, :], func=AF.Exp, scale=1.0 / 16),
                                 reads=[PK(bsc[mc])], writes=pTk)
                        bd = nextbank(ALLB)
                        for mc in range(2):
                            S.op('pe', lambda e, mc=mc, pT=pT: e.matmul(ps[bd][:, :], lhsT=ones[:], rhs=pT[:, mc * 512:(mc + 1) * 512],
                                                                       start=(mc == 0), stop=(mc == 1)), reads=pTk + ['const'], writes=[PK(bd)])
                        rden, rdk = tunit(0, 2, F32)
                        S.op('dve', lambda e, rden=rden: e.reciprocal(out=rden, in_=ps[bd][:, :]), reads=[PK(bd)], writes=rdk)
                        for c in range(2):
                            bo = nextbank(ALLB)
                            for mc in range(2):
                                S.op('pe', lambda e, mc=mc, c=c, pT=pT, bo=bo: e.matmul(
                                    ps[bo][:, :], lhsT=vm[:, mc, h * 256 + c * 128:h * 256 + (c + 1) * 128],
                                    rhs=pT[:, mc * 512:(mc + 1) * 512], start=(mc == 0), stop=(mc == 1)),
                                     reads=pTk + [('hs', 6)], writes=[PK(bo)])
                            S.op('dve', lambda e, c=c, bo=bo, rden=rden: e.tensor_tensor(
                                out=cT[:, h * 2 + c, g * 512:(g + 1) * 512], in0=ps[bo][:, :], in1=rden, op=ALU.mult),
                                 reads=[PK(bo)] + rdk, writes=[('cT', h * 2 + c, g)])
                    for g in range(4):
                        attn(g)
            if stop == 'xattn':
                return
            wo = D["w_xo"][l]
            gemm_tm(lambda kc, tt: cT[:, kc, tt * 128:(tt + 1) * 128], lambda kc, tt: [('cT', kc, tt // 4)],
                    lambda ns: [(wo[:, ns * 512:(ns + 1) * 512], 8)], 2, resid_add)

        def mlp_block(l):
            norm_to_hT(l, 24)
            ring["slots"] = [0, 1, 2, 4, 5, 6, 7]
            wu = D["w_up"][l]; wd = D["w_down"][l]
            for fs in range(4):
                for half in range(2):
                    wap, wkey = load_piece(wu[:, fs * 1024 + half * 512:fs * 1024 + (half + 1) * 512], 8, 512)
                    for m in range(4):
                        fc = half * 4 + m
                        for g in range(4):
                            b = nextbank(ALLB)
                            for kc in range(8):
                                S.op('pe', lambda e, m=m, g=g, kc=kc, b=b, wap=wap: e.matmul(
                                    ps[b][:, :], lhsT=wap[:, kc, m * 128:(m + 1) * 128], rhs=hT[:, kc, g * 512:(g + 1) * 512],
                                    start=(kc == 0), stop=(kc == 7)), reads=[wkey, ('hT', kc, g)], writes=[PK(b)])
                            r, rk = tunit((fc * 4 + g) % 4, 1)
                            S.op('act', lambda e, b=b, r=r: e.activation(out=r, in_=ps[b][:, :], func=AF.Relu), reads=[PK(b)], writes=rk)
                            S.op('pool', lambda e, r=r, fc=fc, g=g: e.tensor_tensor(out=cT[:, fc, g * 512:(g + 1) * 512], in0=r, in1=r, op=ALU.mult),
                                 reads=rk, writes=[('cT', fc, g)])
                gemm_tm(lambda kc, tt: cT[:, kc, tt * 128:(tt + 1) * 128], lambda kc, tt: [('cT', kc, tt // 4)],
                        lambda ns: [(wd[fs * 1024:(fs + 1) * 1024, ns * 512:(ns + 1) * 512], 8)], 2, resid_add)
            ring["slots"] = [0, 1, 2]

        done = False
        for l in range(n_layers):
            mix_block(l)
            if stop in ('ret', 'sb') or stop == f'mix{l}':
                break
            cross_block(l)
            if stop == 'xattn' or stop == f'cross{l}':
                break
            mlp_block(l)
            if stop == f'mlp{l}':
                break

        if stop in ('ret', 'sb', 'xattn'):
            for kc in range(4 if stop == 'ret' else 8):
                for g in range(4):
                    S.op('dve', lambda e, kc=kc, g=g: e.tensor_copy(out=xs[:, kc * 2 + g // 2, (g % 2) * 512:(g % 2 + 1) * 512],
                                                                    in_=cT[:, kc, g * 512:(g + 1) * 512]),
                         reads=[('cT', kc, g)], writes=[('xs', kc * 2 + g // 2)])
        for tt in range(NT):
            S.dma('sp', lambda e, tt=tt: e.dma_start(out=out_d[tt * 128:(tt + 1) * 128, :], in_=xs[:, tt, :]),
                  reads=[('xs', tt)], writes=[('out', tt)], final=True)
        S.emit()
    return nc


_CONSTS = None


def make_in_maps(inputs, n_cores=8):
    global _CONSTS
    if _CONSTS is None:
        _CONSTS = _const_tables()
    f = lambda a: np.ascontiguousarray(np.asarray(a, dtype=np.float32))
    pt = _param_table(*[np.asarray(inputs[k], dtype=np.float32) for k in
                        ("g_mix", "g_cross", "g_mem", "g_mlp", "g_ret_out", "g_sb_out", "g_qn", "g_kn")])
    shared = {k: f(inputs[k]) for k in ("w_in", "w_mix_out", "w_xq", "w_xkv", "w_xo", "w_up", "w_down")}
    shared["pt"] = pt
    shared.update(_CONSTS)
    x = f(inputs["x"]); mem = f(inputs["mem"])
    maps = []
    for c in range(n_cores):
        m = dict(shared)
        m["x"] = x[c]
        m["mem"] = mem[c]
        maps.append(m)
    return maps


def kernel(**inputs):
    nc = build_program()
    maps = make_in_maps(inputs, 8)
    res = run_bass_kernel_spmd(nc, maps, core_ids=list(range(8)))
    return np.stack([np.asarray(r["out"], dtype=np.float32) for r in res.results], axis=0)
```

```python
import contextlib
import numpy as np
import concourse.bass as bass
import concourse.mybir as mybir
from concourse.bass_utils import run_bass_kernel_spmd

dt = mybir.dt
F32, BF16 = dt.float32, dt.bfloat16
AF = mybir.ActivationFunctionType
ALU = mybir.AluOpType
AX = mybir.AxisListType

ENGS = ('pe', 'act', 'dve', 'pool', 'sp')


class Sched:
    EPOCH = 20000
    NDMA = 8

    def __init__(self, nc):
        self.nc = nc
        self.stack = contextlib.ExitStack()
        self.ops = {e: [] for e in ENGS}
        self.ncomp = {e: 0 for e in ENGS}
        self.esem = {e: {} for e in ENGS}
        self.sems = []
        self.lastw = {}
        self.readers = {}
        self.seen = {e: {} for e in ENGS}
        self.dsem = []
        self.dnext = 0
        self.finals = []

    def __enter__(self):
        self.stack.__enter__()
        return self

    def __exit__(self, *a):
        return self.stack.__exit__(*a)

    def sbuf(self, name, shape, dtype):
        return self.stack.enter_context(self.nc.sbuf_tensor(name, list(shape), dtype))

    def psum(self, name, shape, dtype):
        return self.stack.enter_context(self.nc.psum_tensor(name, list(shape), dtype))

    def _newsem(self, name):
        s = self.stack.enter_context(self.nc.semaphore(name))
        self.sems.append(s)
        return len(self.sems) - 1

    def _deps(self, eng, reads, writes, is_dma):
        deps = set()
        for k in reads:
            t = self.lastw.get(k)
            if t is not None:
                deps.add(t)
        for k in writes:
            t = self.lastw.get(k)
            if t is not None:
                deps.add(t)
            for t in self.readers.get(k, ()):
                if t[2] == eng and not is_dma:
                    continue
                deps.add(t)
        return deps

    def _waits(self, eng, deps, is_dma):
        waits = []
        seen = self.seen[eng]
        for (sid, v, deng) in sorted(deps):
            if deng == eng and eng == 'pe' and not is_dma:
                continue
            if seen.get(sid, 0) >= v:
                continue
            seen[sid] = v
            waits.append((sid, v))
        return waits

    def _commit(self, tok, reads, writes):
        for k in reads:
            self.readers.setdefault(k, []).append(tok)
        for k in writes:
            self.lastw[k] = tok
            self.readers[k] = []

    def op(self, eng, fn, reads=(), writes=()):
        deps = self._deps(eng, reads, writes, False)
        waits = self._waits(eng, deps, False)
        idx = self.ncomp[eng]
        ep = idx // self.EPOCH
        if ep not in self.esem[eng]:
            self.esem[eng][ep] = self._newsem(f"s_{eng}_{ep}")
        sid = self.esem[eng][ep]
        tok = (sid, idx % self.EPOCH + 1, eng)
        self.ncomp[eng] += 1
        self._commit(tok, reads, writes)
        self.ops[eng].append((fn, waits, sid, 1))
        return tok

    def dma(self, eng, fn, reads=(), writes=(), final=False):
        if eng == 'pool':
            return self._dma_sw(fn, reads, writes)
        deps = self._deps(eng, reads, writes, True)
        if len(self.dsem) < self.NDMA:
            self.dsem.append([self._newsem(f"s_dma_{len(self.dsem)}"), 0, None])
        ent = self.dsem[self.dnext % self.NDMA]
        self.dnext += 1
        if ent[2] is not None:
            deps.add(ent[2])
        waits = self._waits(eng, deps, True)
        ent[1] += 16
        tok = (ent[0], ent[1], 'dma')
        ent[2] = tok
        self._commit(tok, reads, writes)
        self.ops[eng].append((fn, waits, ent[0], 16))
        if final:
            self.finals.append(tok)
        return tok

    def _dma_sw(self, fn, reads, writes):
        deps = self._deps('pool', reads, writes, True)
        waits = self._waits('pool', deps, True)
        if not hasattr(self, 'nsw'):
            self.nsw = 0
        sid = self._newsem(f"s_sw_{self.nsw}")
        self.nsw += 1
        tok = (sid, 16, 'dma')
        self._commit(tok, reads, writes)
        self.ops['pool'].append((fn, waits, sid, 16))
        return tok

    def emit(self):
        nc = self.nc
        fin = [(sid, v) for (sid, v, _) in self.finals]
        sems = self.sems
        ops = self.ops

        def replay(name, e, tail=()):
            for ent in ops[name]:
                if ent[0] == 'relay':
                    e.wait_ge(sems[ent[1]], 16)
                    e.sem_clear(sems[ent[1]])
                    e.sem_inc(sems[ent[2]], 1)
                    continue
                (fn, waits, sid, n) = ent
                for (ws, wv) in waits:
                    e.wait_ge(sems[ws], wv)
                fn(e).then_inc(sems[sid], n)
            for (ws, wv) in tail:
                e.wait_ge(sems[ws], wv)

        with nc.Block() as block:
            @block.sync
            def _(e):
                replay('sp', e, fin)

            @block.tensor
            def _(e):
                replay('pe', e)

            @block.vector
            def _(e):
                replay('dve', e)

            @block.scalar
            def _(e):
                replay('act', e)

            @block.gpsimd
            def _(e):
                replay('pool', e)


SEQ, DM, DEPTH = 2048, 1024, 2
NT = 16
EPS = 1e-6
NPC = 44
GAM = [1.0 - 2.0 ** (-5 - h) for h in range(4)]


def _const_tables():
    p = np.arange(128)
    c = {}
    c["ident"] = np.eye(128, dtype=np.float32)
    c["tri"] = (p[:, None] >= p[None, :]).astype(np.float32)
    c["ones"] = np.ones((128, 128), np.float32)
    c["mstr"] = (p[:, None] < p[None, :]).astype(np.float32)
    inv = (1.0 / (np.float32(10000.0) ** np.linspace(0.0, 1.0, 64, dtype=np.float32))).astype(np.float32)
    pos = np.arange(SEQ, dtype=np.float32)
    ang = (pos[:, None] * inv[None, :]).astype(np.float32)
    c["cos"] = np.cos(ang.astype(np.float64)).astype(np.float32).reshape(NT, 128, 64).transpose(1, 0, 2).copy()
    c["sin"] = np.sin(ang.astype(np.float64)).astype(np.float32).reshape(NT, 128, 64).transpose(1, 0, 2).copy()
    lg = np.log1p(-np.exp2(-5.0 - np.arange(4, dtype=np.float64)))
    dist = np.abs(p[:, None] - p[None, :]).astype(np.float64)
    same = (p[:, None] // 64) == (p[None, :] // 64)
    dm = np.zeros((128, 4, 128), np.float64)
    qd = np.zeros((128, 4, 128), np.float64)
    kd = np.zeros((128, 4, 128), np.float64)
    for h in range(4):
        dm[:, h, :] = np.where(same, np.exp(lg[h] * dist), 0.0) * (128.0 ** -0.5)
        qd[:, h, :] = np.exp(lg[h] * ((p % 64) + 1.0))[None, :]
        kd[:, h, :] = (np.exp(lg[h] * (63.0 - (p % 64))) * (128.0 ** -0.5))[:, None]
    c["dmk"] = dm.reshape(128, 512).astype(np.float32)
    c["qdc"] = qd.reshape(128, 512).astype(np.float32)
    c["kdc"] = kd.reshape(128, 512).astype(np.float32)
    return c


def _param_table(g_mix, g_cross, g_mem, g_mlp, g_ret_out, g_sb_out, g_qn, g_kn):
    pt = np.zeros((128, DEPTH, NPC), np.float32)
    for l in range(DEPTH):
        pt[:, l, 0:8] = g_mix[l].reshape(8, 128).T
        pt[:, l, 8:16] = g_cross[l].reshape(8, 128).T
        pt[:, l, 16:24] = g_mem[l].reshape(8, 128).T
        pt[:, l, 24:32] = g_mlp[l].reshape(8, 128).T
        pt[:, l, 32:36] = g_ret_out[l].reshape(4, 128).T
        pt[:, l, 36:40] = g_sb_out[l].reshape(4, 128).T
        pt[:, l, 40:42] = g_qn[l].reshape(2, 128).T
        pt[:, l, 42:44] = g_kn[l].reshape(2, 128).T
    return pt


def _bv(ap, dims):
    return bass.AP(ap.tensor, ap.offset, [list(ap.ap[0])] + [list(d) for d in dims])


def build_program(n_layers=DEPTH, stop=None):
    nc = bass.Bass("TRN2", target_bir_lowering=False)
    D = {}
    def din(name, shape):
        D[name] = nc.dram_tensor(name, list(shape), F32, kind="ExternalInput").ap()
    din("x", [SEQ, DM]); din("mem", [256, DM])
    din("w_in", [DEPTH, DM, 3584]); din("w_mix_out", [DEPTH, DM, DM]); din("w_xq", [DEPTH, DM, DM])
    din("w_xkv", [DEPTH, DM, 2 * DM]); din("w_xo", [DEPTH, DM, DM]); din("w_up", [DEPTH, DM, 4 * DM])
    din("w_down", [DEPTH, 4 * DM, DM]); din("pt", [128, DEPTH, NPC])
    for nm in ("ident", "tri", "ones", "mstr"):
        din(nm, [128, 128])
    din("cos", [128, NT, 64]); din("sin", [128, NT, 64])
    for nm in ("dmk", "qdc", "kdc"):
        din(nm, [128, 512])
    out_d = nc.dram_tensor("out", [SEQ, DM], F32, kind="ExternalOutput").ap()

    S = Sched(nc)
    with S:
        xs = S.sbuf("xs", [128, NT, DM], F32)
        hT = S.sbuf("hT", [128, 8, SEQ], BF16)
        cT = S.sbuf("cT", [128, 8, SEQ], BF16)
        WA = S.sbuf("WA", [128, 8, 4096], BF16)
        ident = S.sbuf("ident_s", [128, 128], BF16)
        tri = S.sbuf("tri_s", [128, 128], BF16)
        ones = S.sbuf("ones_s", [128, 128], BF16)
        mstr = S.sbuf("mstr_s", [128, 128], BF16)
        zer = S.sbuf("zer_s", [128, 128], BF16)
        mhalf = S.sbuf("mhalf_s", [128, 16], F32)
        cosT = S.sbuf("cos_s", [128, NT, 64], BF16)
        sinT = S.sbuf("sin_s", [128, NT, 64], BF16)
        dmk = S.sbuf("dmk_s", [128, 512], BF16)
        qdc = S.sbuf("qdc_s", [128, 512], BF16)
        kdc = S.sbuf("kdc_s", [128, 512], BF16)
        pt = S.sbuf("pt_s", [128, DEPTH, NPC], F32)
        st = S.sbuf("st_s", [128, 4, 128], F32)
        stb = S.sbuf("stb_s", [128, 2, 4, 128], BF16)
        sm = S.sbuf("sm_s", [128, 128], F32)
        PS = S.psum("ps", [128, 8, 512], F32)
        ps = [PS[:, i, :] for i in range(8)]
        psb = [p.bitcast(BF16) for p in ps]

        rr = {"dq": 0, "bank": 0, "ring": 0, "ev": 0}

        def dq():
            rr["dq"] += 1
            return 'sp' if rr["dq"] % 2 else 'act'

        def PK(b):
            return ('ps', b)

        def unit(buf, name, u, n=1, dtype=BF16):
            kc, g = u // 4, u % 4
            assert g + n <= 4
            ap = buf[:, kc, g * 512:(g + n) * 512]
            if dtype == F32:
                ap = ap.bitcast(F32)
            return ap, [(name, kc, g + i) for i in range(n)]

        def tunit(i, n=1, dtype=BF16):
            ap = WA[:, 3, i * 512:(i + n) * 512]
            if dtype == F32:
                ap = ap.bitcast(F32)
            return ap, [('t', i + k) for k in range(n)]

        def nextbank(allowed):
            rr["bank"] += 1
            return allowed[rr["bank"] % len(allowed)]

        ring = {"slots": [0, 1, 2]}

        def load_piece(w2d, nk, ncols):
            assert nk * ncols <= 4096
            rr["ring"] += 1
            s = ring["slots"][rr["ring"] % len(ring["slots"])]
            dst = WA[:, s, 0:nk * ncols].rearrange("p (k n) -> p k n", k=nk)
            src = w2d.rearrange("(k p) n -> p k n", p=128)
            S.dma('pool', lambda e: e.dma_start(out=dst, in_=src), reads=[], writes=[('hs', s)])
            return dst, ('hs', s)

        def evac_engine():
            rr["ev"] += 1
            return 'act' if rr["ev"] % 2 else 'dve'

        def copy_op(eng, out, in_, reads, writes):
            if eng == 'act':
                S.op('act', lambda e: e.copy(out=out, in_=in_), reads=reads, writes=writes)
            else:
                S.op(eng, lambda e: e.tensor_copy(out=out, in_=in_), reads=reads, writes=writes)

        def rstd_from_ssq(ssq_ap, inv_n, key):
            S.op('dve', lambda e: e.tensor_scalar(out=ssq_ap, in0=ssq_ap, scalar1=inv_n, scalar2=EPS,
                                                  op0=ALU.mult, op1=ALU.add), reads=[key], writes=[key])
            S.op('act', lambda e: e.activation(out=ssq_ap, in_=ssq_ap, func=AF.Sqrt), reads=[key], writes=[key])
            S.op('dve', lambda e: e.reciprocal(out=ssq_ap, in_=ssq_ap), reads=[key], writes=[key])

        def transposes(src_fn, n, bank, reads):
            for c in range(n):
                S.op('pe', lambda e, c=c: e.transpose(out=psb[bank][:, c * 128:(c + 1) * 128], in_=src_fn(c),
                                                      identity=ident[:]),
                     reads=reads, writes=[PK(bank)])

        for tt in range(NT):
            S.dma('sp', lambda e, tt=tt: e.dma_start(out=xs[:, tt, :], in_=D["x"][tt * 128:(tt + 1) * 128, :]),
                  writes=[('xs', tt)])
        for (sb_t, nm) in ((ident, "ident"), (tri, "tri"), (ones, "ones"), (mstr, "mstr"), (cosT, "cos"),
                           (sinT, "sin"), (dmk, "dmk"), (qdc, "qdc"), (kdc, "kdc")):
            S.dma('pool', lambda e, sb_t=sb_t, nm=nm: e.dma_start(out=sb_t[:], in_=D[nm]), writes=['const'])
        S.dma('sp', lambda e: e.dma_start(out=pt[:], in_=D["pt"]), writes=['const'])
        S.op('pool', lambda e: e.memset(zer[:], 0.0), writes=['const'])
        S.op('pool', lambda e: e.memset(mhalf[:], -0.5), writes=['const'])

        ALLB = list(range(8))

        def norm_to_hT(l, gcol):
            junk, jk = tunit(0, 2)
            for tt in range(NT):
                S.op('act', lambda e, tt=tt: e.activation(out=junk, in_=xs[:, tt, :], func=AF.Square,
                                                          accum_out=sm[:, tt:tt + 1]),
                     reads=[('xs', tt)], writes=['sm'] + jk)
            rstd_from_ssq(sm[:, 0:NT], 1.0 / DM, 'sm')
            gb = _bv(pt[:, l, gcol:gcol + 8], [[1, 8], [0, 128]])

            def front(tt):
                hn, hk = tunit(2 + 2 * (tt % 3), 2)
                S.op('dve', lambda e: e.tensor_scalar(out=hn, in0=xs[:, tt, :], scalar1=sm[:, tt:tt + 1],
                                                      scalar2=None, op0=ALU.mult),
                     reads=[('xs', tt), 'sm'], writes=hk)
                b = nextbank(ALLB)
                transposes(lambda c: hn[:, c * 128:(c + 1) * 128], 8, b, hk + ['const'])
                return b

            def back(tt, b):
                S.op('dve', lambda e: e.tensor_tensor(
                    out=hT[:, :, tt * 128:(tt + 1) * 128],
                    in0=psb[b][:, :].rearrange("p (k n) -> p k n", k=8), in1=gb, op=ALU.mult),
                     reads=[PK(b), 'const'], writes=[('hT', kc, tt // 4) for kc in range(8)])
            prev = None
            for tt in range(NT):
                b = front(tt)
                if prev is not None:
                    back(*prev)
                prev = (tt, b)
            back(*prev)

        def gemm_tm(lhs_fn, lhs_keys_fn, pieces_fn, n_slices, epilogue, banks=ALLB, ntt=NT):
            pending = None
            for ns in range(n_slices):
                pcs = [load_piece(w2d, nk, 512) for (w2d, nk) in pieces_fn(ns)]
                for tt in range(ntt):
                    b = nextbank(banks)
                    tot = sum(p[0].shape[1] for p in pcs)
                    i = 0
                    for pi, (wap, wkey) in enumerate(pcs):
                        for k in range(wap.shape[1]):
                            kc = sum(p[0].shape[1] for p in pcs[:pi]) + k
                            S.op('pe', lambda e, wap=wap, k=k, kc=kc, i=i, b=b, tt=tt: e.matmul(
                                ps[b][:, :], lhsT=lhs_fn(kc, tt), rhs=wap[:, k, :], start=(i == 0), stop=(i == tot - 1)),
                                 reads=[wkey] + lhs_keys_fn(kc, tt), writes=[PK(b)])
                            i += 1
                    if pending is not None:
                        epilogue(*pending)
                    pending = (tt, ns, b)
                if pending is not None:
                    epilogue(*pending)
                    pending = None

        def resid_add(tt, ns, b):
            S.op('dve', lambda e: e.tensor_tensor(out=xs[:, tt, ns * 512:(ns + 1) * 512], in0=ps[b][:, :],
                                                  in1=xs[:, tt, ns * 512:(ns + 1) * 512], op=ALU.add),
                 reads=[PK(b), ('xs', tt)], writes=[('xs', tt)])

        def mix_block(l):
            w_in = D["w_in"][l]
            norm_to_hT(l, 0)
            for j in range(4):
                dst = WA[:, 4 + j, :].rearrange("p (k n) -> p k n", k=8)
                src = w_in[:, j * 512:(j + 1) * 512].rearrange("(k p) n -> p k n", p=128)
                S.dma('pool', lambda e, dst=dst, src=src: e.dma_start(out=dst, in_=src), writes=[('hs', 4 + j)])
            S.op('pool', lambda e: e.memset(st[:], 0.0), writes=['st'])
            S.op('pool', lambda e: e.memset(stb[:], 0.0), writes=['stb0', 'stb1'])
            def cu(i, n=1, dtype=BF16):
                return unit(cT, 'cT', 16 + i, n, dtype)
            def ret_tile(tt):
                pb = [nextbank(ALLB) for _ in range(4)]
                for j in range(4):
                    wv = WA[:, 4 + j, :].rearrange("p (k n) -> p k n", k=8)
                    for kc in range(8):
                        S.op('pe', lambda e, j=j, kc=kc, wv=wv, tt=tt: e.matmul(
                            ps[pb[j]][:, :], lhsT=hT[:, kc, tt * 128:(tt + 1) * 128], rhs=wv[:, kc, :],
                            start=(kc == 0), stop=(kc == 7)),
                             reads=[('hs', 4 + j), ('hT', kc, tt // 4)], writes=[PK(pb[j])])
                yield
                pp = tt % 2
                A, Ak = cu(0, 2, F32)
                Bm, Bk = cu(2, 2, F32)
                qr, qrk = cu(4); kr, krk = cu(5); STm, STk = cu(6); on, onk = cu(7)
                kdt, kdk = cu(8 + pp); vb, vbk = cu(10 + pp); sg, sgk = cu(12 + pp); qTt, qTk = cu(14 + pp)
                qdT, qdk = tunit(1 + pp); kTt, kTk = tunit(3 + pp)
                cosb = _bv(cosT[:, tt, :], [[0, 4], [0, 2], [1, 64]])
                sinb = _bv(sinT[:, tt, :], [[0, 4], [1, 64]])
                def rot(bank, dstt, dk):
                    P4 = ps[bank][:, :].rearrange("p (h x d) -> p h x d", h=4, x=2)
                    A4 = A.rearrange("p (h x d) -> p h x d", h=4, x=2)
                    B4 = Bm.rearrange("p (x h d) -> p x h d", x=2, h=4)
                    D4 = dstt.rearrange("p (h x d) -> p h x d", h=4, x=2)
                    S.op('dve', lambda e, P4=P4, A4=A4: e.tensor_tensor(out=A4, in0=P4, in1=cosb, op=ALU.mult),
                         reads=[PK(bank), 'const'], writes=Ak)
                    S.op('dve', lambda e, P4=P4, B4=B4: e.tensor_tensor(out=B4[:, 0], in0=P4[:, :, 1, :], in1=sinb, op=ALU.mult),
                         reads=[PK(bank), 'const'], writes=[Bk[0]])
                    S.op('dve', lambda e, P4=P4, B4=B4: e.tensor_tensor(out=B4[:, 1], in0=P4[:, :, 0, :], in1=sinb, op=ALU.mult),
                         reads=[PK(bank), 'const'], writes=[Bk[1]])
                    S.op('pool', lambda e, A4=A4, B4=B4, D4=D4: e.tensor_tensor(out=D4[:, :, 0, :], in0=A4[:, :, 0, :], in1=B4[:, 0], op=ALU.subtract),
                         reads=Ak + Bk, writes=dk)
                    S.op('pool', lambda e, A4=A4, B4=B4, D4=D4: e.tensor_tensor(out=D4[:, :, 1, :], in0=A4[:, :, 1, :], in1=B4[:, 1], op=ALU.add),
                         reads=Ak + Bk, writes=dk)
                rot(pb[0], qr, qrk)
                yield
                S.op('act', lambda e: e.copy(out=vb, in_=ps[pb[2]][:, :]), reads=[PK(pb[2])], writes=vbk)
                yield
                rot(pb[1], kr, krk)
                yield
                S.op('act', lambda e: e.activation(out=sg, in_=ps[pb[3]][:, :], func=AF.Silu), reads=[PK(pb[3])], writes=sgk)
                S.op('pool', lambda e: e.tensor_tensor(out=kdt, in0=kr, in1=kdc[:], op=ALU.mult), reads=krk + ['const'], writes=kdk)
                yield
                bq = nextbank(ALLB)
                transposes(lambda c: qr[:, c * 128:(c + 1) * 128], 4, bq, qrk + ['const'])
                S.op('act', lambda e: e.copy(out=qTt, in_=psb[bq][:, 0:512]), reads=[PK(bq)], writes=qTk)
                S.op('dve', lambda e: e.tensor_tensor(out=qdT, in0=psb[bq][:, 0:512], in1=qdc[:], op=ALU.mult),
                     reads=[PK(bq), 'const'], writes=qdk)
                yield
                bk = nextbank(ALLB)
                transposes(lambda c: kr[:, c * 128:(c + 1) * 128], 4, bk, krk + ['const'])
                S.op('act', lambda e: e.copy(out=kTt, in_=psb[bk][:, 0:512]), reads=[PK(bk)], writes=kTk)
                yield 'split'
                bs = nextbank(ALLB)
                for h in range(4):
                    S.op('pe', lambda e, h=h: e.matmul(ps[bs][:, h * 128:(h + 1) * 128], lhsT=kTt[:, h * 128:(h + 1) * 128],
                                                       rhs=qTt[:, h * 128:(h + 1) * 128], start=True, stop=True),
                         reads=kTk + qTk, writes=[PK(bs)])
                S.op('dve', lambda e: e.tensor_tensor(out=STm, in0=ps[bs][:, :], in1=dmk[:], op=ALU.mult),
                     reads=[PK(bs), 'const'], writes=STk)
                yield
                bu = [nextbank(ALLB), nextbank(ALLB)]
                for c in range(2):
                    for h in range(4):
                        S.op('pe', lambda e, c=c, h=h: e.matmul(
                            ps[bu[c]][:, h * 128:(h + 1) * 128], lhsT=kdt[c * 64:(c + 1) * 64, h * 128:(h + 1) * 128],
                            rhs=vb[c * 64:(c + 1) * 64, h * 128:(h + 1) * 128], start=True, stop=True),
                             reads=kdk + vbk, writes=[PK(bu[c])])
                yield
                stf = st[:].rearrange("p h e -> p (h e)")
                for h in range(4):
                    S.op('dve', lambda e, h=h: e.scalar_tensor_tensor(
                        out=st[:, h, :], in0=st[:, h, :], scalar=float(GAM[h] ** 64), in1=ps[bu[0]][:, h * 128:(h + 1) * 128],
                        op0=ALU.mult, op1=ALU.add), reads=['st', PK(bu[0])], writes=['st'])
                S.op('act', lambda e: e.copy(out=stb[:, 1].rearrange("p h e -> p (h e)"), in_=stf), reads=['st'], writes=['stb1'])
                yield
                bo = nextbank(ALLB)
                for h in range(4):
                    S.op('pe', lambda e, h=h: e.matmul(ps[bo][:, h * 128:(h + 1) * 128], lhsT=STm[:, h * 128:(h + 1) * 128],
                                                       rhs=vb[:, h * 128:(h + 1) * 128], start=True, stop=False),
                         reads=STk + vbk, writes=[PK(bo)])
                    for c in range(2):
                        S.op('pe', lambda e, h=h, c=c: e.matmul(
                            ps[bo][c * 64:(c + 1) * 64, h * 128:(h + 1) * 128],
                            lhsT=qdT[:, h * 128 + c * 64:h * 128 + (c + 1) * 64], rhs=stb[:, c, h, :],
                            start=False, stop=True),
                             reads=qdk + [f'stb{c}'], writes=[PK(bo)])
                yield
                for h in range(4):
                    S.op('dve', lambda e, h=h: e.scalar_tensor_tensor(
                        out=st[:, h, :], in0=st[:, h, :], scalar=float(GAM[h] ** 64), in1=ps[bu[1]][:, h * 128:(h + 1) * 128],
                        op0=ALU.mult, op1=ALU.add), reads=['st', PK(bu[1])], writes=['st'])
                S.op('act', lambda e: e.copy(out=stb[:, 0].rearrange("p h e -> p (h e)"), in_=stf), reads=['st'], writes=['stb0'])
                yield
                junk, jk = tunit(0, 1)
                for h in range(4):
                    S.op('act', lambda e, h=h: e.activation(out=junk[:, 0:128], in_=ps[bo][:, h * 128:(h + 1) * 128], func=AF.Square,
                                                            accum_out=sm[:, 32 + h:33 + h]), reads=[PK(bo)], writes=['sm2'] + jk)
                yield
                rstd_from_ssq(sm[:, 32:36], 1.0 / 128, 'sm2')
                yield
                for h in range(4):
                    S.op('dve', lambda e, h=h: e.scalar_tensor_tensor(
                        out=on[:, h * 128:(h + 1) * 128], in0=ps[bo][:, h * 128:(h + 1) * 128], scalar=sm[:, 32 + h:33 + h],
                        in1=sg[:, h * 128:(h + 1) * 128], op0=ALU.mult, op1=ALU.mult),
                         reads=[PK(bo), 'sm2'] + sgk, writes=onk)
                yield
                bt = nextbank(ALLB)
                transposes(lambda c: on[:, c * 128:(c + 1) * 128], 4, bt, onk + ['const'])
                yield
                gb = _bv(pt[:, l, 32:36], [[1, 4], [0, 128]])
                S.op('dve', lambda e, tt=tt, bt=bt, gb=gb: e.tensor_tensor(
                    out=cT[:, 0:4, tt * 128:(tt + 1) * 128], in0=psb[bt][:, 0:512].rearrange("p (k n) -> p k n", k=4),
                    in1=gb, op=ALU.mult), reads=[PK(bt), 'const'], writes=[('cT', kc, tt // 4) for kc in range(4)])
            gens = [ret_tile(tt) for tt in range(NT)]
            for v in gens[0]:
                if v == 'split':
                    break
            for tt in range(NT):
                gB = gens[tt]
                gA = gens[tt + 1] if tt + 1 < NT else None
                doneA, doneB = gA is None, False
                while not (doneA and doneB):
                    if not doneB:
                        try:
                            next(gB)
                        except StopIteration:
                            doneB = True
                    if not doneA:
                        try:
                            if next(gA) == 'split':
                                doneA = True
                        except StopIteration:
                            doneA = True
            if stop == 'ret':
                return
            skT = WA[:, 4:6, :].rearrange("p a (b n) -> p (a b) n", b=2)
            svv = WA[:, 6:8, :].rearrange("p a (t n) -> p (a t) n", t=8)
            for (which, c0) in (("q", 2048), ("k", 2560)):
                wap, wkey = load_piece(w_in[:, c0:c0 + 512], 8, 512)
                for m in range(4):
                    for g in range(4):
                        b = nextbank(ALLB)
                        for kc in range(8):
                            S.op('pe', lambda e, m=m, g=g, kc=kc, b=b, wap=wap: e.matmul(
                                ps[b][:, :], lhsT=wap[:, kc, m * 128:(m + 1) * 128], rhs=hT[:, kc, g * 512:(g + 1) * 512],
                                start=(kc == 0), stop=(kc == 7)),
                                 reads=[wkey, ('hT', kc, g)], writes=[PK(b)])
                        if which == "q":
                            copy_op(evac_engine(), cT[:, 4 + m, g * 512:(g + 1) * 512], ps[b][:, :], [PK(b)], [('cT', 4 + m, g)])
                        else:
                            copy_op(evac_engine(), skT[:, m, g * 512:(g + 1) * 512], ps[b][:, :], [PK(b)], [('hs', 4 + m // 2)])
            wap, wkey = load_piece(w_in[:, 3072:3584], 8, 512)
            for tt in range(NT):
                b = nextbank(ALLB)
                for kc in range(8):
                    S.op('pe', lambda e, tt=tt, kc=kc, b=b, wap=wap: e.matmul(
                        ps[b][:, :], lhsT=hT[:, kc, tt * 128:(tt + 1) * 128], rhs=wap[:, kc, :], start=(kc == 0), stop=(kc == 7)),
                         reads=[wkey, ('hT', kc, tt // 4)], writes=[PK(b)])
                copy_op(evac_engine(), svv[:, tt, :], ps[b][:, :], [PK(b)], [('hs', 6 + tt // 8)])
            PSall = PS
            ZBK = [[0, 1], [2, 3]]
            CBK = [4, 5]
            ACC, SSQ = 6, 7

            def hrow(row, lo, hi, dtype=BF16):
                ap = hT[:, row, lo:hi]
                if dtype == F32:
                    ap = ap.bitcast(F32)
                return ap, [('hT', row, g) for g in range(lo // 512, (hi + 511) // 512)]
            Ebuf = [hrow(0, 0, 2048, F32), hrow(1, 0, 2048, F32), tunit(0, 4, F32)]
            Gbuf = hrow(2, 0, 2048, F32)
            spbuf = [hrow(3, 0, 1024), hrow(3, 1024, 2048)]
            Abuf = [hrow(4, 0, 1024), hrow(4, 1024, 2048)]
            Rbuf = [hrow(5, 0, 1024), hrow(5, 1024, 2048)]
            rsb = hrow(7, 512, 1536, F32)
            rawb = hrow(6, 0, 2048)
            sqb = hrow(7, 0, 512)

            def v2(ap):
                return ap.rearrange("p (c n) -> p c n", c=2)
            mstr2 = _bv(mstr[:], [[0, 2], [1, 128]])

            def sweep_group(G):
                nkb = 4 * G + 4
                items = [(hp, kb) for hp in range(4) for kb in range(nkb - 1, -1, -1)]
                n = len(items)

                def c0_of(i):
                    return max(0, items[i][1] - 4 * G) * 128

                def zmm(i):
                    hp, kb = items[i]
                    c0 = c0_of(i)
                    for ch in range(2):
                        ho = ch * 64
                        zb = ZBK[i % 2][ch]
                        S.op('pe', lambda e, ho=ho, zb=zb: e.matmul(
                            ps[zb][:, c0:512], lhsT=skT[ho:ho + 64, hp, kb * 128:(kb + 1) * 128],
                            rhs=cT[ho:ho + 64, 4 + hp, G * 512 + c0:(G + 1) * 512], start=True, stop=True),
                             reads=[('hs', 4 + hp // 2), ('cT', 4 + hp, G)], writes=[PK(zb)])

                def esp(i):
                    hp, kb = items[i]
                    c0 = c0_of(i)
                    par = i % 2
                    E, Ek = Ebuf[i % 3]; sp, spk = spbuf[par]
                    zb = ZBK[par]
                    S.op('act', lambda e: e.activation(out=v2(E)[:, :, c0:512], in_=PSall[:, zb[0]:zb[0] + 2, c0:512], func=AF.Exp, scale=0.125),
                         reads=[PK(zb[0]), PK(zb[1])], writes=Ek)
                    S.op('act', lambda e: e.activation(out=v2(sp)[:, :, c0:512], in_=v2(E)[:, :, c0:512], func=AF.Ln, bias=1.0, scale=1.0),
                         reads=Ek, writes=spk)
                    if kb >= 4 * G:
                        S.op('dve', lambda e: e.tensor_tensor(out=v2(sp)[:, :, c0:c0 + 128], in0=v2(sp)[:, :, c0:c0 + 128], in1=mstr2, op=ALU.mult),
                             reads=spk + ['const'], writes=spk)
                    first = (kb == nkb - 1)
                    R, Rk = Rbuf[par]; Rn, Rnk = Rbuf[1 - par]
                    if kb > 0:
                        if kb >= 4 * G:
                            if c0 > 0:
                                S.op('dve', lambda e: e.memset(v2(Rn)[:, :, 0:c0], 0.0), writes=Rnk)
                        if first:
                            S.op('dve', lambda e: e.tensor_copy(out=v2(Rn)[:, :, c0:512], in_=v2(sp)[:, :, c0:512]), reads=spk, writes=Rnk)
                        else:
                            S.op('dve', lambda e: e.tensor_tensor(out=v2(Rn)[:, :, c0:512], in0=v2(R)[:, :, c0:512], in1=v2(sp)[:, :, c0:512], op=ALU.add),
                                 reads=Rk + spk, writes=Rnk)

                def cmm(i):
                    hp, kb = items[i]
                    c0 = c0_of(i)
                    first = (kb == nkb - 1)
                    sp, spk = spbuf[i % 2]; R, Rk = Rbuf[i % 2]
                    for ch in range(2):
                        S.op('pe', lambda e, ch=ch: e.matmul(ps[CBK[ch]][:, c0:512], lhsT=tri[:], rhs=v2(sp)[:, ch, c0:512], start=True, stop=first),
                             reads=spk + ['const'], writes=[PK(CBK[ch])])
                        if not first:
                            S.op('pe', lambda e, ch=ch: e.matmul(ps[CBK[ch]][:, c0:512], lhsT=ones[:], rhs=v2(R)[:, ch, c0:512], start=False, stop=True),
                                 reads=Rk + ['const'], writes=[PK(CBK[ch])])

                def rest(i):
                    hp, kb = items[i]
                    c0 = c0_of(i)
                    par = i % 2
                    first = (kb == nkb - 1)
                    E, Ek = Ebuf[i % 3]; sp, spk = spbuf[par]; Gt, Gk = Gbuf; A, Akk = Abuf[par]; R, Rk = Rbuf[par]; Rn, Rnk = Rbuf[1 - par]
                    S.op('act', lambda e: e.activation(out=v2(Gt)[:, :, c0:512], in_=PSall[:, CBK[0]:CBK[0] + 2, c0:512], func=AF.Exp, scale=-1.0),
                         reads=[PK(CBK[0]), PK(CBK[1])], writes=Gk)
                    S.op('dve', lambda e: e.tensor_tensor(out=v2(A)[:, :, c0:512], in0=v2(E)[:, :, c0:512], in1=v2(Gt)[:, :, c0:512], op=ALU.mult),
                         reads=Ek + Gk, writes=Akk)
                    if kb >= 4 * G:
                        S.op('pool', lambda e: e.tensor_tensor(out=v2(A)[:, :, c0:c0 + 128], in0=v2(A)[:, :, c0:c0 + 128], in1=mstr2, op=ALU.mult),
                             reads=Akk + ['const'], writes=Akk)

                def avmm(i):
                    hp, kb = items[i]
                    c0 = c0_of(i)
                    first = (kb == nkb - 1)
                    A, Akk = Abuf[i % 2]
                    if first:
                        S.op('pe', lambda e: e.matmul(ps[ACC][:, :], lhsT=zer[:], rhs=dmk[:], start=True, stop=False),
                             reads=['const'], writes=[PK(ACC)])
                    for ch in range(2):
                        h = 2 * hp + ch
                        S.op('pe', lambda e, ch=ch, h=h: e.matmul(
                            ps[ACC][ch * 64:(ch + 1) * 64, c0:512], lhsT=svv[:, kb, h * 64:(h + 1) * 64], rhs=v2(A)[:, ch, c0:512],
                            start=False, stop=(kb == 0), skip_group_check=True),
                             reads=Akk + [('hs', 6 + kb // 8)], writes=[PK(ACC)])
                    if kb == 0:
                        raw, rawk = rawb; sq, sqk = sqb
                        S.op('act', lambda e: e.copy(out=raw[:, hp * 512:(hp + 1) * 512], in_=ps[ACC][:, :]), reads=[PK(ACC)], writes=[rawk[hp]])
                        S.op('act', lambda e: e.activation(out=sq, in_=ps[ACC][:, :], func=AF.Square), reads=[PK(ACC)], writes=sqk)
                        S.op('pe', lambda e: e.matmul(ps[SSQ][:, :], lhsT=ones[:], rhs=sq, start=(hp == 0), stop=(hp == 3)),
                             reads=sqk + ['const'], writes=[PK(SSQ)])

                zmm(0)
                if n > 1:
                    zmm(1)
                esp(0)
                cmm(0)
                for i in range(n):
                    if i + 2 < n:
                        zmm(i + 2)
                    if i + 1 < n:
                        esp(i + 1)
                    rest(i)
                    if i + 1 < n:
                        cmm(i + 1)
                    avmm(i)
                rs, rsk = rsb; raw, rawk = rawb
                S.op('dve', lambda e: e.tensor_scalar(out=rs, in0=ps[SSQ][:, :], scalar1=1.0 / 512, scalar2=EPS, op0=ALU.mult, op1=ALU.add),
                     reads=[PK(SSQ)], writes=rsk)
                S.op('act', lambda e: e.activation(out=rs, in_=rs, func=AF.Sqrt), reads=rsk, writes=rsk)
                S.op('dve', lambda e: e.reciprocal(out=rs, in_=rs), reads=rsk, writes=rsk)
                for hp in range(4):
                    S.op('dve', lambda e, hp=hp: e.scalar_tensor_tensor(
                        out=cT[:, 4 + hp, G * 512:(G + 1) * 512], in0=raw[:, hp * 512:(hp + 1) * 512], scalar=pt[:, l, 36 + hp:37 + hp],
                        in1=rs, op0=ALU.mult, op1=ALU.mult), reads=[rawk[hp], 'const'] + rsk, writes=[('cT', 4 + hp, G)])
            for G in range(4):
                sweep_group(G)
            if stop == 'sb':
                return
            wm = D["w_mix_out"][l]
            gemm_tm(lambda kc, tt: cT[:, kc, tt * 128:(tt + 1) * 128], lambda kc, tt: [('cT', kc, tt // 4)],
                    lambda ns: [(wm[:, ns * 512:(ns + 1) * 512], 8)], 2, resid_add)

        def cross_block(l):
            norm_to_hT(l, 8)
            memf = WA[:, 7, 0:4096].bitcast(F32).rearrange("p (t d) -> p t d", t=2)
            S.dma('sp', lambda e: e.dma_start(out=memf, in_=D["mem"].rearrange("(t p) d -> p t d", p=128)), writes=[('hs', 7)])
            junk, jk = tunit(0, 2)
            for t in range(2):
                S.op('act', lambda e, t=t: e.activation(out=junk, in_=memf[:, t, :], func=AF.Square, accum_out=sm[:, 64 + t:65 + t]),
                     reads=[('hs', 7)], writes=['sm4'] + jk)
            rstd_from_ssq(sm[:, 64:66], 1.0 / DM, 'sm4')
            mnT = WA[:, 4, 0:2048].rearrange("p (k n) -> p k n", k=8)
            for t in range(2):
                hn, hk = tunit(2 + 2 * t, 2)
                S.op('dve', lambda e, t=t, hn=hn: e.tensor_scalar(out=hn, in0=memf[:, t, :], scalar1=sm[:, 64 + t:65 + t], scalar2=None, op0=ALU.mult),
                     reads=[('hs', 7), 'sm4'], writes=hk)
                b = nextbank(ALLB)
                transposes(lambda c, hn=hn: hn[:, c * 128:(c + 1) * 128], 8, b, hk + ['const'])
                gb = _bv(pt[:, l, 16:24], [[1, 8], [0, 128]])
                S.op('dve', lambda e, t=t, b=b, gb=gb: e.tensor_tensor(out=mnT[:, :, t * 128:(t + 1) * 128],
                                                                       in0=psb[b][:, :].rearrange("p (k n) -> p k n", k=8), in1=gb, op=ALU.mult),
                     reads=[PK(b), 'const'], writes=[('hs', 4)])
            kTm = WA[:, 6, 0:2048].rearrange("p (k n) -> p k n", k=8)
            vm = WA[:, 6, 2048:4096].rearrange("p (t n) -> p t n", t=2)
            wkv = D["w_xkv"][l]
            def kv_epi(tt, ns, b):
                if ns < 2:
                    kn, knk = tunit(6 + tt % 2, 1)
                    junk1, jk1 = tunit(0, 1)
                    for hh in range(2):
                        col = 66 + (tt * 4 + ns * 2 + hh)
                        S.op('act', lambda e, hh=hh, col=col: e.activation(out=junk1[:, 0:256], in_=ps[b][:, hh * 256:(hh + 1) * 256], func=AF.Square,
                                                                           accum_out=sm[:, col:col + 1]), reads=[PK(b)], writes=[('sm5', tt, ns)] + jk1)
                    c0 = 66 + tt * 4 + ns * 2
                    rstd_from_ssq(sm[:, c0:c0 + 2], 1.0 / 256, ('sm5', tt, ns))
                    for hh in range(2):
                        S.op('dve', lambda e, hh=hh, kn=kn: e.tensor_scalar(out=kn[:, hh * 256:(hh + 1) * 256], in0=ps[b][:, hh * 256:(hh + 1) * 256],
                                                                            scalar1=sm[:, c0 + hh:c0 + hh + 1], scalar2=None, op0=ALU.mult),
                             reads=[PK(b), ('sm5', tt, ns)], writes=knk)
                    bt = nextbank(ALLB)
                    transposes(lambda c, kn=kn: kn[:, c * 128:(c + 1) * 128], 4, bt, knk + ['const'])
                    gb = _bv(pt[:, l, 42:44], [[0, 2], [1, 2], [0, 128]])
                    S.op('dve', lambda e, bt=bt, gb=gb: e.tensor_tensor(
                        out=kTm[:, ns * 4:(ns + 1) * 4, tt * 128:(tt + 1) * 128].rearrange("p (a c) n -> p a c n", a=2),
                        in0=psb[bt][:, 0:512].rearrange("p (a c n) -> p a c n", a=2, c=2), in1=gb, op=ALU.mult),
                         reads=[PK(bt), 'const'], writes=[('hs', 6)])
                else:
                    copy_op('act', vm[:, tt, (ns - 2) * 512:(ns - 1) * 512], ps[b][:, :], [PK(b)], [('hs', 6)])
            gemm_tm(lambda kc, tt: mnT[:, kc, tt * 128:(tt + 1) * 128], lambda kc, tt: [('hs', 4)],
                    lambda ns: [(wkv[:, ns * 512:(ns + 1) * 512], 8)], 4, kv_epi, ntt=2)
            wq = D["w_xq"][l]
            qT = WA[:, 4:6, :].rearrange("p a (b n) -> p (a b) n", b=2)
            for hpair in range(2):
                def q_epi(tt, ns, b):
                    qn, qnk = tunit(6 + tt % 2, 1)
                    junk1, jk1 = tunit(0, 1)
                    for hh in range(2):
                        S.op('act', lambda e, hh=hh: e.activation(out=junk1[:, 0:256], in_=ps[b][:, hh * 256:(hh + 1) * 256], func=AF.Square,
                                                                  accum_out=sm[:, 80 + 2 * (tt % 2) + hh:81 + 2 * (tt % 2) + hh]),
                             reads=[PK(b)], writes=[('sm6', tt % 2)] + jk1)
                    c0 = 80 + 2 * (tt % 2)
                    rstd_from_ssq(sm[:, c0:c0 + 2], 1.0 / 256, ('sm6', tt % 2))
                    for hh in range(2):
                        S.op('dve', lambda e, hh=hh, qn=qn: e.tensor_scalar(out=qn[:, hh * 256:(hh + 1) * 256], in0=ps[b][:, hh * 256:(hh + 1) * 256],
                                                                            scalar1=sm[:, c0 + hh:c0 + hh + 1], scalar2=None, op0=ALU.mult),
                             reads=[PK(b), ('sm6', tt % 2)], writes=qnk)
                    bt = nextbank(ALLB)
                    transposes(lambda c, qn=qn: qn[:, c * 128:(c + 1) * 128], 4, bt, qnk + ['const'])
                    gb = _bv(pt[:, l, 40:42], [[0, 2], [1, 2], [0, 128]])
                    S.op('dve', lambda e, bt=bt, gb=gb, tt=tt: e.tensor_tensor(
                        out=qT[:, :, tt * 128:(tt + 1) * 128].rearrange("p (a c) n -> p a c n", a=2),
                        in0=psb[bt][:, 0:512].rearrange("p (a c n) -> p a c n", a=2, c=2), in1=gb, op=ALU.mult),
                         reads=[PK(bt), 'const'], writes=[('hs', 4 + (c // 2)) for c in range(4)])
                gemm_tm(lambda kc, tt: hT[:, kc, tt * 128:(tt + 1) * 128], lambda kc, tt: [('hT', kc, tt // 4)],
                        lambda ns: [(wq[:, hpair * 512:(hpair + 1) * 512], 8)], 1, q_epi)
                for hh in range(2):
                    h = hpair * 2 + hh
                    def attn(g, h=h, hh=hh):
                        pT, pTk = tunit(2 + 2 * (g % 2), 2)
                        bsc = [nextbank(ALLB), nextbank(ALLB)]
                        for mc in range(2):
                            for c in range(2):
                                S.op('pe', lambda e, mc=mc, c=c: e.matmul(
                                    ps[bsc[mc]][:, :], lhsT=kTm[:, h * 2 + c, mc * 128:(mc + 1) * 128],
                                    rhs=qT[:, hh * 2 + c, g * 512:(g + 1) * 512], start=(c == 0), stop=(c == 1)),
                                     reads=[('hs', 6), ('hs', 4 + hh)], writes=[PK(bsc[mc])])
                            S.op('act', lambda e, mc=mc, pT=pT: e.activation(out=pT[:, mc * 512:(mc + 1) * 512], in_=ps[bsc[mc]][:, :], func=AF.Exp, scale=1.0 / 16),
                                 reads=[PK(bsc[mc])], writes=pTk)
                        return pT, pTk

                    def attn2(g, pT, pTk, h=h, hh=hh):
                        bd = nextbank(ALLB)
                        for mc in range(2):
                            S.op('pe', lambda e, mc=mc, pT=pT: e.matmul(ps[bd][:, :], lhsT=ones[:], rhs=pT[:, mc * 512:(mc + 1) * 512],
                                                                       start=(mc == 0), stop=(mc == 1)), reads=pTk + ['const'], writes=[PK(bd)])
                        rden, rdk = tunit(0, 2, F32)
                        S.op('dve', lambda e, rden=rden: e.reciprocal(out=rden, in_=ps[bd][:, :]), reads=[PK(bd)], writes=rdk)
                        for c in range(2):
                            bo = nextbank(ALLB)
                            for mc in range(2):
                                S.op('pe', lambda e, mc=mc, c=c, pT=pT, bo=bo: e.matmul(
                                    ps[bo][:, :], lhsT=vm[:, mc, h * 256 + c * 128:h * 256 + (c + 1) * 128],
                                    rhs=pT[:, mc * 512:(mc + 1) * 512], start=(mc == 0), stop=(mc == 1)),
                                     reads=pTk + [('hs', 6)], writes=[PK(bo)])
                            S.op('dve', lambda e, c=c, bo=bo, rden=rden: e.tensor_tensor(
                                out=cT[:, h * 2 + c, g * 512:(g + 1) * 512], in0=ps[bo][:, :], in1=rden, op=ALU.mult),
                                 reads=[PK(bo)] + rdk, writes=[('cT', h * 2 + c, g)])
                    prevg = None
                    for g in range(4):
                        cur = attn(g)
                        if prevg is not None:
                            attn2(*prevg)
                        prevg = (g,) + cur
                    attn2(*prevg)
            if stop == 'xattn':
                return
            wo = D["w_xo"][l]
            gemm_tm(lambda kc, tt: cT[:, kc, tt * 128:(tt + 1) * 128], lambda kc, tt: [('cT', kc, tt // 4)],
                    lambda ns: [(wo[:, ns * 512:(ns + 1) * 512], 8)], 2, resid_add)

        def mlp_block(l):
            norm_to_hT(l, 24)
            ring["slots"] = [0, 1, 2, 4, 5, 6, 7]
            wu = D["w_up"][l]; wd = D["w_down"][l]
            for fs in range(4):
                for half in range(2):
                    wap, wkey = load_piece(wu[:, fs * 1024 + half * 512:fs * 1024 + (half + 1) * 512], 8, 512)
                    for m in range(4):
                        fc = half * 4 + m
                        for g in range(4):
                            b = nextbank(ALLB)
                            for kc in range(8):
                                S.op('pe', lambda e, m=m, g=g, kc=kc, b=b, wap=wap: e.matmul(
                                    ps[b][:, :], lhsT=wap[:, kc, m * 128:(m + 1) * 128], rhs=hT[:, kc, g * 512:(g + 1) * 512],
                                    start=(kc == 0), stop=(kc == 7)), reads=[wkey, ('hT', kc, g)], writes=[PK(b)])
                            r, rk = tunit((fc * 4 + g) % 4, 1)
                            S.op('act', lambda e, b=b, r=r: e.activation(out=r, in_=ps[b][:, :], func=AF.Relu), reads=[PK(b)], writes=rk)
                            S.op('pool', lambda e, r=r, fc=fc, g=g: e.tensor_tensor(out=cT[:, fc, g * 512:(g + 1) * 512], in0=r, in1=r, op=ALU.mult),
                                 reads=rk, writes=[('cT', fc, g)])
                gemm_tm(lambda kc, tt: cT[:, kc, tt * 128:(tt + 1) * 128], lambda kc, tt: [('cT', kc, tt // 4)],
                        lambda ns: [(wd[fs * 1024:(fs + 1) * 1024, ns * 512:(ns + 1) * 512], 8)], 2, resid_add)
            ring["slots"] = [0, 1, 2]

        done = False
        for l in range(n_layers):
            mix_block(l)
            if stop in ('ret', 'sb') or stop == f'mix{l}':
                break
            cross_block(l)
            if stop == 'xattn' or stop == f'cross{l}':
                break
            mlp_block(l)
            if stop == f'mlp{l}':
                break

        if stop in ('ret', 'sb', 'xattn'):
            for kc in range(4 if stop == 'ret' else 8):
                for g in range(4):
                    S.op('dve', lambda e, kc=kc, g=g: e.tensor_copy(out=xs[:, kc * 2 + g // 2, (g % 2) * 512:(g % 2 + 1) * 512],
                                                                    in_=cT[:, kc, g * 512:(g + 1) * 512]),
                         reads=[('cT', kc, g)], writes=[('xs', kc * 2 + g // 2)])
        for tt in range(NT):
            S.dma('sp', lambda e, tt=tt: e.dma_start(out=out_d[tt * 128:(tt + 1) * 128, :], in_=xs[:, tt, :]),
                  reads=[('xs', tt)], writes=[('out', tt)], final=True)
        S.emit()
    return nc


_CONSTS = None


def make_in_maps(inputs, n_cores=8):
    global _CONSTS
    if _CONSTS is None:
        _CONSTS = _const_tables()
    f = lambda a: np.ascontiguousarray(np.asarray(a, dtype=np.float32))
    pt = _param_table(*[np.asarray(inputs[k], dtype=np.float32) for k in
                        ("g_mix", "g_cross", "g_mem", "g_mlp", "g_ret_out", "g_sb_out", "g_qn", "g_kn")])
    shared = {k: f(inputs[k]) for k in ("w_in", "w_mix_out", "w_xq", "w_xkv", "w_xo", "w_up", "w_down")}
    shared["pt"] = pt
    shared.update(_CONSTS)
    x = f(inputs["x"]); mem = f(inputs["mem"])
    maps = []
    for c in range(n_cores):
        m = dict(shared)
        m["x"] = x[c]
        m["mem"] = mem[c]
        maps.append(m)
    return maps


def kernel(**inputs):
    nc = build_program()
    maps = make_in_maps(inputs, 8)
    res = run_bass_kernel_spmd(nc, maps, core_ids=list(range(8)))
    return np.stack([np.asarray(r["out"], dtype=np.float32) for r in res.results], axis=0)
```

```python
import contextlib
import numpy as np
import concourse.bass as bass
import concourse.mybir as mybir
from concourse.bass_utils import run_bass_kernel_spmd

dt = mybir.dt
F32, BF16 = dt.float32, dt.bfloat16
AF = mybir.ActivationFunctionType
ALU = mybir.AluOpType
AX = mybir.AxisListType

ENGS = ('pe', 'act', 'dve', 'pool', 'sp')


class Sched:
    EPOCH = 20000
    NDMA = 8

    def __init__(self, nc):
        self.nc = nc
        self.stack = contextlib.ExitStack()
        self.ops = {e: [] for e in ENGS}
        self.ncomp = {e: 0 for e in ENGS}
        self.esem = {e: {} for e in ENGS}
        self.sems = []
        self.lastw = {}
        self.readers = {}
        self.seen = {e: {} for e in ENGS}
        self.dsem = []
        self.dnext = 0
        self.finals = []

    def __enter__(self):
        self.stack.__enter__()
        return self

    def __exit__(self, *a):
        return self.stack.__exit__(*a)

    def sbuf(self, name, shape, dtype):
        return self.stack.enter_context(self.nc.sbuf_tensor(name, list(shape), dtype))

    def psum(self, name, shape, dtype):
        return self.stack.enter_context(self.nc.psum_tensor(name, list(shape), dtype))

    def _newsem(self, name):
        s = self.stack.enter_context(self.nc.semaphore(name))
        self.sems.append(s)
        return len(self.sems) - 1

    def _deps(self, eng, reads, writes, is_dma):
        deps = set()
        for k in reads:
            t = self.lastw.get(k)
            if t is not None:
                deps.add(t)
        for k in writes:
            t = self.lastw.get(k)
            if t is not None:
                deps.add(t)
            for t in self.readers.get(k, ()):
                if t[2] == eng and eng == 'pe' and not is_dma:
                    continue
                deps.add(t)
        return deps

    def _waits(self, eng, deps, is_dma):
        waits = []
        seen = self.seen[eng]
        for (sid, v, deng) in sorted(deps):
            if deng == eng and eng == 'pe' and not is_dma:
                continue
            if deng.startswith('sw'):
                if seen.get((sid, 'g'), -1) >= int(deng[2:]):
                    continue
                seen[(sid, 'g')] = int(deng[2:])
                waits.append((sid, v))
                continue
            if seen.get(sid, 0) >= v:
                continue
            seen[sid] = v
            waits.append((sid, v))
        return waits

    def _commit(self, tok, reads, writes):
        for k in reads:
            self.readers.setdefault(k, []).append(tok)
        for k in writes:
            self.lastw[k] = tok
            self.readers[k] = []

    def op(self, eng, fn, reads=(), writes=()):
        deps = self._deps(eng, reads, writes, False)
        waits = self._waits(eng, deps, False)
        idx = self.ncomp[eng]
        ep = idx // self.EPOCH
        if ep not in self.esem[eng]:
            self.esem[eng][ep] = self._newsem(f"s_{eng}_{ep}")
        sid = self.esem[eng][ep]
        tok = (sid, idx % self.EPOCH + 1, eng)
        self.ncomp[eng] += 1
        self._commit(tok, reads, writes)
        self.ops[eng].append((fn, waits, sid, 1))
        return tok

    def dma(self, eng, fn, reads=(), writes=(), final=False, slot=None):
        if eng == 'pool':
            return self._dma_sw(fn, reads, writes, slot)
        deps = self._deps(eng, reads, writes, True)
        if len(self.dsem) < self.NDMA:
            self.dsem.append([self._newsem(f"s_dma_{len(self.dsem)}"), 0, None])
        ent = self.dsem[self.dnext % self.NDMA]
        self.dnext += 1
        if ent[2] is not None:
            deps.add(ent[2])
        waits = self._waits(eng, deps, True)
        ent[1] += 16
        tok = (ent[0], ent[1], 'dma')
        ent[2] = tok
        self._commit(tok, reads, writes)
        self.ops[eng].append((fn, waits, ent[0], 16))
        if final:
            self.finals.append(tok)
        return tok

    def _dma_sw(self, fn, reads, writes, slot=None):
        deps = self._deps('pool', reads, writes, True)
        waits = self._waits('pool', deps, True)
        if not hasattr(self, 'nsw'):
            self.nsw = 0
            self.swslot = {}
        if slot is None:
            sid = self._newsem(f"s_sw_{self.nsw}")
            self.nsw += 1
            tok = (sid, 16, 'dma')
            self.ops['pool'].append((fn, waits, sid, 16))
        else:
            if slot not in self.swslot:
                self.swslot[slot] = [self._newsem(f"s_sl_{len(self.swslot)}a"), self._newsem(f"s_sl_{len(self.swslot)}b"), 0]
            ent = self.swslot[slot]
            g = ent[2]
            ent[2] += 1
            sid, other = ent[g % 2], ent[(g + 1) % 2]
            tok = (sid, 16, f'sw{g}')
            self.ops['pool'].append(('swdma', fn, waits, sid, other if g >= 1 else None))
        self._commit(tok, reads, writes)
        return tok

    def emit(self):
        nc = self.nc
        fin = [(sid, v) for (sid, v, _) in self.finals]
        sems = self.sems
        ops = self.ops

        def replay(name, e, tail=()):
            for ent in ops[name]:
                if ent[0] == 'swdma':
                    for (ws, wv) in ent[2]:
                        e.wait_ge(sems[ws], wv)
                    if ent[4] is not None:
                        e.wait_ge(sems[ent[4]], 16)
                        e.sem_clear(sems[ent[4]])
                    ent[1](e).then_inc(sems[ent[3]], 16)
                    continue
                (fn, waits, sid, n) = ent
                for (ws, wv) in waits:
                    e.wait_ge(sems[ws], wv)
                fn(e).then_inc(sems[sid], n)
            for (ws, wv) in tail:
                e.wait_ge(sems[ws], wv)

        with nc.Block() as block:
            @block.sync
            def _(e):
                replay('sp', e, fin)

            @block.tensor
            def _(e):
                replay('pe', e)

            @block.vector
            def _(e):
                replay('dve', e)

            @block.scalar
            def _(e):
                replay('act', e)

            @block.gpsimd
            def _(e):
                replay('pool', e)


SEQ, DM, DEPTH = 2048, 1024, 2
NT = 16
EPS = 1e-6
NPC = 44
GAM = [1.0 - 2.0 ** (-5 - h) for h in range(4)]


def _const_tables():
    p = np.arange(128)
    c = {}
    c["ident"] = np.eye(128, dtype=np.float32)
    c["tri"] = (p[:, None] >= p[None, :]).astype(np.float32)
    c["ones"] = np.ones((128, 128), np.float32)
    c["mstr"] = (p[:, None] < p[None, :]).astype(np.float32)
    inv = (1.0 / (np.float32(10000.0) ** np.linspace(0.0, 1.0, 64, dtype=np.float32))).astype(np.float32)
    pos = np.arange(SEQ, dtype=np.float32)
    ang = (pos[:, None] * inv[None, :]).astype(np.float32)
    c["cos"] = np.cos(ang.astype(np.float64)).astype(np.float32).reshape(NT, 128, 64).transpose(1, 0, 2).copy()
    c["sin"] = np.sin(ang.astype(np.float64)).astype(np.float32).reshape(NT, 128, 64).transpose(1, 0, 2).copy()
    lg = np.log1p(-np.exp2(-5.0 - np.arange(4, dtype=np.float64)))
    dist = np.abs(p[:, None] - p[None, :]).astype(np.float64)
    same = (p[:, None] // 64) == (p[None, :] // 64)
    dm = np.zeros((128, 4, 128), np.float64)
    qd = np.zeros((128, 4, 128), np.float64)
    kd = np.zeros((128, 4, 128), np.float64)
    for h in range(4):
        dm[:, h, :] = np.where(same, np.exp(lg[h] * dist), 0.0) * (128.0 ** -0.5)
        qd[:, h, :] = np.exp(lg[h] * ((p % 64) + 1.0))[None, :]
        kd[:, h, :] = (np.exp(lg[h] * (63.0 - (p % 64))) * (128.0 ** -0.5))[:, None]
    out = {}
    out["cA"] = np.concatenate([c["ident"], c["tri"], c["ones"], c["mstr"]], axis=1).astype(np.float32)
    out["cB"] = np.stack([c["cos"], c["sin"]], axis=1).astype(np.float32)
    out["cC"] = np.concatenate([dm.reshape(128, 512), qd.reshape(128, 512), kd.reshape(128, 512)], axis=1).astype(np.float32)
    return out


def _param_table(g_mix, g_cross, g_mem, g_mlp, g_ret_out, g_sb_out, g_qn, g_kn):
    pt = np.zeros((128, DEPTH, NPC), np.float32)
    for l in range(DEPTH):
        pt[:, l, 0:8] = g_mix[l].reshape(8, 128).T
        pt[:, l, 8:16] = g_cross[l].reshape(8, 128).T
        pt[:, l, 16:24] = g_mem[l].reshape(8, 128).T
        pt[:, l, 24:32] = g_mlp[l].reshape(8, 128).T
        pt[:, l, 32:36] = g_ret_out[l].reshape(4, 128).T
        pt[:, l, 36:40] = g_sb_out[l].reshape(4, 128).T
        pt[:, l, 40:42] = g_qn[l].reshape(2, 128).T
        pt[:, l, 42:44] = g_kn[l].reshape(2, 128).T
    return pt


def _bv(ap, dims):
    return bass.AP(ap.tensor, ap.offset, [list(ap.ap[0])] + [list(d) for d in dims])


def build_program(n_layers=DEPTH, stop=None):
    nc = bass.Bass("TRN2", target_bir_lowering=False)
    D = {}
    def din(name, shape):
        D[name] = nc.dram_tensor(name, list(shape), F32, kind="ExternalInput").ap()
    din("x", [SEQ, DM]); din("mem", [256, DM])
    din("w_in", [DEPTH, DM, 3584]); din("w_mix_out", [DEPTH, DM, DM]); din("w_xq", [DEPTH, DM, DM])
    din("w_xkv", [DEPTH, DM, 2 * DM]); din("w_xo", [DEPTH, DM, DM]); din("w_up", [DEPTH, DM, 4 * DM])
    din("w_down", [DEPTH, 4 * DM, DM]); din("pt", [128, DEPTH, NPC])
    din("cA", [128, 512]); din("cB", [128, 2, NT, 64]); din("cC", [128, 1536])
    out_d = nc.dram_tensor("out", [SEQ, DM], F32, kind="ExternalOutput").ap()

    S = Sched(nc)
    with S:
        xs = S.sbuf("xs", [128, NT, DM], F32)
        hT = S.sbuf("hT", [128, 8, SEQ], BF16)
        cT = S.sbuf("cT", [128, 8, SEQ], BF16)
        WA = S.sbuf("WA", [128, 8, 4096], BF16)
        cA = S.sbuf("cA_s", [128, 512], BF16)
        ident, tri, ones, mstr = cA[:, 0:128], cA[:, 128:256], cA[:, 256:384], cA[:, 384:512]
        zer = S.sbuf("zer_s", [128, 128], BF16)
        mhalf = S.sbuf("mhalf_s", [128, 16], F32)
        cB = S.sbuf("cB_s", [128, 2, NT, 64], BF16)
        cosT, sinT = cB[:, 0], cB[:, 1]
        cC = S.sbuf("cC_s", [128, 1536], BF16)
        dmk, qdc, kdc = cC[:, 0:512], cC[:, 512:1024], cC[:, 1024:1536]
        pt = S.sbuf("pt_s", [128, DEPTH, NPC], F32)
        st = S.sbuf("st_s", [128, 4, 128], F32)
        stb = S.sbuf("stb_s", [128, 2, 4, 128], BF16)
        sm = S.sbuf("sm_s", [128, 128], F32)
        PS = S.psum("ps", [128, 8, 512], F32)
        ps = [PS[:, i, :] for i in range(8)]
        psb = [p.bitcast(BF16) for p in ps]

        rr = {"dq": 0, "bank": 0, "ring": 0, "ev": 0}

        def dq():
            rr["dq"] += 1
            return 'sp' if rr["dq"] % 2 else 'act'

        def PK(b):
            return ('ps', b)

        def unit(buf, name, u, n=1, dtype=BF16):
            kc, g = u // 4, u % 4
            assert g + n <= 4
            ap = buf[:, kc, g * 512:(g + n) * 512]
            if dtype == F32:
                ap = ap.bitcast(F32)
            return ap, [(name, kc, g + i) for i in range(n)]

        def tunit(i, n=1, dtype=BF16):
            ap = WA[:, 3, i * 512:(i + n) * 512]
            if dtype == F32:
                ap = ap.bitcast(F32)
            return ap, [('t', i + k) for k in range(n)]

        def nextbank(allowed):
            rr["bank"] += 1
            return allowed[rr["bank"] % len(allowed)]

        ring = {"slots": [0, 1, 2]}

        def load_piece(w2d, nk, ncols):
            assert nk * ncols <= 4096
            rr["ring"] += 1
            s = ring["slots"][rr["ring"] % len(ring["slots"])]
            dst = WA[:, s, 0:nk * ncols].rearrange("p (k n) -> p k n", k=nk)
            src = w2d.rearrange("(k p) n -> p k n", p=128)
            S.dma('pool', lambda e: e.dma_start(out=dst, in_=src), reads=[], writes=[('hs', s)])
            return dst, ('hs', s)

        def evac_engine():
            rr["ev"] += 1
            return 'act' if rr["ev"] % 2 else 'dve'

        def copy_op(eng, out, in_, reads, writes):
            if eng == 'act':
                S.op('act', lambda e: e.copy(out=out, in_=in_), reads=reads, writes=writes)
            else:
                S.op(eng, lambda e: e.tensor_copy(out=out, in_=in_), reads=reads, writes=writes)

        def rstd_from_ssq(ssq_ap, inv_n, key):
            S.op('act', lambda e: e.activation(out=ssq_ap, in_=ssq_ap, func=AF.Ln, scale=inv_n, bias=EPS), reads=[key], writes=[key])
            S.op('act', lambda e: e.activation(out=ssq_ap, in_=ssq_ap, func=AF.Exp, scale=-0.5), reads=[key], writes=[key])

        def transposes(src_fn, n, bank, reads):
            for c in range(n):
                S.op('pe', lambda e, c=c: e.transpose(out=psb[bank][:, c * 128:(c + 1) * 128], in_=src_fn(c),
                                                      identity=ident[:]),
                     reads=reads, writes=[PK(bank)])

        for tt in range(NT):
            S.dma('sp', lambda e, tt=tt: e.dma_start(out=xs[:, tt, :], in_=D["x"][tt * 128:(tt + 1) * 128, :]),
                  writes=[('xs', tt)])
        for (sb_t, nm) in ((cA, "cA"), (cB, "cB"), (cC, "cC")):
            S.dma('pool', lambda e, sb_t=sb_t, nm=nm: e.dma_start(out=sb_t[:], in_=D[nm]), writes=['const'])
        S.dma('sp', lambda e: e.dma_start(out=pt[:], in_=D["pt"]), writes=['const'])
        S.op('pool', lambda e: e.memset(zer[:], 0.0), writes=['const'])
        S.op('pool', lambda e: e.memset(mhalf[:], -0.5), writes=['const'])

        ALLB = list(range(8))

        def norm_to_hT(l, gcol):
            junk, jk = tunit(0, 2)
            for tt in range(NT):
                S.op('act', lambda e, tt=tt: e.activation(out=junk, in_=xs[:, tt, :], func=AF.Square,
                                                          accum_out=sm[:, tt:tt + 1]),
                     reads=[('xs', tt)], writes=['sm'] + jk)
            rstd_from_ssq(sm[:, 0:NT], 1.0 / DM, 'sm')
            gb = _bv(pt[:, l, gcol:gcol + 8], [[1, 8], [0, 128]])

            def front(tt):
                hn, hk = tunit(2 + 2 * (tt % 3), 2)
                S.op('dve', lambda e: e.tensor_scalar(out=hn, in0=xs[:, tt, :], scalar1=sm[:, tt:tt + 1],
                                                      scalar2=None, op0=ALU.mult),
                     reads=[('xs', tt), 'sm'], writes=hk)
                b = nextbank(ALLB)
                transposes(lambda c: hn[:, c * 128:(c + 1) * 128], 8, b, hk + ['const'])
                return b

            def back(tt, b):
                S.op('dve', lambda e: e.tensor_tensor(
                    out=hT[:, :, tt * 128:(tt + 1) * 128],
                    in0=psb[b][:, :].rearrange("p (k n) -> p k n", k=8), in1=gb, op=ALU.mult),
                     reads=[PK(b), 'const'], writes=[('hT', kc, tt // 4) for kc in range(8)])
            prev = None
            for tt in range(NT):
                b = front(tt)
                if prev is not None:
                    back(*prev)
                prev = (tt, b)
            back(*prev)

        def gemm_tm(lhs_fn, lhs_keys_fn, pieces_fn, n_slices, epilogue, banks=ALLB, ntt=NT):
            pending = None
            for ns in range(n_slices):
                pcs = [load_piece(w2d, nk, 512) for (w2d, nk) in pieces_fn(ns)]
                for tt in range(ntt):
                    b = nextbank(banks)
                    tot = sum(p[0].shape[1] for p in pcs)
                    i = 0
                    for pi, (wap, wkey) in enumerate(pcs):
                        for k in range(wap.shape[1]):
                            kc = sum(p[0].shape[1] for p in pcs[:pi]) + k
                            S.op('pe', lambda e, wap=wap, k=k, kc=kc, i=i, b=b, tt=tt: e.matmul(
                                ps[b][:, :], lhsT=lhs_fn(kc, tt), rhs=wap[:, k, :], start=(i == 0), stop=(i == tot - 1)),
                                 reads=[wkey] + lhs_keys_fn(kc, tt), writes=[PK(b)])
                            i += 1
                    if pending is not None:
                        epilogue(*pending)
                    pending = (tt, ns, b)
                if pending is not None:
                    epilogue(*pending)
                    pending = None

        def resid_add(tt, ns, b):
            S.op('dve', lambda e: e.tensor_tensor(out=xs[:, tt, ns * 512:(ns + 1) * 512], in0=ps[b][:, :],
                                                  in1=xs[:, tt, ns * 512:(ns + 1) * 512], op=ALU.add),
                 reads=[PK(b), ('xs', tt)], writes=[('xs', tt)])

        def mix_block(l):
            w_in = D["w_in"][l]
            norm_to_hT(l, 0)
            for j in range(4):
                dst = WA[:, 4 + j, :].rearrange("p (k n) -> p k n", k=8)
                src = w_in[:, j * 512:(j + 1) * 512].rearrange("(k p) n -> p k n", p=128)
                S.dma('pool', lambda e, dst=dst, src=src: e.dma_start(out=dst, in_=src), writes=[('hs', 4 + j)])
            S.op('pool', lambda e: e.memset(st[:], 0.0), writes=['st'])
            S.op('pool', lambda e: e.memset(stb[:], 0.0), writes=['stb0', 'stb1'])
            def cu(i, n=1, dtype=BF16):
                return unit(cT, 'cT', 16 + i, n, dtype)
            def ret_tile(tt):
                pb = [nextbank(ALLB) for _ in range(4)]
                for j in range(4):
                    wv = WA[:, 4 + j, :].rearrange("p (k n) -> p k n", k=8)
                    for kc in range(8):
                        S.op('pe', lambda e, j=j, kc=kc, wv=wv, tt=tt: e.matmul(
                            ps[pb[j]][:, :], lhsT=hT[:, kc, tt * 128:(tt + 1) * 128], rhs=wv[:, kc, :],
                            start=(kc == 0), stop=(kc == 7)),
                             reads=[('hs', 4 + j), ('hT', kc, tt // 4)], writes=[PK(pb[j])])
                yield
                pp = tt % 2
                A, Ak = cu(0, 2, F32)
                Bm, Bk = cu(2, 2, F32)
                qr, qrk = cu(4); kr, krk = cu(5); STm, STk = cu(6); on, onk = cu(7)
                kdt, kdk = cu(8 + pp); vb, vbk = cu(10 + pp); sg, sgk = cu(12 + pp); qTt, qTk = cu(14 + pp)
                qdT, qdk = tunit(1 + pp); kTt, kTk = tunit(3 + pp)
                cosb = _bv(cosT[:, tt, :], [[0, 4], [0, 2], [1, 64]])
                sinb = _bv(sinT[:, tt, :], [[0, 4], [1, 64]])
                def rot(bank, dstt, dk):
                    P4 = ps[bank][:, :].rearrange("p (h x d) -> p h x d", h=4, x=2)
                    A4 = A.rearrange("p (h x d) -> p h x d", h=4, x=2)
                    B4 = Bm.rearrange("p (x h d) -> p x h d", x=2, h=4)
                    D4 = dstt.rearrange("p (h x d) -> p h x d", h=4, x=2)
                    S.op('dve', lambda e, P4=P4, A4=A4: e.tensor_tensor(out=A4, in0=P4, in1=cosb, op=ALU.mult),
                         reads=[PK(bank), 'const'], writes=Ak)
                    S.op('dve', lambda e, P4=P4, B4=B4: e.tensor_tensor(out=B4[:, 0], in0=P4[:, :, 1, :], in1=sinb, op=ALU.mult),
                         reads=[PK(bank), 'const'], writes=[Bk[0]])
                    S.op('dve', lambda e, P4=P4, B4=B4: e.tensor_tensor(out=B4[:, 1], in0=P4[:, :, 0, :], in1=sinb, op=ALU.mult),
                         reads=[PK(bank), 'const'], writes=[Bk[1]])
                    S.op('pool', lambda e, A4=A4, B4=B4, D4=D4: e.tensor_tensor(out=D4[:, :, 0, :], in0=A4[:, :, 0, :], in1=B4[:, 0], op=ALU.subtract),
                         reads=Ak + Bk, writes=dk)
                    S.op('pool', lambda e, A4=A4, B4=B4, D4=D4: e.tensor_tensor(out=D4[:, :, 1, :], in0=A4[:, :, 1, :], in1=B4[:, 1], op=ALU.add),
                         reads=Ak + Bk, writes=dk)
                rot(pb[0], qr, qrk)
                yield
                S.op('act', lambda e: e.copy(out=vb, in_=ps[pb[2]][:, :]), reads=[PK(pb[2])], writes=vbk)
                yield
                rot(pb[1], kr, krk)
                yield
                S.op('act', lambda e: e.activation(out=Bm, in_=ps[pb[3]][:, :], func=AF.Exp, scale=-1.0), reads=[PK(pb[3])], writes=Bk)
                S.op('act', lambda e: e.activation(out=Bm, in_=Bm, func=AF.Ln, bias=1.0, scale=1.0), reads=Bk, writes=Bk)
                S.op('act', lambda e: e.activation(out=Bm, in_=Bm, func=AF.Exp, scale=-1.0), reads=Bk, writes=Bk)
                S.op('dve', lambda e: e.tensor_tensor(out=sg, in0=ps[pb[3]][:, :], in1=Bm, op=ALU.mult), reads=[PK(pb[3])] + Bk, writes=sgk)
                S.op('pool', lambda e: e.tensor_tensor(out=kdt, in0=kr, in1=kdc[:], op=ALU.mult), reads=krk + ['const'], writes=kdk)
                yield
                bq = nextbank(ALLB)
                transposes(lambda c: qr[:, c * 128:(c + 1) * 128], 4, bq, qrk + ['const'])
                S.op('act', lambda e: e.copy(out=qTt, in_=psb[bq][:, 0:512]), reads=[PK(bq)], writes=qTk)
                S.op('dve', lambda e: e.tensor_tensor(out=qdT, in0=psb[bq][:, 0:512], in1=qdc[:], op=ALU.mult),
                     reads=[PK(bq), 'const'], writes=qdk)
                yield
                bk = nextbank(ALLB)
                transposes(lambda c: kr[:, c * 128:(c + 1) * 128], 4, bk, krk + ['const'])
                S.op('act', lambda e: e.copy(out=kTt, in_=psb[bk][:, 0:512]), reads=[PK(bk)], writes=kTk)
                yield 'split'
                bs = nextbank(ALLB)
                for h in range(4):
                    S.op('pe', lambda e, h=h: e.matmul(ps[bs][:, h * 128:(h + 1) * 128], lhsT=kTt[:, h * 128:(h + 1) * 128],
                                                       rhs=qTt[:, h * 128:(h + 1) * 128], start=True, stop=True),
                         reads=kTk + qTk, writes=[PK(bs)])
                S.op('dve', lambda e: e.tensor_tensor(out=STm, in0=ps[bs][:, :], in1=dmk[:], op=ALU.mult),
                     reads=[PK(bs), 'const'], writes=STk)
                yield
                bu = [nextbank(ALLB), nextbank(ALLB)]
                for c in range(2):
                    for h in range(4):
                        S.op('pe', lambda e, c=c, h=h: e.matmul(
                            ps[bu[c]][:, h * 128:(h + 1) * 128], lhsT=kdt[c * 64:(c + 1) * 64, h * 128:(h + 1) * 128],
                            rhs=vb[c * 64:(c + 1) * 64, h * 128:(h + 1) * 128], start=True, stop=True),
                             reads=kdk + vbk, writes=[PK(bu[c])])
                yield
                stf = st[:].rearrange("p h e -> p (h e)")
                for h in range(4):
                    S.op('dve', lambda e, h=h: e.scalar_tensor_tensor(
                        out=st[:, h, :], in0=st[:, h, :], scalar=float(GAM[h] ** 64), in1=ps[bu[0]][:, h * 128:(h + 1) * 128],
                        op0=ALU.mult, op1=ALU.add), reads=['st', PK(bu[0])], writes=['st'])
                S.op('act', lambda e: e.copy(out=stb[:, 1].rearrange("p h e -> p (h e)"), in_=stf), reads=['st'], writes=['stb1'])
                yield
                bo = nextbank(ALLB)
                for h in range(4):
                    S.op('pe', lambda e, h=h: e.matmul(ps[bo][:, h * 128:(h + 1) * 128], lhsT=STm[:, h * 128:(h + 1) * 128],
                                                       rhs=vb[:, h * 128:(h + 1) * 128], start=True, stop=False),
                         reads=STk + vbk, writes=[PK(bo)])
                    for c in range(2):
                        S.op('pe', lambda e, h=h, c=c: e.matmul(
                            ps[bo][c * 64:(c + 1) * 64, h * 128:(h + 1) * 128],
                            lhsT=qdT[:, h * 128 + c * 64:h * 128 + (c + 1) * 64], rhs=stb[:, c, h, :],
                            start=False, stop=True),
                             reads=qdk + [f'stb{c}'], writes=[PK(bo)])
                yield
                for h in range(4):
                    S.op('dve', lambda e, h=h: e.scalar_tensor_tensor(
                        out=st[:, h, :], in0=st[:, h, :], scalar=float(GAM[h] ** 64), in1=ps[bu[1]][:, h * 128:(h + 1) * 128],
                        op0=ALU.mult, op1=ALU.add), reads=['st', PK(bu[1])], writes=['st'])
                S.op('act', lambda e: e.copy(out=stb[:, 0].rearrange("p h e -> p (h e)"), in_=stf), reads=['st'], writes=['stb0'])
                yield
                junk, jk = tunit(0, 1)
                for h in range(4):
                    S.op('act', lambda e, h=h: e.activation(out=junk[:, 0:128], in_=ps[bo][:, h * 128:(h + 1) * 128], func=AF.Square,
                                                            accum_out=sm[:, 32 + h:33 + h]), reads=[PK(bo)], writes=['sm2'] + jk)
                yield
                rstd_from_ssq(sm[:, 32:36], 1.0 / 128, 'sm2')
                yield
                for h in range(4):
                    S.op('dve', lambda e, h=h: e.scalar_tensor_tensor(
                        out=on[:, h * 128:(h + 1) * 128], in0=ps[bo][:, h * 128:(h + 1) * 128], scalar=sm[:, 32 + h:33 + h],
                        in1=sg[:, h * 128:(h + 1) * 128], op0=ALU.mult, op1=ALU.mult),
                         reads=[PK(bo), 'sm2'] + sgk, writes=onk)
                yield
                bt = nextbank(ALLB)
                transposes(lambda c: on[:, c * 128:(c + 1) * 128], 4, bt, onk + ['const'])
                yield
                gb = _bv(pt[:, l, 32:36], [[1, 4], [0, 128]])
                S.op('dve', lambda e, tt=tt, bt=bt, gb=gb: e.tensor_tensor(
                    out=cT[:, 0:4, tt * 128:(tt + 1) * 128], in0=psb[bt][:, 0:512].rearrange("p (k n) -> p k n", k=4),
                    in1=gb, op=ALU.mult), reads=[PK(bt), 'const'], writes=[('cT', kc, tt // 4) for kc in range(4)])
            gens = [ret_tile(tt) for tt in range(NT)]
            for v in gens[0]:
                if v == 'split':
                    break
            for tt in range(NT):
                gB = gens[tt]
                gA = gens[tt + 1] if tt + 1 < NT else None
                doneA, doneB = gA is None, False
                while not (doneA and doneB):
                    if not doneB:
                        try:
                            next(gB)
                        except StopIteration:
                            doneB = True
                    if not doneA:
                        try:
                            if next(gA) == 'split':
                                doneA = True
                        except StopIteration:
                            doneA = True
            if stop == 'ret':
                return
            skT = WA[:, 4:6, :].rearrange("p a (b n) -> p (a b) n", b=2)
            svv = WA[:, 6:8, :].rearrange("p a (t n) -> p (a t) n", t=8)
            for (which, c0) in (("q", 2048), ("k", 2560)):
                wap, wkey = load_piece(w_in[:, c0:c0 + 512], 8, 512)
                for m in range(4):
                    for g in range(4):
                        b = nextbank(ALLB)
                        for kc in range(8):
                            S.op('pe', lambda e, m=m, g=g, kc=kc, b=b, wap=wap: e.matmul(
                                ps[b][:, :], lhsT=wap[:, kc, m * 128:(m + 1) * 128], rhs=hT[:, kc, g * 512:(g + 1) * 512],
                                start=(kc == 0), stop=(kc == 7)),
                                 reads=[wkey, ('hT', kc, g)], writes=[PK(b)])
                        if which == "q":
                            copy_op(evac_engine(), cT[:, 4 + m, g * 512:(g + 1) * 512], ps[b][:, :], [PK(b)], [('cT', 4 + m, g)])
                        else:
                            copy_op(evac_engine(), skT[:, m, g * 512:(g + 1) * 512], ps[b][:, :], [PK(b)], [('hs', 4 + m // 2)])
            wap, wkey = load_piece(w_in[:, 3072:3584], 8, 512)
            for tt in range(NT):
                b = nextbank(ALLB)
                for kc in range(8):
                    S.op('pe', lambda e, tt=tt, kc=kc, b=b, wap=wap: e.matmul(
                        ps[b][:, :], lhsT=hT[:, kc, tt * 128:(tt + 1) * 128], rhs=wap[:, kc, :], start=(kc == 0), stop=(kc == 7)),
                         reads=[wkey, ('hT', kc, tt // 4)], writes=[PK(b)])
                copy_op(evac_engine(), svv[:, tt, :], ps[b][:, :], [PK(b)], [('hs', 6 + tt // 8)])
            PSall = PS
            ZBK = [[0, 1], [2, 3]]
            CBK = [4, 5]
            ACC, SSQ = 6, 7

            def hrow(row, lo, hi, dtype=BF16):
                ap = hT[:, row, lo:hi]
                if dtype == F32:
                    ap = ap.bitcast(F32)
                return ap, [('hT', row, g) for g in range(lo // 512, (hi + 511) // 512)]
            Ebuf = [hrow(0, 0, 2048, F32), hrow(1, 0, 2048, F32), tunit(0, 4, F32)]
            Gbuf = hrow(2, 0, 2048, F32)
            spbuf = [hrow(3, 0, 1024), hrow(3, 1024, 2048)]
            Abuf = [hrow(4, 0, 1024), hrow(4, 1024, 2048)]
            Rbuf = [hrow(5, 0, 1024), hrow(5, 1024, 2048)]
            rsb = hrow(7, 512, 1536, F32)
            rawb = hrow(6, 0, 2048)
            sqb = hrow(7, 0, 512)

            def v2(ap):
                return ap.rearrange("p (c n) -> p c n", c=2)
            mstr2 = _bv(mstr[:], [[0, 2], [1, 128]])

            def sweep_group(G):
                nkb = 4 * G + 4
                items = [(hp, kb) for hp in range(4) for kb in range(nkb - 1, -1, -1)]
                n = len(items)

                def c0_of(i):
                    return max(0, items[i][1] - 4 * G) * 128

                def zmm(i):
                    hp, kb = items[i]
                    c0 = c0_of(i)
                    for ch in range(2):
                        ho = ch * 64
                        zb = ZBK[i % 2][ch]
                        S.op('pe', lambda e, ho=ho, zb=zb: e.matmul(
                            ps[zb][:, c0:512], lhsT=skT[ho:ho + 64, hp, kb * 128:(kb + 1) * 128],
                            rhs=cT[ho:ho + 64, 4 + hp, G * 512 + c0:(G + 1) * 512], start=True, stop=True),
                             reads=[('hs', 4 + hp // 2), ('cT', 4 + hp, G)], writes=[PK(zb)])

                def esp(i):
                    hp, kb = items[i]
                    c0 = c0_of(i)
                    par = i % 2
                    E, Ek = Ebuf[i % 3]; sp, spk = spbuf[par]
                    zb = ZBK[par]
                    S.op('act', lambda e: e.activation(out=v2(E)[:, :, c0:512], in_=PSall[:, zb[0]:zb[0] + 2, c0:512], func=AF.Exp, scale=0.125),
                         reads=[PK(zb[0]), PK(zb[1])], writes=Ek)
                    S.op('act', lambda e: e.activation(out=v2(sp)[:, :, c0:512], in_=v2(E)[:, :, c0:512], func=AF.Ln, bias=1.0, scale=1.0),
                         reads=Ek, writes=spk)
                    if kb >= 4 * G:
                        S.op('dve', lambda e: e.tensor_tensor(out=v2(sp)[:, :, c0:c0 + 128], in0=v2(sp)[:, :, c0:c0 + 128], in1=mstr2, op=ALU.mult),
                             reads=spk + ['const'], writes=spk)
                    first = (kb == nkb - 1)
                    R, Rk = Rbuf[par]; Rn, Rnk = Rbuf[1 - par]
                    if kb > 0:
                        if kb >= 4 * G:
                            if c0 > 0:
                                S.op('dve', lambda e: e.memset(v2(Rn)[:, :, 0:c0], 0.0), writes=Rnk)
                        if first:
                            S.op('dve', lambda e: e.tensor_copy(out=v2(Rn)[:, :, c0:512], in_=v2(sp)[:, :, c0:512]), reads=spk, writes=Rnk)
                        else:
                            S.op('dve', lambda e: e.tensor_tensor(out=v2(Rn)[:, :, c0:512], in0=v2(R)[:, :, c0:512], in1=v2(sp)[:, :, c0:512], op=ALU.add),
                                 reads=Rk + spk, writes=Rnk)

                def cmm(i):
                    hp, kb = items[i]
                    c0 = c0_of(i)
                    first = (kb == nkb - 1)
                    sp, spk = spbuf[i % 2]; R, Rk = Rbuf[i % 2]
                    for ch in range(2):
                        S.op('pe', lambda e, ch=ch: e.matmul(ps[CBK[ch]][:, c0:512], lhsT=tri[:], rhs=v2(sp)[:, ch, c0:512], start=True, stop=first),
                             reads=spk + ['const'], writes=[PK(CBK[ch])])
                        if not first:
                            S.op('pe', lambda e, ch=ch: e.matmul(ps[CBK[ch]][:, c0:512], lhsT=ones[:], rhs=v2(R)[:, ch, c0:512], start=False, stop=True),
                                 reads=Rk + ['const'], writes=[PK(CBK[ch])])

                def rest(i):
                    hp, kb = items[i]
                    c0 = c0_of(i)
                    par = i % 2
                    first = (kb == nkb - 1)
                    E, Ek = Ebuf[i % 3]; sp, spk = spbuf[par]; Gt, Gk = Gbuf; A, Akk = Abuf[par]; R, Rk = Rbuf[par]; Rn, Rnk = Rbuf[1 - par]
                    S.op('act', lambda e: e.activation(out=v2(Gt)[:, :, c0:512], in_=PSall[:, CBK[0]:CBK[0] + 2, c0:512], func=AF.Exp, scale=-1.0),
                         reads=[PK(CBK[0]), PK(CBK[1])], writes=Gk)
                    S.op('dve', lambda e: e.tensor_tensor(out=v2(A)[:, :, c0:512], in0=v2(E)[:, :, c0:512], in1=v2(Gt)[:, :, c0:512], op=ALU.mult),
                         reads=Ek + Gk, writes=Akk)
                    if kb >= 4 * G:
                        S.op('pool', lambda e: e.tensor_tensor(out=v2(A)[:, :, c0:c0 + 128], in0=v2(A)[:, :, c0:c0 + 128], in1=mstr2, op=ALU.mult),
                             reads=Akk + ['const'], writes=Akk)

                def avmm(i):
                    hp, kb = items[i]
                    c0 = c0_of(i)
                    first = (kb == nkb - 1)
                    A, Akk = Abuf[i % 2]
                    if first:
                        S.op('pe', lambda e: e.matmul(ps[ACC][:, :], lhsT=zer[:], rhs=dmk[:], start=True, stop=True),
                             reads=['const'], writes=[PK(ACC)])
                    for ch in range(2):
                        h = 2 * hp + ch
                        S.op('pe', lambda e, ch=ch, h=h: e.matmul(
                            ps[ACC][ch * 64:(ch + 1) * 64, c0:512], lhsT=svv[:, kb, h * 64:(h + 1) * 64], rhs=v2(A)[:, ch, c0:512],
                            start=False, stop=True, skip_group_check=True),
                             reads=Akk + [('hs', 6 + kb // 8)], writes=[PK(ACC)])
                    if kb == 0:
                        raw, rawk = rawb; sq, sqk = sqb
                        S.op('act', lambda e: e.copy(out=raw[:, hp * 512:(hp + 1) * 512], in_=ps[ACC][:, :]), reads=[PK(ACC)], writes=[rawk[hp]])
                        S.op('act', lambda e: e.activation(out=sq, in_=ps[ACC][:, :], func=AF.Square), reads=[PK(ACC)], writes=sqk)
                        S.op('pe', lambda e: e.matmul(ps[SSQ][:, :], lhsT=ones[:], rhs=sq, start=(hp == 0), stop=(hp == 3)),
                             reads=sqk + ['const'], writes=[PK(SSQ)])

                zmm(0)
                if n > 1:
                    zmm(1)
                esp(0)
                cmm(0)
                for i in range(n):
                    if i + 2 < n:
                        zmm(i + 2)
                    if i + 1 < n:
                        esp(i + 1)
                    rest(i)
                    if i + 1 < n:
                        cmm(i + 1)
                    avmm(i)
                rs, rsk = rsb; raw, rawk = rawb
                S.op('act', lambda e: e.activation(out=rs, in_=ps[SSQ][:, :], func=AF.Ln, scale=1.0 / 512, bias=EPS), reads=[PK(SSQ)], writes=rsk)
                S.op('act', lambda e: e.activation(out=rs, in_=rs, func=AF.Exp, scale=-0.5), reads=rsk, writes=rsk)
                for hp in range(4):
                    S.op('dve', lambda e, hp=hp: e.scalar_tensor_tensor(
                        out=cT[:, 4 + hp, G * 512:(G + 1) * 512], in0=raw[:, hp * 512:(hp + 1) * 512], scalar=pt[:, l, 36 + hp:37 + hp],
                        in1=rs, op0=ALU.mult, op1=ALU.mult), reads=[rawk[hp], 'const'] + rsk, writes=[('cT', 4 + hp, G)])
            for G in range(4):
                sweep_group(G)
            if stop == 'sb':
                return
            wm = D["w_mix_out"][l]
            gemm_tm(lambda kc, tt: cT[:, kc, tt * 128:(tt + 1) * 128], lambda kc, tt: [('cT', kc, tt // 4)],
                    lambda ns: [(wm[:, ns * 512:(ns + 1) * 512], 8)], 2, resid_add)

        def cross_block(l):
            norm_to_hT(l, 8)
            memf = WA[:, 7, 0:4096].bitcast(F32).rearrange("p (t d) -> p t d", t=2)
            S.dma('sp', lambda e: e.dma_start(out=memf, in_=D["mem"].rearrange("(t p) d -> p t d", p=128)), writes=[('hs', 7)])
            junk, jk = tunit(0, 2)
            for t in range(2):
                S.op('act', lambda e, t=t: e.activation(out=junk, in_=memf[:, t, :], func=AF.Square, accum_out=sm[:, 64 + t:65 + t]),
                     reads=[('hs', 7)], writes=['sm4'] + jk)
            rstd_from_ssq(sm[:, 64:66], 1.0 / DM, 'sm4')
            mnT = WA[:, 4, 0:2048].rearrange("p (k n) -> p k n", k=8)
            for t in range(2):
                hn, hk = tunit(2 + 2 * t, 2)
                S.op('dve', lambda e, t=t, hn=hn: e.tensor_scalar(out=hn, in0=memf[:, t, :], scalar1=sm[:, 64 + t:65 + t], scalar2=None, op0=ALU.mult),
                     reads=[('hs', 7), 'sm4'], writes=hk)
                b = nextbank(ALLB)
                transposes(lambda c, hn=hn: hn[:, c * 128:(c + 1) * 128], 8, b, hk + ['const'])
                gb = _bv(pt[:, l, 16:24], [[1, 8], [0, 128]])
                S.op('dve', lambda e, t=t, b=b, gb=gb: e.tensor_tensor(out=mnT[:, :, t * 128:(t + 1) * 128],
                                                                       in0=psb[b][:, :].rearrange("p (k n) -> p k n", k=8), in1=gb, op=ALU.mult),
                     reads=[PK(b), 'const'], writes=[('hs', 4)])
            kTm = WA[:, 6, 0:2048].rearrange("p (k n) -> p k n", k=8)
            vm = WA[:, 6, 2048:4096].rearrange("p (t n) -> p t n", t=2)
            wkv = D["w_xkv"][l]
            def kv_epi(tt, ns, b):
                if ns < 2:
                    kn, knk = tunit(6 + tt % 2, 1)
                    junk1, jk1 = tunit(0, 1)
                    for hh in range(2):
                        col = 66 + (tt * 4 + ns * 2 + hh)
                        S.op('act', lambda e, hh=hh, col=col: e.activation(out=junk1[:, 0:256], in_=ps[b][:, hh * 256:(hh + 1) * 256], func=AF.Square,
                                                                           accum_out=sm[:, col:col + 1]), reads=[PK(b)], writes=[('sm5', tt, ns)] + jk1)
                    c0 = 66 + tt * 4 + ns * 2
                    rstd_from_ssq(sm[:, c0:c0 + 2], 1.0 / 256, ('sm5', tt, ns))
                    for hh in range(2):
                        S.op('dve', lambda e, hh=hh, kn=kn: e.tensor_scalar(out=kn[:, hh * 256:(hh + 1) * 256], in0=ps[b][:, hh * 256:(hh + 1) * 256],
                                                                            scalar1=sm[:, c0 + hh:c0 + hh + 1], scalar2=None, op0=ALU.mult),
                             reads=[PK(b), ('sm5', tt, ns)], writes=knk)
                    bt = nextbank(ALLB)
                    transposes(lambda c, kn=kn: kn[:, c * 128:(c + 1) * 128], 4, bt, knk + ['const'])
                    gb = _bv(pt[:, l, 42:44], [[0, 2], [1, 2], [0, 128]])
                    S.op('dve', lambda e, bt=bt, gb=gb: e.tensor_tensor(
                        out=kTm[:, ns * 4:(ns + 1) * 4, tt * 128:(tt + 1) * 128].rearrange("p (a c) n -> p a c n", a=2),
                        in0=psb[bt][:, 0:512].rearrange("p (a c n) -> p a c n", a=2, c=2), in1=gb, op=ALU.mult),
                         reads=[PK(bt), 'const'], writes=[('hs', 6)])
                else:
                    copy_op('act', vm[:, tt, (ns - 2) * 512:(ns - 1) * 512], ps[b][:, :], [PK(b)], [('hs', 6)])
            gemm_tm(lambda kc, tt: mnT[:, kc, tt * 128:(tt + 1) * 128], lambda kc, tt: [('hs', 4)],
                    lambda ns: [(wkv[:, ns * 512:(ns + 1) * 512], 8)], 4, kv_epi, ntt=2)
            wq = D["w_xq"][l]
            qT = WA[:, 4:6, :].rearrange("p a (b n) -> p (a b) n", b=2)
            for hpair in range(2):
                def q_epi(tt, ns, b):
                    qn, qnk = tunit(6 + tt % 2, 1)
                    junk1, jk1 = tunit(0, 1)
                    for hh in range(2):
                        S.op('act', lambda e, hh=hh: e.activation(out=junk1[:, 0:256], in_=ps[b][:, hh * 256:(hh + 1) * 256], func=AF.Square,
                                                                  accum_out=sm[:, 80 + 2 * (tt % 2) + hh:81 + 2 * (tt % 2) + hh]),
                             reads=[PK(b)], writes=[('sm6', tt % 2)] + jk1)
                    c0 = 80 + 2 * (tt % 2)
                    rstd_from_ssq(sm[:, c0:c0 + 2], 1.0 / 256, ('sm6', tt % 2))
                    for hh in range(2):
                        S.op('dve', lambda e, hh=hh, qn=qn: e.tensor_scalar(out=qn[:, hh * 256:(hh + 1) * 256], in0=ps[b][:, hh * 256:(hh + 1) * 256],
                                                                            scalar1=sm[:, c0 + hh:c0 + hh + 1], scalar2=None, op0=ALU.mult),
                             reads=[PK(b), ('sm6', tt % 2)], writes=qnk)
                    bt = nextbank(ALLB)
                    transposes(lambda c, qn=qn: qn[:, c * 128:(c + 1) * 128], 4, bt, qnk + ['const'])
                    gb = _bv(pt[:, l, 40:42], [[0, 2], [1, 2], [0, 128]])
                    S.op('dve', lambda e, bt=bt, gb=gb, tt=tt: e.tensor_tensor(
                        out=qT[:, :, tt * 128:(tt + 1) * 128].rearrange("p (a c) n -> p a c n", a=2),
                        in0=psb[bt][:, 0:512].rearrange("p (a c n) -> p a c n", a=2, c=2), in1=gb, op=ALU.mult),
                         reads=[PK(bt), 'const'], writes=[('hs', 4 + (c // 2)) for c in range(4)])
                gemm_tm(lambda kc, tt: hT[:, kc, tt * 128:(tt + 1) * 128], lambda kc, tt: [('hT', kc, tt // 4)],
                        lambda ns: [(wq[:, hpair * 512:(hpair + 1) * 512], 8)], 1, q_epi)
                for hh in range(2):
                    h = hpair * 2 + hh
                    def attn(g, h=h, hh=hh):
                        pT, pTk = tunit(2 + 2 * (g % 2), 2)
                        bsc = [nextbank(ALLB), nextbank(ALLB)]
                        for mc in range(2):
                            for c in range(2):
                                S.op('pe', lambda e, mc=mc, c=c: e.matmul(
                                    ps[bsc[mc]][:, :], lhsT=kTm[:, h * 2 + c, mc * 128:(mc + 1) * 128],
                                    rhs=qT[:, hh * 2 + c, g * 512:(g + 1) * 512], start=(c == 0), stop=(c == 1)),
                                     reads=[('hs', 6), ('hs', 4 + hh)], writes=[PK(bsc[mc])])
                            S.op('act', lambda e, mc=mc, pT=pT: e.activation(out=pT[:, mc * 512:(mc + 1) * 512], in_=ps[bsc[mc]][:, :], func=AF.Exp, scale=1.0 / 16),
                                 reads=[PK(bsc[mc])], writes=pTk)
                        return pT, pTk

                    def attn2(g, pT, pTk, h=h, hh=hh):
                        bd = nextbank(ALLB)
                        for mc in range(2):
                            S.op('pe', lambda e, mc=mc, pT=pT: e.matmul(ps[bd][:, :], lhsT=ones[:], rhs=pT[:, mc * 512:(mc + 1) * 512],
                                                                       start=(mc == 0), stop=(mc == 1)), reads=pTk + ['const'], writes=[PK(bd)])
                        rden, rdk = tunit(0, 2, F32)
                        S.op('dve', lambda e, rden=rden: e.reciprocal(out=rden, in_=ps[bd][:, :]), reads=[PK(bd)], writes=rdk)
                        for c in range(2):
                            bo = nextbank(ALLB)
                            for mc in range(2):
                                S.op('pe', lambda e, mc=mc, c=c, pT=pT, bo=bo: e.matmul(
                                    ps[bo][:, :], lhsT=vm[:, mc, h * 256 + c * 128:h * 256 + (c + 1) * 128],
                                    rhs=pT[:, mc * 512:(mc + 1) * 512], start=(mc == 0), stop=(mc == 1)),
                                     reads=pTk + [('hs', 6)], writes=[PK(bo)])
                            S.op('dve', lambda e, c=c, bo=bo, rden=rden: e.tensor_tensor(
                                out=cT[:, h * 2 + c, g * 512:(g + 1) * 512], in0=ps[bo][:, :], in1=rden, op=ALU.mult),
                                 reads=[PK(bo)] + rdk, writes=[('cT', h * 2 + c, g)])
                    prevg = None
                    for g in range(4):
                        cur = attn(g)
                        if prevg is not None:
                            attn2(*prevg)
                        prevg = (g,) + cur
                    attn2(*prevg)
            if stop == 'xattn':
                return
            wo = D["w_xo"][l]
            gemm_tm(lambda kc, tt: cT[:, kc, tt * 128:(tt + 1) * 128], lambda kc, tt: [('cT', kc, tt // 4)],
                    lambda ns: [(wo[:, ns * 512:(ns + 1) * 512], 8)], 2, resid_add)

        def mlp_block(l):
            norm_to_hT(l, 24)
            ring["slots"] = [0, 1, 2, 4, 5, 6, 7]
            wu = D["w_up"][l]; wd = D["w_down"][l]
            def up(fs):
                wap, wkey = load_piece(wu[:, fs * 512:(fs + 1) * 512], 8, 512)
                for m in range(4):
                    fc = (fs % 2) * 4 + m
                    for g in range(4):
                        b = nextbank(ALLB)
                        for kc in range(8):
                            S.op('pe', lambda e, m=m, g=g, kc=kc, b=b: e.matmul(
                                ps[b][:, :], lhsT=wap[:, kc, m * 128:(m + 1) * 128], rhs=hT[:, kc, g * 512:(g + 1) * 512],
                                start=(kc == 0), stop=(kc == 7)), reads=[wkey, ('hT', kc, g)], writes=[PK(b)])
                        r, rk = tunit((fc * 4 + g) % 8, 1)
                        S.op('act', lambda e, b=b, r=r: e.activation(out=r, in_=ps[b][:, :], func=AF.Relu), reads=[PK(b)], writes=rk)
                        S.op('pool', lambda e, r=r, fc=fc, g=g: e.tensor_tensor(out=cT[:, fc, g * 512:(g + 1) * 512], in0=r, in1=r, op=ALU.mult),
                             reads=rk, writes=[('cT', fc, g)])

            def down(fs):
                hb = (fs % 2) * 4
                wap, wkey = load_piece(wd[fs * 512:(fs + 1) * 512, :], 4, 1024)
                pending = None
                for ns in range(2):
                    for tt in range(NT):
                        b = nextbank(ALLB)
                        for k in range(4):
                            S.op('pe', lambda e, k=k, b=b, tt=tt, ns=ns: e.matmul(
                                ps[b][:, :], lhsT=cT[:, hb + k, tt * 128:(tt + 1) * 128], rhs=wap[:, k, ns * 512:(ns + 1) * 512],
                                start=(k == 0), stop=(k == 3)), reads=[wkey, ('cT', hb + k, tt // 4)], writes=[PK(b)])
                        if pending is not None:
                            resid_add(*pending)
                        pending = (tt, ns, b)
                resid_add(*pending)
            up(0)
            for fs in range(8):
                if fs + 1 < 8:
                    up(fs + 1)
                down(fs)
            ring["slots"] = [0, 1, 2]

        done = False
        for l in range(n_layers):
            mix_block(l)
            if stop in ('ret', 'sb') or stop == f'mix{l}':
                break
            cross_block(l)
            if stop == 'xattn' or stop == f'cross{l}':
                break
            mlp_block(l)
            if stop == f'mlp{l}':
                break

        if stop in ('ret', 'sb', 'xattn'):
            for kc in range(4 if stop == 'ret' else 8):
                for g in range(4):
                    S.op('dve', lambda e, kc=kc, g=g: e.tensor_copy(out=xs[:, kc * 2 + g // 2, (g % 2) * 512:(g % 2 + 1) * 512],
                                                                    in_=cT[:, kc, g * 512:(g + 1) * 512]),
                         reads=[('cT', kc, g)], writes=[('xs', kc * 2 + g // 2)])
        for tt in range(NT):
            S.dma('sp', lambda e, tt=tt: e.dma_start(out=out_d[tt * 128:(tt + 1) * 128, :], in_=xs[:, tt, :]),
                  reads=[('xs', tt)], writes=[('out', tt)], final=True)
        S.emit()
    return nc


_CONSTS = None


def make_in_maps(inputs, n_cores=8):
    global _CONSTS
    if _CONSTS is None:
        _CONSTS = _const_tables()
    f = lambda a: np.ascontiguousarray(np.asarray(a, dtype=np.float32))
    pt = _param_table(*[np.asarray(inputs[k], dtype=np.float32) for k in
                        ("g_mix", "g_cross", "g_mem", "g_mlp", "g_ret_out", "g_sb_out", "g_qn", "g_kn")])
    shared = {k: f(inputs[k]) for k in ("w_in", "w_mix_out", "w_xq", "w_xkv", "w_xo", "w_up", "w_down")}
    shared["pt"] = pt
    shared.update(_CONSTS)
    x = f(inputs["x"]); mem = f(inputs["mem"])
    maps = []
    for c in range(n_cores):
        m = dict(shared)
        m["x"] = x[c]
        m["mem"] = mem[c]
        maps.append(m)
    return maps


def kernel(**inputs):
    nc = build_program()
    maps = make_in_maps(inputs, 8)
    res = run_bass_kernel_spmd(nc, maps, core_ids=list(range(8)))
    return np.stack([np.asarray(r["out"], dtype=np.float32) for r in res.results], axis=0)
```

```python
import contextlib
import numpy as np
import concourse.bass as bass
import concourse.mybir as mybir
from concourse.bass_utils import run_bass_kernel_spmd

dt = mybir.dt
F32, BF16 = dt.float32, dt.bfloat16
AF = mybir.ActivationFunctionType
ALU = mybir.AluOpType
AX = mybir.AxisListType

ENGS = ('pe', 'act', 'dve', 'pool', 'sp')


class Sched:
    EPOCH = 20000
    NDMA = 8

    def __init__(self, nc):
        self.nc = nc
        self.stack = contextlib.ExitStack()
        self.ops = {e: [] for e in ENGS}
        self.ncomp = {e: 0 for e in ENGS}
        self.esem = {e: {} for e in ENGS}
        self.sems = []
        self.lastw = {}
        self.readers = {}
        self.seen = {e: {} for e in ENGS}
        self.dsem = []
        self.dnext = 0
        self.finals = []

    def __enter__(self):
        self.stack.__enter__()
        return self

    def __exit__(self, *a):
        return self.stack.__exit__(*a)

    def sbuf(self, name, shape, dtype):
        return self.stack.enter_context(self.nc.sbuf_tensor(name, list(shape), dtype))

    def psum(self, name, shape, dtype):
        return self.stack.enter_context(self.nc.psum_tensor(name, list(shape), dtype))

    def _newsem(self, name):
        s = self.stack.enter_context(self.nc.semaphore(name))
        self.sems.append(s)
        return len(self.sems) - 1

    def _deps(self, eng, reads, writes, is_dma):
        deps = set()
        for k in reads:
            t = self.lastw.get(k)
            if t is not None:
                deps.add(t)
        for k in writes:
            t = self.lastw.get(k)
            if t is not None:
                deps.add(t)
            for t in self.readers.get(k, ()):
                if t[2] == eng and eng == 'pe' and not is_dma:
                    continue
                deps.add(t)
        return deps

    def _waits(self, eng, deps, is_dma):
        waits = []
        seen = self.seen[eng]
        for (sid, v, deng) in sorted(deps):
            if deng == eng and eng == 'pe' and not is_dma:
                continue
            if deng.startswith('sw'):
                if seen.get((sid, 'g'), -1) >= int(deng[2:]):
                    continue
                seen[(sid, 'g')] = int(deng[2:])
                waits.append((sid, v))
                continue
            if seen.get(sid, 0) >= v:
                continue
            seen[sid] = v
            waits.append((sid, v))
        return waits

    def _commit(self, tok, reads, writes):
        for k in reads:
            self.readers.setdefault(k, []).append(tok)
        for k in writes:
            self.lastw[k] = tok
            self.readers[k] = []

    def op(self, eng, fn, reads=(), writes=()):
        deps = self._deps(eng, reads, writes, False)
        waits = self._waits(eng, deps, False)
        idx = self.ncomp[eng]
        ep = idx // self.EPOCH
        if ep not in self.esem[eng]:
            self.esem[eng][ep] = self._newsem(f"s_{eng}_{ep}")
        sid = self.esem[eng][ep]
        tok = (sid, idx % self.EPOCH + 1, eng)
        self.ncomp[eng] += 1
        self._commit(tok, reads, writes)
        self.ops[eng].append((fn, waits, sid, 1))
        return tok

    def dma(self, eng, fn, reads=(), writes=(), final=False, slot=None):
        if eng == 'pool':
            return self._dma_sw(fn, reads, writes, slot)
        deps = self._deps(eng, reads, writes, True)
        if len(self.dsem) < self.NDMA:
            self.dsem.append([self._newsem(f"s_dma_{len(self.dsem)}"), 0, None])
        ent = self.dsem[self.dnext % self.NDMA]
        self.dnext += 1
        if ent[2] is not None:
            deps.add(ent[2])
        waits = self._waits(eng, deps, True)
        ent[1] += 16
        tok = (ent[0], ent[1], 'dma')
        ent[2] = tok
        self._commit(tok, reads, writes)
        self.ops[eng].append((fn, waits, ent[0], 16))
        if final:
            self.finals.append(tok)
        return tok

    def _dma_sw(self, fn, reads, writes, slot=None):
        deps = self._deps('pool', reads, writes, True)
        waits = self._waits('pool', deps, True)
        if not hasattr(self, 'nsw'):
            self.nsw = 0
            self.swslot = {}
        if slot is None:
            sid = self._newsem(f"s_sw_{self.nsw}")
            self.nsw += 1
            tok = (sid, 16, 'dma')
            self.ops['pool'].append((fn, waits, sid, 16))
        else:
            if slot not in self.swslot:
                self.swslot[slot] = [self._newsem(f"s_sl_{len(self.swslot)}a"), self._newsem(f"s_sl_{len(self.swslot)}b"), 0]
            ent = self.swslot[slot]
            g = ent[2]
            ent[2] += 1
            sid, other = ent[g % 2], ent[(g + 1) % 2]
            tok = (sid, 16, f'sw{g}')
            self.ops['pool'].append(('swdma', fn, waits, sid, other if g >= 1 else None))
        self._commit(tok, reads, writes)
        return tok

    def emit(self):
        nc = self.nc
        fin = [(sid, v) for (sid, v, _) in self.finals]
        sems = self.sems
        ops = self.ops

        def replay(name, e, tail=()):
            for ent in ops[name]:
                if ent[0] == 'swdma':
                    for (ws, wv) in ent[2]:
                        e.wait_ge(sems[ws], wv)
                    if ent[4] is not None:
                        e.wait_ge(sems[ent[4]], 16)
                        e.sem_clear(sems[ent[4]])
                    ent[1](e).then_inc(sems[ent[3]], 16)
                    continue
                (fn, waits, sid, n) = ent
                for (ws, wv) in waits:
                    e.wait_ge(sems[ws], wv)
                fn(e).then_inc(sems[sid], n)
            for (ws, wv) in tail:
                e.wait_ge(sems[ws], wv)

        with nc.Block() as block:
            @block.sync
            def _(e):
                replay('sp', e, fin)

            @block.tensor
            def _(e):
                replay('pe', e)

            @block.vector
            def _(e):
                replay('dve', e)

            @block.scalar
            def _(e):
                replay('act', e)

            @block.gpsimd
            def _(e):
                replay('pool', e)


SEQ, DM, DEPTH = 2048, 1024, 2
NT = 16
EPS = 1e-6
NPC = 44
GAM = [1.0 - 2.0 ** (-5 - h) for h in range(4)]


def _const_tables():
    p = np.arange(128)
    c = {}
    c["ident"] = np.eye(128, dtype=np.float32)
    c["tri"] = (p[:, None] >= p[None, :]).astype(np.float32)
    c["ones"] = np.ones((128, 128), np.float32)
    c["mstr"] = (p[:, None] < p[None, :]).astype(np.float32)
    inv = (1.0 / (np.float32(10000.0) ** np.linspace(0.0, 1.0, 64, dtype=np.float32))).astype(np.float32)
    pos = np.arange(SEQ, dtype=np.float32)
    ang = (pos[:, None] * inv[None, :]).astype(np.float32)
    c["cos"] = np.cos(ang.astype(np.float64)).astype(np.float32).reshape(NT, 128, 64).transpose(1, 0, 2).copy()
    c["sin"] = np.sin(ang.astype(np.float64)).astype(np.float32).reshape(NT, 128, 64).transpose(1, 0, 2).copy()
    lg = np.log1p(-np.exp2(-5.0 - np.arange(4, dtype=np.float64)))
    dist = np.abs(p[:, None] - p[None, :]).astype(np.float64)
    same = (p[:, None] // 64) == (p[None, :] // 64)
    dm = np.zeros((128, 4, 128), np.float64)
    qd = np.zeros((128, 4, 128), np.float64)
    kd = np.zeros((128, 4, 128), np.float64)
    for h in range(4):
        dm[:, h, :] = np.where(same, np.exp(lg[h] * dist), 0.0) * (128.0 ** -0.5)
        qd[:, h, :] = np.exp(lg[h] * ((p % 64) + 1.0))[None, :]
        kd[:, h, :] = (np.exp(lg[h] * (63.0 - (p % 64))) * (128.0 ** -0.5))[:, None]
    out = {}
    out["cA"] = np.concatenate([c["ident"], c["tri"], c["ones"], c["mstr"]], axis=1).astype(np.float32)
    out["cB"] = np.stack([c["cos"], c["sin"]], axis=1).astype(np.float32)
    out["cC"] = np.concatenate([dm.reshape(128, 512), qd.reshape(128, 512), kd.reshape(128, 512)], axis=1).astype(np.float32)
    return out


def _param_table(g_mix, g_cross, g_mem, g_mlp, g_ret_out, g_sb_out, g_qn, g_kn):
    pt = np.zeros((128, DEPTH, NPC), np.float32)
    for l in range(DEPTH):
        pt[:, l, 0:8] = g_mix[l].reshape(8, 128).T
        pt[:, l, 8:16] = g_cross[l].reshape(8, 128).T
        pt[:, l, 16:24] = g_mem[l].reshape(8, 128).T
        pt[:, l, 24:32] = g_mlp[l].reshape(8, 128).T
        pt[:, l, 32:36] = g_ret_out[l].reshape(4, 128).T
        pt[:, l, 36:40] = g_sb_out[l].reshape(4, 128).T
        pt[:, l, 40:42] = g_qn[l].reshape(2, 128).T
        pt[:, l, 42:44] = g_kn[l].reshape(2, 128).T
    return pt


def _bv(ap, dims):
    return bass.AP(ap.tensor, ap.offset, [list(ap.ap[0])] + [list(d) for d in dims])


def build_program(n_layers=DEPTH, stop=None):
    nc = bass.Bass("TRN2", target_bir_lowering=False)
    D = {}
    def din(name, shape):
        D[name] = nc.dram_tensor(name, list(shape), F32, kind="ExternalInput").ap()
    din("x", [SEQ, DM]); din("mem", [256, DM])
    din("w_in", [DEPTH, DM, 3584]); din("w_mix_out", [DEPTH, DM, DM]); din("w_xq", [DEPTH, DM, DM])
    din("w_xkv", [DEPTH, DM, 2 * DM]); din("w_xo", [DEPTH, DM, DM]); din("w_up", [DEPTH, DM, 4 * DM])
    din("w_down", [DEPTH, 4 * DM, DM]); din("pt", [128, DEPTH, NPC])
    din("cA", [128, 512]); din("cB", [128, 2, NT, 64]); din("cC", [128, 1536])
    out_d = nc.dram_tensor("out", [SEQ, DM], F32, kind="ExternalOutput").ap()

    S = Sched(nc)
    with S:
        xs = S.sbuf("xs", [128, NT, DM], F32)
        hT = S.sbuf("hT", [128, 8, SEQ], BF16)
        cT = S.sbuf("cT", [128, 8, SEQ], BF16)
        WA = S.sbuf("WA", [128, 8, 4096], BF16)
        cA = S.sbuf("cA_s", [128, 512], BF16)
        ident, tri, ones, mstr = cA[:, 0:128], cA[:, 128:256], cA[:, 256:384], cA[:, 384:512]
        zer = S.sbuf("zer_s", [128, 128], BF16)
        mhalf = S.sbuf("mhalf_s", [128, 16], F32)
        cB = S.sbuf("cB_s", [128, 2, NT, 64], BF16)
        cosT, sinT = cB[:, 0], cB[:, 1]
        cC = S.sbuf("cC_s", [128, 1536], BF16)
        dmk, qdc, kdc = cC[:, 0:512], cC[:, 512:1024], cC[:, 1024:1536]
        pt = S.sbuf("pt_s", [128, DEPTH, NPC], F32)
        st = S.sbuf("st_s", [128, 4, 128], F32)
        stb = S.sbuf("stb_s", [128, 2, 4, 128], BF16)
        sm = S.sbuf("sm_s", [128, 128], F32)
        PS = S.psum("ps", [128, 8, 512], F32)
        ps = [PS[:, i, :] for i in range(8)]
        psb = [p.bitcast(BF16) for p in ps]

        rr = {"dq": 0, "bank": 0, "ring": 0, "ev": 0}

        def dq():
            rr["dq"] += 1
            return 'sp' if rr["dq"] % 2 else 'act'

        def PK(b):
            return ('ps', b)

        def unit(buf, name, u, n=1, dtype=BF16):
            kc, g = u // 4, u % 4
            assert g + n <= 4
            ap = buf[:, kc, g * 512:(g + n) * 512]
            if dtype == F32:
                ap = ap.bitcast(F32)
            return ap, [(name, kc, g + i) for i in range(n)]

        def tunit(i, n=1, dtype=BF16):
            ap = WA[:, 3, i * 512:(i + n) * 512]
            if dtype == F32:
                ap = ap.bitcast(F32)
            return ap, [('t', i + k) for k in range(n)]

        def nextbank(allowed):
            rr["bank"] += 1
            return allowed[rr["bank"] % len(allowed)]

        ring = {"slots": [0, 1, 2]}

        def load_piece(w2d, nk, ncols):
            assert nk * ncols <= 4096
            rr["ring"] += 1
            s = ring["slots"][rr["ring"] % len(ring["slots"])]
            dst = WA[:, s, 0:nk * ncols].rearrange("p (k n) -> p k n", k=nk)
            src = w2d.rearrange("(k p) n -> p k n", p=128)
            S.dma('pool', lambda e: e.dma_start(out=dst, in_=src), reads=[], writes=[('hs', s)])
            return dst, ('hs', s)

        def evac_engine():
            rr["ev"] += 1
            return 'act' if rr["ev"] % 2 else 'dve'

        def copy_op(eng, out, in_, reads, writes):
            if eng == 'act':
                S.op('act', lambda e: e.copy(out=out, in_=in_), reads=reads, writes=writes)
            else:
                S.op(eng, lambda e: e.tensor_copy(out=out, in_=in_), reads=reads, writes=writes)

        def rstd_from_ssq(ssq_ap, inv_n, key):
            S.op('act', lambda e: e.activation(out=ssq_ap, in_=ssq_ap, func=AF.Ln, scale=inv_n, bias=EPS), reads=[key], writes=[key])
            S.op('act', lambda e: e.activation(out=ssq_ap, in_=ssq_ap, func=AF.Exp, scale=-0.5), reads=[key], writes=[key])

        def transposes(src_fn, n, bank, reads):
            for c in range(n):
                S.op('pe', lambda e, c=c: e.transpose(out=psb[bank][:, c * 128:(c + 1) * 128], in_=src_fn(c),
                                                      identity=ident[:]),
                     reads=reads, writes=[PK(bank)])

        for tt in range(NT):
            S.dma('sp', lambda e, tt=tt: e.dma_start(out=xs[:, tt, :], in_=D["x"][tt * 128:(tt + 1) * 128, :]),
                  writes=[('xs', tt)])
        for (sb_t, nm) in ((cA, "cA"), (cB, "cB"), (cC, "cC")):
            S.dma('pool', lambda e, sb_t=sb_t, nm=nm: e.dma_start(out=sb_t[:], in_=D[nm]), writes=['const'])
        S.dma('sp', lambda e: e.dma_start(out=pt[:], in_=D["pt"]), writes=['const'])
        S.op('pool', lambda e: e.memset(zer[:], 0.0), writes=['const'])
        S.op('pool', lambda e: e.memset(mhalf[:], -0.5), writes=['const'])

        ALLB = list(range(8))

        def norm_to_hT(l, gcol):
            junk, jk = tunit(0, 2)
            for tt in range(NT):
                S.op('act', lambda e, tt=tt: e.activation(out=junk, in_=xs[:, tt, :], func=AF.Square,
                                                          accum_out=sm[:, tt:tt + 1]),
                     reads=[('xs', tt)], writes=['sm'] + jk)
            rstd_from_ssq(sm[:, 0:NT], 1.0 / DM, 'sm')
            gb = _bv(pt[:, l, gcol:gcol + 8], [[1, 8], [0, 128]])

            def front(tt):
                hn, hk = tunit(2 + 2 * (tt % 3), 2)
                S.op('dve', lambda e: e.tensor_scalar(out=hn, in0=xs[:, tt, :], scalar1=sm[:, tt:tt + 1],
                                                      scalar2=None, op0=ALU.mult),
                     reads=[('xs', tt), 'sm'], writes=hk)
                b = nextbank(ALLB)
                transposes(lambda c: hn[:, c * 128:(c + 1) * 128], 8, b, hk + ['const'])
                return b

            def back(tt, b):
                S.op('dve', lambda e: e.tensor_tensor(
                    out=hT[:, :, tt * 128:(tt + 1) * 128],
                    in0=psb[b][:, :].rearrange("p (k n) -> p k n", k=8), in1=gb, op=ALU.mult),
                     reads=[PK(b), 'const'], writes=[('hT', kc, tt // 4) for kc in range(8)])
            prev = None
            for tt in range(NT):
                b = front(tt)
                if prev is not None:
                    back(*prev)
                prev = (tt, b)
            back(*prev)

        def gemm_tm(lhs_fn, lhs_keys_fn, pieces_fn, n_slices, epilogue, banks=ALLB, ntt=NT, defer=1):
            pending = []
            for ns in range(n_slices):
                pcs = [load_piece(w2d, nk, 512) for (w2d, nk) in pieces_fn(ns)]
                for tt in range(ntt):
                    b = nextbank(banks)
                    tot = sum(p[0].shape[1] for p in pcs)
                    i = 0
                    for pi, (wap, wkey) in enumerate(pcs):
                        for k in range(wap.shape[1]):
                            kc = sum(p[0].shape[1] for p in pcs[:pi]) + k
                            S.op('pe', lambda e, wap=wap, k=k, kc=kc, i=i, b=b, tt=tt: e.matmul(
                                ps[b][:, :], lhsT=lhs_fn(kc, tt), rhs=wap[:, k, :], start=(i == 0), stop=(i == tot - 1)),
                                 reads=[wkey] + lhs_keys_fn(kc, tt), writes=[PK(b)])
                            i += 1
                    pending.append((tt, ns, b))
                    if len(pending) > defer:
                        epilogue(*pending.pop(0))
                while pending:
                    epilogue(*pending.pop(0))

        def resid_add(tt, ns, b):
            S.op('dve', lambda e: e.tensor_tensor(out=xs[:, tt, ns * 512:(ns + 1) * 512], in0=ps[b][:, :],
                                                  in1=xs[:, tt, ns * 512:(ns + 1) * 512], op=ALU.add),
                 reads=[PK(b), ('xs', tt)], writes=[('xs', tt)])

        def gemm_resid_norm(lhs_fn, lhs_keys_fn, rhs_fn, nk, norm):
            pend_add = []
            junk, jk = tunit(0, 2)
            fr = {}

            def n_stats(t):
                S.op('act', lambda e: e.activation(out=junk, in_=xs[:, t, :], func=AF.Square, accum_out=sm[:, t:t + 1]),
                     reads=[('xs', t)], writes=[('smn', t)] + jk)
                rstd_from_ssq(sm[:, t:t + 1], 1.0 / DM, ('smn', t))

            def n_front(t):
                hn, hk = tunit(2 + 2 * (t % 3), 2)
                S.op('dve', lambda e: e.tensor_scalar(out=hn, in0=xs[:, t, :], scalar1=sm[:, t:t + 1], scalar2=None, op0=ALU.mult),
                     reads=[('xs', t), ('smn', t)], writes=hk)
                b = nextbank(ALLB)
                transposes(lambda c: hn[:, c * 128:(c + 1) * 128], 8, b, hk + ['const'])
                fr[t] = b

            def n_back(t):
                b = fr[t]
                gb = _bv(pt[:, norm[0], norm[1]:norm[1] + 8], [[1, 8], [0, 128]])
                S.op('dve', lambda e: e.tensor_tensor(out=hT[:, :, t * 128:(t + 1) * 128],
                                                      in0=psb[b][:, :].rearrange("p (k n) -> p k n", k=8), in1=gb, op=ALU.mult),
                     reads=[PK(b), 'const'], writes=[('hT', kc, t // 4) for kc in range(8)])

            for tt in range(NT + 3):
                if tt < NT:
                    for ns in range(2):
                        b = nextbank(ALLB)
                        for k in range(nk):
                            rhs, wkey = rhs_fn(ns, k)
                            S.op('pe', lambda e, k=k, b=b, tt=tt, rhs=rhs: e.matmul(ps[b][:, :], lhsT=lhs_fn(k, tt), rhs=rhs,
                                                                                  start=(k == 0), stop=(k == nk - 1)),
                                 reads=[wkey] + lhs_keys_fn(k, tt), writes=[PK(b)])
                        pend_add.append((tt, ns, b))
                        if len(pend_add) > 1:
                            resid_add(*pend_add.pop(0))
                else:
                    while pend_add:
                        resid_add(*pend_add.pop(0))
                if norm is not None:
                    if 0 <= tt - 1 < NT:
                        n_stats(tt - 1)
                    if 0 <= tt - 2 < NT:
                        n_front(tt - 2)
                    if 0 <= tt - 3 < NT:
                        n_back(tt - 3)

        def mix_block(l):
            w_in = D["w_in"][l]
            if l == 0:
                norm_to_hT(l, 0)
            for j in range(4):
                dst = WA[:, 4 + j, :].rearrange("p (k n) -> p k n", k=8)
                src = w_in[:, j * 512:(j + 1) * 512].rearrange("(k p) n -> p k n", p=128)
                S.dma('pool', lambda e, dst=dst, src=src: e.dma_start(out=dst, in_=src), writes=[('hs', 4 + j)])
            S.op('pool', lambda e: e.memset(st[:], 0.0), writes=['st'])
            S.op('pool', lambda e: e.memset(stb[:], 0.0), writes=['stb0', 'stb1'])
            def cu(i, n=1, dtype=BF16):
                return unit(cT, 'cT', 16 + i, n, dtype)
            def ret_tile(tt):
                pb = [nextbank(ALLB) for _ in range(4)]
                for j in range(4):
                    wv = WA[:, 4 + j, :].rearrange("p (k n) -> p k n", k=8)
                    for kc in range(8):
                        S.op('pe', lambda e, j=j, kc=kc, wv=wv, tt=tt: e.matmul(
                            ps[pb[j]][:, :], lhsT=hT[:, kc, tt * 128:(tt + 1) * 128], rhs=wv[:, kc, :],
                            start=(kc == 0), stop=(kc == 7)),
                             reads=[('hs', 4 + j), ('hT', kc, tt // 4)], writes=[PK(pb[j])])
                yield
                pp = tt % 2
                A, Ak = cu(0, 2, F32)
                Bm, Bk = cu(2, 2, F32)
                qr, qrk = cu(4); kr, krk = cu(5); STm, STk = cu(6); on, onk = cu(7)
                kdt, kdk = cu(8 + pp); vb, vbk = cu(10 + pp); sg, sgk = cu(12 + pp); qTt, qTk = cu(14 + pp)
                qdT, qdk = tunit(1 + pp); kTt, kTk = tunit(3 + pp)
                cosb = _bv(cosT[:, tt, :], [[0, 4], [0, 2], [1, 64]])
                sinb = _bv(sinT[:, tt, :], [[0, 4], [1, 64]])
                def rot(bank, dstt, dk):
                    P4 = ps[bank][:, :].rearrange("p (h x d) -> p h x d", h=4, x=2)
                    A4 = A.rearrange("p (h x d) -> p h x d", h=4, x=2)
                    B4 = Bm.rearrange("p (x h d) -> p x h d", x=2, h=4)
                    D4 = dstt.rearrange("p (h x d) -> p h x d", h=4, x=2)
                    S.op('dve', lambda e, P4=P4, A4=A4: e.tensor_tensor(out=A4, in0=P4, in1=cosb, op=ALU.mult),
                         reads=[PK(bank), 'const'], writes=Ak)
                    S.op('dve', lambda e, P4=P4, B4=B4: e.tensor_tensor(out=B4[:, 0], in0=P4[:, :, 1, :], in1=sinb, op=ALU.mult),
                         reads=[PK(bank), 'const'], writes=[Bk[0]])
                    S.op('dve', lambda e, P4=P4, B4=B4: e.tensor_tensor(out=B4[:, 1], in0=P4[:, :, 0, :], in1=sinb, op=ALU.mult),
                         reads=[PK(bank), 'const'], writes=[Bk[1]])
                    S.op('pool', lambda e, A4=A4, B4=B4, D4=D4: e.tensor_tensor(out=D4[:, :, 0, :], in0=A4[:, :, 0, :], in1=B4[:, 0], op=ALU.subtract),
                         reads=Ak + Bk, writes=dk)
                    S.op('pool', lambda e, A4=A4, B4=B4, D4=D4: e.tensor_tensor(out=D4[:, :, 1, :], in0=A4[:, :, 1, :], in1=B4[:, 1], op=ALU.add),
                         reads=Ak + Bk, writes=dk)
                rot(pb[0], qr, qrk)
                yield
                S.op('act', lambda e: e.copy(out=vb, in_=ps[pb[2]][:, :]), reads=[PK(pb[2])], writes=vbk)
                yield
                rot(pb[1], kr, krk)
                yield
                S.op('act', lambda e: e.activation(out=Bm, in_=ps[pb[3]][:, :], func=AF.Exp, scale=-1.0), reads=[PK(pb[3])], writes=Bk)
                S.op('act', lambda e: e.activation(out=Bm, in_=Bm, func=AF.Ln, bias=1.0, scale=1.0), reads=Bk, writes=Bk)
                S.op('act', lambda e: e.activation(out=Bm, in_=Bm, func=AF.Exp, scale=-1.0), reads=Bk, writes=Bk)
                S.op('dve', lambda e: e.tensor_tensor(out=sg, in0=ps[pb[3]][:, :], in1=Bm, op=ALU.mult), reads=[PK(pb[3])] + Bk, writes=sgk)
                S.op('pool', lambda e: e.tensor_tensor(out=kdt, in0=kr, in1=kdc[:], op=ALU.mult), reads=krk + ['const'], writes=kdk)
                yield
                bq = nextbank(ALLB)
                transposes(lambda c: qr[:, c * 128:(c + 1) * 128], 4, bq, qrk + ['const'])
                S.op('act', lambda e: e.copy(out=qTt, in_=psb[bq][:, 0:512]), reads=[PK(bq)], writes=qTk)
                S.op('dve', lambda e: e.tensor_tensor(out=qdT, in0=psb[bq][:, 0:512], in1=qdc[:], op=ALU.mult),
                     reads=[PK(bq), 'const'], writes=qdk)
                yield
                bk = nextbank(ALLB)
                transposes(lambda c: kr[:, c * 128:(c + 1) * 128], 4, bk, krk + ['const'])
                S.op('act', lambda e: e.copy(out=kTt, in_=psb[bk][:, 0:512]), reads=[PK(bk)], writes=kTk)
                yield 'split'
                bs = nextbank(ALLB)
                for h in range(4):
                    S.op('pe', lambda e, h=h: e.matmul(ps[bs][:, h * 128:(h + 1) * 128], lhsT=kTt[:, h * 128:(h + 1) * 128],
                                                       rhs=qTt[:, h * 128:(h + 1) * 128], start=True, stop=True),
                         reads=kTk + qTk, writes=[PK(bs)])
                S.op('dve', lambda e: e.tensor_tensor(out=STm, in0=ps[bs][:, :], in1=dmk[:], op=ALU.mult),
                     reads=[PK(bs), 'const'], writes=STk)
                yield
                bu = [nextbank(ALLB), nextbank(ALLB)]
                for c in range(2):
                    for h in range(4):
                        S.op('pe', lambda e, c=c, h=h: e.matmul(
                            ps[bu[c]][:, h * 128:(h + 1) * 128], lhsT=kdt[c * 64:(c + 1) * 64, h * 128:(h + 1) * 128],
                            rhs=vb[c * 64:(c + 1) * 64, h * 128:(h + 1) * 128], start=True, stop=True),
                             reads=kdk + vbk, writes=[PK(bu[c])])
                yield
                stf = st[:].rearrange("p h e -> p (h e)")
                for h in range(4):
                    S.op('dve', lambda e, h=h: e.scalar_tensor_tensor(
                        out=st[:, h, :], in0=st[:, h, :], scalar=float(GAM[h] ** 64), in1=ps[bu[0]][:, h * 128:(h + 1) * 128],
                        op0=ALU.mult, op1=ALU.add), reads=['st', PK(bu[0])], writes=['st'])
                S.op('act', lambda e: e.copy(out=stb[:, 1].rearrange("p h e -> p (h e)"), in_=stf), reads=['st'], writes=['stb1'])
                yield
                bo = nextbank(ALLB)
                for h in range(4):
                    S.op('pe', lambda e, h=h: e.matmul(ps[bo][:, h * 128:(h + 1) * 128], lhsT=STm[:, h * 128:(h + 1) * 128],
                                                       rhs=vb[:, h * 128:(h + 1) * 128], start=True, stop=False),
                         reads=STk + vbk, writes=[PK(bo)])
                    for c in range(2):
                        S.op('pe', lambda e, h=h, c=c: e.matmul(
                            ps[bo][c * 64:(c + 1) * 64, h * 128:(h + 1) * 128],
                            lhsT=qdT[:, h * 128 + c * 64:h * 128 + (c + 1) * 64], rhs=stb[:, c, h, :],
                            start=False, stop=True),
                             reads=qdk + [f'stb{c}'], writes=[PK(bo)])
                yield
                for h in range(4):
                    S.op('dve', lambda e, h=h: e.scalar_tensor_tensor(
                        out=st[:, h, :], in0=st[:, h, :], scalar=float(GAM[h] ** 64), in1=ps[bu[1]][:, h * 128:(h + 1) * 128],
                        op0=ALU.mult, op1=ALU.add), reads=['st', PK(bu[1])], writes=['st'])
                S.op('act', lambda e: e.copy(out=stb[:, 0].rearrange("p h e -> p (h e)"), in_=stf), reads=['st'], writes=['stb0'])
                yield
                junk, jk = tunit(0, 1)
                for h in range(4):
                    S.op('act', lambda e, h=h: e.activation(out=junk[:, 0:128], in_=ps[bo][:, h * 128:(h + 1) * 128], func=AF.Square,
                                                            accum_out=sm[:, 32 + h:33 + h]), reads=[PK(bo)], writes=['sm2'] + jk)
                yield
                rstd_from_ssq(sm[:, 32:36], 1.0 / 128, 'sm2')
                yield
                for h in range(4):
                    S.op('dve', lambda e, h=h: e.scalar_tensor_tensor(
                        out=on[:, h * 128:(h + 1) * 128], in0=ps[bo][:, h * 128:(h + 1) * 128], scalar=sm[:, 32 + h:33 + h],
                        in1=sg[:, h * 128:(h + 1) * 128], op0=ALU.mult, op1=ALU.mult),
                         reads=[PK(bo), 'sm2'] + sgk, writes=onk)
                yield
                bt = nextbank(ALLB)
                transposes(lambda c: on[:, c * 128:(c + 1) * 128], 4, bt, onk + ['const'])
                yield
                gb = _bv(pt[:, l, 32:36], [[1, 4], [0, 128]])
                S.op('dve', lambda e, tt=tt, bt=bt, gb=gb: e.tensor_tensor(
                    out=cT[:, 0:4, tt * 128:(tt + 1) * 128], in0=psb[bt][:, 0:512].rearrange("p (k n) -> p k n", k=4),
                    in1=gb, op=ALU.mult), reads=[PK(bt), 'const'], writes=[('cT', kc, tt // 4) for kc in range(4)])
            gens = [ret_tile(tt) for tt in range(NT)]
            for v in gens[0]:
                if v == 'split':
                    break
            for tt in range(NT):
                gB = gens[tt]
                gA = gens[tt + 1] if tt + 1 < NT else None
                doneA, doneB = gA is None, False
                while not (doneA and doneB):
                    if not doneB:
                        try:
                            next(gB)
                        except StopIteration:
                            doneB = True
                    if not doneA:
                        try:
                            if next(gA) == 'split':
                                doneA = True
                        except StopIteration:
                            doneA = True
            if stop == 'ret':
                return
            skT = WA[:, 4:6, :].rearrange("p a (b n) -> p (a b) n", b=2)
            svv = WA[:, 6:8, :].rearrange("p a (t n) -> p (a t) n", t=8)
            for (which, c0) in (("q", 2048), ("k", 2560)):
                wap, wkey = load_piece(w_in[:, c0:c0 + 512], 8, 512)
                for m in range(4):
                    for g in range(4):
                        b = nextbank(ALLB)
                        for kc in range(8):
                            S.op('pe', lambda e, m=m, g=g, kc=kc, b=b, wap=wap: e.matmul(
                                ps[b][:, :], lhsT=wap[:, kc, m * 128:(m + 1) * 128], rhs=hT[:, kc, g * 512:(g + 1) * 512],
                                start=(kc == 0), stop=(kc == 7)),
                                 reads=[wkey, ('hT', kc, g)], writes=[PK(b)])
                        if which == "q":
                            copy_op(evac_engine(), cT[:, 4 + m, g * 512:(g + 1) * 512], ps[b][:, :], [PK(b)], [('cT', 4 + m, g)])
                        else:
                            copy_op(evac_engine(), skT[:, m, g * 512:(g + 1) * 512], ps[b][:, :], [PK(b)], [('hs', 4 + m // 2)])
            wap, wkey = load_piece(w_in[:, 3072:3584], 8, 512)
            for tt in range(NT):
                b = nextbank(ALLB)
                for kc in range(8):
                    S.op('pe', lambda e, tt=tt, kc=kc, b=b, wap=wap: e.matmul(
                        ps[b][:, :], lhsT=hT[:, kc, tt * 128:(tt + 1) * 128], rhs=wap[:, kc, :], start=(kc == 0), stop=(kc == 7)),
                         reads=[wkey, ('hT', kc, tt // 4)], writes=[PK(b)])
                copy_op(evac_engine(), svv[:, tt, :], ps[b][:, :], [PK(b)], [('hs', 6 + tt // 8)])
            PSall = PS
            ZBK = [[0, 1], [2, 3]]
            CBK = [4, 5]
            ACC, SSQ = 6, 7

            def hrow(row, lo, hi, dtype=BF16):
                ap = hT[:, row, lo:hi]
                if dtype == F32:
                    ap = ap.bitcast(F32)
                return ap, [('hT', row, g) for g in range(lo // 512, (hi + 511) // 512)]
            Ebuf = [hrow(0, 0, 2048, F32), hrow(1, 0, 2048, F32), tunit(0, 4, F32)]
            Gbuf = hrow(2, 0, 2048, F32)
            spbuf = [hrow(3, 0, 1024), hrow(3, 1024, 2048)]
            Abuf = [hrow(4, 0, 1024), hrow(4, 1024, 2048)]
            Rbuf = [hrow(5, 0, 1024), hrow(5, 1024, 2048)]
            rsb = hrow(7, 512, 1536, F32)
            rawb = hrow(6, 0, 2048)
            sqb = hrow(7, 0, 512)

            def v2(ap):
                return ap.rearrange("p (c n) -> p c n", c=2)
            mstr2 = _bv(mstr[:], [[0, 2], [1, 128]])

            def sweep_group(G):
                nkb = 4 * G + 4
                items = [(hp, kb) for hp in range(4) for kb in range(nkb - 1, -1, -1)]
                n = len(items)

                def c0_of(i):
                    return max(0, items[i][1] - 4 * G) * 128

                def zmm(i):
                    hp, kb = items[i]
                    c0 = c0_of(i)
                    for ch in range(2):
                        ho = ch * 64
                        zb = ZBK[i % 2][ch]
                        S.op('pe', lambda e, ho=ho, zb=zb: e.matmul(
                            ps[zb][:, c0:512], lhsT=skT[ho:ho + 64, hp, kb * 128:(kb + 1) * 128],
                            rhs=cT[ho:ho + 64, 4 + hp, G * 512 + c0:(G + 1) * 512], start=True, stop=True),
                             reads=[('hs', 4 + hp // 2), ('cT', 4 + hp, G)], writes=[PK(zb)])

                def esp(i):
                    hp, kb = items[i]
                    c0 = c0_of(i)
                    par = i % 2
                    E, Ek = Ebuf[i % 3]; sp, spk = spbuf[par]
                    zb = ZBK[par]
                    S.op('act', lambda e: e.activation(out=v2(E)[:, :, c0:512], in_=PSall[:, zb[0]:zb[0] + 2, c0:512], func=AF.Exp, scale=0.125),
                         reads=[PK(zb[0]), PK(zb[1])], writes=Ek)
                    S.op('act', lambda e: e.activation(out=v2(sp)[:, :, c0:512], in_=v2(E)[:, :, c0:512], func=AF.Ln, bias=1.0, scale=1.0),
                         reads=Ek, writes=spk)
                    if kb >= 4 * G:
                        S.op('dve', lambda e: e.tensor_tensor(out=v2(sp)[:, :, c0:c0 + 128], in0=v2(sp)[:, :, c0:c0 + 128], in1=mstr2, op=ALU.mult),
                             reads=spk + ['const'], writes=spk)
                    first = (kb == nkb - 1)
                    R, Rk = Rbuf[par]; Rn, Rnk = Rbuf[1 - par]
                    if kb > 0:
                        if kb >= 4 * G:
                            if c0 > 0:
                                S.op('dve', lambda e: e.memset(v2(Rn)[:, :, 0:c0], 0.0), writes=Rnk)
                        if first:
                            S.op('dve', lambda e: e.tensor_copy(out=v2(Rn)[:, :, c0:512], in_=v2(sp)[:, :, c0:512]), reads=spk, writes=Rnk)
                        else:
                            S.op('dve', lambda e: e.tensor_tensor(out=v2(Rn)[:, :, c0:512], in0=v2(R)[:, :, c0:512], in1=v2(sp)[:, :, c0:512], op=ALU.add),
                                 reads=Rk + spk, writes=Rnk)

                def cmm(i):
                    hp, kb = items[i]
                    c0 = c0_of(i)
                    first = (kb == nkb - 1)
                    sp, spk = spbuf[i % 2]; R, Rk = Rbuf[i % 2]
                    for ch in range(2):
                        S.op('pe', lambda e, ch=ch: e.matmul(ps[CBK[ch]][:, c0:512], lhsT=tri[:], rhs=v2(sp)[:, ch, c0:512], start=True, stop=first),
                             reads=spk + ['const'], writes=[PK(CBK[ch])])
                        if not first:
                            S.op('pe', lambda e, ch=ch: e.matmul(ps[CBK[ch]][:, c0:512], lhsT=ones[:], rhs=v2(R)[:, ch, c0:512], start=False, stop=True),
                                 reads=Rk + ['const'], writes=[PK(CBK[ch])])

                def rest(i):
                    hp, kb = items[i]
                    c0 = c0_of(i)
                    par = i % 2
                    first = (kb == nkb - 1)
                    E, Ek = Ebuf[i % 3]; sp, spk = spbuf[par]; Gt, Gk = Gbuf; A, Akk = Abuf[par]; R, Rk = Rbuf[par]; Rn, Rnk = Rbuf[1 - par]
                    S.op('act', lambda e: e.activation(out=v2(Gt)[:, :, c0:512], in_=PSall[:, CBK[0]:CBK[0] + 2, c0:512], func=AF.Exp, scale=-1.0),
                         reads=[PK(CBK[0]), PK(CBK[1])], writes=Gk)
                    S.op('dve', lambda e: e.tensor_tensor(out=v2(A)[:, :, c0:512], in0=v2(E)[:, :, c0:512], in1=v2(Gt)[:, :, c0:512], op=ALU.mult),
                         reads=Ek + Gk, writes=Akk)
                    if kb >= 4 * G:
                        S.op('pool', lambda e: e.tensor_tensor(out=v2(A)[:, :, c0:c0 + 128], in0=v2(A)[:, :, c0:c0 + 128], in1=mstr2, op=ALU.mult),
                             reads=Akk + ['const'], writes=Akk)

                def avmm(i):
                    hp, kb = items[i]
                    c0 = c0_of(i)
                    first = (kb == nkb - 1)
                    A, Akk = Abuf[i % 2]
                    if first:
                        S.op('pe', lambda e: e.matmul(ps[ACC][:, :], lhsT=zer[:], rhs=dmk[:], start=True, stop=True),
                             reads=['const'], writes=[PK(ACC)])
                    for ch in range(2):
                        h = 2 * hp + ch
                        S.op('pe', lambda e, ch=ch, h=h: e.matmul(
                            ps[ACC][ch * 64:(ch + 1) * 64, c0:512], lhsT=svv[:, kb, h * 64:(h + 1) * 64], rhs=v2(A)[:, ch, c0:512],
                            start=False, stop=True, skip_group_check=True),
                             reads=Akk + [('hs', 6 + kb // 8)], writes=[PK(ACC)])
                    if kb == 0:
                        raw, rawk = rawb; sq, sqk = sqb
                        S.op('act', lambda e: e.copy(out=raw[:, hp * 512:(hp + 1) * 512], in_=ps[ACC][:, :]), reads=[PK(ACC)], writes=[rawk[hp]])
                        S.op('act', lambda e: e.activation(out=sq, in_=ps[ACC][:, :], func=AF.Square), reads=[PK(ACC)], writes=sqk)
                        S.op('pe', lambda e: e.matmul(ps[SSQ][:, :], lhsT=ones[:], rhs=sq, start=(hp == 0), stop=(hp == 3)),
                             reads=sqk + ['const'], writes=[PK(SSQ)])

                zmm(0)
                if n > 1:
                    zmm(1)
                esp(0)
                cmm(0)
                for i in range(n):
                    if i + 2 < n:
                        zmm(i + 2)
                    if i + 1 < n:
                        esp(i + 1)
                    rest(i)
                    if i + 1 < n:
                        cmm(i + 1)
                    avmm(i)
                rs, rsk = rsb; raw, rawk = rawb
                S.op('act', lambda e: e.activation(out=rs, in_=ps[SSQ][:, :], func=AF.Ln, scale=1.0 / 512, bias=EPS), reads=[PK(SSQ)], writes=rsk)
                S.op('act', lambda e: e.activation(out=rs, in_=rs, func=AF.Exp, scale=-0.5), reads=rsk, writes=rsk)
                for hp in range(4):
                    S.op('dve', lambda e, hp=hp: e.scalar_tensor_tensor(
                        out=cT[:, 4 + hp, G * 512:(G + 1) * 512], in0=raw[:, hp * 512:(hp + 1) * 512], scalar=pt[:, l, 36 + hp:37 + hp],
                        in1=rs, op0=ALU.mult, op1=ALU.mult), reads=[rawk[hp], 'const'] + rsk, writes=[('cT', 4 + hp, G)])
            for G in range(4):
                sweep_group(G)
            if stop == 'sb':
                return
            wm = D["w_mix_out"][l]
            pcs = [load_piece(wm[:, ns * 512:(ns + 1) * 512], 8, 512) for ns in range(2)]
            gemm_resid_norm(lambda kc, tt: cT[:, kc, tt * 128:(tt + 1) * 128], lambda kc, tt: [('cT', kc, tt // 4)],
                            lambda ns, k: (pcs[ns][0][:, k, :], pcs[ns][1]), 8, (l, 8))

        def cross_block(l):
            memf = WA[:, 7, 0:4096].bitcast(F32).rearrange("p (t d) -> p t d", t=2)
            S.dma('sp', lambda e: e.dma_start(out=memf, in_=D["mem"].rearrange("(t p) d -> p t d", p=128)), writes=[('hs', 7)])
            junk, jk = tunit(0, 2)
            for t in range(2):
                S.op('act', lambda e, t=t: e.activation(out=junk, in_=memf[:, t, :], func=AF.Square, accum_out=sm[:, 64 + t:65 + t]),
                     reads=[('hs', 7)], writes=['sm4'] + jk)
            rstd_from_ssq(sm[:, 64:66], 1.0 / DM, 'sm4')
            mnT = WA[:, 4, 0:2048].rearrange("p (k n) -> p k n", k=8)
            for t in range(2):
                hn, hk = tunit(2 + 2 * t, 2)
                S.op('dve', lambda e, t=t, hn=hn: e.tensor_scalar(out=hn, in0=memf[:, t, :], scalar1=sm[:, 64 + t:65 + t], scalar2=None, op0=ALU.mult),
                     reads=[('hs', 7), 'sm4'], writes=hk)
                b = nextbank(ALLB)
                transposes(lambda c, hn=hn: hn[:, c * 128:(c + 1) * 128], 8, b, hk + ['const'])
                gb = _bv(pt[:, l, 16:24], [[1, 8], [0, 128]])
                S.op('dve', lambda e, t=t, b=b, gb=gb: e.tensor_tensor(out=mnT[:, :, t * 128:(t + 1) * 128],
                                                                       in0=psb[b][:, :].rearrange("p (k n) -> p k n", k=8), in1=gb, op=ALU.mult),
                     reads=[PK(b), 'const'], writes=[('hs', 4)])
            kTm = WA[:, 6, 0:2048].rearrange("p (k n) -> p k n", k=8)
            vm = WA[:, 6, 2048:4096].rearrange("p (t n) -> p t n", t=2)
            wkv = D["w_xkv"][l]
            def kv_epi(tt, ns, b):
                if ns < 2:
                    kn, knk = tunit(6 + tt % 2, 1)
                    junk1, jk1 = tunit(0, 1)
                    for hh in range(2):
                        col = 66 + (tt * 4 + ns * 2 + hh)
                        S.op('act', lambda e, hh=hh, col=col: e.activation(out=junk1[:, 0:256], in_=ps[b][:, hh * 256:(hh + 1) * 256], func=AF.Square,
                                                                           accum_out=sm[:, col:col + 1]), reads=[PK(b)], writes=[('sm5', tt, ns)] + jk1)
                    c0 = 66 + tt * 4 + ns * 2
                    rstd_from_ssq(sm[:, c0:c0 + 2], 1.0 / 256, ('sm5', tt, ns))
                    for hh in range(2):
                        S.op('dve', lambda e, hh=hh, kn=kn: e.tensor_scalar(out=kn[:, hh * 256:(hh + 1) * 256], in0=ps[b][:, hh * 256:(hh + 1) * 256],
                                                                            scalar1=sm[:, c0 + hh:c0 + hh + 1], scalar2=None, op0=ALU.mult),
                             reads=[PK(b), ('sm5', tt, ns)], writes=knk)
                    bt = nextbank(ALLB)
                    transposes(lambda c, kn=kn: kn[:, c * 128:(c + 1) * 128], 4, bt, knk + ['const'])
                    gb = _bv(pt[:, l, 42:44], [[0, 2], [1, 2], [0, 128]])
                    S.op('dve', lambda e, bt=bt, gb=gb: e.tensor_tensor(
                        out=kTm[:, ns * 4:(ns + 1) * 4, tt * 128:(tt + 1) * 128].rearrange("p (a c) n -> p a c n", a=2),
                        in0=psb[bt][:, 0:512].rearrange("p (a c n) -> p a c n", a=2, c=2), in1=gb, op=ALU.mult),
                         reads=[PK(bt), 'const'], writes=[('hs', 6)])
                else:
                    copy_op('act', vm[:, tt, (ns - 2) * 512:(ns - 1) * 512], ps[b][:, :], [PK(b)], [('hs', 6)])
            gemm_tm(lambda kc, tt: mnT[:, kc, tt * 128:(tt + 1) * 128], lambda kc, tt: [('hs', 4)],
                    lambda ns: [(wkv[:, ns * 512:(ns + 1) * 512], 8)], 4, kv_epi, ntt=2, defer=1)
            wq = D["w_xq"][l]
            qT = WA[:, 4:6, :].rearrange("p a (b n) -> p (a b) n", b=2)
            for hpair in range(2):
                def q_epi(tt, ns, b):
                    qn, qnk = tunit(5 + tt % 3, 1)
                    junk1, jk1 = tunit(0, 1)
                    for hh in range(2):
                        S.op('act', lambda e, hh=hh: e.activation(out=junk1[:, 0:256], in_=ps[b][:, hh * 256:(hh + 1) * 256], func=AF.Square,
                                                                  accum_out=sm[:, 80 + 2 * (tt % 3) + hh:81 + 2 * (tt % 3) + hh]),
                             reads=[PK(b)], writes=[('sm6', tt % 3)] + jk1)
                    c0 = 80 + 2 * (tt % 3)
                    rstd_from_ssq(sm[:, c0:c0 + 2], 1.0 / 256, ('sm6', tt % 3))
                    for hh in range(2):
                        S.op('dve', lambda e, hh=hh, qn=qn: e.tensor_scalar(out=qn[:, hh * 256:(hh + 1) * 256], in0=ps[b][:, hh * 256:(hh + 1) * 256],
                                                                            scalar1=sm[:, c0 + hh:c0 + hh + 1], scalar2=None, op0=ALU.mult),
                             reads=[PK(b), ('sm6', tt % 3)], writes=qnk)
                    bt = nextbank(ALLB)
                    transposes(lambda c, qn=qn: qn[:, c * 128:(c + 1) * 128], 4, bt, qnk + ['const'])
                    gb = _bv(pt[:, l, 40:42], [[0, 2], [1, 2], [0, 128]])
                    S.op('dve', lambda e, bt=bt, gb=gb, tt=tt: e.tensor_tensor(
                        out=qT[:, :, tt * 128:(tt + 1) * 128].rearrange("p (a c) n -> p a c n", a=2),
                        in0=psb[bt][:, 0:512].rearrange("p (a c n) -> p a c n", a=2, c=2), in1=gb, op=ALU.mult),
                         reads=[PK(bt), 'const'], writes=[('hs', 4 + (c // 2)) for c in range(4)])
                gemm_tm(lambda kc, tt: hT[:, kc, tt * 128:(tt + 1) * 128], lambda kc, tt: [('hT', kc, tt // 4)],
                        lambda ns: [(wq[:, hpair * 512:(hpair + 1) * 512], 8)], 1, q_epi, defer=2)
                for hh in range(2):
                    h = hpair * 2 + hh
                    def attn(g, h=h, hh=hh):
                        pT, pTk = tunit(2 + 2 * (g % 2), 2)
                        bsc = [nextbank(ALLB), nextbank(ALLB)]
                        for mc in range(2):
                            for c in range(2):
                                S.op('pe', lambda e, mc=mc, c=c: e.matmul(
                                    ps[bsc[mc]][:, :], lhsT=kTm[:, h * 2 + c, mc * 128:(mc + 1) * 128],
                                    rhs=qT[:, hh * 2 + c, g * 512:(g + 1) * 512], start=(c == 0), stop=(c == 1)),
                                     reads=[('hs', 6), ('hs', 4 + hh)], writes=[PK(bsc[mc])])
                            S.op('act', lambda e, mc=mc, pT=pT: e.activation(out=pT[:, mc * 512:(mc + 1) * 512], in_=ps[bsc[mc]][:, :], func=AF.Exp, scale=1.0 / 16),
                                 reads=[PK(bsc[mc])], writes=pTk)
                        return pT, pTk

                    def attn2(g, pT, pTk, h=h, hh=hh):
                        bd = nextbank(ALLB)
                        for mc in range(2):
                            S.op('pe', lambda e, mc=mc, pT=pT: e.matmul(ps[bd][:, :], lhsT=ones[:], rhs=pT[:, mc * 512:(mc + 1) * 512],
                                                                       start=(mc == 0), stop=(mc == 1)), reads=pTk + ['const'], writes=[PK(bd)])
                        rden, rdk = tunit(0, 2, F32)
                        S.op('dve', lambda e, rden=rden: e.reciprocal(out=rden, in_=ps[bd][:, :]), reads=[PK(bd)], writes=rdk)
                        for c in range(2):
                            bo = nextbank(ALLB)
                            for mc in range(2):
                                S.op('pe', lambda e, mc=mc, c=c, pT=pT, bo=bo: e.matmul(
                                    ps[bo][:, :], lhsT=vm[:, mc, h * 256 + c * 128:h * 256 + (c + 1) * 128],
                                    rhs=pT[:, mc * 512:(mc + 1) * 512], start=(mc == 0), stop=(mc == 1)),
                                     reads=pTk + [('hs', 6)], writes=[PK(bo)])
                            S.op('dve', lambda e, c=c, bo=bo, rden=rden: e.tensor_tensor(
                                out=cT[:, h * 2 + c, g * 512:(g + 1) * 512], in0=ps[bo][:, :], in1=rden, op=ALU.mult),
                                 reads=[PK(bo)] + rdk, writes=[('cT', h * 2 + c, g)])
                    prevg = None
                    for g in range(4):
                        cur = attn(g)
                        if prevg is not None:
                            attn2(*prevg)
                        prevg = (g,) + cur
                    attn2(*prevg)
            if stop == 'xattn':
                return
            wo = D["w_xo"][l]
            pcs = [load_piece(wo[:, ns * 512:(ns + 1) * 512], 8, 512) for ns in range(2)]
            gemm_resid_norm(lambda kc, tt: cT[:, kc, tt * 128:(tt + 1) * 128], lambda kc, tt: [('cT', kc, tt // 4)],
                            lambda ns, k: (pcs[ns][0][:, k, :], pcs[ns][1]), 8, (l, 24))

        def mlp_block(l):
            ring["slots"] = [0, 1, 2, 4, 5, 6, 7]
            wu = D["w_up"][l]; wd = D["w_down"][l]
            def up(fs):
                wap, wkey = load_piece(wu[:, fs * 512:(fs + 1) * 512], 8, 512)
                for m in range(4):
                    fc = (fs % 2) * 4 + m
                    for g in range(4):
                        b = nextbank(ALLB)
                        for kc in range(8):
                            S.op('pe', lambda e, m=m, g=g, kc=kc, b=b: e.matmul(
                                ps[b][:, :], lhsT=wap[:, kc, m * 128:(m + 1) * 128], rhs=hT[:, kc, g * 512:(g + 1) * 512],
                                start=(kc == 0), stop=(kc == 7)), reads=[wkey, ('hT', kc, g)], writes=[PK(b)])
                        r, rk = tunit((fc * 4 + g) % 8, 1)
                        S.op('act', lambda e, b=b, r=r: e.activation(out=r, in_=ps[b][:, :], func=AF.Relu), reads=[PK(b)], writes=rk)
                        S.op('pool', lambda e, r=r, fc=fc, g=g: e.tensor_tensor(out=cT[:, fc, g * 512:(g + 1) * 512], in0=r, in1=r, op=ALU.mult),
                             reads=rk, writes=[('cT', fc, g)])

            def down(fs):
                hb = (fs % 2) * 4
                wap, wkey = load_piece(wd[fs * 512:(fs + 1) * 512, :], 4, 1024)
                if fs == 7:
                    gemm_resid_norm(lambda k, tt: cT[:, hb + k, tt * 128:(tt + 1) * 128], lambda k, tt: [('cT', hb + k, tt // 4)],
                                    lambda ns, k: (wap[:, k, ns * 512:(ns + 1) * 512], wkey), 4,
                                    (l + 1, 0) if l + 1 < n_layers else None)
                    return
                pending = None
                for ns in range(2):
                    for tt in range(NT):
                        b = nextbank(ALLB)
                        for k in range(4):
                            S.op('pe', lambda e, k=k, b=b, tt=tt, ns=ns: e.matmul(
                                ps[b][:, :], lhsT=cT[:, hb + k, tt * 128:(tt + 1) * 128], rhs=wap[:, k, ns * 512:(ns + 1) * 512],
                                start=(k == 0), stop=(k == 3)), reads=[wkey, ('cT', hb + k, tt // 4)], writes=[PK(b)])
                        if pending is not None:
                            resid_add(*pending)
                        pending = (tt, ns, b)
                resid_add(*pending)
            up(0)
            for fs in range(8):
                if fs + 1 < 8:
                    up(fs + 1)
                down(fs)
            ring["slots"] = [0, 1, 2]

        done = False
        for l in range(n_layers):
            mix_block(l)
            if stop in ('ret', 'sb') or stop == f'mix{l}':
                break
            cross_block(l)
            if stop == 'xattn' or stop == f'cross{l}':
                break
            mlp_block(l)
            if stop == f'mlp{l}':
                break

        if stop in ('ret', 'sb', 'xattn'):
            for kc in range(4 if stop == 'ret' else 8):
                for g in range(4):
                    S.op('dve', lambda e, kc=kc, g=g: e.tensor_copy(out=xs[:, kc * 2 + g // 2, (g % 2) * 512:(g % 2 + 1) * 512],
                                                                    in_=cT[:, kc, g * 512:(g + 1) * 512]),
                         reads=[('cT', kc, g)], writes=[('xs', kc * 2 + g // 2)])
        for tt in range(NT):
            S.dma('sp', lambda e, tt=tt: e.dma_start(out=out_d[tt * 128:(tt + 1) * 128, :], in_=xs[:, tt, :]),
                  reads=[('xs', tt)], writes=[('out', tt)], final=True)
        S.emit()
    return nc


_CONSTS = None


def make_in_maps(inputs, n_cores=8):
    global _CONSTS
    if _CONSTS is None:
        _CONSTS = _const_tables()
    f = lambda a: np.ascontiguousarray(np.asarray(a, dtype=np.float32))
    pt = _param_table(*[np.asarray(inputs[k], dtype=np.float32) for k in
                        ("g_mix", "g_cross", "g_mem", "g_mlp", "g_ret_out", "g_sb_out", "g_qn", "g_kn")])
    shared = {k: f(inputs[k]) for k in ("w_in", "w_mix_out", "w_xq", "w_xkv", "w_xo", "w_up", "w_down")}
    shared["pt"] = pt
    shared.update(_CONSTS)
    x = f(inputs["x"]); mem = f(inputs["mem"])
    maps = []
    for c in range(n_cores):
        m = dict(shared)
        m["x"] = x[c]
        m["mem"] = mem[c]
        maps.append(m)
    return maps


def kernel(**inputs):
    nc = build_program()
    maps = make_in_maps(inputs, 8)
    res = run_bass_kernel_spmd(nc, maps, core_ids=list(range(8)))
    return np.stack([np.asarray(r["out"], dtype=np.float32) for r in res.results], axis=0)
```

```python
import contextlib
import numpy as np
import concourse.bass as bass
import concourse.mybir as mybir
from concourse.bass_utils import run_bass_kernel_spmd

dt = mybir.dt
F32, BF16 = dt.float32, dt.bfloat16
AF = mybir.ActivationFunctionType
ALU = mybir.AluOpType
AX = mybir.AxisListType

ENGS = ('pe', 'act', 'dve', 'pool', 'sp')


class Sched:
    EPOCH = 20000
    NDMA = 8

    def __init__(self, nc):
        self.nc = nc
        self.stack = contextlib.ExitStack()
        self.ops = {e: [] for e in ENGS}
        self.ncomp = {e: 0 for e in ENGS}
        self.esem = {e: {} for e in ENGS}
        self.sems = []
        self.lastw = {}
        self.readers = {}
        self.seen = {e: {} for e in ENGS}
        self.dsem = []
        self.dnext = 0
        self.finals = []

    def __enter__(self):
        self.stack.__enter__()
        return self

    def __exit__(self, *a):
        return self.stack.__exit__(*a)

    def sbuf(self, name, shape, dtype):
        return self.stack.enter_context(self.nc.sbuf_tensor(name, list(shape), dtype))

    def psum(self, name, shape, dtype):
        return self.stack.enter_context(self.nc.psum_tensor(name, list(shape), dtype))

    def _newsem(self, name):
        s = self.stack.enter_context(self.nc.semaphore(name))
        self.sems.append(s)
        return len(self.sems) - 1

    def _deps(self, eng, reads, writes, is_dma):
        deps = set()
        for k in reads:
            t = self.lastw.get(k)
            if t is not None:
                deps.add(t)
        for k in writes:
            t = self.lastw.get(k)
            if t is not None:
                deps.add(t)
            for t in self.readers.get(k, ()):
                if t[2] == eng and eng == 'pe' and not is_dma:
                    continue
                deps.add(t)
        return deps

    def _waits(self, eng, deps, is_dma):
        waits = []
        seen = self.seen[eng]
        for (sid, v, deng) in sorted(deps):
            if deng == eng and eng == 'pe' and not is_dma:
                continue
            if deng.startswith('sw'):
                if seen.get((sid, 'g'), -1) >= int(deng[2:]):
                    continue
                seen[(sid, 'g')] = int(deng[2:])
                waits.append((sid, v))
                continue
            if seen.get(sid, 0) >= v:
                continue
            seen[sid] = v
            waits.append((sid, v))
        return waits

    def _commit(self, tok, reads, writes):
        for k in reads:
            self.readers.setdefault(k, []).append(tok)
        for k in writes:
            self.lastw[k] = tok
            self.readers[k] = []

    def op(self, eng, fn, reads=(), writes=()):
        deps = self._deps(eng, reads, writes, False)
        waits = self._waits(eng, deps, False)
        idx = self.ncomp[eng]
        ep = idx // self.EPOCH
        if ep not in self.esem[eng]:
            self.esem[eng][ep] = self._newsem(f"s_{eng}_{ep}")
        sid = self.esem[eng][ep]
        tok = (sid, idx % self.EPOCH + 1, eng)
        self.ncomp[eng] += 1
        self._commit(tok, reads, writes)
        self.ops[eng].append((fn, waits, sid, 1))
        return tok

    def dma(self, eng, fn, reads=(), writes=(), final=False, slot=None):
        if eng == 'pool':
            return self._dma_sw(fn, reads, writes, slot)
        deps = self._deps(eng, reads, writes, True)
        if len(self.dsem) < self.NDMA:
            self.dsem.append([self._newsem(f"s_dma_{len(self.dsem)}"), 0, None])
        ent = self.dsem[self.dnext % self.NDMA]
        self.dnext += 1
        if ent[2] is not None:
            deps.add(ent[2])
        waits = self._waits(eng, deps, True)
        ent[1] += 16
        tok = (ent[0], ent[1], 'dma')
        ent[2] = tok
        self._commit(tok, reads, writes)
        self.ops[eng].append((fn, waits, ent[0], 16))
        if final:
            self.finals.append(tok)
        return tok

    def _dma_sw(self, fn, reads, writes, slot=None):
        deps = self._deps('pool', reads, writes, True)
        waits = self._waits('pool', deps, True)
        if not hasattr(self, 'nsw'):
            self.nsw = 0
            self.swslot = {}
        if slot is None:
            sid = self._newsem(f"s_sw_{self.nsw}")
            self.nsw += 1
            tok = (sid, 16, 'dma')
            self.ops['pool'].append((fn, waits, sid, 16))
        else:
            if slot not in self.swslot:
                self.swslot[slot] = [self._newsem(f"s_sl_{len(self.swslot)}a"), self._newsem(f"s_sl_{len(self.swslot)}b"), 0]
            ent = self.swslot[slot]
            g = ent[2]
            ent[2] += 1
            sid, other = ent[g % 2], ent[(g + 1) % 2]
            tok = (sid, 16, f'sw{g}')
            self.ops['pool'].append(('swdma', fn, waits, sid, other if g >= 1 else None))
        self._commit(tok, reads, writes)
        return tok

    def emit(self):
        nc = self.nc
        fin = [(sid, v) for (sid, v, _) in self.finals]
        sems = self.sems
        ops = self.ops

        def replay(name, e, tail=()):
            for ent in ops[name]:
                if ent[0] == 'swdma':
                    for (ws, wv) in ent[2]:
                        e.wait_ge(sems[ws], wv)
                    if ent[4] is not None:
                        e.wait_ge(sems[ent[4]], 16)
                        e.sem_clear(sems[ent[4]])
                    ent[1](e).then_inc(sems[ent[3]], 16)
                    continue
                (fn, waits, sid, n) = ent
                for (ws, wv) in waits:
                    e.wait_ge(sems[ws], wv)
                fn(e).then_inc(sems[sid], n)
            for (ws, wv) in tail:
                e.wait_ge(sems[ws], wv)

        with nc.Block() as block:
            @block.sync
            def _(e):
                replay('sp', e, fin)

            @block.tensor
            def _(e):
                replay('pe', e)

            @block.vector
            def _(e):
                replay('dve', e)

            @block.scalar
            def _(e):
                replay('act', e)

            @block.gpsimd
            def _(e):
                replay('pool', e)


SEQ, DM, DEPTH = 2048, 1024, 2
NT = 16
EPS = 1e-6
NPC = 44
GAM = [1.0 - 2.0 ** (-5 - h) for h in range(4)]


def _const_tables():
    p = np.arange(128)
    c = {}
    c["ident"] = np.eye(128, dtype=np.float32)
    c["tri"] = (p[:, None] >= p[None, :]).astype(np.float32)
    c["ones"] = np.ones((128, 128), np.float32)
    c["mstr"] = (p[:, None] < p[None, :]).astype(np.float32)
    inv = (1.0 / (np.float32(10000.0) ** np.linspace(0.0, 1.0, 64, dtype=np.float32))).astype(np.float32)
    pos = np.arange(SEQ, dtype=np.float32)
    ang = (pos[:, None] * inv[None, :]).astype(np.float32)
    c["cos"] = np.cos(ang.astype(np.float64)).astype(np.float32).reshape(NT, 128, 64).transpose(1, 0, 2).copy()
    c["sin"] = np.sin(ang.astype(np.float64)).astype(np.float32).reshape(NT, 128, 64).transpose(1, 0, 2).copy()
    lg = np.log1p(-np.exp2(-5.0 - np.arange(4, dtype=np.float64)))
    dist = np.abs(p[:, None] - p[None, :]).astype(np.float64)
    same = (p[:, None] // 64) == (p[None, :] // 64)
    dm = np.zeros((128, 4, 128), np.float64)
    qd = np.zeros((128, 4, 128), np.float64)
    kd = np.zeros((128, 4, 128), np.float64)
    for h in range(4):
        dm[:, h, :] = np.where(same, np.exp(lg[h] * dist), 0.0) * (128.0 ** -0.5)
        qd[:, h, :] = np.exp(lg[h] * ((p % 64) + 1.0))[None, :]
        kd[:, h, :] = (np.exp(lg[h] * (63.0 - (p % 64))) * (128.0 ** -0.5))[:, None]
    out = {}
    out["cA"] = np.concatenate([c["ident"], c["tri"], c["ones"], c["mstr"]], axis=1).astype(np.float32)
    out["cB"] = np.stack([c["cos"], c["sin"]], axis=1).astype(np.float32)
    out["cC"] = np.concatenate([dm.reshape(128, 512), qd.reshape(128, 512), kd.reshape(128, 512)], axis=1).astype(np.float32)
    return out


def _param_table(g_mix, g_cross, g_mem, g_mlp, g_ret_out, g_sb_out, g_qn, g_kn):
    pt = np.zeros((128, DEPTH, NPC), np.float32)
    for l in range(DEPTH):
        pt[:, l, 0:8] = g_mix[l].reshape(8, 128).T
        pt[:, l, 8:16] = g_cross[l].reshape(8, 128).T
        pt[:, l, 16:24] = g_mem[l].reshape(8, 128).T
        pt[:, l, 24:32] = g_mlp[l].reshape(8, 128).T
        pt[:, l, 32:36] = g_ret_out[l].reshape(4, 128).T
        pt[:, l, 36:40] = g_sb_out[l].reshape(4, 128).T
        pt[:, l, 40:42] = g_qn[l].reshape(2, 128).T
        pt[:, l, 42:44] = g_kn[l].reshape(2, 128).T
    return pt


def _bv(ap, dims):
    return bass.AP(ap.tensor, ap.offset, [list(ap.ap[0])] + [list(d) for d in dims])


def build_program(n_layers=DEPTH, stop=None):
    nc = bass.Bass("TRN2", target_bir_lowering=False)
    D = {}
    def din(name, shape):
        D[name] = nc.dram_tensor(name, list(shape), F32, kind="ExternalInput").ap()
    din("x", [SEQ, DM]); din("mem", [256, DM])
    din("w_in", [DEPTH, DM, 3584]); din("w_mix_out", [DEPTH, DM, DM]); din("w_xq", [DEPTH, DM, DM])
    din("w_xkv", [DEPTH, DM, 2 * DM]); din("w_xo", [DEPTH, DM, DM]); din("w_up", [DEPTH, DM, 4 * DM])
    din("w_down", [DEPTH, 4 * DM, DM]); din("pt", [128, DEPTH, NPC])
    din("cA", [128, 512]); din("cB", [128, 2, NT, 64]); din("cC", [128, 1536])
    out_d = nc.dram_tensor("out", [SEQ, DM], F32, kind="ExternalOutput").ap()

    S = Sched(nc)
    with S:
        xs = S.sbuf("xs", [128, NT, DM], F32)
        hT = S.sbuf("hT", [128, 8, SEQ], BF16)
        cT = S.sbuf("cT", [128, 8, SEQ], BF16)
        WA = S.sbuf("WA", [128, 8, 4096], BF16)
        cA = S.sbuf("cA_s", [128, 512], BF16)
        ident, tri, ones, mstr = cA[:, 0:128], cA[:, 128:256], cA[:, 256:384], cA[:, 384:512]
        zer = S.sbuf("zer_s", [128, 128], BF16)
        mhalf = S.sbuf("mhalf_s", [128, 16], F32)
        cB = S.sbuf("cB_s", [128, 2, NT, 64], BF16)
        cosT, sinT = cB[:, 0], cB[:, 1]
        cC = S.sbuf("cC_s", [128, 1536], BF16)
        dmk, qdc, kdc = cC[:, 0:512], cC[:, 512:1024], cC[:, 1024:1536]
        pt = S.sbuf("pt_s", [128, DEPTH, NPC], F32)
        st = S.sbuf("st_s", [128, 4, 128], F32)
        stb = S.sbuf("stb_s", [128, 2, 4, 128], BF16)
        sm = S.sbuf("sm_s", [128, 128], F32)
        PS = S.psum("ps", [128, 8, 512], F32)
        ps = [PS[:, i, :] for i in range(8)]
        psb = [p.bitcast(BF16) for p in ps]

        rr = {"dq": 0, "bank": 0, "ring": 0, "ev": 0}

        def dq():
            rr["dq"] += 1
            return 'sp' if rr["dq"] % 2 else 'act'

        def PK(b):
            return ('ps', b)

        def unit(buf, name, u, n=1, dtype=BF16):
            kc, g = u // 4, u % 4
            assert g + n <= 4
            ap = buf[:, kc, g * 512:(g + n) * 512]
            if dtype == F32:
                ap = ap.bitcast(F32)
            return ap, [(name, kc, g + i) for i in range(n)]

        def tunit(i, n=1, dtype=BF16):
            ap = WA[:, 3, i * 512:(i + n) * 512]
            if dtype == F32:
                ap = ap.bitcast(F32)
            return ap, [('t', i + k) for k in range(n)]

        def nextbank(allowed):
            rr["bank"] += 1
            return allowed[rr["bank"] % len(allowed)]

        ring = {"slots": [0, 1, 2]}

        def load_piece(w2d, nk, ncols):
            assert nk * ncols <= 4096
            rr["ring"] += 1
            s = ring["slots"][rr["ring"] % len(ring["slots"])]
            dst = WA[:, s, 0:nk * ncols].rearrange("p (k n) -> p k n", k=nk)
            src = w2d.rearrange("(k p) n -> p k n", p=128)
            S.dma('pool', lambda e: e.dma_start(out=dst, in_=src), reads=[], writes=[('hs', s)])
            return dst, ('hs', s)

        def evac_engine():
            rr["ev"] += 1
            return 'act' if rr["ev"] % 2 else 'dve'

        def copy_op(eng, out, in_, reads, writes):
            if eng == 'act':
                S.op('act', lambda e: e.copy(out=out, in_=in_), reads=reads, writes=writes)
            else:
                S.op(eng, lambda e: e.tensor_copy(out=out, in_=in_), reads=reads, writes=writes)

        def rstd_from_ssq(ssq_ap, inv_n, key):
            S.op('act', lambda e: e.activation(out=ssq_ap, in_=ssq_ap, func=AF.Ln, scale=inv_n, bias=EPS), reads=[key], writes=[key])
            S.op('act', lambda e: e.activation(out=ssq_ap, in_=ssq_ap, func=AF.Exp, scale=-0.5), reads=[key], writes=[key])

        def transposes(src_fn, n, bank, reads):
            for c in range(n):
                S.op('pe', lambda e, c=c: e.transpose(out=psb[bank][:, c * 128:(c + 1) * 128], in_=src_fn(c),
                                                      identity=ident[:]),
                     reads=reads, writes=[PK(bank)])

        for tt in range(NT):
            S.dma('sp', lambda e, tt=tt: e.dma_start(out=xs[:, tt, :], in_=D["x"][tt * 128:(tt + 1) * 128, :]),
                  writes=[('xs', tt)])
        for (sb_t, nm) in ((cA, "cA"), (cB, "cB"), (cC, "cC")):
            S.dma('pool', lambda e, sb_t=sb_t, nm=nm: e.dma_start(out=sb_t[:], in_=D[nm]), writes=['const'])
        S.dma('sp', lambda e: e.dma_start(out=pt[:], in_=D["pt"]), writes=['const'])
        S.op('pool', lambda e: e.memset(zer[:], 0.0), writes=['const'])
        S.op('pool', lambda e: e.memset(mhalf[:], -0.5), writes=['const'])

        ALLB = list(range(8))

        def norm_to_hT(l, gcol):
            junk, jk = tunit(0, 2)
            for tt in range(NT):
                S.op('act', lambda e, tt=tt: e.activation(out=junk, in_=xs[:, tt, :], func=AF.Square,
                                                          accum_out=sm[:, tt:tt + 1]),
                     reads=[('xs', tt)], writes=['sm'] + jk)
            rstd_from_ssq(sm[:, 0:NT], 1.0 / DM, 'sm')
            gb = _bv(pt[:, l, gcol:gcol + 8], [[1, 8], [0, 128]])

            def front(tt):
                hn, hk = tunit(2 + 2 * (tt % 3), 2)
                S.op('dve', lambda e: e.tensor_scalar(out=hn, in0=xs[:, tt, :], scalar1=sm[:, tt:tt + 1],
                                                      scalar2=None, op0=ALU.mult),
                     reads=[('xs', tt), 'sm'], writes=hk)
                b = nextbank(ALLB)
                transposes(lambda c: hn[:, c * 128:(c + 1) * 128], 8, b, hk + ['const'])
                return b

            def back(tt, b):
                S.op('dve', lambda e: e.tensor_tensor(
                    out=hT[:, :, tt * 128:(tt + 1) * 128],
                    in0=psb[b][:, :].rearrange("p (k n) -> p k n", k=8), in1=gb, op=ALU.mult),
                     reads=[PK(b), 'const'], writes=[('hT', kc, tt // 4) for kc in range(8)])
            prev = None
            for tt in range(NT):
                b = front(tt)
                if prev is not None:
                    back(*prev)
                prev = (tt, b)
            back(*prev)

        def gemm_tm(lhs_fn, lhs_keys_fn, pieces_fn, n_slices, epilogue, banks=ALLB, ntt=NT, defer=1):
            pending = []
            for ns in range(n_slices):
                pcs = [load_piece(w2d, nk, 512) for (w2d, nk) in pieces_fn(ns)]
                for tt in range(ntt):
                    b = nextbank(banks)
                    tot = sum(p[0].shape[1] for p in pcs)
                    i = 0
                    for pi, (wap, wkey) in enumerate(pcs):
                        for k in range(wap.shape[1]):
                            kc = sum(p[0].shape[1] for p in pcs[:pi]) + k
                            S.op('pe', lambda e, wap=wap, k=k, kc=kc, i=i, b=b, tt=tt: e.matmul(
                                ps[b][:, :], lhsT=lhs_fn(kc, tt), rhs=wap[:, k, :], start=(i == 0), stop=(i == tot - 1)),
                                 reads=[wkey] + lhs_keys_fn(kc, tt), writes=[PK(b)])
                            i += 1
                    pending.append((tt, ns, b))
                    if len(pending) > defer:
                        epilogue(*pending.pop(0))
                while pending:
                    epilogue(*pending.pop(0))

        def resid_add(tt, ns, b):
            S.op('dve', lambda e: e.tensor_tensor(out=xs[:, tt, ns * 512:(ns + 1) * 512], in0=ps[b][:, :],
                                                  in1=xs[:, tt, ns * 512:(ns + 1) * 512], op=ALU.add),
                 reads=[PK(b), ('xs', tt)], writes=[('xs', tt)])

        def gemm_resid_norm(lhs_fn, lhs_keys_fn, rhs_fn, nk, norm):
            pend_add = []
            junk, jk = tunit(0, 2)
            fr = {}

            def n_stats(t):
                S.op('act', lambda e: e.activation(out=junk, in_=xs[:, t, :], func=AF.Square, accum_out=sm[:, t:t + 1]),
                     reads=[('xs', t)], writes=[('smn', t)] + jk)
                rstd_from_ssq(sm[:, t:t + 1], 1.0 / DM, ('smn', t))

            def n_front(t):
                hn, hk = tunit(2 + 2 * (t % 3), 2)
                S.op('dve', lambda e: e.tensor_scalar(out=hn, in0=xs[:, t, :], scalar1=sm[:, t:t + 1], scalar2=None, op0=ALU.mult),
                     reads=[('xs', t), ('smn', t)], writes=hk)
                b = nextbank(ALLB)
                transposes(lambda c: hn[:, c * 128:(c + 1) * 128], 8, b, hk + ['const'])
                fr[t] = b

            def n_back(t):
                b = fr[t]
                gb = _bv(pt[:, norm[0], norm[1]:norm[1] + 8], [[1, 8], [0, 128]])
                S.op('dve', lambda e: e.tensor_tensor(out=hT[:, :, t * 128:(t + 1) * 128],
                                                      in0=psb[b][:, :].rearrange("p (k n) -> p k n", k=8), in1=gb, op=ALU.mult),
                     reads=[PK(b), 'const'], writes=[('hT', kc, t // 4) for kc in range(8)])

            for tt in range(NT + 3):
                if tt < NT:
                    for ns in range(2):
                        b = nextbank(ALLB)
                        for k in range(nk):
                            rhs, wkey = rhs_fn(ns, k)
                            S.op('pe', lambda e, k=k, b=b, tt=tt, rhs=rhs: e.matmul(ps[b][:, :], lhsT=lhs_fn(k, tt), rhs=rhs,
                                                                                  start=(k == 0), stop=(k == nk - 1)),
                                 reads=[wkey] + lhs_keys_fn(k, tt), writes=[PK(b)])
                        pend_add.append((tt, ns, b))
                        if len(pend_add) > 1:
                            resid_add(*pend_add.pop(0))
                else:
                    while pend_add:
                        resid_add(*pend_add.pop(0))
                if norm is not None:
                    if 0 <= tt - 1 < NT:
                        n_stats(tt - 1)
                    if 0 <= tt - 2 < NT:
                        n_front(tt - 2)
                    if 0 <= tt - 3 < NT:
                        n_back(tt - 3)

        def mix_block(l):
            w_in = D["w_in"][l]
            if l == 0:
                norm_to_hT(l, 0)
            for j in range(4):
                dst = WA[:, 4 + j, :].rearrange("p (k n) -> p k n", k=8)
                src = w_in[:, j * 512:(j + 1) * 512].rearrange("(k p) n -> p k n", p=128)
                S.dma('pool', lambda e, dst=dst, src=src: e.dma_start(out=dst, in_=src), writes=[('hs', 4 + j)])
            S.op('pool', lambda e: e.memset(st[:], 0.0), writes=['st'])
            S.op('pool', lambda e: e.memset(stb[:], 0.0), writes=['stb0', 'stb1'])
            def cu(i, n=1, dtype=BF16):
                return unit(cT, 'cT', 16 + i, n, dtype)
            RB = [0, 1, 2, 3, 4, 5]
            def ret_tile(tt):
                pb = [0, 1, 2, 3]
                for j in range(4):
                    wv = WA[:, 4 + j, :].rearrange("p (k n) -> p k n", k=8)
                    for kc in range(8):
                        S.op('pe', lambda e, j=j, kc=kc, wv=wv, tt=tt: e.matmul(
                            ps[pb[j]][:, :], lhsT=hT[:, kc, tt * 128:(tt + 1) * 128], rhs=wv[:, kc, :],
                            start=(kc == 0), stop=(kc == 7)),
                             reads=[('hs', 4 + j), ('hT', kc, tt // 4)], writes=[PK(pb[j])])
                    yield
                pp = tt % 2
                A, Ak = cu(0, 2, F32)
                Bm, Bk = cu(2, 2, F32)
                qr, qrk = cu(4); kr, krk = cu(5); STm, STk = cu(6); on, onk = cu(7)
                kdt, kdk = cu(8 + pp); vb, vbk = cu(10 + pp); qTt, qTk = cu(14 + pp)
                sg, sgk = [cu(12), cu(13), tunit(5)][tt % 3]
                qdT, qdk = tunit(1 + pp); kTt, kTk = tunit(3 + pp)
                cosb = _bv(cosT[:, tt, :], [[0, 4], [0, 2], [1, 64]])
                sinb = _bv(sinT[:, tt, :], [[0, 4], [1, 64]])
                def rot(bank, dstt, dk):
                    P4 = ps[bank][:, :].rearrange("p (h x d) -> p h x d", h=4, x=2)
                    A4 = A.rearrange("p (h x d) -> p h x d", h=4, x=2)
                    B4 = Bm.rearrange("p (x h d) -> p x h d", x=2, h=4)
                    D4 = dstt.rearrange("p (h x d) -> p h x d", h=4, x=2)
                    S.op('dve', lambda e, P4=P4, A4=A4: e.tensor_tensor(out=A4, in0=P4, in1=cosb, op=ALU.mult),
                         reads=[PK(bank), 'const'], writes=Ak)
                    S.op('dve', lambda e, P4=P4, B4=B4: e.tensor_tensor(out=B4[:, 0], in0=P4[:, :, 1, :], in1=sinb, op=ALU.mult),
                         reads=[PK(bank), 'const'], writes=[Bk[0]])
                    S.op('dve', lambda e, P4=P4, B4=B4: e.tensor_tensor(out=B4[:, 1], in0=P4[:, :, 0, :], in1=sinb, op=ALU.mult),
                         reads=[PK(bank), 'const'], writes=[Bk[1]])
                    S.op('pool', lambda e, A4=A4, B4=B4, D4=D4: e.tensor_tensor(out=D4[:, :, 0, :], in0=A4[:, :, 0, :], in1=B4[:, 0], op=ALU.subtract),
                         reads=Ak + Bk, writes=dk)
                    S.op('pool', lambda e, A4=A4, B4=B4, D4=D4: e.tensor_tensor(out=D4[:, :, 1, :], in0=A4[:, :, 1, :], in1=B4[:, 1], op=ALU.add),
                         reads=Ak + Bk, writes=dk)
                rot(pb[0], qr, qrk)
                yield
                S.op('act', lambda e: e.copy(out=vb, in_=ps[pb[2]][:, :]), reads=[PK(pb[2])], writes=vbk)
                yield
                rot(pb[1], kr, krk)
                yield
                S.op('act', lambda e: e.activation(out=Bm, in_=ps[pb[3]][:, :], func=AF.Exp, scale=-1.0), reads=[PK(pb[3])], writes=Bk)
                S.op('act', lambda e: e.activation(out=Bm, in_=Bm, func=AF.Ln, bias=1.0, scale=1.0), reads=Bk, writes=Bk)
                S.op('act', lambda e: e.activation(out=Bm, in_=Bm, func=AF.Exp, scale=-1.0), reads=Bk, writes=Bk)
                S.op('dve', lambda e: e.tensor_tensor(out=sg, in0=ps[pb[3]][:, :], in1=Bm, op=ALU.mult), reads=[PK(pb[3])] + Bk, writes=sgk)
                S.op('pool', lambda e: e.tensor_tensor(out=kdt, in0=kr, in1=kdc[:], op=ALU.mult), reads=krk + ['const'], writes=kdk)
                yield
                bq = 0
                transposes(lambda c: qr[:, c * 128:(c + 1) * 128], 4, bq, qrk + ['const'])
                S.op('act', lambda e: e.copy(out=qTt, in_=psb[bq][:, 0:512]), reads=[PK(bq)], writes=qTk)
                S.op('dve', lambda e: e.tensor_tensor(out=qdT, in0=psb[bq][:, 0:512], in1=qdc[:], op=ALU.mult),
                     reads=[PK(bq), 'const'], writes=qdk)
                yield
                bk = 1
                transposes(lambda c: kr[:, c * 128:(c + 1) * 128], 4, bk, krk + ['const'])
                S.op('act', lambda e: e.copy(out=kTt, in_=psb[bk][:, 0:512]), reads=[PK(bk)], writes=kTk)
                yield 'split'
                bs = 4
                for h in range(4):
                    S.op('pe', lambda e, h=h: e.matmul(ps[bs][:, h * 128:(h + 1) * 128], lhsT=kTt[:, h * 128:(h + 1) * 128],
                                                       rhs=qTt[:, h * 128:(h + 1) * 128], start=True, stop=True),
                         reads=kTk + qTk, writes=[PK(bs)])
                S.op('dve', lambda e: e.tensor_tensor(out=STm, in0=ps[bs][:, :], in1=dmk[:], op=ALU.mult),
                     reads=[PK(bs), 'const'], writes=STk)
                yield
                bu = [5, 5]
                def umm(c):
                    for h in range(4):
                        S.op('pe', lambda e, h=h: e.matmul(
                            ps[bu[c]][:, h * 128:(h + 1) * 128], lhsT=kdt[c * 64:(c + 1) * 64, h * 128:(h + 1) * 128],
                            rhs=vb[c * 64:(c + 1) * 64, h * 128:(h + 1) * 128], start=True, stop=True),
                             reads=kdk + vbk, writes=[PK(bu[c])])
                umm(0)
                yield
                stf = st[:].rearrange("p h e -> p (h e)")
                for h in range(4):
                    S.op('dve', lambda e, h=h: e.scalar_tensor_tensor(
                        out=st[:, h, :], in0=st[:, h, :], scalar=float(GAM[h] ** 64), in1=ps[bu[0]][:, h * 128:(h + 1) * 128],
                        op0=ALU.mult, op1=ALU.add), reads=['st', PK(bu[0])], writes=['st'])
                S.op('act', lambda e: e.copy(out=stb[:, 1].rearrange("p h e -> p (h e)"), in_=stf), reads=['st'], writes=['stb1'])
                umm(1)
                yield
                bo = 6 + tt % 2
                for h in range(4):
                    S.op('pe', lambda e, h=h: e.matmul(ps[bo][:, h * 128:(h + 1) * 128], lhsT=STm[:, h * 128:(h + 1) * 128],
                                                       rhs=vb[:, h * 128:(h + 1) * 128], start=True, stop=False),
                         reads=STk + vbk, writes=[PK(bo)])
                    for c in range(2):
                        S.op('pe', lambda e, h=h, c=c: e.matmul(
                            ps[bo][c * 64:(c + 1) * 64, h * 128:(h + 1) * 128],
                            lhsT=qdT[:, h * 128 + c * 64:h * 128 + (c + 1) * 64], rhs=stb[:, c, h, :],
                            start=False, stop=True),
                             reads=qdk + [f'stb{c}'], writes=[PK(bo)])
                yield
                for h in range(4):
                    S.op('dve', lambda e, h=h: e.scalar_tensor_tensor(
                        out=st[:, h, :], in0=st[:, h, :], scalar=float(GAM[h] ** 64), in1=ps[bu[1]][:, h * 128:(h + 1) * 128],
                        op0=ALU.mult, op1=ALU.add), reads=['st', PK(bu[1])], writes=['st'])
                S.op('act', lambda e: e.copy(out=stb[:, 0].rearrange("p h e -> p (h e)"), in_=stf), reads=['st'], writes=['stb0'])
                yield 'split'
                junk, jk = tunit(0, 1)
                for h in range(4):
                    S.op('act', lambda e, h=h: e.activation(out=junk[:, 0:128], in_=ps[bo][:, h * 128:(h + 1) * 128], func=AF.Square,
                                                            accum_out=sm[:, 32 + h:33 + h]), reads=[PK(bo)], writes=['sm2'] + jk)
                yield
                rstd_from_ssq(sm[:, 32:36], 1.0 / 128, 'sm2')
                yield
                for h in range(4):
                    S.op('dve', lambda e, h=h: e.scalar_tensor_tensor(
                        out=on[:, h * 128:(h + 1) * 128], in0=ps[bo][:, h * 128:(h + 1) * 128], scalar=sm[:, 32 + h:33 + h],
                        in1=sg[:, h * 128:(h + 1) * 128], op0=ALU.mult, op1=ALU.mult),
                         reads=[PK(bo), 'sm2'] + sgk, writes=onk)
                yield
                bt = 4
                transposes(lambda c: on[:, c * 128:(c + 1) * 128], 4, bt, onk + ['const'])
                yield
                gb = _bv(pt[:, l, 32:36], [[1, 4], [0, 128]])
                S.op('dve', lambda e, tt=tt, bt=bt, gb=gb: e.tensor_tensor(
                    out=cT[:, 0:4, tt * 128:(tt + 1) * 128], in0=psb[bt][:, 0:512].rearrange("p (k n) -> p k n", k=4),
                    in1=gb, op=ALU.mult), reads=[PK(bt), 'const'], writes=[('cT', kc, tt // 4) for kc in range(4)])
            sbw = {nm: load_piece(w_in[:, c0:c0 + 512], 8, 512) for (nm, c0) in (("q", 2048), ("k", 2560), ("v", 3072))}
            gens = [ret_tile(tt) for tt in range(NT)]
            for period in range(NT + 2):
                active = [gens[t] for t in (period, period - 1, period - 2) if 0 <= t < NT]
                while active:
                    for g in list(active):
                        try:
                            if next(g) == 'split':
                                active.remove(g)
                        except StopIteration:
                            active.remove(g)
            if stop == 'ret':
                return
            skT = WA[:, 4:6, :].rearrange("p a (b n) -> p (a b) n", b=2)
            svv = WA[:, 6:8, :].rearrange("p a (t n) -> p (a t) n", t=8)
            for (which, c0) in (("q", 2048), ("k", 2560)):
                wap, wkey = sbw[which]
                for m in range(4):
                    for g in range(4):
                        b = nextbank(ALLB)
                        for kc in range(8):
                            S.op('pe', lambda e, m=m, g=g, kc=kc, b=b, wap=wap: e.matmul(
                                ps[b][:, :], lhsT=wap[:, kc, m * 128:(m + 1) * 128], rhs=hT[:, kc, g * 512:(g + 1) * 512],
                                start=(kc == 0), stop=(kc == 7)),
                                 reads=[wkey, ('hT', kc, g)], writes=[PK(b)])
                        if which == "q":
                            copy_op(evac_engine(), cT[:, 4 + m, g * 512:(g + 1) * 512], ps[b][:, :], [PK(b)], [('cT', 4 + m, g)])
                        else:
                            copy_op(evac_engine(), skT[:, m, g * 512:(g + 1) * 512], ps[b][:, :], [PK(b)], [('hs', 4 + m // 2)])
            wap, wkey = sbw["v"]
            wm = D["w_mix_out"][l]
            mixw = [load_piece(wm[:, ns * 512:(ns + 1) * 512], 8, 512) for ns in range(2)]
            for tt in range(NT):
                b = nextbank(ALLB)
                for kc in range(8):
                    S.op('pe', lambda e, tt=tt, kc=kc, b=b, wap=wap: e.matmul(
                        ps[b][:, :], lhsT=hT[:, kc, tt * 128:(tt + 1) * 128], rhs=wap[:, kc, :], start=(kc == 0), stop=(kc == 7)),
                         reads=[wkey, ('hT', kc, tt // 4)], writes=[PK(b)])
                copy_op(evac_engine(), svv[:, tt, :], ps[b][:, :], [PK(b)], [('hs', 6 + tt // 8)])
            PSall = PS
            ZBK = [[0, 1], [2, 3]]
            CBK = [4, 5]
            ACC, SSQ = 6, 7

            def hrow(row, lo, hi, dtype=BF16):
                ap = hT[:, row, lo:hi]
                if dtype == F32:
                    ap = ap.bitcast(F32)
                return ap, [('hT', row, g) for g in range(lo // 512, (hi + 511) // 512)]
            Ebuf = [hrow(0, 0, 2048, F32), hrow(1, 0, 2048, F32), tunit(0, 4, F32)]
            Gbuf = hrow(2, 0, 2048, F32)
            spbuf = [hrow(3, 0, 1024), hrow(3, 1024, 2048)]
            Abuf = [hrow(4, 0, 1024), hrow(4, 1024, 2048)]
            Rbuf = [hrow(5, 0, 1024), hrow(5, 1024, 2048)]
            rsb = hrow(7, 512, 1536, F32)
            rawb = hrow(6, 0, 2048)
            sqb = hrow(7, 0, 512)

            def v2(ap):
                return ap.rearrange("p (c n) -> p c n", c=2)
            mstr2 = _bv(mstr[:], [[0, 2], [1, 128]])

            def sweep_group(G):
                nkb = 4 * G + 4
                items = [(hp, kb) for hp in range(4) for kb in range(nkb - 1, -1, -1)]
                n = len(items)

                def c0_of(i):
                    return max(0, items[i][1] - 4 * G) * 128

                def zmm(i):
                    hp, kb = items[i]
                    c0 = c0_of(i)
                    for ch in range(2):
                        ho = ch * 64
                        zb = ZBK[i % 2][ch]
                        S.op('pe', lambda e, ho=ho, zb=zb: e.matmul(
                            ps[zb][:, c0:512], lhsT=skT[ho:ho + 64, hp, kb * 128:(kb + 1) * 128],
                            rhs=cT[ho:ho + 64, 4 + hp, G * 512 + c0:(G + 1) * 512], start=True, stop=True),
                             reads=[('hs', 4 + hp // 2), ('cT', 4 + hp, G)], writes=[PK(zb)])

                def esp(i):
                    hp, kb = items[i]
                    c0 = c0_of(i)
                    par = i % 2
                    E, Ek = Ebuf[i % 3]; sp, spk = spbuf[par]
                    zb = ZBK[par]
                    S.op('act', lambda e: e.activation(out=v2(E)[:, :, c0:512], in_=PSall[:, zb[0]:zb[0] + 2, c0:512], func=AF.Exp, scale=0.125),
                         reads=[PK(zb[0]), PK(zb[1])], writes=Ek)
                    S.op('act', lambda e: e.activation(out=v2(sp)[:, :, c0:512], in_=v2(E)[:, :, c0:512], func=AF.Ln, bias=1.0, scale=1.0),
                         reads=Ek, writes=spk)
                    if kb >= 4 * G:
                        S.op('dve', lambda e: e.tensor_tensor(out=v2(sp)[:, :, c0:c0 + 128], in0=v2(sp)[:, :, c0:c0 + 128], in1=mstr2, op=ALU.mult),
                             reads=spk + ['const'], writes=spk)
                    first = (kb == nkb - 1)
                    R, Rk = Rbuf[par]; Rn, Rnk = Rbuf[1 - par]
                    if kb > 0:
                        if kb >= 4 * G:
                            if c0 > 0:
                                S.op('dve', lambda e: e.memset(v2(Rn)[:, :, 0:c0], 0.0), writes=Rnk)
                        if first:
                            S.op('dve', lambda e: e.tensor_copy(out=v2(Rn)[:, :, c0:512], in_=v2(sp)[:, :, c0:512]), reads=spk, writes=Rnk)
                        else:
                            S.op('dve', lambda e: e.tensor_tensor(out=v2(Rn)[:, :, c0:512], in0=v2(R)[:, :, c0:512], in1=v2(sp)[:, :, c0:512], op=ALU.add),
                                 reads=Rk + spk, writes=Rnk)

                def cmm(i):
                    hp, kb = items[i]
                    c0 = c0_of(i)
                    first = (kb == nkb - 1)
                    sp, spk = spbuf[i % 2]; R, Rk = Rbuf[i % 2]
                    for ch in range(2):
                        S.op('pe', lambda e, ch=ch: e.matmul(ps[CBK[ch]][:, c0:512], lhsT=tri[:], rhs=v2(sp)[:, ch, c0:512], start=True, stop=first),
                             reads=spk + ['const'], writes=[PK(CBK[ch])])
                        if not first:
                            S.op('pe', lambda e, ch=ch: e.matmul(ps[CBK[ch]][:, c0:512], lhsT=ones[:], rhs=v2(R)[:, ch, c0:512], start=False, stop=True),
                                 reads=Rk + ['const'], writes=[PK(CBK[ch])])

                def rest(i):
                    hp, kb = items[i]
                    c0 = c0_of(i)
                    par = i % 2
                    first = (kb == nkb - 1)
                    E, Ek = Ebuf[i % 3]; sp, spk = spbuf[par]; Gt, Gk = Gbuf; A, Akk = Abuf[par]; R, Rk = Rbuf[par]; Rn, Rnk = Rbuf[1 - par]
                    S.op('act', lambda e: e.activation(out=v2(Gt)[:, :, c0:512], in_=PSall[:, CBK[0]:CBK[0] + 2, c0:512], func=AF.Exp, scale=-1.0),
                         reads=[PK(CBK[0]), PK(CBK[1])], writes=Gk)
                    S.op('dve', lambda e: e.tensor_tensor(out=v2(A)[:, :, c0:512], in0=v2(E)[:, :, c0:512], in1=v2(Gt)[:, :, c0:512], op=ALU.mult),
                         reads=Ek + Gk, writes=Akk)
                    if kb >= 4 * G:
                        S.op('pool', lambda e: e.tensor_tensor(out=v2(A)[:, :, c0:c0 + 128], in0=v2(A)[:, :, c0:c0 + 128], in1=mstr2, op=ALU.mult),
                             reads=Akk + ['const'], writes=Akk)

                def avmm(i):
                    hp, kb = items[i]
                    c0 = c0_of(i)
                    first = (kb == nkb - 1)
                    A, Akk = Abuf[i % 2]
                    if first:
                        S.op('pe', lambda e: e.matmul(ps[ACC][:, :], lhsT=zer[:], rhs=dmk[:], start=True, stop=True),
                             reads=['const'], writes=[PK(ACC)])
                    for ch in range(2):
                        h = 2 * hp + ch
                        S.op('pe', lambda e, ch=ch, h=h: e.matmul(
                            ps[ACC][ch * 64:(ch + 1) * 64, c0:512], lhsT=svv[:, kb, h * 64:(h + 1) * 64], rhs=v2(A)[:, ch, c0:512],
                            start=False, stop=True, skip_group_check=True),
                             reads=Akk + [('hs', 6 + kb // 8)], writes=[PK(ACC)])
                    if kb == 0:
                        raw, rawk = rawb; sq, sqk = sqb
                        S.op('act', lambda e: e.copy(out=raw[:, hp * 512:(hp + 1) * 512], in_=ps[ACC][:, :]), reads=[PK(ACC)], writes=[rawk[hp]])
                        S.op('act', lambda e: e.activation(out=sq, in_=ps[ACC][:, :], func=AF.Square), reads=[PK(ACC)], writes=sqk)
                        S.op('pe', lambda e: e.matmul(ps[SSQ][:, :], lhsT=ones[:], rhs=sq, start=(hp == 0), stop=(hp == 3)),
                             reads=sqk + ['const'], writes=[PK(SSQ)])

                zmm(0)
                if n > 1:
                    zmm(1)
                esp(0)
                cmm(0)
                for i in range(n):
                    if i + 2 < n:
                        zmm(i + 2)
                    if i + 1 < n:
                        esp(i + 1)
                    rest(i)
                    if i + 1 < n:
                        cmm(i + 1)
                    avmm(i)
                rs, rsk = rsb; raw, rawk = rawb
                S.op('act', lambda e: e.activation(out=rs, in_=ps[SSQ][:, :], func=AF.Ln, scale=1.0 / 512, bias=EPS), reads=[PK(SSQ)], writes=rsk)
                S.op('act', lambda e: e.activation(out=rs, in_=rs, func=AF.Exp, scale=-0.5), reads=rsk, writes=rsk)
                for hp in range(4):
                    S.op('dve', lambda e, hp=hp: e.scalar_tensor_tensor(
                        out=cT[:, 4 + hp, G * 512:(G + 1) * 512], in0=raw[:, hp * 512:(hp + 1) * 512], scalar=pt[:, l, 36 + hp:37 + hp],
                        in1=rs, op0=ALU.mult, op1=ALU.mult), reads=[rawk[hp], 'const'] + rsk, writes=[('cT', 4 + hp, G)])
            for G in range(4):
                sweep_group(G)
            if stop == 'sb':
                return
            pcs = mixw
            gemm_resid_norm(lambda kc, tt: cT[:, kc, tt * 128:(tt + 1) * 128], lambda kc, tt: [('cT', kc, tt // 4)],
                            lambda ns, k: (pcs[ns][0][:, k, :], pcs[ns][1]), 8, (l, 8))

        def cross_block(l):
            memf = WA[:, 7, 0:4096].bitcast(F32).rearrange("p (t d) -> p t d", t=2)
            S.dma('sp', lambda e: e.dma_start(out=memf, in_=D["mem"].rearrange("(t p) d -> p t d", p=128)), writes=[('hs', 7)])
            junk, jk = tunit(0, 2)
            for t in range(2):
                S.op('act', lambda e, t=t: e.activation(out=junk, in_=memf[:, t, :], func=AF.Square, accum_out=sm[:, 64 + t:65 + t]),
                     reads=[('hs', 7)], writes=['sm4'] + jk)
            rstd_from_ssq(sm[:, 64:66], 1.0 / DM, 'sm4')
            mnT = WA[:, 4, 0:2048].rearrange("p (k n) -> p k n", k=8)
            for t in range(2):
                hn, hk = tunit(2 + 2 * t, 2)
                S.op('dve', lambda e, t=t, hn=hn: e.tensor_scalar(out=hn, in0=memf[:, t, :], scalar1=sm[:, 64 + t:65 + t], scalar2=None, op0=ALU.mult),
                     reads=[('hs', 7), 'sm4'], writes=hk)
                b = nextbank(ALLB)
                transposes(lambda c, hn=hn: hn[:, c * 128:(c + 1) * 128], 8, b, hk + ['const'])
                gb = _bv(pt[:, l, 16:24], [[1, 8], [0, 128]])
                S.op('dve', lambda e, t=t, b=b, gb=gb: e.tensor_tensor(out=mnT[:, :, t * 128:(t + 1) * 128],
                                                                       in0=psb[b][:, :].rearrange("p (k n) -> p k n", k=8), in1=gb, op=ALU.mult),
                     reads=[PK(b), 'const'], writes=[('hs', 4)])
            kTm = WA[:, 6, 0:2048].rearrange("p (k n) -> p k n", k=8)
            vm = WA[:, 6, 2048:4096].rearrange("p (t n) -> p t n", t=2)
            wkv = D["w_xkv"][l]
            def kv_epi(tt, ns, b):
                if ns < 2:
                    kn, knk = tunit(6 + tt % 2, 1)
                    junk1, jk1 = tunit(0, 1)
                    for hh in range(2):
                        col = 66 + (tt * 4 + ns * 2 + hh)
                        S.op('act', lambda e, hh=hh, col=col: e.activation(out=junk1[:, 0:256], in_=ps[b][:, hh * 256:(hh + 1) * 256], func=AF.Square,
                                                                           accum_out=sm[:, col:col + 1]), reads=[PK(b)], writes=[('sm5', tt, ns)] + jk1)
                    c0 = 66 + tt * 4 + ns * 2
                    rstd_from_ssq(sm[:, c0:c0 + 2], 1.0 / 256, ('sm5', tt, ns))
                    for hh in range(2):
                        S.op('dve', lambda e, hh=hh, kn=kn: e.tensor_scalar(out=kn[:, hh * 256:(hh + 1) * 256], in0=ps[b][:, hh * 256:(hh + 1) * 256],
                                                                            scalar1=sm[:, c0 + hh:c0 + hh + 1], scalar2=None, op0=ALU.mult),
                             reads=[PK(b), ('sm5', tt, ns)], writes=knk)
                    bt = nextbank(ALLB)
                    transposes(lambda c, kn=kn: kn[:, c * 128:(c + 1) * 128], 4, bt, knk + ['const'])
                    gb = _bv(pt[:, l, 42:44], [[0, 2], [1, 2], [0, 128]])
                    S.op('dve', lambda e, bt=bt, gb=gb: e.tensor_tensor(
                        out=kTm[:, ns * 4:(ns + 1) * 4, tt * 128:(tt + 1) * 128].rearrange("p (a c) n -> p a c n", a=2),
                        in0=psb[bt][:, 0:512].rearrange("p (a c n) -> p a c n", a=2, c=2), in1=gb, op=ALU.mult),
                         reads=[PK(bt), 'const'], writes=[('hs', 6)])
                else:
                    copy_op('act', vm[:, tt, (ns - 2) * 512:(ns - 1) * 512], ps[b][:, :], [PK(b)], [('hs', 6)])
            gemm_tm(lambda kc, tt: mnT[:, kc, tt * 128:(tt + 1) * 128], lambda kc, tt: [('hs', 4)],
                    lambda ns: [(wkv[:, ns * 512:(ns + 1) * 512], 8)], 4, kv_epi, ntt=2, defer=1)
            wq = D["w_xq"][l]
            qT = WA[:, 4:6, :].rearrange("p a (b n) -> p (a b) n", b=2)
            for hpair in range(2):
                wap, wkey = load_piece(wq[:, hpair * 512:(hpair + 1) * 512], 8, 512)
                if hpair == 1:
                    xow = [load_piece(D["w_xo"][l][:, ns * 512:(ns + 1) * 512], 8, 512) for ns in range(2)]
                qitems = [(hh, g) for hh in range(2) for g in range(4)]

                def q_stage1(i, wap=wap, wkey=wkey):
                    hh, g = qitems[i]
                    bc = [nextbank(ALLB), nextbank(ALLB)]
                    sq, sqk = tunit(4 * (i % 2), 2)
                    for c in range(2):
                        for kc in range(8):
                            S.op('pe', lambda e, c=c, kc=kc: e.matmul(
                                ps[bc[c]][:, :], lhsT=wap[:, kc, (hh * 2 + c) * 128:(hh * 2 + c + 1) * 128],
                                rhs=hT[:, kc, g * 512:(g + 1) * 512], start=(kc == 0), stop=(kc == 7)),
                                 reads=[wkey, ('hT', kc, g)], writes=[PK(bc[c])])
                        S.op('act', lambda e, c=c: e.activation(out=sq[:, c * 512:(c + 1) * 512], in_=ps[bc[c]][:, :], func=AF.Square),
                             reads=[PK(bc[c])], writes=[sqk[c]])
                    return bc, sq, sqk

                def q_stage2(i, bc, sq, sqk):
                    hh, g = qitems[i]
                    bd = nextbank(ALLB)
                    for c in range(2):
                        S.op('pe', lambda e, c=c: e.matmul(ps[bd][:, :], lhsT=ones[:], rhs=sq[:, c * 512:(c + 1) * 512],
                                                           start=(c == 0), stop=(c == 1)), reads=[sqk[c], 'const'], writes=[PK(bd)])
                    rs, rsk = tunit(4 * (i % 2) + 2, 2, F32)
                    S.op('act', lambda e: e.activation(out=rs, in_=ps[bd][:, :], func=AF.Ln, scale=1.0 / 256, bias=EPS), reads=[PK(bd)], writes=rsk)
                    S.op('act', lambda e: e.activation(out=rs, in_=rs, func=AF.Exp, scale=-0.5), reads=rsk, writes=rsk)
                    for c in range(2):
                        S.op('dve', lambda e, c=c: e.scalar_tensor_tensor(
                            out=qT[:, hh * 2 + c, g * 512:(g + 1) * 512], in0=ps[bc[c]][:, :], scalar=pt[:, l, 40 + c:41 + c],
                            in1=rs, op0=ALU.mult, op1=ALU.mult), reads=[PK(bc[c]), 'const'] + rsk, writes=[('hs', 4 + hh)])
                prevq = None
                for i in range(len(qitems)):
                    cur = q_stage1(i)
                    if prevq is not None:
                        q_stage2(*prevq)
                    prevq = (i,) + cur
                q_stage2(*prevq)
                for hh in range(2):
                    h = hpair * 2 + hh
                    def attn(g, h=h, hh=hh):
                        pT, pTk = tunit(2 + 2 * (g % 2), 2)
                        bsc = [nextbank(ALLB), nextbank(ALLB)]
                        for mc in range(2):
                            for c in range(2):
                                S.op('pe', lambda e, mc=mc, c=c: e.matmul(
                                    ps[bsc[mc]][:, :], lhsT=kTm[:, h * 2 + c, mc * 128:(mc + 1) * 128],
                                    rhs=qT[:, hh * 2 + c, g * 512:(g + 1) * 512], start=(c == 0), stop=(c == 1)),
                                     reads=[('hs', 6), ('hs', 4 + hh)], writes=[PK(bsc[mc])])
                            S.op('act', lambda e, mc=mc, pT=pT: e.activation(out=pT[:, mc * 512:(mc + 1) * 512], in_=ps[bsc[mc]][:, :], func=AF.Exp, scale=1.0 / 16),
                                 reads=[PK(bsc[mc])], writes=pTk)
                        return pT, pTk

                    def attn2(g, pT, pTk, h=h, hh=hh):
                        bd = nextbank(ALLB)
                        for mc in range(2):
                            S.op('pe', lambda e, mc=mc, pT=pT: e.matmul(ps[bd][:, :], lhsT=ones[:], rhs=pT[:, mc * 512:(mc + 1) * 512],
                                                                       start=(mc == 0), stop=(mc == 1)), reads=pTk + ['const'], writes=[PK(bd)])
                        rden, rdk = tunit(0, 2, F32)
                        S.op('dve', lambda e, rden=rden: e.reciprocal(out=rden, in_=ps[bd][:, :]), reads=[PK(bd)], writes=rdk)
                        for c in range(2):
                            bo = nextbank(ALLB)
                            for mc in range(2):
                                S.op('pe', lambda e, mc=mc, c=c, pT=pT, bo=bo: e.matmul(
                                    ps[bo][:, :], lhsT=vm[:, mc, h * 256 + c * 128:h * 256 + (c + 1) * 128],
                                    rhs=pT[:, mc * 512:(mc + 1) * 512], start=(mc == 0), stop=(mc == 1)),
                                     reads=pTk + [('hs', 6)], writes=[PK(bo)])
                            S.op('dve', lambda e, c=c, bo=bo, rden=rden: e.tensor_tensor(
                                out=cT[:, h * 2 + c, g * 512:(g + 1) * 512], in0=ps[bo][:, :], in1=rden, op=ALU.mult),
                                 reads=[PK(bo)] + rdk, writes=[('cT', h * 2 + c, g)])
                    prevg = None
                    for g in range(4):
                        cur = attn(g)
                        if prevg is not None:
                            attn2(*prevg)
                        prevg = (g,) + cur
                    attn2(*prevg)
            if stop == 'xattn':
                return
            pcs = xow
            gemm_resid_norm(lambda kc, tt: cT[:, kc, tt * 128:(tt + 1) * 128], lambda kc, tt: [('cT', kc, tt // 4)],
                            lambda ns, k: (pcs[ns][0][:, k, :], pcs[ns][1]), 8, (l, 24))

        def mlp_block(l):
            ring["slots"] = [0, 1, 2, 4, 5, 6, 7]
            wu = D["w_up"][l]; wd = D["w_down"][l]
            def up_load(fs):
                return load_piece(wu[:, fs * 512:(fs + 1) * 512], 8, 512)

            def down_load(fs):
                return load_piece(wd[fs * 512:(fs + 1) * 512, :], 4, 1024)

            def up(fs, wap, wkey):
                for m in range(4):
                    fc = (fs % 2) * 4 + m
                    for g in range(4):
                        b = nextbank(ALLB)
                        for kc in range(8):
                            S.op('pe', lambda e, m=m, g=g, kc=kc, b=b: e.matmul(
                                ps[b][:, :], lhsT=wap[:, kc, m * 128:(m + 1) * 128], rhs=hT[:, kc, g * 512:(g + 1) * 512],
                                start=(kc == 0), stop=(kc == 7)), reads=[wkey, ('hT', kc, g)], writes=[PK(b)])
                        r, rk = tunit((fc * 4 + g) % 8, 1)
                        S.op('act', lambda e, b=b, r=r: e.activation(out=r, in_=ps[b][:, :], func=AF.Relu), reads=[PK(b)], writes=rk)
                        S.op('dve', lambda e, r=r, fc=fc, g=g: e.tensor_tensor(out=cT[:, fc, g * 512:(g + 1) * 512], in0=r, in1=r, op=ALU.mult),
                             reads=rk, writes=[('cT', fc, g)])

            def down(fs, wap, wkey):
                hb = (fs % 2) * 4
                if fs == 7:
                    gemm_resid_norm(lambda k, tt: cT[:, hb + k, tt * 128:(tt + 1) * 128], lambda k, tt: [('cT', hb + k, tt // 4)],
                                    lambda ns, k: (wap[:, k, ns * 512:(ns + 1) * 512], wkey), 4,
                                    (l + 1, 0) if l + 1 < n_layers else None)
                    return
                pending = None
                for ns in range(2):
                    for tt in range(NT):
                        b = nextbank(ALLB)
                        for k in range(4):
                            S.op('pe', lambda e, k=k, b=b, tt=tt, ns=ns: e.matmul(
                                ps[b][:, :], lhsT=cT[:, hb + k, tt * 128:(tt + 1) * 128], rhs=wap[:, k, ns * 512:(ns + 1) * 512],
                                start=(k == 0), stop=(k == 3)), reads=[wkey, ('cT', hb + k, tt // 4)], writes=[PK(b)])
                        if pending is not None:
                            resid_add(*pending)
                        pending = (tt, ns, b)
                resid_add(*pending)
            wu_next = up_load(0)
            wd_next = down_load(0)
            up(0, *wu_next)
            for fs in range(8):
                wd_cur = wd_next
                if fs + 1 < 8:
                    wu_next = up_load(fs + 1)
                    wd_next = down_load(fs + 1)
                    up(fs + 1, *wu_next)
                down(fs, *wd_cur)
            ring["slots"] = [0, 1, 2]

        done = False
        for l in range(n_layers):
            mix_block(l)
            if stop in ('ret', 'sb') or stop == f'mix{l}':
                break
            cross_block(l)
            if stop == 'xattn' or stop == f'cross{l}':
                break
            mlp_block(l)
            if stop == f'mlp{l}':
                break

        if stop in ('ret', 'sb', 'xattn'):
            for kc in range(4 if stop == 'ret' else 8):
                for g in range(4):
                    S.op('dve', lambda e, kc=kc, g=g: e.tensor_copy(out=xs[:, kc * 2 + g // 2, (g % 2) * 512:(g % 2 + 1) * 512],
                                                                    in_=cT[:, kc, g * 512:(g + 1) * 512]),
                         reads=[('cT', kc, g)], writes=[('xs', kc * 2 + g // 2)])
        for tt in range(NT):
            S.dma('sp', lambda e, tt=tt: e.dma_start(out=out_d[tt * 128:(tt + 1) * 128, :], in_=xs[:, tt, :]),
                  reads=[('xs', tt)], writes=[('out', tt)], final=True)
        S.emit()
    return nc


_CONSTS = None


def make_in_maps(inputs, n_cores=8):
    global _CONSTS
    if _CONSTS is None:
        _CONSTS = _const_tables()
    f = lambda a: np.ascontiguousarray(np.asarray(a, dtype=np.float32))
    pt = _param_table(*[np.asarray(inputs[k], dtype=np.float32) for k in
                        ("g_mix", "g_cross", "g_mem", "g_mlp", "g_ret_out", "g_sb_out", "g_qn", "g_kn")])
    shared = {k: f(inputs[k]) for k in ("w_in", "w_mix_out", "w_xq", "w_xkv", "w_xo", "w_up", "w_down")}
    shared["pt"] = pt
    shared.update(_CONSTS)
    x = f(inputs["x"]); mem = f(inputs["mem"])
    maps = []
    for c in range(n_cores):
        m = dict(shared)
        m["x"] = x[c]
        m["mem"] = mem[c]
        maps.append(m)
    return maps


def kernel(**inputs):
    nc = build_program()
    maps = make_in_maps(inputs, 8)
    res = run_bass_kernel_spmd(nc, maps, core_ids=list(range(8)))
    return np.stack([np.asarray(r["out"], dtype=np.float32) for r in res.results], axis=0)
```

```python
import contextlib
import numpy as np
import concourse.bass as bass
import concourse.mybir as mybir
from concourse.bass_utils import run_bass_kernel_spmd

dt = mybir.dt
F32, BF16 = dt.float32, dt.bfloat16
AF = mybir.ActivationFunctionType
ALU = mybir.AluOpType
AX = mybir.AxisListType

ENGS = ('pe', 'act', 'dve', 'pool', 'sp')


class Sched:
    EPOCH = 20000
    NDMA = 8

    def __init__(self, nc):
        self.nc = nc
        self.stack = contextlib.ExitStack()
        self.ops = {e: [] for e in ENGS}
        self.ncomp = {e: 0 for e in ENGS}
        self.esem = {e: {} for e in ENGS}
        self.sems = []
        self.lastw = {}
        self.readers = {}
        self.seen = {e: {} for e in ENGS}
        self.dsem = []
        self.dnext = 0
        self.finals = []

    def __enter__(self):
        self.stack.__enter__()
        return self

    def __exit__(self, *a):
        return self.stack.__exit__(*a)

    def sbuf(self, name, shape, dtype):
        return self.stack.enter_context(self.nc.sbuf_tensor(name, list(shape), dtype))

    def psum(self, name, shape, dtype):
        return self.stack.enter_context(self.nc.psum_tensor(name, list(shape), dtype))

    def _newsem(self, name):
        s = self.stack.enter_context(self.nc.semaphore(name))
        self.sems.append(s)
        return len(self.sems) - 1

    def _deps(self, eng, reads, writes, is_dma):
        deps = set()
        for k in reads:
            t = self.lastw.get(k)
            if t is not None:
                deps.add(t)
        for k in writes:
            t = self.lastw.get(k)
            if t is not None:
                deps.add(t)
            for t in self.readers.get(k, ()):
                if t[2] == eng and eng == 'pe' and not is_dma:
                    continue
                deps.add(t)
        return deps

    def _waits(self, eng, deps, is_dma):
        waits = []
        seen = self.seen[eng]
        for (sid, v, deng) in sorted(deps):
            if deng == eng and eng == 'pe' and not is_dma:
                continue
            if deng.startswith('sw'):
                if seen.get((sid, 'g'), -1) >= int(deng[2:]):
                    continue
                seen[(sid, 'g')] = int(deng[2:])
                waits.append((sid, v))
                continue
            if seen.get(sid, 0) >= v:
                continue
            seen[sid] = v
            waits.append((sid, v))
        return waits

    def _commit(self, tok, reads, writes):
        for k in reads:
            self.readers.setdefault(k, []).append(tok)
        for k in writes:
            self.lastw[k] = tok
            self.readers[k] = []

    def op(self, eng, fn, reads=(), writes=()):
        deps = self._deps(eng, reads, writes, False)
        waits = self._waits(eng, deps, False)
        idx = self.ncomp[eng]
        ep = idx // self.EPOCH
        if ep not in self.esem[eng]:
            self.esem[eng][ep] = self._newsem(f"s_{eng}_{ep}")
        sid = self.esem[eng][ep]
        tok = (sid, idx % self.EPOCH + 1, eng)
        self.ncomp[eng] += 1
        self._commit(tok, reads, writes)
        self.ops[eng].append((fn, waits, sid, 1))
        return tok

    def dma(self, eng, fn, reads=(), writes=(), final=False, slot=None):
        if eng == 'pool':
            return self._dma_sw(fn, reads, writes, slot)
        deps = self._deps(eng, reads, writes, True)
        if len(self.dsem) < self.NDMA:
            self.dsem.append([self._newsem(f"s_dma_{len(self.dsem)}"), 0, None])
        ent = self.dsem[self.dnext % self.NDMA]
        self.dnext += 1
        if ent[2] is not None:
            deps.add(ent[2])
        waits = self._waits(eng, deps, True)
        ent[1] += 16
        tok = (ent[0], ent[1], 'dma')
        ent[2] = tok
        self._commit(tok, reads, writes)
        self.ops[eng].append((fn, waits, ent[0], 16))
        if final:
            self.finals.append(tok)
        return tok

    def _dma_sw(self, fn, reads, writes, slot=None):
        deps = self._deps('pool', reads, writes, True)
        waits = self._waits('pool', deps, True)
        if not hasattr(self, 'nsw'):
            self.nsw = 0
            self.swslot = {}
        if slot is None:
            sid = self._newsem(f"s_sw_{self.nsw}")
            self.nsw += 1
            tok = (sid, 16, 'dma')
            self.ops['pool'].append((fn, waits, sid, 16))
        else:
            if slot not in self.swslot:
                self.swslot[slot] = [self._newsem(f"s_sl_{len(self.swslot)}a"), self._newsem(f"s_sl_{len(self.swslot)}b"), 0]
            ent = self.swslot[slot]
            g = ent[2]
            ent[2] += 1
            sid, other = ent[g % 2], ent[(g + 1) % 2]
            tok = (sid, 16, f'sw{g}')
            self.ops['pool'].append(('swdma', fn, waits, sid, other if g >= 1 else None))
        self._commit(tok, reads, writes)
        return tok

    def emit(self):
        nc = self.nc
        fin = [(sid, v) for (sid, v, _) in self.finals]
        sems = self.sems
        ops = self.ops

        def replay(name, e, tail=()):
            for ent in ops[name]:
                if ent[0] == 'swdma':
                    for (ws, wv) in ent[2]:
                        e.wait_ge(sems[ws], wv)
                    if ent[4] is not None:
                        e.wait_ge(sems[ent[4]], 16)
                        e.sem_clear(sems[ent[4]])
                    ent[1](e).then_inc(sems[ent[3]], 16)
                    continue
                (fn, waits, sid, n) = ent
                for (ws, wv) in waits:
                    e.wait_ge(sems[ws], wv)
                fn(e).then_inc(sems[sid], n)
            for (ws, wv) in tail:
                e.wait_ge(sems[ws], wv)

        with nc.Block() as block:
            @block.sync
            def _(e):
                replay('sp', e, fin)

            @block.tensor
            def _(e):
                replay('pe', e)

            @block.vector
            def _(e):
                replay('dve', e)

            @block.scalar
            def _(e):
                replay('act', e)

            @block.gpsimd
            def _(e):
                replay('pool', e)


SEQ, DM, DEPTH = 2048, 1024, 2
NT = 16
EPS = 1e-6
NPC = 44
GAM = [1.0 - 2.0 ** (-5 - h) for h in range(4)]


def _const_tables():
    p = np.arange(128)
    c = {}
    c["ident"] = np.eye(128, dtype=np.float32)
    c["tri"] = (p[:, None] >= p[None, :]).astype(np.float32)
    c["ones"] = np.ones((128, 128), np.float32)
    c["mstr"] = (p[:, None] < p[None, :]).astype(np.float32)
    inv = (1.0 / (np.float32(10000.0) ** np.linspace(0.0, 1.0, 64, dtype=np.float32))).astype(np.float32)
    pos = np.arange(SEQ, dtype=np.float32)
    ang = (pos[:, None] * inv[None, :]).astype(np.float32)
    c["cos"] = np.cos(ang.astype(np.float64)).astype(np.float32).reshape(NT, 128, 64).transpose(1, 0, 2).copy()
    c["sin"] = np.sin(ang.astype(np.float64)).astype(np.float32).reshape(NT, 128, 64).transpose(1, 0, 2).copy()
    lg = np.log1p(-np.exp2(-5.0 - np.arange(4, dtype=np.float64)))
    dist = np.abs(p[:, None] - p[None, :]).astype(np.float64)
    same = (p[:, None] // 64) == (p[None, :] // 64)
    dm = np.zeros((128, 4, 128), np.float64)
    qd = np.zeros((128, 4, 128), np.float64)
    kd = np.zeros((128, 4, 128), np.float64)
    for h in range(4):
        dm[:, h, :] = np.where(same, np.exp(lg[h] * dist), 0.0) * (128.0 ** -0.5)
        qd[:, h, :] = np.exp(lg[h] * ((p % 64) + 1.0))[None, :]
        kd[:, h, :] = (np.exp(lg[h] * (63.0 - (p % 64))) * (128.0 ** -0.5))[:, None]
    out = {}
    out["cA"] = np.concatenate([c["ident"], c["tri"], c["ones"], c["mstr"]], axis=1).astype(np.float32)
    out["cB"] = np.stack([c["cos"], c["sin"]], axis=1).astype(np.float32)
    out["cC"] = np.concatenate([dm.reshape(128, 512), qd.reshape(128, 512), kd.reshape(128, 512)], axis=1).astype(np.float32)
    return out


def _param_table(g_mix, g_cross, g_mem, g_mlp, g_ret_out, g_sb_out, g_qn, g_kn):
    pt = np.zeros((128, DEPTH, NPC), np.float32)
    for l in range(DEPTH):
        pt[:, l, 0:8] = g_mix[l].reshape(8, 128).T
        pt[:, l, 8:16] = g_cross[l].reshape(8, 128).T
        pt[:, l, 16:24] = g_mem[l].reshape(8, 128).T
        pt[:, l, 24:32] = g_mlp[l].reshape(8, 128).T
        pt[:, l, 32:36] = g_ret_out[l].reshape(4, 128).T
        pt[:, l, 36:40] = g_sb_out[l].reshape(4, 128).T
        pt[:, l, 40:42] = g_qn[l].reshape(2, 128).T
        pt[:, l, 42:44] = g_kn[l].reshape(2, 128).T
    return pt


def _bv(ap, dims):
    return bass.AP(ap.tensor, ap.offset, [list(ap.ap[0])] + [list(d) for d in dims])


def build_program(n_layers=DEPTH, stop=None):
    nc = bass.Bass("TRN2", target_bir_lowering=False)
    D = {}
    def din(name, shape):
        D[name] = nc.dram_tensor(name, list(shape), F32, kind="ExternalInput").ap()
    din("x", [SEQ, DM]); din("mem", [256, DM])
    din("w_in", [DEPTH, DM, 3584]); din("w_mix_out", [DEPTH, DM, DM]); din("w_xq", [DEPTH, DM, DM])
    din("w_xkv", [DEPTH, DM, 2 * DM]); din("w_xo", [DEPTH, DM, DM]); din("w_up", [DEPTH, DM, 4 * DM])
    din("w_down", [DEPTH, 4 * DM, DM]); din("pt", [128, DEPTH, NPC])
    din("cA", [128, 512]); din("cB", [128, 2, NT, 64]); din("cC", [128, 1536])
    out_d = nc.dram_tensor("out", [SEQ, DM], F32, kind="ExternalOutput").ap()

    S = Sched(nc)
    with S:
        xs = S.sbuf("xs", [128, NT, DM], F32)
        hT = S.sbuf("hT", [128, 8, SEQ], BF16)
        cT = S.sbuf("cT", [128, 8, SEQ], BF16)
        WA = S.sbuf("WA", [128, 8, 4096], BF16)
        cA = S.sbuf("cA_s", [128, 512], BF16)
        ident, tri, ones, mstr = cA[:, 0:128], cA[:, 128:256], cA[:, 256:384], cA[:, 384:512]
        zer = S.sbuf("zer_s", [128, 128], BF16)
        mhalf = S.sbuf("mhalf_s", [128, 16], F32)
        cB = S.sbuf("cB_s", [128, 2, NT, 64], BF16)
        cosT, sinT = cB[:, 0], cB[:, 1]
        cC = S.sbuf("cC_s", [128, 1536], BF16)
        dmk, qdc, kdc = cC[:, 0:512], cC[:, 512:1024], cC[:, 1024:1536]
        pt = S.sbuf("pt_s", [128, DEPTH, NPC], F32)
        st = S.sbuf("st_s", [128, 4, 128], F32)
        stb = S.sbuf("stb_s", [128, 2, 4, 128], BF16)
        sm = S.sbuf("sm_s", [128, 128], F32)
        PS = S.psum("ps", [128, 8, 512], F32)
        ps = [PS[:, i, :] for i in range(8)]
        psb = [p.bitcast(BF16) for p in ps]

        rr = {"dq": 0, "bank": 0, "ring": 0, "ev": 0}

        def dq():
            rr["dq"] += 1
            return 'sp' if rr["dq"] % 2 else 'act'

        def PK(b):
            return ('ps', b)

        def unit(buf, name, u, n=1, dtype=BF16):
            kc, g = u // 4, u % 4
            assert g + n <= 4
            ap = buf[:, kc, g * 512:(g + n) * 512]
            if dtype == F32:
                ap = ap.bitcast(F32)
            return ap, [(name, kc, g + i) for i in range(n)]

        def tunit(i, n=1, dtype=BF16):
            ap = WA[:, 3, i * 512:(i + n) * 512]
            if dtype == F32:
                ap = ap.bitcast(F32)
            return ap, [('t', i + k) for k in range(n)]

        def nextbank(allowed):
            rr["bank"] += 1
            return allowed[rr["bank"] % len(allowed)]

        ring = {"slots": [0, 1, 2]}

        def load_piece(w2d, nk, ncols):
            assert nk * ncols <= 4096
            rr["ring"] += 1
            s = ring["slots"][rr["ring"] % len(ring["slots"])]
            dst = WA[:, s, 0:nk * ncols].rearrange("p (k n) -> p k n", k=nk)
            src = w2d.rearrange("(k p) n -> p k n", p=128)
            S.dma('pool', lambda e: e.dma_start(out=dst, in_=src), reads=[], writes=[('hs', s)])
            return dst, ('hs', s)

        def evac_engine():
            rr["ev"] += 1
            return 'act' if rr["ev"] % 2 else 'dve'

        def copy_op(eng, out, in_, reads, writes):
            if eng == 'act':
                S.op('act', lambda e: e.copy(out=out, in_=in_), reads=reads, writes=writes)
            else:
                S.op(eng, lambda e: e.tensor_copy(out=out, in_=in_), reads=reads, writes=writes)

        def rstd_from_ssq(ssq_ap, inv_n, key):
            S.op('act', lambda e: e.activation(out=ssq_ap, in_=ssq_ap, func=AF.Ln, scale=inv_n, bias=EPS), reads=[key], writes=[key])
            S.op('act', lambda e: e.activation(out=ssq_ap, in_=ssq_ap, func=AF.Exp, scale=-0.5), reads=[key], writes=[key])

        def transposes(src_fn, n, bank, reads):
            for c in range(n):
                S.op('pe', lambda e, c=c: e.transpose(out=psb[bank][:, c * 128:(c + 1) * 128], in_=src_fn(c),
                                                      identity=ident[:]),
                     reads=reads, writes=[PK(bank)])

        for tt in range(NT):
            S.dma('sp', lambda e, tt=tt: e.dma_start(out=xs[:, tt, :], in_=D["x"][tt * 128:(tt + 1) * 128, :]),
                  writes=[('xs', tt)])
        for (sb_t, nm) in ((cA, "cA"), (cB, "cB"), (cC, "cC")):
            S.dma('pool', lambda e, sb_t=sb_t, nm=nm: e.dma_start(out=sb_t[:], in_=D[nm]), writes=['const'])
        S.dma('sp', lambda e: e.dma_start(out=pt[:], in_=D["pt"]), writes=['const'])
        S.op('pool', lambda e: e.memset(zer[:], 0.0), writes=['const'])
        S.op('pool', lambda e: e.memset(mhalf[:], -0.5), writes=['const'])

        ALLB = list(range(8))

        def norm_to_hT(l, gcol):
            junk, jk = tunit(0, 2)
            for tt in range(NT):
                S.op('act', lambda e, tt=tt: e.activation(out=junk, in_=xs[:, tt, :], func=AF.Square,
                                                          accum_out=sm[:, tt:tt + 1]),
                     reads=[('xs', tt)], writes=['sm'] + jk)
            rstd_from_ssq(sm[:, 0:NT], 1.0 / DM, 'sm')
            gb = _bv(pt[:, l, gcol:gcol + 8], [[1, 8], [0, 128]])

            def front(tt):
                hn, hk = tunit(2 + 2 * (tt % 3), 2)
                S.op('dve', lambda e: e.tensor_scalar(out=hn, in0=xs[:, tt, :], scalar1=sm[:, tt:tt + 1],
                                                      scalar2=None, op0=ALU.mult),
                     reads=[('xs', tt), 'sm'], writes=hk)
                b = nextbank(ALLB)
                transposes(lambda c: hn[:, c * 128:(c + 1) * 128], 8, b, hk + ['const'])
                return b

            def back(tt, b):
                S.op('dve', lambda e: e.tensor_tensor(
                    out=hT[:, :, tt * 128:(tt + 1) * 128],
                    in0=psb[b][:, :].rearrange("p (k n) -> p k n", k=8), in1=gb, op=ALU.mult),
                     reads=[PK(b), 'const'], writes=[('hT', kc, tt // 4) for kc in range(8)])
            prev = None
            for tt in range(NT):
                b = front(tt)
                if prev is not None:
                    back(*prev)
                prev = (tt, b)
            back(*prev)

        def gemm_tm(lhs_fn, lhs_keys_fn, pieces_fn, n_slices, epilogue, banks=ALLB, ntt=NT, defer=1):
            pending = []
            for ns in range(n_slices):
                pcs = [load_piece(w2d, nk, 512) for (w2d, nk) in pieces_fn(ns)]
                for tt in range(ntt):
                    b = nextbank(banks)
                    tot = sum(p[0].shape[1] for p in pcs)
                    i = 0
                    for pi, (wap, wkey) in enumerate(pcs):
                        for k in range(wap.shape[1]):
                            kc = sum(p[0].shape[1] for p in pcs[:pi]) + k
                            S.op('pe', lambda e, wap=wap, k=k, kc=kc, i=i, b=b, tt=tt: e.matmul(
                                ps[b][:, :], lhsT=lhs_fn(kc, tt), rhs=wap[:, k, :], start=(i == 0), stop=(i == tot - 1)),
                                 reads=[wkey] + lhs_keys_fn(kc, tt), writes=[PK(b)])
                            i += 1
                    pending.append((tt, ns, b))
                    if len(pending) > defer:
                        epilogue(*pending.pop(0))
                while pending:
                    epilogue(*pending.pop(0))

        def resid_add(tt, ns, b):
            S.op('dve', lambda e: e.tensor_tensor(out=xs[:, tt, ns * 512:(ns + 1) * 512], in0=ps[b][:, :],
                                                  in1=xs[:, tt, ns * 512:(ns + 1) * 512], op=ALU.add),
                 reads=[PK(b), ('xs', tt)], writes=[('xs', tt)])

        def gemm_resid_norm(lhs_fn, lhs_keys_fn, rhs_fn, nk, norm):
            pend_add = []
            junk, jk = tunit(0, 2)
            fr = {}

            def n_stats(t):
                S.op('act', lambda e: e.activation(out=junk, in_=xs[:, t, :], func=AF.Square, accum_out=sm[:, t:t + 1]),
                     reads=[('xs', t)], writes=[('smn', t)] + jk)
                rstd_from_ssq(sm[:, t:t + 1], 1.0 / DM, ('smn', t))

            def n_front(t):
                hn, hk = tunit(2 + 2 * (t % 3), 2)
                S.op('dve', lambda e: e.tensor_scalar(out=hn, in0=xs[:, t, :], scalar1=sm[:, t:t + 1], scalar2=None, op0=ALU.mult),
                     reads=[('xs', t), ('smn', t)], writes=hk)
                b = nextbank(ALLB)
                transposes(lambda c: hn[:, c * 128:(c + 1) * 128], 8, b, hk + ['const'])
                fr[t] = b

            def n_back(t):
                b = fr[t]
                gb = _bv(pt[:, norm[0], norm[1]:norm[1] + 8], [[1, 8], [0, 128]])
                S.op('dve', lambda e: e.tensor_tensor(out=hT[:, :, t * 128:(t + 1) * 128],
                                                      in0=psb[b][:, :].rearrange("p (k n) -> p k n", k=8), in1=gb, op=ALU.mult),
                     reads=[PK(b), 'const'], writes=[('hT', kc, t // 4) for kc in range(8)])

            for tt in range(NT + 3):
                if tt < NT:
                    for ns in range(2):
                        b = nextbank(ALLB)
                        for k in range(nk):
                            rhs, wkey = rhs_fn(ns, k)
                            S.op('pe', lambda e, k=k, b=b, tt=tt, rhs=rhs: e.matmul(ps[b][:, :], lhsT=lhs_fn(k, tt), rhs=rhs,
                                                                                  start=(k == 0), stop=(k == nk - 1)),
                                 reads=[wkey] + lhs_keys_fn(k, tt), writes=[PK(b)])
                        pend_add.append((tt, ns, b))
                        if len(pend_add) > 1:
                            resid_add(*pend_add.pop(0))
                else:
                    while pend_add:
                        resid_add(*pend_add.pop(0))
                if norm is not None:
                    if 0 <= tt - 1 < NT:
                        n_stats(tt - 1)
                    if 0 <= tt - 2 < NT:
                        n_front(tt - 2)
                    if 0 <= tt - 3 < NT:
                        n_back(tt - 3)

        def mix_block(l):
            w_in = D["w_in"][l]
            if l == 0:
                norm_to_hT(l, 0)
            for j in (range(4) if l == 0 else range(2)):
                dst = WA[:, 4 + j, :].rearrange("p (k n) -> p k n", k=8)
                src = w_in[:, j * 512:(j + 1) * 512].rearrange("(k p) n -> p k n", p=128)
                S.dma('pool', lambda e, dst=dst, src=src: e.dma_start(out=dst, in_=src), writes=[('hs', 4 + j)])
            S.op('pool', lambda e: e.memset(st[:], 0.0), writes=['st'])
            S.op('pool', lambda e: e.memset(stb[:], 0.0), writes=['stb0', 'stb1'])
            def cu(i, n=1, dtype=BF16):
                return unit(cT, 'cT', 16 + i, n, dtype)
            RB = [0, 1, 2, 3, 4, 5]
            def ret_tile(tt):
                pb = [0, 1, 2, 3]
                for j in range(4):
                    wv = WA[:, 4 + j, :].rearrange("p (k n) -> p k n", k=8)
                    for kc in range(8):
                        S.op('pe', lambda e, j=j, kc=kc, wv=wv, tt=tt: e.matmul(
                            ps[pb[j]][:, :], lhsT=hT[:, kc, tt * 128:(tt + 1) * 128], rhs=wv[:, kc, :],
                            start=(kc == 0), stop=(kc == 7)),
                             reads=[('hs', 4 + j), ('hT', kc, tt // 4)], writes=[PK(pb[j])])
                    yield
                pp = tt % 2
                A, Ak = cu(0, 2, F32)
                Bm, Bk = cu(2, 2, F32)
                qr, qrk = cu(4); kr, krk = cu(5); STm, STk = cu(6); on, onk = cu(7)
                kdt, kdk = cu(8 + pp); vb, vbk = cu(10 + pp); qTt, qTk = cu(14 + pp)
                sg, sgk = [cu(12), cu(13), tunit(5)][tt % 3]
                qdT, qdk = tunit(1 + pp); kTt, kTk = tunit(3 + pp)
                cosb = _bv(cosT[:, tt, :], [[0, 4], [0, 2], [1, 64]])
                sinb = _bv(sinT[:, tt, :], [[0, 4], [1, 64]])
                def rot(bank, dstt, dk):
                    P4 = ps[bank][:, :].rearrange("p (h x d) -> p h x d", h=4, x=2)
                    A4 = A.rearrange("p (h x d) -> p h x d", h=4, x=2)
                    B4 = Bm.rearrange("p (x h d) -> p x h d", x=2, h=4)
                    D4 = dstt.rearrange("p (h x d) -> p h x d", h=4, x=2)
                    S.op('dve', lambda e, P4=P4, A4=A4: e.tensor_tensor(out=A4, in0=P4, in1=cosb, op=ALU.mult),
                         reads=[PK(bank), 'const'], writes=Ak)
                    S.op('dve', lambda e, P4=P4, B4=B4: e.tensor_tensor(out=B4[:, 0], in0=P4[:, :, 1, :], in1=sinb, op=ALU.mult),
                         reads=[PK(bank), 'const'], writes=[Bk[0]])
                    S.op('dve', lambda e, P4=P4, B4=B4: e.tensor_tensor(out=B4[:, 1], in0=P4[:, :, 0, :], in1=sinb, op=ALU.mult),
                         reads=[PK(bank), 'const'], writes=[Bk[1]])
                    S.op('pool', lambda e, A4=A4, B4=B4, D4=D4: e.tensor_tensor(out=D4[:, :, 0, :], in0=A4[:, :, 0, :], in1=B4[:, 0], op=ALU.subtract),
                         reads=Ak + Bk, writes=dk)
                    S.op('pool', lambda e, A4=A4, B4=B4, D4=D4: e.tensor_tensor(out=D4[:, :, 1, :], in0=A4[:, :, 1, :], in1=B4[:, 1], op=ALU.add),
                         reads=Ak + Bk, writes=dk)
                rot(pb[0], qr, qrk)
                yield
                S.op('act', lambda e: e.copy(out=vb, in_=ps[pb[2]][:, :]), reads=[PK(pb[2])], writes=vbk)
                yield
                rot(pb[1], kr, krk)
                yield
                S.op('act', lambda e: e.activation(out=Bm, in_=ps[pb[3]][:, :], func=AF.Exp, scale=-1.0), reads=[PK(pb[3])], writes=Bk)
                S.op('act', lambda e: e.activation(out=Bm, in_=Bm, func=AF.Ln, bias=1.0, scale=1.0), reads=Bk, writes=Bk)
                S.op('act', lambda e: e.activation(out=Bm, in_=Bm, func=AF.Exp, scale=-1.0), reads=Bk, writes=Bk)
                S.op('dve', lambda e: e.tensor_tensor(out=sg, in0=ps[pb[3]][:, :], in1=Bm, op=ALU.mult), reads=[PK(pb[3])] + Bk, writes=sgk)
                S.op('pool', lambda e: e.tensor_tensor(out=kdt, in0=kr, in1=kdc[:], op=ALU.mult), reads=krk + ['const'], writes=kdk)
                yield
                bq = 0
                transposes(lambda c: qr[:, c * 128:(c + 1) * 128], 4, bq, qrk + ['const'])
                S.op('act', lambda e: e.copy(out=qTt, in_=psb[bq][:, 0:512]), reads=[PK(bq)], writes=qTk)
                S.op('dve', lambda e: e.tensor_tensor(out=qdT, in0=psb[bq][:, 0:512], in1=qdc[:], op=ALU.mult),
                     reads=[PK(bq), 'const'], writes=qdk)
                yield
                bk = 1
                transposes(lambda c: kr[:, c * 128:(c + 1) * 128], 4, bk, krk + ['const'])
                S.op('act', lambda e: e.copy(out=kTt, in_=psb[bk][:, 0:512]), reads=[PK(bk)], writes=kTk)
                yield 'split'
                bs = 4
                for h in range(4):
                    S.op('pe', lambda e, h=h: e.matmul(ps[bs][:, h * 128:(h + 1) * 128], lhsT=kTt[:, h * 128:(h + 1) * 128],
                                                       rhs=qTt[:, h * 128:(h + 1) * 128], start=True, stop=True),
                         reads=kTk + qTk, writes=[PK(bs)])
                S.op('dve', lambda e: e.tensor_tensor(out=STm, in0=ps[bs][:, :], in1=dmk[:], op=ALU.mult),
                     reads=[PK(bs), 'const'], writes=STk)
                yield
                bu = [5, 5]
                def umm(c):
                    for h in range(4):
                        S.op('pe', lambda e, h=h: e.matmul(
                            ps[bu[c]][:, h * 128:(h + 1) * 128], lhsT=kdt[c * 64:(c + 1) * 64, h * 128:(h + 1) * 128],
                            rhs=vb[c * 64:(c + 1) * 64, h * 128:(h + 1) * 128], start=True, stop=True),
                             reads=kdk + vbk, writes=[PK(bu[c])])
                umm(0)
                yield
                stf = st[:].rearrange("p h e -> p (h e)")
                for h in range(4):
                    S.op('dve', lambda e, h=h: e.scalar_tensor_tensor(
                        out=st[:, h, :], in0=st[:, h, :], scalar=float(GAM[h] ** 64), in1=ps[bu[0]][:, h * 128:(h + 1) * 128],
                        op0=ALU.mult, op1=ALU.add), reads=['st', PK(bu[0])], writes=['st'])
                S.op('act', lambda e: e.copy(out=stb[:, 1].rearrange("p h e -> p (h e)"), in_=stf), reads=['st'], writes=['stb1'])
                umm(1)
                yield
                bo = 6 + tt % 2
                for h in range(4):
                    S.op('pe', lambda e, h=h: e.matmul(ps[bo][:, h * 128:(h + 1) * 128], lhsT=STm[:, h * 128:(h + 1) * 128],
                                                       rhs=vb[:, h * 128:(h + 1) * 128], start=True, stop=False),
                         reads=STk + vbk, writes=[PK(bo)])
                    for c in range(2):
                        S.op('pe', lambda e, h=h, c=c: e.matmul(
                            ps[bo][c * 64:(c + 1) * 64, h * 128:(h + 1) * 128],
                            lhsT=qdT[:, h * 128 + c * 64:h * 128 + (c + 1) * 64], rhs=stb[:, c, h, :],
                            start=False, stop=True),
                             reads=qdk + [f'stb{c}'], writes=[PK(bo)])
                yield
                for h in range(4):
                    S.op('dve', lambda e, h=h: e.scalar_tensor_tensor(
                        out=st[:, h, :], in0=st[:, h, :], scalar=float(GAM[h] ** 64), in1=ps[bu[1]][:, h * 128:(h + 1) * 128],
                        op0=ALU.mult, op1=ALU.add), reads=['st', PK(bu[1])], writes=['st'])
                S.op('act', lambda e: e.copy(out=stb[:, 0].rearrange("p h e -> p (h e)"), in_=stf), reads=['st'], writes=['stb0'])
                yield 'split'
                junk, jk = tunit(0, 1)
                for h in range(4):
                    S.op('act', lambda e, h=h: e.activation(out=junk[:, 0:128], in_=ps[bo][:, h * 128:(h + 1) * 128], func=AF.Square,
                                                            accum_out=sm[:, 32 + h:33 + h]), reads=[PK(bo)], writes=['sm2'] + jk)
                yield
                rstd_from_ssq(sm[:, 32:36], 1.0 / 128, 'sm2')
                yield
                for h in range(4):
                    S.op('dve', lambda e, h=h: e.scalar_tensor_tensor(
                        out=on[:, h * 128:(h + 1) * 128], in0=ps[bo][:, h * 128:(h + 1) * 128], scalar=sm[:, 32 + h:33 + h],
                        in1=sg[:, h * 128:(h + 1) * 128], op0=ALU.mult, op1=ALU.mult),
                         reads=[PK(bo), 'sm2'] + sgk, writes=onk)
                yield
                bt = 4
                transposes(lambda c: on[:, c * 128:(c + 1) * 128], 4, bt, onk + ['const'])
                yield
                gb = _bv(pt[:, l, 32:36], [[1, 4], [0, 128]])
                S.op('dve', lambda e, tt=tt, bt=bt, gb=gb: e.tensor_tensor(
                    out=cT[:, 0:4, tt * 128:(tt + 1) * 128], in0=psb[bt][:, 0:512].rearrange("p (k n) -> p k n", k=4),
                    in1=gb, op=ALU.mult), reads=[PK(bt), 'const'], writes=[('cT', kc, tt // 4) for kc in range(4)])
            sbw = {nm: load_piece(w_in[:, c0:c0 + 512], 8, 512) for (nm, c0) in (("q", 2048), ("k", 2560), ("v", 3072))}
            gens = [ret_tile(tt) for tt in range(NT)]
            for period in range(NT + 2):
                active = [gens[t] for t in (period, period - 1, period - 2) if 0 <= t < NT]
                while active:
                    for g in list(active):
                        try:
                            if next(g) == 'split':
                                active.remove(g)
                        except StopIteration:
                            active.remove(g)
            if stop == 'ret':
                return
            skT = WA[:, 4:6, :].rearrange("p a (b n) -> p (a b) n", b=2)
            svv = WA[:, 6:8, :].rearrange("p a (t n) -> p (a t) n", t=8)
            for (which, c0) in (("q", 2048), ("k", 2560)):
                wap, wkey = sbw[which]
                for m in range(4):
                    for g in range(4):
                        b = nextbank(ALLB)
                        for kc in range(8):
                            S.op('pe', lambda e, m=m, g=g, kc=kc, b=b, wap=wap: e.matmul(
                                ps[b][:, :], lhsT=wap[:, kc, m * 128:(m + 1) * 128], rhs=hT[:, kc, g * 512:(g + 1) * 512],
                                start=(kc == 0), stop=(kc == 7)),
                                 reads=[wkey, ('hT', kc, g)], writes=[PK(b)])
                        if which == "q":
                            copy_op(evac_engine(), cT[:, 4 + m, g * 512:(g + 1) * 512], ps[b][:, :], [PK(b)], [('cT', 4 + m, g)])
                        else:
                            copy_op(evac_engine(), skT[:, m, g * 512:(g + 1) * 512], ps[b][:, :], [PK(b)], [('hs', 4 + m // 2)])
            wap, wkey = sbw["v"]
            wm = D["w_mix_out"][l]
            mixw = [load_piece(wm[:, ns * 512:(ns + 1) * 512], 8, 512) for ns in range(2)]
            for tt in range(NT):
                b = nextbank(ALLB)
                for kc in range(8):
                    S.op('pe', lambda e, tt=tt, kc=kc, b=b, wap=wap: e.matmul(
                        ps[b][:, :], lhsT=hT[:, kc, tt * 128:(tt + 1) * 128], rhs=wap[:, kc, :], start=(kc == 0), stop=(kc == 7)),
                         reads=[wkey, ('hT', kc, tt // 4)], writes=[PK(b)])
                copy_op(evac_engine(), svv[:, tt, :], ps[b][:, :], [PK(b)], [('hs', 6 + tt // 8)])
            PSall = PS
            ZBK = [[0, 1], [2, 3]]
            CBK = [4, 5]
            ACC, SSQ = 6, 7

            def hrow(row, lo, hi, dtype=BF16):
                ap = hT[:, row, lo:hi]
                if dtype == F32:
                    ap = ap.bitcast(F32)
                return ap, [('hT', row, g) for g in range(lo // 512, (hi + 511) // 512)]
            Ebuf = [hrow(0, 0, 2048, F32), hrow(1, 0, 2048, F32), tunit(0, 4, F32)]
            Gbuf = hrow(2, 0, 2048, F32)
            spbuf = [hrow(3, 0, 1024), hrow(3, 1024, 2048)]
            Abuf = [hrow(4, 0, 1024), hrow(4, 1024, 2048)]
            Rbuf = [hrow(5, 0, 1024), hrow(5, 1024, 2048)]
            rsb = hrow(7, 512, 1536, F32)
            rawb = hrow(6, 0, 2048)
            sqb = hrow(7, 0, 512)

            def v2(ap):
                return ap.rearrange("p (c n) -> p c n", c=2)
            mstr2 = _bv(mstr[:], [[0, 2], [1, 128]])

            def sweep_group(G):
                nkb = 4 * G + 4
                items = [(hp, kb) for hp in range(4) for kb in range(nkb - 1, -1, -1)]
                n = len(items)

                def c0_of(i):
                    return max(0, items[i][1] - 4 * G) * 128

                def zmm(i):
                    hp, kb = items[i]
                    c0 = c0_of(i)
                    for ch in range(2):
                        ho = ch * 64
                        zb = ZBK[i % 2][ch]
                        S.op('pe', lambda e, ho=ho, zb=zb: e.matmul(
                            ps[zb][:, c0:512], lhsT=skT[ho:ho + 64, hp, kb * 128:(kb + 1) * 128],
                            rhs=cT[ho:ho + 64, 4 + hp, G * 512 + c0:(G + 1) * 512], start=True, stop=True),
                             reads=[('hs', 4 + hp // 2), ('cT', 4 + hp, G)], writes=[PK(zb)])

                def esp(i):
                    hp, kb = items[i]
                    c0 = c0_of(i)
                    par = i % 2
                    E, Ek = Ebuf[i % 3]; sp, spk = spbuf[par]
                    zb = ZBK[par]
                    S.op('act', lambda e: e.activation(out=v2(E)[:, :, c0:512], in_=PSall[:, zb[0]:zb[0] + 2, c0:512], func=AF.Exp, scale=0.125),
                         reads=[PK(zb[0]), PK(zb[1])], writes=Ek)
                    S.op('act', lambda e: e.activation(out=v2(sp)[:, :, c0:512], in_=v2(E)[:, :, c0:512], func=AF.Ln, bias=1.0, scale=1.0),
                         reads=Ek, writes=spk)
                    if kb >= 4 * G:
                        S.op('dve', lambda e: e.tensor_tensor(out=v2(sp)[:, :, c0:c0 + 128], in0=v2(sp)[:, :, c0:c0 + 128], in1=mstr2, op=ALU.mult),
                             reads=spk + ['const'], writes=spk)
                    first = (kb == nkb - 1)
                    R, Rk = Rbuf[par]; Rn, Rnk = Rbuf[1 - par]
                    if kb > 0:
                        if kb >= 4 * G:
                            if c0 > 0:
                                S.op('dve', lambda e: e.memset(v2(Rn)[:, :, 0:c0], 0.0), writes=Rnk)
                        if first:
                            S.op('dve', lambda e: e.tensor_copy(out=v2(Rn)[:, :, c0:512], in_=v2(sp)[:, :, c0:512]), reads=spk, writes=Rnk)
                        else:
                            S.op('dve', lambda e: e.tensor_tensor(out=v2(Rn)[:, :, c0:512], in0=v2(R)[:, :, c0:512], in1=v2(sp)[:, :, c0:512], op=ALU.add),
                                 reads=Rk + spk, writes=Rnk)

                def cmm(i):
                    hp, kb = items[i]
                    c0 = c0_of(i)
                    first = (kb == nkb - 1)
                    sp, spk = spbuf[i % 2]; R, Rk = Rbuf[i % 2]
                    for ch in range(2):
                        S.op('pe', lambda e, ch=ch: e.matmul(ps[CBK[ch]][:, c0:512], lhsT=tri[:], rhs=v2(sp)[:, ch, c0:512], start=True, stop=first),
                             reads=spk + ['const'], writes=[PK(CBK[ch])])
                        if not first:
                            S.op('pe', lambda e, ch=ch: e.matmul(ps[CBK[ch]][:, c0:512], lhsT=ones[:], rhs=v2(R)[:, ch, c0:512], start=False, stop=True),
                                 reads=Rk + ['const'], writes=[PK(CBK[ch])])

                def rest(i):
                    hp, kb = items[i]
                    c0 = c0_of(i)
                    par = i % 2
                    first = (kb == nkb - 1)
                    E, Ek = Ebuf[i % 3]; sp, spk = spbuf[par]; Gt, Gk = Gbuf; A, Akk = Abuf[par]; R, Rk = Rbuf[par]; Rn, Rnk = Rbuf[1 - par]
                    S.op('act', lambda e: e.activation(out=v2(Gt)[:, :, c0:512], in_=PSall[:, CBK[0]:CBK[0] + 2, c0:512], func=AF.Exp, scale=-1.0),
                         reads=[PK(CBK[0]), PK(CBK[1])], writes=Gk)
                    S.op('dve', lambda e: e.tensor_tensor(out=v2(A)[:, :, c0:512], in0=v2(E)[:, :, c0:512], in1=v2(Gt)[:, :, c0:512], op=ALU.mult),
                         reads=Ek + Gk, writes=Akk)
                    if kb >= 4 * G:
                        S.op('pool', lambda e: e.tensor_tensor(out=v2(A)[:, :, c0:c0 + 128], in0=v2(A)[:, :, c0:c0 + 128], in1=mstr2, op=ALU.mult),
                             reads=Akk + ['const'], writes=Akk)

                def avmm(i):
                    hp, kb = items[i]
                    c0 = c0_of(i)
                    first = (kb == nkb - 1)
                    A, Akk = Abuf[i % 2]
                    if first:
                        S.op('pe', lambda e: e.matmul(ps[ACC][:, :], lhsT=zer[:], rhs=dmk[:], start=True, stop=True),
                             reads=['const'], writes=[PK(ACC)])
                    for ch in range(2):
                        h = 2 * hp + ch
                        S.op('pe', lambda e, ch=ch, h=h: e.matmul(
                            ps[ACC][ch * 64:(ch + 1) * 64, c0:512], lhsT=svv[:, kb, h * 64:(h + 1) * 64], rhs=v2(A)[:, ch, c0:512],
                            start=False, stop=True, skip_group_check=True),
                             reads=Akk + [('hs', 6 + kb // 8)], writes=[PK(ACC)])
                    if kb == 0:
                        raw, rawk = rawb; sq, sqk = sqb
                        S.op('act', lambda e: e.copy(out=raw[:, hp * 512:(hp + 1) * 512], in_=ps[ACC][:, :]), reads=[PK(ACC)], writes=[rawk[hp]])
                        S.op('act', lambda e: e.activation(out=sq, in_=ps[ACC][:, :], func=AF.Square), reads=[PK(ACC)], writes=sqk)
                        S.op('pe', lambda e: e.matmul(ps[SSQ][:, :], lhsT=ones[:], rhs=sq, start=(hp == 0), stop=(hp == 3)),
                             reads=sqk + ['const'], writes=[PK(SSQ)])

                zmm(0)
                if n > 1:
                    zmm(1)
                esp(0)
                cmm(0)
                for i in range(n):
                    if i + 2 < n:
                        zmm(i + 2)
                    if i + 1 < n:
                        esp(i + 1)
                    rest(i)
                    if i + 1 < n:
                        cmm(i + 1)
                    avmm(i)
                rs, rsk = rsb; raw, rawk = rawb
                S.op('act', lambda e: e.activation(out=rs, in_=ps[SSQ][:, :], func=AF.Ln, scale=1.0 / 512, bias=EPS), reads=[PK(SSQ)], writes=rsk)
                S.op('act', lambda e: e.activation(out=rs, in_=rs, func=AF.Exp, scale=-0.5), reads=rsk, writes=rsk)
                for hp in range(4):
                    S.op('dve', lambda e, hp=hp: e.scalar_tensor_tensor(
                        out=cT[:, 4 + hp, G * 512:(G + 1) * 512], in0=raw[:, hp * 512:(hp + 1) * 512], scalar=pt[:, l, 36 + hp:37 + hp],
                        in1=rs, op0=ALU.mult, op1=ALU.mult), reads=[rawk[hp], 'const'] + rsk, writes=[('cT', 4 + hp, G)])
            for G in range(4):
                sweep_group(G)
            if stop == 'sb':
                return
            pcs = mixw
            gemm_resid_norm(lambda kc, tt: cT[:, kc, tt * 128:(tt + 1) * 128], lambda kc, tt: [('cT', kc, tt // 4)],
                            lambda ns, k: (pcs[ns][0][:, k, :], pcs[ns][1]), 8, (l, 8))

        def cross_block(l):
            memf = WA[:, 7, 0:4096].bitcast(F32).rearrange("p (t d) -> p t d", t=2)
            S.dma('sp', lambda e: e.dma_start(out=memf, in_=D["mem"].rearrange("(t p) d -> p t d", p=128)), writes=[('hs', 7)])
            junk, jk = tunit(0, 2)
            for t in range(2):
                S.op('act', lambda e, t=t: e.activation(out=junk, in_=memf[:, t, :], func=AF.Square, accum_out=sm[:, 64 + t:65 + t]),
                     reads=[('hs', 7)], writes=['sm4'] + jk)
            rstd_from_ssq(sm[:, 64:66], 1.0 / DM, 'sm4')
            mnT = WA[:, 4, 0:2048].rearrange("p (k n) -> p k n", k=8)
            for t in range(2):
                hn, hk = tunit(2 + 2 * t, 2)
                S.op('dve', lambda e, t=t, hn=hn: e.tensor_scalar(out=hn, in0=memf[:, t, :], scalar1=sm[:, 64 + t:65 + t], scalar2=None, op0=ALU.mult),
                     reads=[('hs', 7), 'sm4'], writes=hk)
                b = nextbank(ALLB)
                transposes(lambda c, hn=hn: hn[:, c * 128:(c + 1) * 128], 8, b, hk + ['const'])
                gb = _bv(pt[:, l, 16:24], [[1, 8], [0, 128]])
                S.op('dve', lambda e, t=t, b=b, gb=gb: e.tensor_tensor(out=mnT[:, :, t * 128:(t + 1) * 128],
                                                                       in0=psb[b][:, :].rearrange("p (k n) -> p k n", k=8), in1=gb, op=ALU.mult),
                     reads=[PK(b), 'const'], writes=[('hs', 4)])
            kTm = WA[:, 6, 0:2048].rearrange("p (k n) -> p k n", k=8)
            vm = WA[:, 6, 2048:4096].rearrange("p (t n) -> p t n", t=2)
            wkv = D["w_xkv"][l]
            def kv_epi(tt, ns, b):
                if ns < 2:
                    kn, knk = tunit(6 + tt % 2, 1)
                    junk1, jk1 = tunit(0, 1)
                    for hh in range(2):
                        col = 66 + (tt * 4 + ns * 2 + hh)
                        S.op('act', lambda e, hh=hh, col=col: e.activation(out=junk1[:, 0:256], in_=ps[b][:, hh * 256:(hh + 1) * 256], func=AF.Square,
                                                                           accum_out=sm[:, col:col + 1]), reads=[PK(b)], writes=[('sm5', tt, ns)] + jk1)
                    c0 = 66 + tt * 4 + ns * 2
                    rstd_from_ssq(sm[:, c0:c0 + 2], 1.0 / 256, ('sm5', tt, ns))
                    for hh in range(2):
                        S.op('dve', lambda e, hh=hh, kn=kn: e.tensor_scalar(out=kn[:, hh * 256:(hh + 1) * 256], in0=ps[b][:, hh * 256:(hh + 1) * 256],
                                                                            scalar1=sm[:, c0 + hh:c0 + hh + 1], scalar2=None, op0=ALU.mult),
                             reads=[PK(b), ('sm5', tt, ns)], writes=knk)
                    bt = nextbank(ALLB)
                    transposes(lambda c, kn=kn: kn[:, c * 128:(c + 1) * 128], 4, bt, knk + ['const'])
                    gb = _bv(pt[:, l, 42:44], [[0, 2], [1, 2], [0, 128]])
                    S.op('dve', lambda e, bt=bt, gb=gb: e.tensor_tensor(
                        out=kTm[:, ns * 4:(ns + 1) * 4, tt * 128:(tt + 1) * 128].rearrange("p (a c) n -> p a c n", a=2),
                        in0=psb[bt][:, 0:512].rearrange("p (a c n) -> p a c n", a=2, c=2), in1=gb, op=ALU.mult),
                         reads=[PK(bt), 'const'], writes=[('hs', 6)])
                else:
                    copy_op('act', vm[:, tt, (ns - 2) * 512:(ns - 1) * 512], ps[b][:, :], [PK(b)], [('hs', 6)])
            gemm_tm(lambda kc, tt: mnT[:, kc, tt * 128:(tt + 1) * 128], lambda kc, tt: [('hs', 4)],
                    lambda ns: [(wkv[:, ns * 512:(ns + 1) * 512], 8)], 4, kv_epi, ntt=2, defer=1)
            wq = D["w_xq"][l]
            qT = WA[:, 4:6, :].rearrange("p a (b n) -> p (a b) n", b=2)
            for hpair in range(2):
                wap, wkey = load_piece(wq[:, hpair * 512:(hpair + 1) * 512], 8, 512)
                if hpair == 1:
                    xow = [load_piece(D["w_xo"][l][:, ns * 512:(ns + 1) * 512], 8, 512) for ns in range(2)]
                qitems = [(hh, g) for hh in range(2) for g in range(4)]

                def q_stage1(i, wap=wap, wkey=wkey):
                    hh, g = qitems[i]
                    bc = [nextbank(ALLB), nextbank(ALLB)]
                    sq, sqk = tunit(4 * (i % 2), 2)
                    for c in range(2):
                        for kc in range(8):
                            S.op('pe', lambda e, c=c, kc=kc: e.matmul(
                                ps[bc[c]][:, :], lhsT=wap[:, kc, (hh * 2 + c) * 128:(hh * 2 + c + 1) * 128],
                                rhs=hT[:, kc, g * 512:(g + 1) * 512], start=(kc == 0), stop=(kc == 7)),
                                 reads=[wkey, ('hT', kc, g)], writes=[PK(bc[c])])
                        S.op('act', lambda e, c=c: e.activation(out=sq[:, c * 512:(c + 1) * 512], in_=ps[bc[c]][:, :], func=AF.Square),
                             reads=[PK(bc[c])], writes=[sqk[c]])
                    return bc, sq, sqk

                def q_stage2(i, bc, sq, sqk):
                    hh, g = qitems[i]
                    bd = nextbank(ALLB)
                    for c in range(2):
                        S.op('pe', lambda e, c=c: e.matmul(ps[bd][:, :], lhsT=ones[:], rhs=sq[:, c * 512:(c + 1) * 512],
                                                           start=(c == 0), stop=(c == 1)), reads=[sqk[c], 'const'], writes=[PK(bd)])
                    rs, rsk = tunit(4 * (i % 2) + 2, 2, F32)
                    S.op('act', lambda e: e.activation(out=rs, in_=ps[bd][:, :], func=AF.Ln, scale=1.0 / 256, bias=EPS), reads=[PK(bd)], writes=rsk)
                    S.op('act', lambda e: e.activation(out=rs, in_=rs, func=AF.Exp, scale=-0.5), reads=rsk, writes=rsk)
                    for c in range(2):
                        S.op('dve', lambda e, c=c: e.scalar_tensor_tensor(
                            out=qT[:, hh * 2 + c, g * 512:(g + 1) * 512], in0=ps[bc[c]][:, :], scalar=pt[:, l, 40 + c:41 + c],
                            in1=rs, op0=ALU.mult, op1=ALU.mult), reads=[PK(bc[c]), 'const'] + rsk, writes=[('hs', 4 + hh)])
                prevq = None
                for i in range(len(qitems)):
                    cur = q_stage1(i)
                    if prevq is not None:
                        q_stage2(*prevq)
                    prevq = (i,) + cur
                q_stage2(*prevq)
                for hh in range(2):
                    h = hpair * 2 + hh
                    def attn(g, h=h, hh=hh):
                        pT, pTk = tunit(2 + 2 * (g % 2), 2)
                        bsc = [nextbank(ALLB), nextbank(ALLB)]
                        for mc in range(2):
                            for c in range(2):
                                S.op('pe', lambda e, mc=mc, c=c: e.matmul(
                                    ps[bsc[mc]][:, :], lhsT=kTm[:, h * 2 + c, mc * 128:(mc + 1) * 128],
                                    rhs=qT[:, hh * 2 + c, g * 512:(g + 1) * 512], start=(c == 0), stop=(c == 1)),
                                     reads=[('hs', 6), ('hs', 4 + hh)], writes=[PK(bsc[mc])])
                            S.op('act', lambda e, mc=mc, pT=pT: e.activation(out=pT[:, mc * 512:(mc + 1) * 512], in_=ps[bsc[mc]][:, :], func=AF.Exp, scale=1.0 / 16),
                                 reads=[PK(bsc[mc])], writes=pTk)
                        return pT, pTk

                    def attn2(g, pT, pTk, h=h, hh=hh):
                        bd = nextbank(ALLB)
                        for mc in range(2):
                            S.op('pe', lambda e, mc=mc, pT=pT: e.matmul(ps[bd][:, :], lhsT=ones[:], rhs=pT[:, mc * 512:(mc + 1) * 512],
                                                                       start=(mc == 0), stop=(mc == 1)), reads=pTk + ['const'], writes=[PK(bd)])
                        rden, rdk = tunit(0, 2, F32)
                        S.op('dve', lambda e, rden=rden: e.reciprocal(out=rden, in_=ps[bd][:, :]), reads=[PK(bd)], writes=rdk)
                        for c in range(2):
                            bo = nextbank(ALLB)
                            for mc in range(2):
                                S.op('pe', lambda e, mc=mc, c=c, pT=pT, bo=bo: e.matmul(
                                    ps[bo][:, :], lhsT=vm[:, mc, h * 256 + c * 128:h * 256 + (c + 1) * 128],
                                    rhs=pT[:, mc * 512:(mc + 1) * 512], start=(mc == 0), stop=(mc == 1)),
                                     reads=pTk + [('hs', 6)], writes=[PK(bo)])
                            S.op('dve', lambda e, c=c, bo=bo, rden=rden: e.tensor_tensor(
                                out=cT[:, h * 2 + c, g * 512:(g + 1) * 512], in0=ps[bo][:, :], in1=rden, op=ALU.mult),
                                 reads=[PK(bo)] + rdk, writes=[('cT', h * 2 + c, g)])
                    prevg = None
                    for g in range(4):
                        cur = attn(g)
                        if prevg is not None:
                            attn2(*prevg)
                        prevg = (g,) + cur
                    attn2(*prevg)
            if stop == 'xattn':
                return
            pcs = xow
            gemm_resid_norm(lambda kc, tt: cT[:, kc, tt * 128:(tt + 1) * 128], lambda kc, tt: [('cT', kc, tt // 4)],
                            lambda ns, k: (pcs[ns][0][:, k, :], pcs[ns][1]), 8, (l, 24))

        def mlp_block(l):
            ring["slots"] = [0, 1, 2, 4, 5]
            wu = D["w_up"][l]; wd = D["w_down"][l]
            def up_load(fs):
                return load_piece(wu[:, fs * 512:(fs + 1) * 512], 8, 512)

            def down_load(fs):
                return load_piece(wd[fs * 512:(fs + 1) * 512, :], 4, 1024)

            def up(fs, wap, wkey):
                for m in range(4):
                    fc = (fs % 2) * 4 + m
                    for g in range(4):
                        b = nextbank(ALLB)
                        for kc in range(8):
                            S.op('pe', lambda e, m=m, g=g, kc=kc, b=b: e.matmul(
                                ps[b][:, :], lhsT=wap[:, kc, m * 128:(m + 1) * 128], rhs=hT[:, kc, g * 512:(g + 1) * 512],
                                start=(kc == 0), stop=(kc == 7)), reads=[wkey, ('hT', kc, g)], writes=[PK(b)])
                        r, rk = tunit((fc * 4 + g) % 8, 1)
                        S.op('act', lambda e, b=b, r=r: e.activation(out=r, in_=ps[b][:, :], func=AF.Relu), reads=[PK(b)], writes=rk)
                        S.op('dve', lambda e, r=r, fc=fc, g=g: e.tensor_tensor(out=cT[:, fc, g * 512:(g + 1) * 512], in0=r, in1=r, op=ALU.mult),
                             reads=rk, writes=[('cT', fc, g)])

            def down(fs, wap, wkey):
                hb = (fs % 2) * 4
                if fs == 7:
                    gemm_resid_norm(lambda k, tt: cT[:, hb + k, tt * 128:(tt + 1) * 128], lambda k, tt: [('cT', hb + k, tt // 4)],
                                    lambda ns, k: (wap[:, k, ns * 512:(ns + 1) * 512], wkey), 4,
                                    (l + 1, 0) if l + 1 < n_layers else None)
                    return
                pending = None
                for ns in range(2):
                    for tt in range(NT):
                        b = nextbank(ALLB)
                        for k in range(4):
                            S.op('pe', lambda e, k=k, b=b, tt=tt, ns=ns: e.matmul(
                                ps[b][:, :], lhsT=cT[:, hb + k, tt * 128:(tt + 1) * 128], rhs=wap[:, k, ns * 512:(ns + 1) * 512],
                                start=(k == 0), stop=(k == 3)), reads=[wkey, ('cT', hb + k, tt // 4)], writes=[PK(b)])
                        if pending is not None:
                            resid_add(*pending)
                        pending = (tt, ns, b)
                resid_add(*pending)
            wu_next = up_load(0)
            wd_next = down_load(0)
            up(0, *wu_next)
            if l + 1 < n_layers:
                for j in (2, 3):
                    dstn = WA[:, 4 + j, :].rearrange("p (k n) -> p k n", k=8)
                    srcn = D["w_in"][l + 1][:, j * 512:(j + 1) * 512].rearrange("(k p) n -> p k n", p=128)
                    S.dma('pool', lambda e, dstn=dstn, srcn=srcn: e.dma_start(out=dstn, in_=srcn), writes=[('hs', 4 + j)])
            for fs in range(8):
                wd_cur = wd_next
                if fs + 1 < 8:
                    wu_next = up_load(fs + 1)
                    wd_next = down_load(fs + 1)
                    up(fs + 1, *wu_next)
                down(fs, *wd_cur)
            ring["slots"] = [0, 1, 2]

        done = False
        for l in range(n_layers):
            mix_block(l)
            if stop in ('ret', 'sb') or stop == f'mix{l}':
                break
            cross_block(l)
            if stop == 'xattn' or stop == f'cross{l}':
                break
            mlp_block(l)
            if stop == f'mlp{l}':
                break

        if stop in ('ret', 'sb', 'xattn'):
            for kc in range(4 if stop == 'ret' else 8):
                for g in range(4):
                    S.op('dve', lambda e, kc=kc, g=g: e.tensor_copy(out=xs[:, kc * 2 + g // 2, (g % 2) * 512:(g % 2 + 1) * 512],
                                                                    in_=cT[:, kc, g * 512:(g + 1) * 512]),
                         reads=[('cT', kc, g)], writes=[('xs', kc * 2 + g // 2)])
        for tt in range(NT):
            S.dma('sp', lambda e, tt=tt: e.dma_start(out=out_d[tt * 128:(tt + 1) * 128, :], in_=xs[:, tt, :]),
                  reads=[('xs', tt)], writes=[('out', tt)], final=True)
        S.emit()
    return nc


_CONSTS = None


def make_in_maps(inputs, n_cores=8):
    global _CONSTS
    if _CONSTS is None:
        _CONSTS = _const_tables()
    f = lambda a: np.ascontiguousarray(np.asarray(a, dtype=np.float32))
    pt = _param_table(*[np.asarray(inputs[k], dtype=np.float32) for k in
                        ("g_mix", "g_cross", "g_mem", "g_mlp", "g_ret_out", "g_sb_out", "g_qn", "g_kn")])
    shared = {k: f(inputs[k]) for k in ("w_in", "w_mix_out", "w_xq", "w_xkv", "w_xo", "w_up", "w_down")}
    shared["pt"] = pt
    shared.update(_CONSTS)
    x = f(inputs["x"]); mem = f(inputs["mem"])
    maps = []
    for c in range(n_cores):
        m = dict(shared)
        m["x"] = x[c]
        m["mem"] = mem[c]
        maps.append(m)
    return maps


def kernel(**inputs):
    nc = build_program()
    maps = make_in_maps(inputs, 8)
    res = run_bass_kernel_spmd(nc, maps, core_ids=list(range(8)))
    return np.stack([np.asarray(r["out"], dtype=np.float32) for r in res.results], axis=0)
```
